# Optimizing a Trainium2 kernel written in Bass

```python
import math
import jax
import jax.numpy as jnp
from jax import lax
import numpy as np

D_MODEL = 2048
BATCH = 8
SEQ = 2048
DEPTH = 4

GRID_W = 64
CTX_LEN = 256
NORM_EPS = 1e-6

HG_W = D_MODEL // 4
HG_HD = 128
HG_HEADS = HG_W // HG_HD
HG_CHUNK = 16
HG_COLS = 5 * HG_W

RW_W = D_MODEL // 4
RW_HD = 64
RW_HEADS = RW_W // RW_HD
RW_DECAY_LORA = 96
RW_A_LORA = 96
RW_G_LORA = 64
RW_GN_EPS = 64e-5
RW_DECAY_MAX = math.exp(-0.5)
RW_SPLITS = (RW_W, RW_W, RW_W, RW_DECAY_LORA, RW_DECAY_LORA, RW_A_LORA, RW_A_LORA, RW_G_LORA)
RW_COLS = sum(RW_SPLITS)

MLA_W = D_MODEL - HG_W - RW_W
MLA_V = 128
MLA_HEADS = MLA_W // MLA_V
MLA_NOPE = 128
MLA_ROPE = 64
MLA_Q_RANK = 512
MLA_KV_RANK = 256
MLA_COLS = MLA_Q_RANK + MLA_KV_RANK + MLA_ROPE
MLA_SCALE = (MLA_NOPE + MLA_ROPE) ** -0.5
ROPE_THETA = 10000.0
Q_BLOCK = 128

IN_COLS = HG_COLS + RW_COLS + MLA_COLS
D_FF = 5632

kernel_name = 'hybrid_hgrn2_rwkv7_mla_dit_block'


def _split_sizes(a, sizes):
    return jnp.split(a, [int(s) for s in np.cumsum(sizes)[:-1]], axis=-1)


def _heads(a, n_heads):
    return a.reshape(a.shape[:-1] + (n_heads, a.shape[-1] // n_heads))


def _merge_heads(a):
    return a.reshape(a.shape[:-2] + (a.shape[-2] * a.shape[-1],))


def _rmsnorm(x, gain):
    xf = x.astype(jnp.float32)
    y = xf * lax.rsqrt(jnp.mean(xf * xf, axis=-1, keepdims=True) + NORM_EPS)
    return (y * gain.astype(jnp.float32)).astype(x.dtype)


def _modulate(h, shift, scale):
    return h * (1.0 + scale) + shift


def _neighbours(u):
    up = jnp.pad(u, ((0, 0), (1, 1), (0, 0)))
    return up[:, :-2], up[:, 2:]


def _token_shift(p, mu):
    prev, nxt = _neighbours(p)
    return p + mu * (0.5 * (prev + nxt) - p)


def _dwconv3(u, w, b):
    prev, nxt = _neighbours(u)
    return prev * w[0] + u * w[1] + nxt * w[2] + b


def _axial_rope_tables(n_tokens):
    rows = n_tokens // GRID_W
    row = jnp.repeat(jnp.arange(rows, dtype=jnp.float32), GRID_W)
    col = jnp.tile(jnp.arange(GRID_W, dtype=jnp.float32), rows)
    axis_dim = MLA_ROPE // 2
    inv_freq = ROPE_THETA ** (-jnp.arange(0, axis_dim, 2, dtype=jnp.float32) / axis_dim)
    ang = jnp.concatenate([row[:, None] * inv_freq, col[:, None] * inv_freq], axis=-1)
    return jnp.cos(ang), jnp.sin(ang)


def _rope(x, cos, sin):
    half = x.shape[-1] // 2
    x1 = x[..., :half].astype(jnp.float32)
    x2 = x[..., half:].astype(jnp.float32)
    return jnp.concatenate([x1 * cos - x2 * sin, x2 * cos + x1 * sin], axis=-1).astype(x.dtype)


def _bidir(scan_fn, ctx_f, lat_f, ctx_b, lat_b):
    n_ctx = ctx_f[0].shape[1]
    fwd = scan_fn(*[jnp.concatenate([c, l], axis=1) for c, l in zip(ctx_f, lat_f)])
    bwd = scan_fn(*[jnp.concatenate([jnp.flip(c, 1), jnp.flip(l, 1)], axis=1) for c, l in zip(ctx_b, lat_b)])
    y_ctx = fwd[:, :n_ctx] + jnp.flip(bwd[:, :n_ctx], 1)
    y_lat = fwd[:, n_ctx:] + jnp.flip(bwd[:, n_ctx:], 1)
    return y_ctx, y_lat


def _hgrn2_chunk_scan(q, k, log_f, v):
    b, t, h, dk = q.shape
    n = t // HG_CHUNK

    def chunks(a):
        return a.reshape(b, n, HG_CHUNK, h, a.shape[-1]).transpose(1, 0, 3, 2, 4)

    q, k, log_f, v = chunks(q), chunks(k), chunks(log_f), chunks(v)
    g = jnp.cumsum(log_f, axis=3)
    causal = jnp.tril(jnp.ones((HG_CHUNK, HG_CHUNK), dtype=bool))[:, :, None]
    decay = jnp.exp(jnp.where(causal, g[..., :, None, :] - g[..., None, :, :], -jnp.inf))
    attn = jnp.einsum('nbhtd,nbhsd,nbhtsd->nbhts', q, k, decay)
    o_intra = jnp.einsum('nbhts,nbhse->nbhte', attn, v)
    g_last = g[..., -1:, :]
    q_dec = q * jnp.exp(g)
    k_dec = k * jnp.exp(g_last - g)
    f_chunk = jnp.exp(g_last[..., 0, :])

    def step(state, xs):
        qd, kd, fc, vc = xs
        o = jnp.einsum('bhtd,bhde->bhte', qd, state)
        state = fc[..., None] * state + jnp.einsum('bhsd,bhse->bhde', kd, vc)
        return state, o

    s0 = jnp.zeros((b, h, dk, v.shape[-1]), q.dtype)
    _, o_inter = lax.scan(step, s0, (q_dec, k_dec, f_chunk, v))
    return (o_intra + o_inter).transpose(1, 0, 3, 2, 4).reshape(b, t, h, -1)


def _hgrn2_mixer(p_ctx, p_lat, lb, gn_w, need_ctx):
    def prep(p):
        q, z_f, z_b, i, g = jnp.split(p.astype(jnp.float32), 5, axis=-1)
        q, i = _heads(q, HG_HEADS), _heads(i, HG_HEADS)
        dirs = []
        for d, z in enumerate((z_f, z_b)):
            f = lb[d] + (1.0 - lb[d]) * jax.nn.sigmoid(z)
            dirs.append((q, _heads(1.0 - f, HG_HEADS), _heads(jnp.log(f), HG_HEADS), i))
        return dirs, _heads(g, HG_HEADS)

    ctx_dirs, g_ctx = prep(p_ctx)
    lat_dirs, g_lat = prep(p_lat)
    o_ctx, o_lat = _bidir(_hgrn2_chunk_scan, ctx_dirs[0], lat_dirs[0], ctx_dirs[1], lat_dirs[1])

    def out(o, g):
        return _merge_heads(_rmsnorm(o, gn_w) * jax.nn.silu(g)).astype(p_lat.dtype)

    return (out(o_ctx, g_ctx) if need_ctx else None), out(o_lat, g_lat)


def _rwkv7_scan(r, w, k, v, a, b):
    def step(state, xs):
        rt, wt, kt, vt, at, bt = xs
        sa = jnp.einsum('bhvk,bhk->bhv', state, at)
        state = state * wt[:, :, None, :] + sa[..., None] * bt[:, :, None, :] + vt[..., None] * kt[:, :, None, :]
        return state, jnp.einsum('bhvk,bhk->bhv', state, rt)

    bsz, _, h, n = r.shape
    xs = tuple(jnp.moveaxis(u, 1, 0) for u in (r, w, k, v, a, b))
    _, o = lax.scan(step, jnp.zeros((bsz, h, n, n), r.dtype), xs)
    return jnp.moveaxis(o, 0, 1)


def _rwkv7_mixer(p_ctx, p_lat, mu, w0, w2, a0, a2, g2, k_k, k_a, r_k, gn_w, gn_b, need_ctx):
    def prep(p):
        s = _token_shift(p.astype(jnp.float32), mu)
        r, k, v, xw_f, xw_b, xa_f, xa_b, xg = _split_sizes(s, RW_SPLITS)
        kk = _heads(k * k_k, RW_HEADS)
        kk = kk / jnp.maximum(jnp.sqrt(jnp.sum(kk * kk, axis=-1, keepdims=True)), 1e-12)
        r, v = _heads(r, RW_HEADS), _heads(v, RW_HEADS)
        dirs = []
        for d, (xw, xa) in enumerate(((xw_f, xa_f), (xw_b, xa_b))):
            decay = jnp.exp(-RW_DECAY_MAX * jax.nn.sigmoid(w0[d] + jnp.tanh(xw) @ w2[d]))
            a = jax.nn.sigmoid(a0[d] + xa @ a2[d])
            k_d = _heads(k * (1.0 + (a - 1.0) * k_a), RW_HEADS)
            dirs.append((r, _heads(decay, RW_HEADS), k_d, v, -kk, kk * _heads(a, RW_HEADS)))
        g = jax.nn.sigmoid(xg) @ g2
        return dirs, g

    ctx_dirs, g_ctx = prep(p_ctx)
    lat_dirs, g_lat = prep(p_lat)
    o_ctx, o_lat = _bidir(_rwkv7_scan, ctx_dirs[0], lat_dirs[0], ctx_dirs[1], lat_dirs[1])

    def out(o, dirs, g):
        mean = jnp.mean(o, axis=-1, keepdims=True)
        var = jnp.mean(jnp.square(o - mean), axis=-1, keepdims=True)
        o = _merge_heads((o - mean) * lax.rsqrt(var + RW_GN_EPS)) * gn_w + gn_b
        r, k_f, v, k_b = dirs[0][0], dirs[0][2], dirs[0][3], dirs[1][2]
        bonus = _merge_heads(jnp.sum(r * (k_f + k_b) * r_k, axis=-1, keepdims=True) * v)
        return ((o + bonus) * g).astype(p_lat.dtype)

    return (out(o_ctx, ctx_dirs, g_ctx) if need_ctx else None), out(o_lat, lat_dirs, g_lat)


def _mla_q(p, q_norm, w_uq, qn_g, qr_g):
    q = _heads(_rmsnorm(p[..., :MLA_Q_RANK], q_norm) @ w_uq, MLA_HEADS)
    return _rmsnorm(q[..., :MLA_NOPE], qn_g), _rmsnorm(q[..., MLA_NOPE:], qr_g)


def _mla_kv(p, kv_norm, w_ukv, kn_g, kr_g):
    c_kv = p[..., MLA_Q_RANK:MLA_Q_RANK + MLA_KV_RANK]
    k_rope = p[..., MLA_Q_RANK + MLA_KV_RANK:]
    kv = _heads(_rmsnorm(c_kv, kv_norm) @ w_ukv, MLA_HEADS)
    return _rmsnorm(kv[..., :MLA_NOPE], kn_g), _rmsnorm(k_rope, kr_g), kv[..., MLA_NOPE:]


def _mla_attend(qn, qr, kn, kr, v):
    s = jnp.einsum('bqhd,bkhd->bhqk', qn, kn) + jnp.einsum('bqhd,bkd->bhqk', qr, kr)
    pr = jax.nn.softmax(s.astype(jnp.float32) * MLA_SCALE, axis=-1).astype(v.dtype)
    return jnp.einsum('bhqk,bkhd->bqhd', pr, v)


def _mla_mixer(p_ctx, p_lat, cos, sin, q_norm, w_uq, kv_norm, w_ukv, qn_g, qr_g, kn_g, kr_g, need_ctx):
    kn_c, kr_c, v_c = _mla_kv(p_ctx, kv_norm, w_ukv, kn_g, kr_g)
    kn_l, kr_l, v_l = _mla_kv(p_lat, kv_norm, w_ukv, kn_g, kr_g)
    kr_l = _rope(kr_l, cos, sin)
    kn = jnp.concatenate([kn_c, kn_l], axis=1)
    kr = jnp.concatenate([kr_c, kr_l], axis=1)
    v = jnp.concatenate([v_c, v_l], axis=1)
    qn_l, qr_l = _mla_q(p_lat, q_norm, w_uq, qn_g, qr_g)
    qr_l = _rope(qr_l, cos[:, None, :], sin[:, None, :])
    bsz, n_lat = p_lat.shape[:2]
    n_blk = n_lat // Q_BLOCK

    def blocks(a):
        return jnp.moveaxis(a.reshape((bsz, n_blk, Q_BLOCK) + a.shape[2:]), 1, 0)

    o_lat = lax.map(lambda qb: _mla_attend(qb[0], qb[1], kn, kr, v), (blocks(qn_l), blocks(qr_l)))
    o_lat = _merge_heads(jnp.moveaxis(o_lat, 0, 1).reshape(bsz, n_lat, MLA_HEADS, MLA_V))
    o_ctx = None
    if need_ctx:
        qn_c, qr_c = _mla_q(p_ctx, q_norm, w_uq, qn_g, qr_g)
        o_ctx = _merge_heads(_mla_attend(qn_c, qr_c, kn_c, kr_c, v_c))
    return o_ctx, o_lat


def _conv_ffn(h, w_up, dw, db, w_down):
    u = _dwconv3(h @ w_up, dw, db)
    a, b = jnp.split(u, 2, axis=-1)
    return (jax.nn.silu(a) * b) @ w_down


def setup_inputs(seed: int = 0) -> dict:
    key = jax.random.key(seed)
    keys = iter(jax.random.split(key, 40))

    def nrm(shape, scale):
        return jax.random.normal(next(keys), shape, jnp.float32) * scale

    def unif(shape, lo, hi):
        return jax.random.uniform(next(keys), shape, jnp.float32, lo, hi)

    L, D = DEPTH, D_MODEL
    return {
        'x': nrm((BATCH, SEQ, D), 1.0),
        'c': nrm((BATCH, D), 1.0),
        'ctx': nrm((BATCH, CTX_LEN, D), 1.0),
        'c_ctx': nrm((D,), 1.0),
        'norm_g': 1.0 + nrm((L, 2, D), 0.05),
        'w_mod': nrm((L, D, 6 * D), 0.5 * D ** -0.5),
        'b_mod': nrm((L, 6 * D), 0.05),
        'w_in': nrm((L, D, IN_COLS), D ** -0.5),
        'w_out': nrm((L, HG_W + RW_W + MLA_W, D), D ** -0.5),
        'hg_lb': nrm((L, 2, HG_W), 1.0),
        'hg_gn': 1.0 + nrm((L, HG_HD), 0.05),
        'rw_mu': unif((L, RW_COLS), 0.0, 1.0),
        'rw_w0': unif((L, 2, RW_W), -6.0, 1.0),
        'rw_w2': nrm((L, 2, RW_DECAY_LORA, RW_W), 0.5 * RW_DECAY_LORA ** -0.5),
        'rw_a0': nrm((L, 2, RW_W), 0.5),
        'rw_a2': nrm((L, 2, RW_A_LORA, RW_W), 0.5 * RW_A_LORA ** -0.5),
        'rw_g2': nrm((L, RW_G_LORA, RW_W), RW_G_LORA ** -0.5),
        'rw_kk': 0.85 + nrm((L, RW_W), 0.05),
        'rw_ka': 1.0 + nrm((L, RW_W), 0.05),
        'rw_rk': nrm((L, RW_HEADS, RW_HD), 0.1),
        'rw_gn_w': 1.0 + nrm((L, RW_W), 0.05),
        'rw_gn_b': nrm((L, RW_W), 0.02),
        'mla_q_norm': 1.0 + nrm((L, MLA_Q_RANK), 0.05),
        'mla_w_uq': nrm((L, MLA_Q_RANK, MLA_HEADS * (MLA_NOPE + MLA_ROPE)), MLA_Q_RANK ** -0.5),
        'mla_kv_norm': 1.0 + nrm((L, MLA_KV_RANK), 0.05),
        'mla_w_ukv': nrm((L, MLA_KV_RANK, MLA_HEADS * (MLA_NOPE + MLA_V)), MLA_KV_RANK ** -0.5),
        'mla_qn_g': 1.0 + nrm((L, MLA_NOPE), 0.05),
        'mla_qr_g': 1.0 + nrm((L, MLA_ROPE), 0.05),
        'mla_kn_g': 1.0 + nrm((L, MLA_NOPE), 0.05),
        'mla_kr_g': 1.0 + nrm((L, MLA_ROPE), 0.05),
        'ffn_up': nrm((L, D, 2 * D_FF), D ** -0.5),
        'ffn_dw': nrm((L, 3, 2 * D_FF), 0.2) + jnp.array([0.0, 1.0, 0.0], jnp.float32)[None, :, None],
        'ffn_db': nrm((L, 2 * D_FF), 0.02),
        'ffn_down': nrm((L, D_FF, D), D_FF ** -0.5),
    }


def reference(x, c, ctx, c_ctx, norm_g, w_mod, b_mod, w_in, w_out, hg_lb, hg_gn, rw_mu, rw_w0, rw_w2,
              rw_a0, rw_a2, rw_g2, rw_kk, rw_ka, rw_rk, rw_gn_w, rw_gn_b, mla_q_norm, mla_w_uq,
              mla_kv_norm, mla_w_ukv, mla_qn_g, mla_qr_g, mla_kn_g, mla_kr_g, ffn_up, ffn_dw, ffn_db,
              ffn_down):
    cos, sin = _axial_rope_tables(x.shape[1])
    lb_p = jax.nn.softmax(hg_lb.astype(jnp.float32), axis=0)
    lower_bounds = jnp.cumsum(lb_p, axis=0) - lb_p[0]
    silu_c = jax.nn.silu(c)[:, None, :]
    silu_cc = jax.nn.silu(c_ctx)[None, None, :]
    xc = ctx
    for l in range(DEPTH):
        need_ctx = l < DEPTH - 1
        mod_l = jnp.split(silu_c @ w_mod[l] + b_mod[l], 6, axis=-1)
        mod_c = jnp.split(silu_cc @ w_mod[l] + b_mod[l], 6, axis=-1)
        p_lat = _modulate(_rmsnorm(x, norm_g[l, 0]), mod_l[0], mod_l[1]) @ w_in[l]
        p_ctx = _modulate(_rmsnorm(xc, norm_g[l, 0]), mod_c[0], mod_c[1]) @ w_in[l]
        hg_c, rw_c, ml_c = _split_sizes(p_ctx, (HG_COLS, RW_COLS, MLA_COLS))
        hg_l, rw_l, ml_l = _split_sizes(p_lat, (HG_COLS, RW_COLS, MLA_COLS))
        hg_oc, hg_ol = _hgrn2_mixer(hg_c, hg_l, lower_bounds[l], hg_gn[l], need_ctx)
        rw_oc, rw_ol = _rwkv7_mixer(rw_c, rw_l, rw_mu[l], rw_w0[l], rw_w2[l], rw_a0[l], rw_a2[l], rw_g2[l],
                                    rw_kk[l], rw_ka[l], rw_rk[l], rw_gn_w[l], rw_gn_b[l], need_ctx)
        ml_oc, ml_ol = _mla_mixer(ml_c, ml_l, cos, sin, mla_q_norm[l], mla_w_uq[l], mla_kv_norm[l],
                                  mla_w_ukv[l], mla_qn_g[l], mla_qr_g[l], mla_kn_g[l], mla_kr_g[l], need_ctx)
        x = x + mod_l[2] * (jnp.concatenate([hg_ol, rw_ol, ml_ol], axis=-1) @ w_out[l])
        h2 = _modulate(_rmsnorm(x, norm_g[l, 1]), mod_l[3], mod_l[4])
        x = x + mod_l[5] * _conv_ffn(h2, ffn_up[l], ffn_dw[l], ffn_db[l], ffn_down[l])
        if need_ctx:
            xc = xc + mod_c[2] * (jnp.concatenate([hg_oc, rw_oc, ml_oc], axis=-1) @ w_out[l])
            h2c = _modulate(_rmsnorm(xc, norm_g[l, 1]), mod_c[3], mod_c[4])
            xc = xc + mod_c[5] * _conv_ffn(h2c, ffn_up[l], ffn_dw[l], ffn_db[l], ffn_down[l])
    return x
```

```python
import math
import numpy as np
import concourse.bass as bass
import concourse.mybir as mybir
from concourse.bass_utils import run_bass_kernel_spmd

F32 = mybir.dt.float32
BF16 = mybir.dt.bfloat16
ALU = mybir.AluOpType
AF = mybir.ActivationFunctionType
ENGS = ["pe", "act", "dve", "pool", "sp"]
NDMASEM = 16

L = 4
D = 2048
KC = 16
NCTX = 256
NLAT = 2048
T = NCTX + NLAT
TILES = [(0, 256), (256, 512), (768, 512), (1280, 512), (1792, 512)]
SEGS = [(0, 256), (256, 2304)]
DFF = 5632
IN_COLS = 5376
EPS = 1e-6
RW0 = 2560
ML0 = 4544
DECAY_MAX = math.exp(-0.5)


class Buf:
    __slots__ = ("name", "h", "writer", "readers", "excl")

    def __init__(self, name, h):
        self.name = name
        self.h = h
        self.writer = None
        self.readers = {}
        self.excl = False

    def __getitem__(self, idx):
        return self.h[idx]


class Pool:
    def __init__(self, bufs):
        self.bufs = bufs
        self.i = 0
        self.held = set()

    def next(self):
        while True:
            b = self.bufs[self.i % len(self.bufs)]
            self.i += 1
            if b.name not in self.held:
                return b

    def hold(self):
        b = self.next()
        self.held.add(b.name)
        return b

    def release(self, b):
        self.held.discard(b.name)


class Prog:
    def __init__(self):
        self.nc = bass.Bass("TRN2", target_bir_lowering=False)
        self.ops = {e: [] for e in ENGS}
        self.qcount = {}
        self.seen = {e: {} for e in ENGS}
        self.sems = {}
        self._stack = []
        self.floor = {}
        self.banks = None
        self.dma_idx = {}

    def sb(self, name, shape, dt=F32):
        g = self.nc.sbuf_tensor(name, list(shape), dt)
        h = g.__enter__()
        self._stack.append(g)
        return Buf(name, h)

    def pool(self, name, shape, dt, n):
        return Pool([self.sb("%s%d" % (name, i), shape, dt) for i in range(n)])

    def dram(self, name, shape, dt=F32, kind="Internal"):
        h = self.nc.dram_tensor(name, list(shape), dt, kind=kind)
        return Buf(name, h.ap())

    def view(self, buf, name=None):
        return Buf(name or buf.name, buf.h)

    def init_psum(self):
        bl = []
        for i in range(8):
            g = self.nc.psum_tensor("psb%d" % i, [128, 512], F32)
            h = g.__enter__()
            self._stack.append(g)
            b_ = Buf("psb%d" % i, h)
            b_.excl = True
            bl.append(b_)
        self.banks = Pool(bl)

    def psb(self):
        return self.banks.next()

    def hold(self):
        return self.banks.hold()

    def release(self, b):
        self.banks.release(b)

    def barrier(self):
        self.floor = dict(self.qcount)

    def emit(self, eng, fn, reads=(), writes=(), dma=False):
        if dma:
            k = self.dma_idx.get(eng, 0)
            self.dma_idx[eng] = k + 1
            q = "dma_%s_%d" % (eng, k % NDMASEM)
        else:
            q = eng
        excl = [b for b in reads if b.excl]
        if excl:
            reads = [b for b in reads if not b.excl]
            writes = list(writes) + excl
        deps = dict(self.floor)
        if dma and self.qcount.get(q, 0) > 0:
            deps[q] = self.qcount[q]

        def need(w):
            if w is None:
                return
            qq, c = w
            if qq == q and q == "pe":
                return
            if deps.get(qq, 0) < c:
                deps[qq] = c

        for b in reads:
            need(b.writer)
        for b in writes:
            need(b.writer)
            for qq, c in b.readers.items():
                need((qq, c))
        waits = []
        seen = self.seen[eng]
        for qq, c in deps.items():
            if qq == "pe" and q == "pe":
                continue
            if seen.get(qq, 0) < c:
                seen[qq] = c
                waits.append((qq, c))
        inc = 16 if dma else 1
        cnt = self.qcount.get(q, 0) + inc
        self.qcount[q] = cnt
        self.ops[eng].append((waits, fn, q, inc))
        for b in reads:
            b.readers[q] = cnt
        for b in writes:
            b.writer = (q, cnt)
            b.readers = {}
        return cnt

    def mm(self, out, lhsT, rhs, start, stop, R, W):
        self.emit("pe", lambda e: e.matmul(out, lhsT, rhs, start=start, stop=stop), R, W)

    def tr(self, out, in_, ident, R, W):
        self.emit("pe", lambda e: e.transpose(out, in_, ident), R, W)

    def act(self, out, in_, func, R, W, scale=1.0, bias=0.0):
        self.emit("act", lambda e: e.activation(out=out, in_=in_, func=func, scale=scale, bias=bias), R, W)

    def tt(self, out, in0, in1, op, R, W, eng="dve"):
        self.emit(eng, lambda e: e.tensor_tensor(out=out, in0=in0, in1=in1, op=op), R, W)

    def ts(self, out, in0, s1, s2, op0, op1, R, W, eng="dve"):
        self.emit(eng, lambda e: e.tensor_scalar(out=out, in0=in0, scalar1=s1, scalar2=s2, op0=op0, op1=op1), R, W)

    def stt(self, out, in0, scalar, in1, op0, op1, R, W):
        self.emit("dve", lambda e: e.scalar_tensor_tensor(out=out, in0=in0, scalar=scalar, in1=in1, op0=op0, op1=op1), R, W)

    def cp(self, out, in_, R, W, eng="dve"):
        if eng == "act":
            self.emit("act", lambda e: e.activation(out=out, in_=in_, func=AF.Copy), R, W)
        else:
            self.emit(eng, lambda e: e.tensor_copy(out=out, in_=in_), R, W)

    def recip(self, out, in_, R, W):
        self.emit("dve", lambda e: e.reciprocal(out=out, in_=in_), R, W)

    def scan(self, out, d0, d1, R, W):
        self.emit("dve", lambda e: e.tensor_tensor_scan(out=out, data0=d0, data1=d1, initial=0.0,
                                                       op0=ALU.mult, op1=ALU.add), R, W)

    def memset(self, ap, val, W, eng="dve"):
        self.emit(eng, lambda e: e.memset(ap, val), (), W)

    def dma(self, eng, out, in_, R, W):
        self.emit(eng, lambda e: e.dma_start(out=out, in_=in_), R, W, dma=True)

    def build(self):
        nc = self.nc
        qs = sorted(self.qcount.keys())
        guards = []
        for q in qs:
            g = nc.semaphore("s_" + q)
            self.sems[q] = g.__enter__()
            guards.append(g)
        ops, sems, qcount = self.ops, self.sems, self.qcount

        def run(engname):
            def body(e):
                for waits, fn, q, inc in ops[engname]:
                    for qq, c in waits:
                        e.wait_ge(sems[qq], c)
                    fn(e).then_inc(sems[q], inc)
                if engname == "sp":
                    for q in qs:
                        e.wait_ge(sems[q], qcount[q])
            return body

        with nc.Block() as block:
            block.sync(run("sp"))
            block.tensor(run("pe"))
            block.scalar(run("act"))
            block.vector(run("dve"))
            block.gpsimd(run("pool"))
        for g in guards:
            g.__exit__(None, None, None)
        while self._stack:
            self._stack.pop().__exit__(None, None, None)
        return nc


V128 = {}
_o = 0
for _n, _w in [("ng1", 16), ("ng2", 16), ("bm", 96), ("hgn", 1), ("mu", 12), ("w0", 8), ("a0", 8), ("kk", 4),
               ("ka", 4), ("rk", 4), ("gnw", 4), ("gnb", 4), ("qnorm", 4), ("kvnorm", 2), ("qn_g", 1),
               ("kn_g", 1), ("dw", 264), ("db", 88), ("mul", 4), ("mug", 1), ("qr_g", 1), ("kr_g", 1)]:
    V128[_n] = (_o, _w)
    _o += _w
NV = _o

C128 = {}
_o = 0
for _n, _w in [("ident", 128), ("ones", 128), ("blk", 128), ("tri_i", 128), ("tri_s", 128), ("tri_sT", 128),
               ("m16", 16), ("rs16", 512), ("rs128", 512), ("rotT", 64)]:
    C128[_n] = (_o, _w)
    _o += _w
NCST = _o


def fm(v, nch):
    return np.ascontiguousarray(np.asarray(v, np.float32).reshape(nch, 128).T)


def host_consts():
    c = np.zeros((128, NCST), np.float32)

    def put(n, a):
        o, w = C128[n]
        c[:a.shape[0], o:o + a.shape[1]] = a

    put("ident", np.eye(128, dtype=np.float32))
    put("ones", np.ones((128, 128), np.float32))
    blk = np.zeros((128, 128), np.float32)
    blk[:64, :64] = 1
    blk[64:, 64:] = 1
    put("blk", blk)
    put("tri_i", np.triu(np.ones((128, 128), np.float32)))
    put("tri_s", np.triu(np.ones((128, 128), np.float32), 1))
    put("tri_sT", np.tril(np.ones((128, 128), np.float32), -1))
    put("m16", np.triu(np.ones((16, 16), np.float32)))
    r16 = np.ones((128, 512), np.float32)
    r16[:, ::16] = 0
    put("rs16", r16)
    r128 = np.ones((128, 512), np.float32)
    r128[:, ::128] = 0
    put("rs128", r128)
    rot = np.zeros((64, 64), np.float32)
    rot[:32, 32:] = -np.eye(32)
    rot[32:, :32] = np.eye(32)
    put("rotT", np.ascontiguousarray(rot.T))
    rows = NLAT // 64
    row = np.repeat(np.arange(rows, dtype=np.float32), 64)
    col = np.tile(np.arange(64, dtype=np.float32), rows)
    inv = (10000.0 ** (-np.arange(0, 32, 2, dtype=np.float32) / 32)).astype(np.float32)
    ang = np.concatenate([row[:, None] * inv, col[:, None] * inv], -1).astype(np.float32)
    cos, sin = np.cos(ang).astype(np.float32), np.sin(ang).astype(np.float32)
    cs = np.zeros((2, 64, NLAT), np.float32)
    cs[0] = np.concatenate([cos.T, cos.T], 0)
    cs[1] = np.concatenate([sin.T, sin.T], 0)
    return c, cs


def host_vecs(inp):
    v = np.zeros((L, 128, NV), np.float32)

    def put(l, n, a):
        o, w = V128[n]
        a = np.asarray(a, np.float32)
        v[l, :a.shape[0], o:o + a.shape[1]] = a

    for l in range(L):
        put(l, "ng1", fm(inp["norm_g"][l, 0], 16))
        put(l, "ng2", fm(inp["norm_g"][l, 1], 16))
        put(l, "bm", fm(inp["b_mod"][l], 96))
        put(l, "hgn", inp["hg_gn"][l][:, None])
        put(l, "mu", fm(inp["rw_mu"][l, :1536], 12))
        put(l, "w0", np.concatenate([fm(inp["rw_w0"][l, 0], 4), fm(inp["rw_w0"][l, 1], 4)], 1))
        put(l, "a0", np.concatenate([fm(inp["rw_a0"][l, 0], 4), fm(inp["rw_a0"][l, 1], 4)], 1))
        put(l, "kk", fm(inp["rw_kk"][l], 4))
        put(l, "ka", fm(inp["rw_ka"][l], 4))
        put(l, "rk", fm(inp["rw_rk"][l].reshape(-1), 4))
        put(l, "gnw", fm(inp["rw_gn_w"][l], 4))
        put(l, "gnb", fm(inp["rw_gn_b"][l], 4))
        put(l, "qnorm", fm(inp["mla_q_norm"][l], 4))
        put(l, "kvnorm", fm(inp["mla_kv_norm"][l], 2))
        put(l, "qn_g", inp["mla_qn_g"][l][:, None])
        put(l, "kn_g", inp["mla_kn_g"][l][:, None])
        dw = inp["ffn_dw"][l]
        put(l, "dw", np.concatenate([fm(dw[0], 88), fm(dw[1], 88), fm(dw[2], 88)], 1))
        put(l, "db", fm(inp["ffn_db"][l], 88))
        mu = inp["rw_mu"][l]
        put(l, "mul", np.stack([mu[1536 + 96 * i:1536 + 96 * (i + 1)] for i in range(4)], 1))
        put(l, "mug", mu[1920:1984][:, None])
        put(l, "qr_g", inp["mla_qr_g"][l][:, None])
        put(l, "kr_g", inp["mla_kr_g"][l][:, None])
    lb = np.zeros((128, 4, 2, L), np.float32)
    for l in range(L):
        for d in range(2):
            lb[:, :, d, l] = fm(inp["hg_lb"][l, d], 4)
    return v, lb.reshape(128, 4 * 2 * L)


class ArenaMem:
    def __init__(self, P, name, nwords):
        self.buf = P.sb(name, [128, nwords], F32)
        self.n = nwords
        self.off = 0


class Arena:
    def __init__(self, mem, dt):
        self.mem = mem
        self.dt = dt

    def reset(self):
        self.mem.off = 0

    def alloc(self, name, shape):
        size = int(np.prod(shape))
        words = size if self.dt == F32 else (size + 1) // 2
        words = (words + 3) // 4 * 4
        m = self.mem
        assert m.off + words <= m.n, (name, m.off, words, m.n)
        ap = m.buf.h[:, m.off:m.off + words]
        m.off += words
        if self.dt != F32:
            ap = ap.bitcast(self.dt)
        ap = ap[:, 0:size]
        if len(shape) == 2:
            ap = ap.rearrange("p (a b) -> p a b", b=shape[1])
        elif len(shape) == 3:
            ap = ap.rearrange("p (a b c) -> p a b c", b=shape[1], c=shape[2])
        return Buf(name, ap)


def scan_tiles(d):
    if d == 0:
        return [(t0, n, False) for (t0, n) in TILES]
    return [(0, 256, True)] + [(t0, 512, True) for t0 in (1792, 1280, 768, 256)]


def tsl(ap2d, t0, n, rev):
    a = ap2d[:, t0:t0 + n]
    return a[:, ::-1] if rev else a


def build_program(nl=L, dbg=()):
    P = Prog()
    P.init_psum()
    EI = "ExternalInput"
    xT_in = P.dram("xT", [D, T], F32, EI)
    cT_in = P.dram("cT", [128, 32], F32, EI)
    vec_in = P.dram("vecs", [L, 128, NV], F32, EI)
    lb_in = P.dram("hglb", [128, 4 * 2 * L], F32, EI)
    cst_in = P.dram("cst", [128, NCST], F32, EI)
    cs_in = P.dram("rope", [2, 64, NLAT], F32, EI)
    w_mod = P.dram("w_mod", [L, D, 6 * D], F32, EI)
    w_in = P.dram("w_in", [L, D, IN_COLS], F32, EI)
    w_out = P.dram("w_out", [L, D, D], F32, EI)
    rw_w2 = P.dram("rw_w2", [L, 2, 96, 512], F32, EI)
    rw_a2 = P.dram("rw_a2", [L, 2, 96, 512], F32, EI)
    rw_g2 = P.dram("rw_g2", [L, 64, 512], F32, EI)
    w_uq = P.dram("mla_w_uq", [L, 512, 1536], F32, EI)
    w_ukv = P.dram("mla_w_ukv", [L, 256, 2048], F32, EI)
    ffn_up = P.dram("ffn_up", [L, D, 2 * DFF], F32, EI)
    ffn_down = P.dram("ffn_down", [L, DFF, D], F32, EI)
    out = P.dram("out", [D, NLAT], F32, "ExternalOutput")
    dbg_out = {}
    for n, shp in dbg:
        dbg_out[n] = P.dram("dbg_" + n, shp, F32, "ExternalOutput")

    xs = P.dram("xs", [D, T], F32)
    yT = P.dram("yT", [D, T], BF16)
    zT = P.dram("zT", [DFF, T], BF16)
    rwaux = P.dram("rwaux", [5, 512, T], F32)
    xs_v = [[P.view(xs, "xs_%d_%d" % (m, ti)) for ti in range(5)] for m in range(KC)]
    xin_v = [[P.view(xT_in, "xin_%d_%d" % (m, ti)) for ti in range(5)] for m in range(KC)]
    yT_v = [[P.view(yT, "yT_%d_%d" % (m, ti)) for ti in range(5)] for m in range(KC)]
    zT_v = [P.view(zT, "zT_%d" % j) for j in range(44)]
    aux_v = [[P.view(rwaux, "aux_%d_%d" % (a, m)) for m in range(4)] for a in range(5)]
    out_v = [[P.view(out, "out_%d_%d" % (m, ti)) for ti in range(5)] for m in range(KC)]

    cst = P.sb("cst_s", [128, NCST], F32)
    cstb = P.sb("cst_b", [128, NCST], BF16)
    P.dma("sp", cst[:], cst_in[:], [cst_in], [cst])
    P.cp(cstb[:], cst[:], [cst], [cstb])

    def C(n, rows=128, bf=False, c0=0, c1=None):
        o, w = C128[n]
        c1 = w if c1 is None else c1
        return (cstb if bf else cst)[0:rows, o + c0:o + c1]

    hT = P.sb("hT", [128, KC, T], BF16)
    hT_v = [P.view(hT, "hT_%d" % ti) for ti in range(5)]
    vec = P.sb("vec", [128, NV], F32)
    modT = P.sb("modT", [128, 96, 2], F32)
    mA = P.sb("mA", [128, 2, KC, 2], F32)
    lbt = P.sb("lbt", [128, 4, 2, L], F32)
    omlb = P.sb("omlb", [128, 4, 2, L], F32)
    cT = P.sb("cTs", [128, 32], F32)
    sT = P.sb("sTs", [128, KC, 2], BF16)
    epsD = P.sb("epsD", [128, 4], F32)
    wpool = P.pool("wp", [128, KC, 128], BF16, 3)
    _mem = ArenaMem(P, "arena", 27600)
    AF_ = Arena(_mem, F32)
    AB_ = Arena(_mem, BF16)

    def V(n, rows=128, c0=0, c1=None):
        o, w = V128[n]
        c1 = w if c1 is None else c1
        return vec[0:rows, o + c0:o + c1]

    P.memset(_mem.buf[:, :], 0.0, [_mem.buf])
    P.memset(hT[:].rearrange('p k t -> p (k t)'), 0.0, [hT])
    P.memset(epsD[:, 0:1], EPS, [epsD])
    P.memset(epsD[:, 1:2], 64e-5, [epsD])
    P.memset(epsD[:, 2:3], 1e-24, [epsD])
    P.memset(epsD[:, 3:4], 0.0, [epsD])

    P.dma("sp", cT[:], cT_in[:], [cT_in], [cT])
    P.act(sT[:].rearrange("p k g -> p (k g)"), cT[:], AF.Silu, [cT], [sT])

    lbe = P.sb("lbe", [128, 8, L], F32)
    lbs = P.sb("lbs", [128, 8], F32)
    P.dma("sp", lbe[:].rearrange("p a l -> p (a l)"), lb_in[:], [lb_in], [lbe])
    P.act(lbe[:], lbe[:], AF.Exp, [lbe], [lbe])
    P.emit("dve", lambda e: e.reduce_sum(out=lbs[:], in_=lbe[:], axis=mybir.AxisListType.X), [lbe], [lbs])
    P.recip(lbs[:], lbs[:], [lbs], [lbs])
    P.tt(lbe[:], lbe[:], lbs[:].unsqueeze(2).to_broadcast([128, 8, L]), ALU.mult, [lbe, lbs], [lbe])
    lb3 = lbt[:].rearrange("p h d l -> p (h d) l")
    P.memset(lb3[:, :, 0:1], 0.0, [lbt])
    for l in range(1, L):
        P.tt(lb3[:, :, l:l + 1], lb3[:, :, l - 1:l], lbe[:, :, l:l + 1], ALU.add, [lbt, lbe], [lbt])
    P.ts(omlb[:], lbt[:], -1.0, 1.0, ALU.mult, ALU.add, [lbt], [omlb])

    widx = [0]

    def load_w(src_buf, src_ap, rows=128, kc=KC, m=128):
        wb = wpool.next()
        P.dma("pool", wb[0:rows, 0:kc, 0:m], src_ap, [src_buf], [wb])
        return wb

    def x_src(l, m, ti):
        return (xin_v if l == 0 else xs_v)[m][ti], (xT_in if l == 0 else xs)

    def stage_mod(l):
        P.dma("sp", vec[:], vec_in[l], [vec_in], [vec])
        for j in range(96):
            wb = load_w(w_mod, w_mod[l][:, j * 128:(j + 1) * 128].rearrange("(k p) n -> p k n", p=128))
            ps = P.psb()
            for k in range(KC):
                P.mm(ps[:, 0:2], wb[:, k, :], sT[:, k, :], k == 0, k == KC - 1, [wb, sT], [ps])
            P.ts(modT[:, j, :], ps[:, 0:2], V("bm", c0=j, c1=j + 1), None, ALU.add, ALU.bypass, [ps, vec], [modT])
        for i, (ng, sc0) in enumerate([("ng1", 16), ("ng2", 64)]):
            P.ts(mA[:, i, :, :], modT[:, sc0:sc0 + 16, :], 1.0, None, ALU.add, ALU.bypass, [modT], [mA])
            P.tt(mA[:, i, :, :], mA[:, i, :, :], V(ng).unsqueeze(2).to_broadcast([128, 16, 2]), ALU.mult,
                 [mA, vec], [mA])

    def stage_norm(l, i):
        P.barrier()
        AF_.reset()
        AB_.reset()
        xts = [AF_.alloc("xt%d" % b, [KC, 512]) for b in range(2)]
        sqs = [AB_.alloc("sq%d" % b, [KC, 512]) for b in range(2)]
        rss = [AF_.alloc("rs%d" % b, [512]) for b in range(2)]
        sh0 = 0 if i == 0 else 48
        for ti, (t0, n) in enumerate(TILES):
            g = 1 if ti == 0 else 0
            xt, sq, rs = xts[ti % 2], sqs[ti % 2], rss[ti % 2]
            srcs = [x_src(l if i == 0 else 99, m, ti)[0] for m in range(KC)]
            src = xT_in if (l == 0 and i == 0) else xs
            P.dma("sp", xt[:, :, 0:n], src[:, t0:t0 + n].rearrange("(k p) t -> p k t", p=128), srcs, [xt])
            P.act(sq[:, :, 0:n], xt[:, :, 0:n], AF.Square, [xt], [sq])
            ps = P.psb()
            for k in range(KC):
                P.mm(ps[:, 0:n], C("ones", bf=True), sq[:, k, 0:n], k == 0, k == KC - 1, [cstb, sq], [ps])
            P.act(rs[:, 0:n], ps[:, 0:n], AF.Sqrt, [ps, epsD], [rs], scale=1.0 / D, bias=epsD[:, 0:1])
            P.recip(rs[:, 0:n], rs[:, 0:n], [rs], [rs])
            P.tt(xt[:, :, 0:n], xt[:, :, 0:n], rs[:, 0:n].unsqueeze(1).to_broadcast([128, KC, n]), ALU.mult,
                 [xt, rs], [xt])
            for k in range(KC):
                P.act(hT[:, k, t0:t0 + n], xt[:, k, 0:n], AF.Identity, [xt, mA, modT], [hT_v[ti]],
                      scale=mA[:, i, k, g:g + 1], bias=modT[:, sh0 + k, g:g + 1])

    def proj(l, c0, m, evac, wsrc=None, tiles=None):
        wb = load_w(w_in, w_in[l][:, c0:c0 + m].rearrange("(k p) n -> p k n", p=128), m=m)
        for ti, (t0, n) in enumerate(TILES):
            ps = P.psb()
            for k in range(KC):
                P.mm(ps[0:m, 0:n], wb[:, k, 0:m], hT[:, k, t0:t0 + n], k == 0, k == KC - 1, [wb, hT_v[ti]], [ps])
            evac(ps, ti, t0, n)

    def proj_full(l, c0, m, dst, eng="act"):
        def ev(ps, ti, t0, n):
            P.cp(dst[0:m, t0:t0 + n], ps[0:m, 0:n], [ps], [dst], eng=eng)
        proj(l, c0, m, ev)

    def rms_over_partitions(src_ap, n, rows, ones_ap, inv_count, eps_col, sqb, rsb, R):
        P.act(sqb[0:rows, 0:n], src_ap, AF.Square, R, [sqb])
        ps = P.psb()
        P.mm(ps[0:rows, 0:n], ones_ap, sqb[0:rows, 0:n], True, True, [cstb, sqb], [ps])
        P.act(rsb[0:rows, 0:n], ps[0:rows, 0:n], AF.Sqrt, [ps, epsD], [rsb], scale=inv_count,
              bias=epsD[0:rows, eps_col:eps_col + 1])
        P.recip(rsb[0:rows, 0:n], rsb[0:rows, 0:n], [rsb], [rsb])

    def stage_out(l):
        P.barrier()
        AF_.reset()
        AB_.reset()
        yts = [AB_.alloc("yt%d" % b, [KC, 512]) for b in range(2)]
        xps = Pool([AF_.alloc("xo%d" % b, [512]) for b in range(4)])
        for ti, (t0, n) in enumerate(TILES):
            g = 1 if ti == 0 else 0
            yt = yts[ti % 2]
            P.dma("sp", yt[:, :, 0:n], yT[:, t0:t0 + n].rearrange("(k p) t -> p k t", p=128),
                  [yT_v[m][ti] for m in range(KC)], [yt])
            for m in range(KC):
                wb = load_w(w_out, w_out[l][:, m * 128:(m + 1) * 128].rearrange("(k p) n -> p k n", p=128))
                xv, xsrc = x_src(l, m, ti)
                xo = xps.next()
                P.dma("sp", xo[:, 0:n], xsrc[m * 128:(m + 1) * 128, t0:t0 + n], [xv], [xo])
                ps = P.psb()
                for k in range(KC):
                    P.mm(ps[:, 0:n], wb[:, k, :], yt[:, k, 0:n], k == 0, k == KC - 1, [wb, yt], [ps])
                P.stt(xo[:, 0:n], ps[:, 0:n], modT[:, 32 + m, g:g + 1], xo[:, 0:n], ALU.mult, ALU.add,
                      [ps, modT, xo], [xo])
                P.dma("sp", xs[m * 128:(m + 1) * 128, t0:t0 + n], xo[:, 0:n], [xo], [xs_v[m][ti]])

    def stage_ffn(l, last):
        P.barrier()
        AF_.reset()
        AB_.reset()
        ups = [AF_.alloc("up%d" % b, [T]) for b in range(2)]
        cvs = [AF_.alloc("cv%d" % b, [T]) for b in range(2)]
        zb = Pool([AB_.alloc("zb%d" % b, [T]) for b in range(2)])
        dwo, _ = V128["dw"]
        for j in range(44):
            for half in range(2):
                cidx = j + 44 * half
                up, cv = ups[half], cvs[half]
                wb = load_w(ffn_up, ffn_up[l][:, cidx * 128:(cidx + 1) * 128].rearrange("(k p) n -> p k n", p=128))
                for ti, (t0, n) in enumerate(TILES):
                    ps = P.psb()
                    for k in range(KC):
                        P.mm(ps[:, 0:n], wb[:, k, :], hT[:, k, t0:t0 + n], k == 0, k == KC - 1, [wb, hT_v[ti]], [ps])
                    P.cp(up[:, t0:t0 + n], ps[:, 0:n], [ps], [up], eng="act" if ti % 2 else "dve")
                P.act(cv[:, :], up[:, :], AF.Identity, [up, vec], [cv],
                      scale=vec[:, dwo + 88 + cidx:dwo + 88 + cidx + 1], bias=V("db", c0=cidx, c1=cidx + 1))
                for (s0, s1) in SEGS:
                    P.stt(cv[:, s0 + 1:s1], up[:, s0:s1 - 1], vec[:, dwo + cidx:dwo + cidx + 1], cv[:, s0 + 1:s1],
                          ALU.mult, ALU.add, [up, vec, cv], [cv])
                    P.stt(cv[:, s0:s1 - 1], up[:, s0 + 1:s1], vec[:, dwo + 176 + cidx:dwo + 176 + cidx + 1],
                          cv[:, s0:s1 - 1], ALU.mult, ALU.add, [up, vec, cv], [cv])
            z = zb.next()
            P.act(cvs[0][:, :], cvs[0][:, :], AF.Silu, [cvs[0]], [cvs[0]])
            P.tt(z[:, :], cvs[0][:, :], cvs[1][:, :], ALU.mult, [cvs[0], cvs[1]], [z])
            P.dma("sp", zT[j * 128:(j + 1) * 128, :], z[:, :], [z], [zT_v[j]])
        P.barrier()
        AF_.reset()
        AB_.reset()
        zt = AB_.alloc("zt", [44, 512])
        wdp = Pool([AB_.alloc("wd%d" % b, [1024]) for b in range(3)])
        xps = Pool([AF_.alloc("xo%d" % b, [512]) for b in range(4)])
        for ti, (t0, n) in enumerate(TILES):
            g = 1 if ti == 0 else 0
            if last and ti == 0:
                continue
            P.dma("sp", zt[:, :, 0:n], zT[:, t0:t0 + n].rearrange("(k p) t -> p k t", p=128), zT_v, [zt])
            for half in range(2):
                pss = [P.psb() for _ in range(8)]
                for k in range(44):
                    wd = wdp.next()
                    P.dma("pool", wd[:, :], ffn_down[l][k * 128:(k + 1) * 128, half * 1024:(half + 1) * 1024],
                          [ffn_down], [wd])
                    for mm_ in range(8):
                        P.mm(pss[mm_][:, 0:n], wd[:, mm_ * 128:(mm_ + 1) * 128], zt[:, k, 0:n], k == 0, k == 43,
                             [wd, zt], [pss[mm_]])
                for mm_ in range(8):
                    m = half * 8 + mm_
                    xo = xps.next()
                    P.dma("sp", xo[:, 0:n], xs[m * 128:(m + 1) * 128, t0:t0 + n], [xs_v[m][ti]], [xo])
                    P.stt(xo[:, 0:n], pss[mm_][:, 0:n], modT[:, 80 + m, g:g + 1], xo[:, 0:n], ALU.mult, ALU.add,
                          [pss[mm_], modT, xo], [xo])
                    if last:
                        P.dma("sp", out[m * 128:(m + 1) * 128, t0 - NCTX:t0 - NCTX + n], xo[:, 0:n], [xo],
                              [out_v[m][ti]])
                    else:
                        P.dma("sp", xs[m * 128:(m + 1) * 128, t0:t0 + n], xo[:, 0:n], [xo], [xs_v[m][ti]])

    def token_shift(pf, sf, m, mu_ap):
        for (s0, s1) in SEGS:
            P.tt(sf[0:m, s0 + 1:s1 - 1], pf[0:m, s0:s1 - 2], pf[0:m, s0 + 2:s1], ALU.add, [pf], [sf])
            P.cp(sf[0:m, s0:s0 + 1], pf[0:m, s0 + 1:s0 + 2], [pf], [sf])
            P.cp(sf[0:m, s1 - 1:s1], pf[0:m, s1 - 2:s1 - 1], [pf], [sf])
        P.stt(sf[0:m, :], sf[0:m, :], 0.5, pf[0:m, :], ALU.mult, ALU.subtract, [sf, pf], [sf])
        P.stt(sf[0:m, :], sf[0:m, :], mu_ap, pf[0:m, :], ALU.mult, ALU.add, [sf, pf, vec], [sf])

    def proj_to(l, c0, m, buf, dst_fn, eng="act"):
        def ev(ps, ti, t0, n):
            P.cp(dst_fn(t0, n), ps[0:m, 0:n], [ps], [buf], eng=eng)
        proj(l, c0, m, ev)

    MLA_SCALE = 192.0 ** -0.5

    def mixer_mla(l):
        P.barrier()
        AF_.reset()
        cq = AB_.alloc("cq", [4, T])
        ckv = AB_.alloc("ckv", [2, T])
        krT = AB_.alloc("krT", [T])
        mark = _mem.off
        scr = AF_.alloc("scr", [4, T])
        sq4 = AB_.alloc("sq4", [4, 512])
        rs1 = AF_.alloc("rs1", [512])
        tA = AF_.alloc("tA", [512])
        tB = AF_.alloc("tB", [512])
        csT = AF_.alloc("csT", [2, 512])
        qb = AB_.alloc("qb", [512])
        sq1p = AB_.alloc("sq1p", [512])

        def rope(dst_bf, dst_buf, src32, src_buf, t0, n):
            P.dma("sp", csT[0:64, :, 0:n], cs_in[:, :, t0 - NCTX:t0 - NCTX + n].rearrange("c p t -> p c t"),
                  [cs_in], [csT])
            P.cp(qb[0:64, 0:n], src32, [src_buf], [qb], eng="act")
            ps = P.psb()
            P.mm(ps[0:64, 0:n], C("rotT", rows=64, bf=True), qb[0:64, 0:n], True, True, [cstb, qb], [ps])
            P.tt(tB[0:64, 0:n], ps[0:64, 0:n], csT[0:64, 1, 0:n], ALU.mult, [ps, csT], [tB])
            P.tt(src32, src32, csT[0:64, 0, 0:n], ALU.mult, [src_buf, csT], [src_buf])
            P.tt(dst_bf, src32, tB[0:64, 0:n], ALU.add, [src_buf, tB], [dst_buf])

        def norm_chunks(src3, nk, dstb, gname, cnt):
            for ti, (t0, n) in enumerate(TILES):
                P.act(sq4[:, 0:nk, 0:n], src3[:, 0:nk, t0:t0 + n], AF.Square, [scr], [sq4])
                ps = P.psb()
                for k in range(nk):
                    P.mm(ps[:, 0:n], C("ones", bf=True), sq4[:, k, 0:n], k == 0, k == nk - 1, [cstb, sq4], [ps])
                P.act(rs1[:, 0:n], ps[:, 0:n], AF.Sqrt, [ps, epsD], [rs1], scale=1.0 / cnt, bias=epsD[:, 0:1])
                P.recip(rs1[:, 0:n], rs1[:, 0:n], [rs1], [rs1])
                for k in range(nk):
                    P.stt(dstb[:, k, t0:t0 + n], src3[:, k, t0:t0 + n], V(gname, c0=k, c1=k + 1), rs1[:, 0:n],
                          ALU.mult, ALU.mult, [scr, vec, rs1], [dstb])

        for k in range(4):
            proj_to(l, ML0 + k * 128, 128, scr, lambda t0, n, k=k: scr[:, k, t0:t0 + n])
        norm_chunks(scr, 4, cq, "qnorm", 512)
        for k in range(2):
            proj_to(l, ML0 + 512 + k * 128, 128, scr, lambda t0, n, k=k: scr[:, k, t0:t0 + n])
        proj_to(l, ML0 + 768, 64, scr, lambda t0, n: scr[0:64, 2, t0:t0 + n])
        norm_chunks(scr, 2, ckv, "kvnorm", 256)
        for ti, (t0, n) in enumerate(TILES):
            rms_over_partitions(scr[0:64, 2, t0:t0 + n], n, 64, C("ones", rows=64, bf=True, c1=64), 1.0 / 64, 0,
                                sq1p, rs1, [scr])
            P.stt(tA[0:64, 0:n], scr[0:64, 2, t0:t0 + n], V("kr_g", rows=64), rs1[0:64, 0:n], ALU.mult, ALU.mult,
                  [scr, vec, rs1], [tA])
            if ti == 0:
                P.cp(krT[0:64, t0:t0 + n], tA[0:64, 0:n], [tA], [krT])
            else:
                rope(krT[0:64, t0:t0 + n], krT, tA[0:64, 0:n], tA, t0, n)
        P.barrier()
        _mem.off = mark
        P.memset(_mem.buf[:, mark:_mem.n], 0.0, [_mem.buf])
        P.barrier()
        qnT = AB_.alloc("qnT", [T])
        qrT = AB_.alloc("qrT", [T])
        knT = AB_.alloc("knT", [T])
        Vt = AB_.alloc("Vt", [18, 128])
        wq = AB_.alloc("wq", [4, 192])
        wkv = AB_.alloc("wkv", [2, 256])
        sq1 = AB_.alloc("sq1", [512])
        rs1 = AF_.alloc("rs1b", [512])
        tA = AF_.alloc("tAb", [512])
        tB = AF_.alloc("tBb", [512])
        csT = AF_.alloc("csTb", [2, 512])
        qb = AB_.alloc("qbb", [512])
        ptp = Pool([AB_.alloc("pt%d" % b, [512]) for b in range(3)])
        yb = Pool([AB_.alloc("yb%d" % b, [512]) for b in range(2)])
        for h in range(8):
            P.dma("pool", wq[:, :, :], w_uq[l][:, h * 192:(h + 1) * 192].rearrange("(k p) n -> p k n", p=128),
                  [w_uq], [wq])
            P.dma("pool", wkv[:, :, :], w_ukv[l][:, h * 256:(h + 1) * 256].rearrange("(k p) n -> p k n", p=128),
                  [w_ukv], [wkv])
            for ti, (t0, n) in enumerate(TILES):
                ps = P.psb()
                for k in range(4):
                    P.mm(ps[:, 0:n], wq[:, k, 0:128], cq[:, k, t0:t0 + n], k == 0, k == 3, [wq, cq], [ps])
                rms_over_partitions(ps[:, 0:n], n, 128, C("ones", bf=True), 1.0 / 128, 0, sq1, rs1, [ps])
                P.stt(qnT[:, t0:t0 + n], ps[:, 0:n], V("qn_g"), rs1[:, 0:n], ALU.mult, ALU.mult, [ps, vec, rs1], [qnT])
                ps = P.psb()
                for k in range(4):
                    P.mm(ps[0:64, 0:n], wq[:, k, 128:192], cq[:, k, t0:t0 + n], k == 0, k == 3, [wq, cq], [ps])
                rms_over_partitions(ps[0:64, 0:n], n, 64, C("ones", rows=64, bf=True, c1=64), 1.0 / 64, 0, sq1, rs1, [ps])
                P.stt(tA[0:64, 0:n], ps[0:64, 0:n], V("qr_g", rows=64), rs1[0:64, 0:n], ALU.mult, ALU.mult,
                      [ps, vec, rs1], [tA])
                if ti == 0:
                    P.cp(qrT[0:64, t0:t0 + n], tA[0:64, 0:n], [tA], [qrT])
                else:
                    rope(qrT[0:64, t0:t0 + n], qrT, tA[0:64, 0:n], tA, t0, n)
                ps = P.psb()
                for k in range(2):
                    P.mm(ps[:, 0:n], wkv[:, k, 0:128], ckv[:, k, t0:t0 + n], k == 0, k == 1, [wkv, ckv], [ps])
                rms_over_partitions(ps[:, 0:n], n, 128, C("ones", bf=True), 1.0 / 128, 0, sq1, rs1, [ps])
                P.stt(knT[:, t0:t0 + n], ps[:, 0:n], V("kn_g"), rs1[:, 0:n], ALU.mult, ALU.mult, [ps, vec, rs1], [knT])
            for b0 in range(0, 18, 4):
                nb = min(4, 18 - b0)
                ps = P.psb()
                for j in range(nb):
                    b = b0 + j
                    for k in range(2):
                        P.mm(ps[:, j * 128:(j + 1) * 128], ckv[:, k, b * 128:(b + 1) * 128], wkv[:, k, 128:256],
                             k == 0, k == 1, [ckv, wkv], [ps])
                P.cp(Vt[:, b0:b0 + nb, :], ps[:, 0:nb * 128].rearrange("p (a b) -> p a b", b=128), [ps], [Vt], eng="act")
            for ti, (t0, n) in enumerate(TILES):
                nkb = 2 if ti == 0 else 18
                pso = P.hold()
                psd = P.hold()
                for kb in range(nkb):
                    pss = P.psb()
                    ks = slice(kb * 128, (kb + 1) * 128)
                    P.mm(pss[:, 0:n], knT[:, ks], qnT[:, t0:t0 + n], True, False, [knT, qnT], [pss])
                    P.mm(pss[:, 0:n], krT[0:64, ks], qrT[0:64, t0:t0 + n], False, True, [krT, qrT], [pss])
                    pt = ptp.next()
                    P.act(pt[:, 0:n], pss[:, 0:n], AF.Exp, [pss], [pt], scale=MLA_SCALE)
                    P.mm(pso[:, 0:n], Vt[:, kb, :], pt[:, 0:n], kb == 0, kb == nkb - 1, [Vt, pt], [pso])
                    P.mm(psd[:, 0:n], C("ones", bf=True), pt[:, 0:n], kb == 0, kb == nkb - 1, [cstb, pt], [psd])
                P.recip(tA[:, 0:n], psd[:, 0:n], [psd], [tA])
                y = yb.next()
                P.tt(y[:, 0:n], pso[:, 0:n], tA[:, 0:n], ALU.mult, [pso, tA], [y])
                P.release(pso)
                P.release(psd)
                P.dma("sp", yT[(8 + h) * 128:(9 + h) * 128, t0:t0 + n], y[:, 0:n], [y], [yT_v[8 + h][ti]])
    def mixer_hgrn2(l):
        for h in range(4):
            P.barrier()
            AF_.reset()
            qf = AF_.alloc("qf", [T])
            vf = AF_.alloc("vf", [T])
            zf = AF_.alloc("zf", [T])
            oT = AF_.alloc("oT", [T])
            t1 = AF_.alloc("t1", [512])
            t2 = AF_.alloc("t2", [512])
            t3 = AF_.alloc("t3", [512])
            Fc = AF_.alloc("Fc", [32])
            qt = AB_.alloc("qt", [512])
            kt = AB_.alloc("kt", [512])
            kb = AB_.alloc("kb", [512])
            vb = AB_.alloc("vb", [512])
            sq1 = AB_.alloc("sq1", [512])
            kvp = Pool([AB_.alloc("kvT%d" % b, [8, 128]) for b in range(3)])
            atp = Pool([AB_.alloc("aT%d" % b, [4, 16]) for b in range(3)])
            S32 = [AF_.alloc("S32_%d" % b, [128]) for b in range(2)]
            Sbf = Pool([AB_.alloc("Sbf%d" % b, [128]) for b in range(3)])
            yb = Pool([AB_.alloc("yb%d" % b, [512]) for b in range(2)])
            proj_to(l, h * 128, 128, qf, lambda t0, n: qf[:, t0:t0 + n])
            proj_to(l, 1536 + h * 128, 128, vf, lambda t0, n: vf[:, t0:t0 + n])
            identb = C("ident", bf=True)
            for d in range(2):
                proj_to(l, 512 * (1 + d) + h * 128, 128, zf, lambda t0, n: zf[:, t0:t0 + n])
                state = None
                cidx = 0
                for (t0, n, rev) in scan_tiles(d):
                    nch = n // 16
                    P.act(t1[:, 0:n], tsl(zf, t0, n, rev), AF.Sigmoid, [zf], [t1])
                    P.ts(t1[:, 0:n], t1[:, 0:n], omlb[:, h, d, l:l + 1], lbt[:, h, d, l:l + 1], ALU.mult, ALU.add,
                         [t1, omlb, lbt], [t1])
                    P.act(t2[:, 0:n], t1[:, 0:n], AF.Ln, [t1], [t2])
                    P.ts(t1[:, 0:n], t1[:, 0:n], -1.0, 1.0, ALU.mult, ALU.add, [t1], [t1])
                    P.scan(t3[:, 0:n], C("rs16", c1=n), t2[:, 0:n], [cst, t2], [t3])
                    P.act(t2[:, 0:n], t3[:, 0:n], AF.Exp, [t3], [t2])
                    P.tt(qt[:, 0:n], tsl(qf, t0, n, rev), t2[:, 0:n], ALU.mult, [qf, t2], [qt])
                    P.act(t2[:, 0:n], t3[:, 0:n], AF.Exp, [t3], [t2], scale=-1.0)
                    P.tt(kt[:, 0:n], t1[:, 0:n], t2[:, 0:n], ALU.mult, [t1, t2], [kt])
                    G3 = t3[:, 0:n].rearrange("p (c i) -> p c i", i=16)
                    P.tt(t2[:, 0:n].rearrange("p (c i) -> p c i", i=16), G3[:, :, 15:16].to_broadcast([128, nch, 16]),
                         G3, ALU.subtract, [t3], [t2])
                    P.act(t2[:, 0:n], t2[:, 0:n], AF.Exp, [t2], [t2])
                    P.tt(kb[:, 0:n], t1[:, 0:n], t2[:, 0:n], ALU.mult, [t1, t2], [kb])
                    P.act(Fc[:, 0:nch], G3[:, :, 15], AF.Exp, [t3], [Fc])
                    P.cp(vb[:, 0:n], tsl(vf, t0, n, rev), [vf], [vb])
                    pso = P.hold()
                    for g0 in range(0, nch, 4):
                        psT = P.psb()
                        psTb = psT[:].bitcast(BF16)
                        for j in range(4):
                            cs = slice((g0 + j) * 16, (g0 + j + 1) * 16)
                            P.tr(psTb[0:16, j * 128:(j + 1) * 128], kb[:, cs], identb, [kb, cstb], [psT])
                            P.tr(psTb[0:16, (4 + j) * 128:(5 + j) * 128], vb[:, cs], identb, [vb, cstb], [psT])
                        kvT = kvp.next()
                        P.cp(kvT[0:16, :, :].rearrange("p a b -> p (a b)"), psTb[0:16, :], [psT], [kvT], eng="act")
                        psA = P.psb()
                        for j in range(4):
                            cs = slice((g0 + j) * 16, (g0 + j + 1) * 16)
                            P.mm(psA[0:16, j * 16:(j + 1) * 16], kt[:, cs], qt[:, cs], True, True, [kt, qt], [psA])
                        aT = atp.next()
                        P.tt(aT[0:16, :, :], psA[0:16, 0:64].rearrange("p (a b) -> p a b", b=16),
                             C("m16", rows=16).unsqueeze(1).to_broadcast([16, 4, 16]), ALU.mult, [psA, cst], [aT])
                        psG = P.psb()
                        for j in range(4):
                            P.mm(psG[:, j * 128:(j + 1) * 128], kvT[0:16, j, :], kvT[0:16, 4 + j, :], True, True,
                                 [kvT], [psG])
                        for j in range(4):
                            c = g0 + j
                            cs = slice(c * 16, (c + 1) * 16)
                            P.mm(pso[:, cs], kvT[0:16, 4 + j, :], aT[0:16, j, :], True, state is None, [kvT, aT], [pso])
                            if state is not None:
                                P.mm(pso[:, cs], state[1][:, :], qt[:, cs], False, True, [state[1], qt], [pso])
                            new32 = S32[cidx % 2]
                            if state is None:
                                P.cp(new32[:, :], psG[:, j * 128:(j + 1) * 128], [psG], [new32])
                            else:
                                P.stt(new32[:, :], state[0][:, :], Fc[:, c:c + 1], psG[:, j * 128:(j + 1) * 128],
                                      ALU.mult, ALU.add, [state[0], Fc, psG], [new32])
                            nbf = Sbf.next()
                            P.cp(nbf[:, :], new32[:, :], [new32], [nbf], eng="act")
                            state = (new32, nbf)
                            cidx += 1
                    if d == 0:
                        P.cp(oT[:, t0:t0 + n], pso[:, 0:n], [pso], [oT])
                    else:
                        P.tt(tsl(oT, t0, n, True), tsl(oT, t0, n, True), pso[:, 0:n], ALU.add, [oT, pso], [oT])
                    P.release(pso)
            proj_to(l, 2048 + h * 128, 128, zf, lambda t0, n: zf[:, t0:t0 + n])
            for ti, (t0, n) in enumerate(TILES):
                rms_over_partitions(oT[:, t0:t0 + n], n, 128, C("ones", bf=True), 1.0 / 128, 0, sq1, t1, [oT])
                P.stt(t2[:, 0:n], oT[:, t0:t0 + n], V("hgn"), t1[:, 0:n], ALU.mult, ALU.mult, [oT, vec, t1], [t2])
                P.act(t3[:, 0:n], zf[:, t0:t0 + n], AF.Silu, [zf], [t3])
                y = yb.next()
                P.tt(y[:, 0:n], t2[:, 0:n], t3[:, 0:n], ALU.mult, [t2, t3], [y])
                P.dma("sp", yT[h * 128:(h + 1) * 128, t0:t0 + n], y[:, 0:n], [y], [yT_v[h][ti]])
    def rwkv_lora(l):
        P.barrier()
        AF_.reset()
        pf = AF_.alloc("pf", [T])
        sf = AF_.alloc("sf", [T])
        xb = AB_.alloc("xb", [T])
        w2b = AB_.alloc("w2b", [512])
        otp = Pool([AF_.alloc("ot%d" % b, [512]) for b in range(3)])
        groups = [(1536, 96, "w", 0, 0), (1632, 96, "w", 1, 1), (1728, 96, "a", 0, 2), (1824, 96, "a", 1, 3),
                  (1920, 64, "g", 0, 4)]
        for gi, (c0, m, kind, d, ai) in enumerate(groups):
            proj_to(l, RW0 + c0, m, pf, lambda t0, n, m=m: pf[0:m, t0:t0 + n])
            mu_ap = V("mul", rows=96, c0=gi, c1=gi + 1) if kind != "g" else V("mug", rows=64)
            token_shift(pf, sf, m, mu_ap)
            func = AF.Tanh if kind == "w" else (AF.Identity if kind == "a" else AF.Sigmoid)
            P.act(xb[0:m, :], sf[0:m, :], func, [sf], [xb])
            if kind == "w":
                wsrc, wb_ = rw_w2[l][d], rw_w2
            elif kind == "a":
                wsrc, wb_ = rw_a2[l][d], rw_a2
            else:
                wsrc, wb_ = rw_g2[l], rw_g2
            P.dma("pool", w2b[0:m, :], wsrc, [wb_], [w2b])
            for mc in range(4):
                for ti, (t0, n) in enumerate(TILES):
                    ps = P.psb()
                    P.mm(ps[:, 0:n], w2b[0:m, mc * 128:(mc + 1) * 128], xb[0:m, t0:t0 + n], True, True, [w2b, xb], [ps])
                    o = otp.next()
                    if kind == "w":
                        P.act(o[:, 0:n], ps[:, 0:n], AF.Sigmoid, [ps, vec], [o], bias=V("w0", c0=d * 4 + mc, c1=d * 4 + mc + 1))
                        P.ts(o[:, 0:n], o[:, 0:n], -DECAY_MAX, None, ALU.mult, ALU.bypass, [o], [o])
                    elif kind == "a":
                        P.act(o[:, 0:n], ps[:, 0:n], AF.Sigmoid, [ps, vec], [o], bias=V("a0", c0=d * 4 + mc, c1=d * 4 + mc + 1))
                    else:
                        P.cp(o[:, 0:n], ps[:, 0:n], [ps], [o])
                    P.dma("sp", rwaux[ai][mc * 128:(mc + 1) * 128, t0:t0 + n], o[:, 0:n], [o], [aux_v[ai][mc]])

    def mixer_rwkv(l):
        rwkv_lora(l)
        for m in range(4):
            P.barrier()
            AF_.reset()
            rf = AF_.alloc("rf", [T])
            kf = AF_.alloc("kf", [T])
            vf = AF_.alloc("vf", [T])
            kkf = AF_.alloc("kkf", [T])
            oT = AF_.alloc("oT", [T])
            pf = AF_.alloc("pf", [T])
            tlw = AF_.alloc("tlw", [512])
            tag = AF_.alloc("tag", [512])
            tcum = AF_.alloc("tcum", [512])
            te = AF_.alloc("te", [512])
            te2 = AF_.alloc("te2", [512])
            tb_ = AF_.alloc("tb", [512])
            tkd = AF_.alloc("tkd", [512])
            WC = AF_.alloc("WC", [4])
            omka = AF_.alloc("omka", [1])
            tmpP = AF_.alloc("tmpP", [128])
            Gp = AF_.alloc("Gp", [64])
            S32 = [AF_.alloc("S32_%d" % b, [64]) for b in range(2)]
            rt = AB_.alloc("rt", [512])
            bt = AB_.alloc("bt", [512])
            kt = AB_.alloc("kt", [512])
            at = AB_.alloc("at", [512])
            bb = AB_.alloc("bb", [512])
            kb = AB_.alloc("kb", [512])
            vb = AB_.alloc("vb", [512])
            sq1 = AB_.alloc("sq1", [512])
            atm = [AB_.alloc("atm%d" % b, [512]) for b in range(2)]
            rtm = [AB_.alloc("rtm%d" % b, [512]) for b in range(2)]
            for b_ in atm + rtm:
                P.memset(b_[:, :], 0.0, [b_])
            Tp = Pool([AF_.alloc("Tj%d" % b, [2, 128]) for b in range(2)])
            Lp = Pool([AF_.alloc("Lj%d" % b, [2, 128]) for b in range(2)])
            Zp = Pool([AF_.alloc("Z%d" % b, [2, 128]) for b in range(3)])
            QeTm = [AB_.alloc("QeTm%d" % b, [128]) for b in range(2)]
            for b_ in QeTm:
                P.memset(b_[:, :], 0.0, [b_])
            identf = C("ident")
            Tak = AB_.alloc("Tak", [2, 128])
            Trb = AB_.alloc("Trb", [2, 128])
            Trk = AB_.alloc("Trk", [2, 128])
            tm = AB_.alloc("tm", [4, 128])
            Z1c = AB_.alloc("Z1c", [128])
            Z2c = AB_.alloc("Z2c", [128])
            PhiT = AB_.alloc("PhiT", [128])
            QeT = AB_.alloc("QeT", [128])
            Sbf = Pool([AB_.alloc("Sbf%d" % b, [64]) for b in range(3)])
            yb = Pool([AB_.alloc("yb%d" % b, [512]) for b in range(2)])
            identb = C("ident", bf=True)
            for (c0, dst, mi) in [(0, rf, m), (512, kf, 4 + m), (1024, vf, 8 + m)]:
                proj_to(l, RW0 + c0 + m * 128, 128, pf, lambda t0, n: pf[:, t0:t0 + n])
                token_shift(pf, dst, 128, V("mu", c0=mi, c1=mi + 1))
            P.ts(omka[:, 0:1], V("ka", c0=m, c1=m + 1), -1.0, 1.0, ALU.mult, ALU.add, [vec], [omka])
            for ti, (t0, n) in enumerate(TILES):
                P.ts(te[:, 0:n], kf[:, t0:t0 + n], V("kk", c0=m, c1=m + 1), None, ALU.mult, ALU.bypass, [kf, vec], [te])
                P.act(sq1[:, 0:n], te[:, 0:n], AF.Square, [te], [sq1])
                ps = P.psb()
                P.mm(ps[:, 0:n], C("blk", bf=True), sq1[:, 0:n], True, True, [cstb, sq1], [ps])
                P.act(te2[:, 0:n], ps[:, 0:n], AF.Sqrt, [ps, epsD], [te2], bias=epsD[:, 2:3])
                P.recip(te2[:, 0:n], te2[:, 0:n], [te2], [te2])
                P.tt(kkf[:, t0:t0 + n], te[:, 0:n], te2[:, 0:n], ALU.mult, [te, te2], [kkf])
            for d in range(2):
                state = None
                cidx = 0
                for (t0, n, rev) in scan_tiles(d):
                    nch = n // 128

                    def S(b_):
                        a_ = b_[:, 0:n]
                        return a_[:, ::-1] if rev else a_
                    P.dma("sp", tlw[:, 0:n], rwaux[d][m * 128:(m + 1) * 128, t0:t0 + n], [aux_v[d][m]], [tlw])
                    P.dma("sp", tag[:, 0:n], rwaux[2 + d][m * 128:(m + 1) * 128, t0:t0 + n], [aux_v[2 + d][m]], [tag])
                    P.scan(tcum[:, 0:n], C("rs128", c1=n), S(tlw), [cst, tlw], [tcum])
                    P.act(te[:, 0:n], tcum[:, 0:n], AF.Exp, [tcum], [te])
                    P.tt(rt[:, 0:n], tsl(rf, t0, n, rev), te[:, 0:n], ALU.mult, [rf, te], [rt])
                    P.act(te[:, 0:n], tcum[:, 0:n], AF.Exp, [tcum], [te], scale=-1.0)
                    P.tt(tb_[:, 0:n], tsl(kkf, t0, n, rev), S(tag), ALU.mult, [kkf, tag], [tb_])
                    P.tt(bt[:, 0:n], tb_[:, 0:n], te[:, 0:n], ALU.mult, [tb_, te], [bt])
                    P.ts(tkd[:, 0:n], S(tag), V("ka", c0=m, c1=m + 1), omka[:, 0:1], ALU.mult, ALU.add, [tag, vec, omka], [tkd])
                    P.tt(tkd[:, 0:n], tkd[:, 0:n], tsl(kf, t0, n, rev), ALU.mult, [tkd, kf], [tkd])
                    P.tt(kt[:, 0:n], tkd[:, 0:n], te[:, 0:n], ALU.mult, [tkd, te], [kt])
                    P.tt(te2[:, 0:n], tcum[:, 0:n], S(tlw), ALU.subtract, [tcum, tlw], [te2])
                    P.act(te2[:, 0:n], te2[:, 0:n], AF.Exp, [te2], [te2])
                    P.stt(at[:, 0:n], tsl(kkf, t0, n, rev), -1.0, te2[:, 0:n], ALU.mult, ALU.mult, [kkf, te2], [at])
                    c3 = tcum[:, 0:n].rearrange("p (c i) -> p c i", i=128)
                    P.tt(te2[:, 0:n].rearrange("p (c i) -> p c i", i=128), c3[:, :, 127:128].to_broadcast([128, nch, 128]),
                         c3, ALU.subtract, [tcum], [te2])
                    P.act(te2[:, 0:n], te2[:, 0:n], AF.Exp, [te2], [te2])
                    P.tt(bb[:, 0:n], tb_[:, 0:n], te2[:, 0:n], ALU.mult, [tb_, te2], [bb])
                    P.tt(kb[:, 0:n], tkd[:, 0:n], te2[:, 0:n], ALU.mult, [tkd, te2], [kb])
                    P.cp(vb[:, 0:n], tsl(vf, t0, n, rev), [vf], [vb])
                    P.act(WC[:, 0:nch], c3[:, :, 127], AF.Exp, [tcum], [WC])
                    for hh in range(2):
                        pr = slice(hh * 64, hh * 64 + 64)
                        P.cp(atm[hh][pr, 0:n], at[pr, 0:n], [at], [atm[hh]])
                        P.cp(rtm[hh][pr, 0:n], rt[pr, 0:n], [rt], [rtm[hh]], eng="act")
                    psO = P.hold()
                    for c in range(nch):
                        cs = slice(c * 128, (c + 1) * 128)
                        A_, B_, C_ = P.psb(), P.psb(), P.psb()
                        for hh in range(2):
                            pr = slice(hh * 64, hh * 64 + 64)
                            o0, o1 = hh * 128, 256 + hh * 128
                            am, rm = atm[hh], rtm[hh]
                            P.mm(A_[:, o0:o0 + 128], bt[:, cs], am[:, cs], True, True, [bt, am], [A_])
                            P.mm(A_[:, o1:o1 + 128], am[:, cs], bt[:, cs], True, True, [bt, am], [A_])
                            P.mm(B_[:, o0:o0 + 128], kt[:, cs], am[:, cs], True, True, [kt, am], [B_])
                            P.mm(B_[:, o1:o1 + 128], bt[:, cs], rm[:, cs], True, True, [bt, rm], [B_])
                            P.mm(C_[:, o0:o0 + 128], kt[:, cs], rm[:, cs], True, True, [kt, rm], [C_])

                        def v3(ap):
                            return ap.rearrange("p (a b) -> p a b", b=128)

                        def msk(nm):
                            return C(nm).unsqueeze(1).to_broadcast([128, 2, 128])
                        Tj, Lj = Tp.next(), Lp.next()
                        P.tt(Tj[:, :, :], v3(A_[:, 0:256]), msk("tri_s"), ALU.mult, [A_, cst], [Tj])
                        P.tt(Lj[:, :, :], v3(A_[:, 256:512]), msk("tri_sT"), ALU.mult, [A_, cst], [Lj])
                        P.tt(Tak[:, :, :], v3(B_[:, 0:256]), msk("tri_s"), ALU.mult, [B_, cst], [Tak])
                        P.tt(Trb[:, :, :], v3(B_[:, 256:512]), msk("tri_i"), ALU.mult, [B_, cst], [Trb])
                        P.tt(Trk[:, :, :], v3(C_[:, 0:256]), msk("tri_i"), ALU.mult, [C_, cst], [Trk])
                        psT = P.psb()
                        psTb = psT[:].bitcast(BF16)
                        for i_, src in enumerate([bb, kb, vb, at]):
                            P.tr(psTb[:, i_ * 128:(i_ + 1) * 128], src[:, cs], identb, [src, cstb], [psT])
                        P.cp(tm[:, :, :].rearrange("p a b -> p (a b)"), psTb[:, 0:512], [psT], [tm], eng="act")
                        psX = P.psb()
                        for hh in range(2):
                            P.mm(psX[:, hh * 64:(hh + 1) * 64], Tak[:, hh, :], tm[:, 2, hh * 64:(hh + 1) * 64], True, True,
                                 [Tak, tm], [psX])
                        Z = Zp.next()
                        P.cp(Z[:, :, 0:64], tm[:, 3, :].rearrange("p (a b) -> p a b", b=64), [tm], [Z], eng="act")
                        P.cp(Z[:, :, 64:128], psX[:, 0:128].rearrange("p (a b) -> p a b", b=64), [psX], [Z])
                        for j in range(7):
                            psZ = P.psb()
                            for hh in range(2):
                                o0 = hh * 128
                                P.mm(psZ[:, o0:o0 + 128], Tj[:, hh, :], Z[:, hh, :], True, False, [Tj, Z], [psZ])
                                P.mm(psZ[:, o0:o0 + 128], identf, Z[:, hh, :], False, True, [cst, Z], [psZ])
                            if j < 6:
                                psS = P.psb()
                                for hh in range(2):
                                    o0, o1 = hh * 128, 256 + hh * 128
                                    P.mm(psS[:, o0:o0 + 128], Lj[:, hh, :], Tj[:, hh, :], True, True, [Lj, Tj], [psS])
                                    P.mm(psS[:, o1:o1 + 128], Tj[:, hh, :], Lj[:, hh, :], True, True, [Lj, Tj], [psS])
                                Zn = Zp.next()
                                P.cp(Zn[:, :, :], v3(psZ[:, 0:256]), [psZ], [Zn], eng="act")
                                Tn, Ln = Tp.next(), Lp.next()
                                P.cp(Tn[:, :, :], v3(psS[:, 0:256]), [psS], [Tn])
                                P.cp(Ln[:, :, :], v3(psS[:, 256:512]), [psS], [Ln], eng="act")
                                Z, Tj, Lj = Zn, Tn, Ln
                            else:
                                z3 = v3(psZ[:, 0:256])
                                P.cp(Z1c[:, :].rearrange("p (a b) -> p a b", b=64), z3[:, :, 0:64], [psZ], [Z1c], eng="act")
                                P.cp(Z2c[:, :].rearrange("p (a b) -> p a b", b=64), z3[:, :, 64:128], [psZ], [Z2c])
                        psP = P.psb()
                        P.mm(psP[:, 0:128], Z1c[:, :], tm[:, 0, :], True, True, [Z1c, tm], [psP])
                        P.mm(psP[:, 128:256], tm[:, 0, :], Z2c[:, :], True, False, [Z2c, tm], [psP])
                        P.mm(psP[:, 128:256], tm[:, 1, :], tm[:, 2, :], False, True, [tm], [psP])
                        P.tt(tmpP[:, :], psP[:, 0:128], C("blk"), ALU.mult, [psP, cst], [tmpP])
                        P.stt(PhiT[:, :], C("ident"), WC[:, c:c + 1], tmpP[:, :], ALU.mult, ALU.add, [cst, WC, tmpP], [PhiT])
                        P.cp(Gp[0:64, :], psP[0:64, 128:192], [psP], [Gp], eng="act")
                        P.cp(Gp[64:128, :], psP[64:128, 192:256], [psP], [Gp])
                        psQ = P.psb()
                        for hh in range(2):
                            pr = slice(hh * 64, hh * 64 + 64)
                            P.mm(psQ[pr, 0:128], Z1c[:, pr], Trb[:, hh, :], True, True, [Z1c, Trb], [psQ])
                        for hh in range(2):
                            pr = slice(hh * 64, hh * 64 + 64)
                            P.tt(QeTm[hh][pr, :], psQ[pr, 0:128], rt[pr, cs], ALU.add, [psQ, rt], [QeTm[hh]])
                        for hh in range(2):
                            pr = slice(hh * 64, hh * 64 + 64)
                            P.mm(psO[pr, cs], Z2c[:, pr], Trb[:, hh, :], True, False, [Z2c, Trb], [psO])
                            P.mm(psO[pr, cs], tm[:, 2, pr], Trk[:, hh, :], False, state is None, [tm, Trk], [psO])
                            if state is not None:
                                P.mm(psO[pr, cs], state[1][:, :], QeTm[hh][:, :], False, True, [state[1], QeTm[hh]], [psO])
                        new32 = S32[cidx % 2]
                        if state is None:
                            P.cp(new32[:, :], Gp[:, :], [Gp], [new32])
                        else:
                            psS2 = P.psb()
                            P.mm(psS2[:, 0:64], PhiT[:, :], state[1][:, :], True, True, [PhiT, state[1]], [psS2])
                            P.tt(new32[:, :], psS2[:, 0:64], Gp[:, :], ALU.add, [psS2, Gp], [new32])
                        nbf = Sbf.next()
                        P.cp(nbf[:, :], new32[:, :], [new32], [nbf], eng="act")
                        state = (new32, nbf)
                        cidx += 1
                    if d == 0:
                        P.cp(oT[:, t0:t0 + n], psO[:, 0:n], [psO], [oT])
                    else:
                        P.tt(tsl(oT, t0, n, True), tsl(oT, t0, n, True), psO[:, 0:n], ALU.add, [oT, psO], [oT])
                    P.release(psO)
            for ti, (t0, n) in enumerate(TILES):
                osl = oT[:, t0:t0 + n]
                P.cp(sq1[:, 0:n], osl, [oT], [sq1], eng="act")
                ps = P.psb()
                P.mm(ps[:, 0:n], C("blk", bf=True), sq1[:, 0:n], True, True, [cstb, sq1], [ps])
                P.stt(te[:, 0:n], ps[:, 0:n], -1.0 / 64, osl, ALU.mult, ALU.add, [ps, oT], [te])
                P.act(sq1[:, 0:n], te[:, 0:n], AF.Square, [te], [sq1])
                ps = P.psb()
                P.mm(ps[:, 0:n], C("blk", bf=True), sq1[:, 0:n], True, True, [cstb, sq1], [ps])
                P.act(te2[:, 0:n], ps[:, 0:n], AF.Sqrt, [ps, epsD], [te2], scale=1.0 / 64, bias=epsD[:, 1:2])
                P.recip(te2[:, 0:n], te2[:, 0:n], [te2], [te2])
                P.tt(te[:, 0:n], te[:, 0:n], te2[:, 0:n], ALU.mult, [te, te2], [te])
                P.ts(te[:, 0:n], te[:, 0:n], V("gnw", c0=m, c1=m + 1), V("gnb", c0=m, c1=m + 1), ALU.mult, ALU.add,
                     [te, vec], [te])
                P.dma("sp", tlw[:, 0:n], rwaux[2][m * 128:(m + 1) * 128, t0:t0 + n], [aux_v[2][m]], [tlw])
                P.dma("sp", tag[:, 0:n], rwaux[3][m * 128:(m + 1) * 128, t0:t0 + n], [aux_v[3][m]], [tag])
                P.tt(tb_[:, 0:n], tlw[:, 0:n], tag[:, 0:n], ALU.add, [tlw, tag], [tb_])
                P.ts(tb_[:, 0:n], tb_[:, 0:n], -2.0, None, ALU.add, ALU.bypass, [tb_], [tb_])
                P.ts(tb_[:, 0:n], tb_[:, 0:n], V("ka", c0=m, c1=m + 1), 2.0, ALU.mult, ALU.add, [tb_, vec], [tb_])
                P.tt(tb_[:, 0:n], tb_[:, 0:n], kf[:, t0:t0 + n], ALU.mult, [tb_, kf], [tb_])
                P.tt(tb_[:, 0:n], tb_[:, 0:n], rf[:, t0:t0 + n], ALU.mult, [tb_, rf], [tb_])
                P.ts(sq1[:, 0:n], tb_[:, 0:n], V("rk", c0=m, c1=m + 1), None, ALU.mult, ALU.bypass, [tb_, vec], [sq1])
                ps = P.psb()
                P.mm(ps[:, 0:n], C("blk", bf=True), sq1[:, 0:n], True, True, [cstb, sq1], [ps])
                P.tt(tkd[:, 0:n], ps[:, 0:n], vf[:, t0:t0 + n], ALU.mult, [ps, vf], [tkd])
                P.tt(te[:, 0:n], te[:, 0:n], tkd[:, 0:n], ALU.add, [te, tkd], [te])
                P.dma("sp", tcum[:, 0:n], rwaux[4][m * 128:(m + 1) * 128, t0:t0 + n], [aux_v[4][m]], [tcum])
                y = yb.next()
                P.tt(y[:, 0:n], te[:, 0:n], tcum[:, 0:n], ALU.mult, [te, tcum], [y])
                P.dma("sp", yT[(4 + m) * 128:(5 + m) * 128, t0:t0 + n], y[:, 0:n], [y], [yT_v[4 + m][ti]])
    for l in range(nl):
        stage_mod(l)
        stage_norm(l, 0)
        mixer_hgrn2(l)
        mixer_rwkv(l)
        mixer_mla(l)
        stage_out(l)
        stage_norm(l, 1)
        stage_ffn(l, l == nl - 1)
    return P


_BIG = ["w_mod", "w_in", "w_out", "rw_w2", "rw_a2", "rw_g2", "mla_w_uq", "mla_w_ukv", "ffn_up", "ffn_down"]


def kernel(**inputs):
    inp = {k: np.asarray(v) for k, v in inputs.items()}
    P = build_program(L)
    nc = P.build()
    cst, cs = host_consts()
    vecs, lb = host_vecs(inp)
    big = {k: np.ascontiguousarray(inp[k], dtype=np.float32) for k in _BIG}
    in_maps = []
    for b in range(8):
        xT = np.ascontiguousarray(np.concatenate([inp["ctx"][b], inp["x"][b]], 0).T.astype(np.float32))
        cT = np.ascontiguousarray(np.stack([fm(inp["c"][b], 16), fm(inp["c_ctx"], 16)], -1).reshape(128, 32))
        in_maps.append(dict(xT=xT, cT=cT, vecs=vecs, hglb=lb, cst=cst, rope=cs, **big))
    res = run_bass_kernel_spmd(nc, in_maps, core_ids=list(range(8)))
    out = np.stack([np.ascontiguousarray(np.asarray(res.results[b]["out"]).T) for b in range(8)], 0)
    return out.astype(np.float32)
```

```python
import math
import numpy as np
import concourse.bass as bass
import concourse.mybir as mybir
from concourse.bass_utils import run_bass_kernel_spmd

F32 = mybir.dt.float32
BF16 = mybir.dt.bfloat16
ALU = mybir.AluOpType
AF = mybir.ActivationFunctionType
ENGS = ["pe", "act", "dve", "pool", "sp"]
NDMASEM = 16

L = 4
D = 2048
KC = 16
NCTX = 256
NLAT = 2048
T = NCTX + NLAT
TILES = [(0, 256), (256, 512), (768, 512), (1280, 512), (1792, 512)]
SEGS = [(0, 256), (256, 2304)]
DFF = 5632
IN_COLS = 5376
EPS = 1e-6
RW0 = 2560
ML0 = 4544
DECAY_MAX = math.exp(-0.5)


class Buf:
    __slots__ = ("name", "h", "writer", "readers", "excl")

    def __init__(self, name, h):
        self.name = name
        self.h = h
        self.writer = None
        self.readers = {}
        self.excl = False

    def __getitem__(self, idx):
        return self.h[idx]


class Pool:
    def __init__(self, bufs):
        self.bufs = bufs
        self.i = 0
        self.held = set()

    def next(self):
        while True:
            b = self.bufs[self.i % len(self.bufs)]
            self.i += 1
            if b.name not in self.held:
                return b

    def hold(self):
        b = self.next()
        self.held.add(b.name)
        return b

    def release(self, b):
        self.held.discard(b.name)


class Prog:
    def __init__(self):
        self.nc = bass.Bass("TRN2", target_bir_lowering=False)
        self.ops = {e: [] for e in ENGS}
        self.qcount = {}
        self.seen = {e: {} for e in ENGS}
        self.sems = {}
        self._stack = []
        self.floor = {}
        self.banks = None
        self.dma_idx = {}

    def sb(self, name, shape, dt=F32):
        g = self.nc.sbuf_tensor(name, list(shape), dt)
        h = g.__enter__()
        self._stack.append(g)
        return Buf(name, h)

    def pool(self, name, shape, dt, n):
        return Pool([self.sb("%s%d" % (name, i), shape, dt) for i in range(n)])

    def dram(self, name, shape, dt=F32, kind="Internal"):
        h = self.nc.dram_tensor(name, list(shape), dt, kind=kind)
        return Buf(name, h.ap())

    def view(self, buf, name=None):
        return Buf(name or buf.name, buf.h)

    def init_psum(self):
        bl = []
        for i in range(8):
            g = self.nc.psum_tensor("psb%d" % i, [128, 512], F32)
            h = g.__enter__()
            self._stack.append(g)
            b_ = Buf("psb%d" % i, h)
            b_.excl = True
            bl.append(b_)
        self.banks = Pool(bl)

    def psb(self):
        return self.banks.next()

    def hold(self):
        return self.banks.hold()

    def release(self, b):
        self.banks.release(b)

    def barrier(self):
        self.floor = dict(self.qcount)

    def emit(self, eng, fn, reads=(), writes=(), dma=False):
        if dma:
            k = self.dma_idx.get(eng, 0)
            self.dma_idx[eng] = k + 1
            q = "dma_%s_%d" % (eng, k % NDMASEM)
        else:
            q = eng
        excl = [b for b in reads if b.excl]
        if excl:
            reads = [b for b in reads if not b.excl]
            writes = list(writes) + excl
        deps = dict(self.floor)
        if dma and self.qcount.get(q, 0) > 0:
            deps[q] = self.qcount[q]

        def need(w):
            if w is None:
                return
            qq, c = w
            if qq == q and q == "pe":
                return
            if deps.get(qq, 0) < c:
                deps[qq] = c

        for b in reads:
            need(b.writer)
        for b in writes:
            need(b.writer)
            for qq, c in b.readers.items():
                need((qq, c))
        waits = []
        seen = self.seen[eng]
        for qq, c in deps.items():
            if qq == "pe" and q == "pe":
                continue
            if seen.get(qq, 0) < c:
                seen[qq] = c
                waits.append((qq, c))
        inc = 16 if dma else 1
        cnt = self.qcount.get(q, 0) + inc
        self.qcount[q] = cnt
        self.ops[eng].append((waits, fn, q, inc))
        for b in reads:
            b.readers[q] = cnt
        for b in writes:
            b.writer = (q, cnt)
            b.readers = {}
        return cnt

    def mm(self, out, lhsT, rhs, start, stop, R, W):
        self.emit("pe", lambda e: e.matmul(out, lhsT, rhs, start=start, stop=stop), R, W)

    def tr(self, out, in_, ident, R, W):
        self.emit("pe", lambda e: e.transpose(out, in_, ident), R, W)

    def act(self, out, in_, func, R, W, scale=1.0, bias=0.0):
        self.emit("act", lambda e: e.activation(out=out, in_=in_, func=func, scale=scale, bias=bias), R, W)

    def tt(self, out, in0, in1, op, R, W, eng="dve"):
        self.emit(eng, lambda e: e.tensor_tensor(out=out, in0=in0, in1=in1, op=op), R, W)

    def ts(self, out, in0, s1, s2, op0, op1, R, W, eng="dve"):
        self.emit(eng, lambda e: e.tensor_scalar(out=out, in0=in0, scalar1=s1, scalar2=s2, op0=op0, op1=op1), R, W)

    def stt(self, out, in0, scalar, in1, op0, op1, R, W):
        self.emit("dve", lambda e: e.scalar_tensor_tensor(out=out, in0=in0, scalar=scalar, in1=in1, op0=op0, op1=op1), R, W)

    def cp(self, out, in_, R, W, eng="dve"):
        if eng == "act":
            self.emit("act", lambda e: e.activation(out=out, in_=in_, func=AF.Copy), R, W)
        else:
            self.emit(eng, lambda e: e.tensor_copy(out=out, in_=in_), R, W)

    def recip(self, out, in_, R, W):
        self.emit("dve", lambda e: e.reciprocal(out=out, in_=in_), R, W)

    def scan(self, out, d0, d1, R, W):
        self.emit("dve", lambda e: e.tensor_tensor_scan(out=out, data0=d0, data1=d1, initial=0.0,
                                                       op0=ALU.mult, op1=ALU.add), R, W)

    def memset(self, ap, val, W, eng="dve"):
        self.emit(eng, lambda e: e.memset(ap, val), (), W)

    def dma(self, eng, out, in_, R, W):
        self.emit(eng, lambda e: e.dma_start(out=out, in_=in_), R, W, dma=True)

    def build(self):
        nc = self.nc
        qs = sorted(self.qcount.keys())
        guards = []
        for q in qs:
            g = nc.semaphore("s_" + q)
            self.sems[q] = g.__enter__()
            guards.append(g)
        ops, sems, qcount = self.ops, self.sems, self.qcount

        def run(engname):
            def body(e):
                for waits, fn, q, inc in ops[engname]:
                    for qq, c in waits:
                        e.wait_ge(sems[qq], c)
                    fn(e).then_inc(sems[q], inc)
                if engname == "sp":
                    for q in qs:
                        e.wait_ge(sems[q], qcount[q])
            return body

        with nc.Block() as block:
            block.sync(run("sp"))
            block.tensor(run("pe"))
            block.scalar(run("act"))
            block.vector(run("dve"))
            block.gpsimd(run("pool"))
        for g in guards:
            g.__exit__(None, None, None)
        while self._stack:
            self._stack.pop().__exit__(None, None, None)
        return nc


V128 = {}
_o = 0
for _n, _w in [("ng1", 16), ("ng2", 16), ("bm", 96), ("hgn", 1), ("mu", 12), ("w0", 8), ("a0", 8), ("kk", 4),
               ("ka", 4), ("rk", 4), ("gnw", 4), ("gnb", 4), ("qnorm", 4), ("kvnorm", 2), ("qn_g", 1),
               ("kn_g", 1), ("dw", 264), ("db", 88), ("mul", 4), ("mug", 1), ("qr_g", 1), ("kr_g", 1)]:
    V128[_n] = (_o, _w)
    _o += _w
NV = _o

C128 = {}
_o = 0
for _n, _w in [("ident", 128), ("ones", 128), ("blk", 128), ("tri_i", 128), ("tri_s", 128), ("tri_sT", 128),
               ("m16", 16), ("rs16", 512), ("rs128", 512), ("rotT", 64)]:
    C128[_n] = (_o, _w)
    _o += _w
NCST = _o


def fm(v, nch):
    return np.ascontiguousarray(np.asarray(v, np.float32).reshape(nch, 128).T)


def host_consts():
    c = np.zeros((128, NCST), np.float32)

    def put(n, a):
        o, w = C128[n]
        c[:a.shape[0], o:o + a.shape[1]] = a

    put("ident", np.eye(128, dtype=np.float32))
    put("ones", np.ones((128, 128), np.float32))
    blk = np.zeros((128, 128), np.float32)
    blk[:64, :64] = 1
    blk[64:, 64:] = 1
    put("blk", blk)
    put("tri_i", np.triu(np.ones((128, 128), np.float32)))
    put("tri_s", np.triu(np.ones((128, 128), np.float32), 1))
    put("tri_sT", np.tril(np.ones((128, 128), np.float32), -1))
    put("m16", np.triu(np.ones((16, 16), np.float32)))
    r16 = np.ones((128, 512), np.float32)
    r16[:, ::16] = 0
    put("rs16", r16)
    r128 = np.ones((128, 512), np.float32)
    r128[:, ::128] = 0
    put("rs128", r128)
    rot = np.zeros((64, 64), np.float32)
    rot[:32, 32:] = -np.eye(32)
    rot[32:, :32] = np.eye(32)
    put("rotT", np.ascontiguousarray(rot.T))
    rows = NLAT // 64
    row = np.repeat(np.arange(rows, dtype=np.float32), 64)
    col = np.tile(np.arange(64, dtype=np.float32), rows)
    inv = (10000.0 ** (-np.arange(0, 32, 2, dtype=np.float32) / 32)).astype(np.float32)
    ang = np.concatenate([row[:, None] * inv, col[:, None] * inv], -1).astype(np.float32)
    cos, sin = np.cos(ang).astype(np.float32), np.sin(ang).astype(np.float32)
    cs = np.zeros((2, 64, NLAT), np.float32)
    cs[0] = np.concatenate([cos.T, cos.T], 0)
    cs[1] = np.concatenate([sin.T, sin.T], 0)
    return c, cs


def host_vecs(inp):
    v = np.zeros((L, 128, NV), np.float32)

    def put(l, n, a):
        o, w = V128[n]
        a = np.asarray(a, np.float32)
        v[l, :a.shape[0], o:o + a.shape[1]] = a

    for l in range(L):
        put(l, "ng1", fm(inp["norm_g"][l, 0], 16))
        put(l, "ng2", fm(inp["norm_g"][l, 1], 16))
        put(l, "bm", fm(inp["b_mod"][l], 96))
        put(l, "hgn", inp["hg_gn"][l][:, None])
        put(l, "mu", fm(inp["rw_mu"][l, :1536], 12))
        put(l, "w0", np.concatenate([fm(inp["rw_w0"][l, 0], 4), fm(inp["rw_w0"][l, 1], 4)], 1))
        put(l, "a0", np.concatenate([fm(inp["rw_a0"][l, 0], 4), fm(inp["rw_a0"][l, 1], 4)], 1))
        put(l, "kk", fm(inp["rw_kk"][l], 4))
        put(l, "ka", fm(inp["rw_ka"][l], 4))
        put(l, "rk", fm(inp["rw_rk"][l].reshape(-1), 4))
        put(l, "gnw", fm(inp["rw_gn_w"][l], 4))
        put(l, "gnb", fm(inp["rw_gn_b"][l], 4))
        put(l, "qnorm", fm(inp["mla_q_norm"][l], 4))
        put(l, "kvnorm", fm(inp["mla_kv_norm"][l], 2))
        put(l, "qn_g", inp["mla_qn_g"][l][:, None])
        put(l, "kn_g", inp["mla_kn_g"][l][:, None])
        dw = inp["ffn_dw"][l]
        put(l, "dw", np.concatenate([fm(dw[0], 88), fm(dw[1], 88), fm(dw[2], 88)], 1))
        put(l, "db", fm(inp["ffn_db"][l], 88))
        mu = inp["rw_mu"][l]
        put(l, "mul", np.stack([mu[1536 + 96 * i:1536 + 96 * (i + 1)] for i in range(4)], 1))
        put(l, "mug", mu[1920:1984][:, None])
        put(l, "qr_g", inp["mla_qr_g"][l][:, None])
        put(l, "kr_g", inp["mla_kr_g"][l][:, None])
    lb = np.zeros((128, 4, 2, L), np.float32)
    for l in range(L):
        for d in range(2):
            lb[:, :, d, l] = fm(inp["hg_lb"][l, d], 4)
    return v, lb.reshape(128, 4 * 2 * L)


class ArenaMem:
    def __init__(self, P, name, nwords):
        self.buf = P.sb(name, [128, nwords], F32)
        self.n = nwords
        self.off = 0


class Arena:
    def __init__(self, mem, dt):
        self.mem = mem
        self.dt = dt

    def reset(self):
        self.mem.off = 0

    def alloc(self, name, shape):
        size = int(np.prod(shape))
        words = size if self.dt == F32 else (size + 1) // 2
        words = (words + 3) // 4 * 4
        m = self.mem
        assert m.off + words <= m.n, (name, m.off, words, m.n)
        ap = m.buf.h[:, m.off:m.off + words]
        m.off += words
        if self.dt != F32:
            ap = ap.bitcast(self.dt)
        ap = ap[:, 0:size]
        if len(shape) == 2:
            ap = ap.rearrange("p (a b) -> p a b", b=shape[1])
        elif len(shape) == 3:
            ap = ap.rearrange("p (a b c) -> p a b c", b=shape[1], c=shape[2])
        return Buf(name, ap)


def scan_tiles(d):
    if d == 0:
        return [(t0, n, False) for (t0, n) in TILES]
    return [(0, 256, True)] + [(t0, 512, True) for t0 in (1792, 1280, 768, 256)]


def tsl(ap2d, t0, n, rev):
    a = ap2d[:, t0:t0 + n]
    return a[:, ::-1] if rev else a


def build_program(nl=L, dbg=()):
    P = Prog()
    P.init_psum()
    EI = "ExternalInput"
    xT_in = P.dram("xT", [D, T], F32, EI)
    cT_in = P.dram("cT", [128, 32], F32, EI)
    vec_in = P.dram("vecs", [L, 128, NV], F32, EI)
    lb_in = P.dram("hglb", [128, 4 * 2 * L], F32, EI)
    cst_in = P.dram("cst", [128, NCST], F32, EI)
    cs_in = P.dram("rope", [2, 64, NLAT], F32, EI)
    w_mod = P.dram("w_mod", [L, D, 6 * D], F32, EI)
    w_in = P.dram("w_in", [L, D, IN_COLS], F32, EI)
    w_out = P.dram("w_out", [L, D, D], F32, EI)
    rw_w2 = P.dram("rw_w2", [L, 2, 96, 512], F32, EI)
    rw_a2 = P.dram("rw_a2", [L, 2, 96, 512], F32, EI)
    rw_g2 = P.dram("rw_g2", [L, 64, 512], F32, EI)
    w_uq = P.dram("mla_w_uq", [L, 512, 1536], F32, EI)
    w_ukv = P.dram("mla_w_ukv", [L, 256, 2048], F32, EI)
    ffn_up = P.dram("ffn_up", [L, D, 2 * DFF], F32, EI)
    ffn_down = P.dram("ffn_down", [L, DFF, D], F32, EI)
    out = P.dram("out", [D, NLAT], F32, "ExternalOutput")
    dbg_out = {}
    for n, shp in dbg:
        dbg_out[n] = P.dram("dbg_" + n, shp, F32, "ExternalOutput")

    xs = P.dram("xs", [D, T], F32)
    yT = P.dram("yT", [D, T], BF16)
    zT = P.dram("zT", [DFF, T], BF16)
    rwaux = P.dram("rwaux", [5, 512, T], F32)
    xs_v = [[P.view(xs, "xs_%d_%d" % (m, ti)) for ti in range(5)] for m in range(KC)]
    xin_v = [[P.view(xT_in, "xin_%d_%d" % (m, ti)) for ti in range(5)] for m in range(KC)]
    yT_v = [[P.view(yT, "yT_%d_%d" % (m, ti)) for ti in range(5)] for m in range(KC)]
    zT_v = [P.view(zT, "zT_%d" % j) for j in range(44)]
    aux_v = [[P.view(rwaux, "aux_%d_%d" % (a, m)) for m in range(4)] for a in range(5)]
    out_v = [[P.view(out, "out_%d_%d" % (m, ti)) for ti in range(5)] for m in range(KC)]

    cst = P.sb("cst_s", [128, NCST], F32)
    cstb = P.sb("cst_b", [128, NCST], BF16)
    P.dma("sp", cst[:], cst_in[:], [cst_in], [cst])
    P.cp(cstb[:], cst[:], [cst], [cstb])

    def C(n, rows=128, bf=False, c0=0, c1=None):
        o, w = C128[n]
        c1 = w if c1 is None else c1
        return (cstb if bf else cst)[0:rows, o + c0:o + c1]

    hT = P.sb("hT", [128, KC, T], BF16)
    hT_v = [P.view(hT, "hT_%d" % ti) for ti in range(5)]
    vec = P.sb("vec", [128, NV], F32)
    modT = P.sb("modT", [128, 96, 2], F32)
    mA = P.sb("mA", [128, 2, KC, 2], F32)
    lbt = P.sb("lbt", [128, 4, 2, L], F32)
    omlb = P.sb("omlb", [128, 4, 2, L], F32)
    cT = P.sb("cTs", [128, 32], F32)
    sT = P.sb("sTs", [128, KC, 2], BF16)
    epsD = P.sb("epsD", [128, 4], F32)
    wpool = P.pool("wp", [128, KC, 128], BF16, 3)
    _mem = ArenaMem(P, "arena", 27600)
    AF_ = Arena(_mem, F32)
    AB_ = Arena(_mem, BF16)

    def V(n, rows=128, c0=0, c1=None):
        o, w = V128[n]
        c1 = w if c1 is None else c1
        return vec[0:rows, o + c0:o + c1]

    P.memset(_mem.buf[:, :], 0.0, [_mem.buf])
    P.memset(hT[:].rearrange('p k t -> p (k t)'), 0.0, [hT])
    P.memset(epsD[:, 0:1], EPS, [epsD])
    P.memset(epsD[:, 1:2], 64e-5, [epsD])
    P.memset(epsD[:, 2:3], 1e-24, [epsD])
    P.memset(epsD[:, 3:4], 0.0, [epsD])

    P.dma("sp", cT[:], cT_in[:], [cT_in], [cT])
    P.act(sT[:].rearrange("p k g -> p (k g)"), cT[:], AF.Silu, [cT], [sT])

    lbe = P.sb("lbe", [128, 8, L], F32)
    lbs = P.sb("lbs", [128, 8], F32)
    P.dma("sp", lbe[:].rearrange("p a l -> p (a l)"), lb_in[:], [lb_in], [lbe])
    P.act(lbe[:], lbe[:], AF.Exp, [lbe], [lbe])
    P.emit("dve", lambda e: e.reduce_sum(out=lbs[:], in_=lbe[:], axis=mybir.AxisListType.X), [lbe], [lbs])
    P.recip(lbs[:], lbs[:], [lbs], [lbs])
    P.tt(lbe[:], lbe[:], lbs[:].unsqueeze(2).to_broadcast([128, 8, L]), ALU.mult, [lbe, lbs], [lbe])
    lb3 = lbt[:].rearrange("p h d l -> p (h d) l")
    P.memset(lb3[:, :, 0:1], 0.0, [lbt])
    for l in range(1, L):
        P.tt(lb3[:, :, l:l + 1], lb3[:, :, l - 1:l], lbe[:, :, l:l + 1], ALU.add, [lbt, lbe], [lbt])
    P.ts(omlb[:], lbt[:], -1.0, 1.0, ALU.mult, ALU.add, [lbt], [omlb])

    widx = [0]

    def load_w(src_buf, src_ap, rows=128, kc=KC, m=128):
        wb = wpool.next()
        P.dma("pool", wb[0:rows, 0:kc, 0:m], src_ap, [src_buf], [wb])
        return wb

    def x_src(l, m, ti):
        return (xin_v if l == 0 else xs_v)[m][ti], (xT_in if l == 0 else xs)

    def stage_mod(l):
        P.dma("sp", vec[:], vec_in[l], [vec_in], [vec])
        for j in range(96):
            wb = load_w(w_mod, w_mod[l][:, j * 128:(j + 1) * 128].rearrange("(k p) n -> p k n", p=128))
            ps = P.psb()
            for k in range(KC):
                P.mm(ps[:, 0:2], wb[:, k, :], sT[:, k, :], k == 0, k == KC - 1, [wb, sT], [ps])
            P.ts(modT[:, j, :], ps[:, 0:2], V("bm", c0=j, c1=j + 1), None, ALU.add, ALU.bypass, [ps, vec], [modT])
        for i, (ng, sc0) in enumerate([("ng1", 16), ("ng2", 64)]):
            P.ts(mA[:, i, :, :], modT[:, sc0:sc0 + 16, :], 1.0, None, ALU.add, ALU.bypass, [modT], [mA])
            P.tt(mA[:, i, :, :], mA[:, i, :, :], V(ng).unsqueeze(2).to_broadcast([128, 16, 2]), ALU.mult,
                 [mA, vec], [mA])

    def stage_norm(l, i):
        P.barrier()
        AF_.reset()
        AB_.reset()
        xts = [AF_.alloc("xt%d" % b, [KC, 512]) for b in range(2)]
        sqs = [AB_.alloc("sq%d" % b, [KC, 512]) for b in range(2)]
        rss = [AF_.alloc("rs%d" % b, [512]) for b in range(2)]
        sh0 = 0 if i == 0 else 48
        for ti, (t0, n) in enumerate(TILES):
            g = 1 if ti == 0 else 0
            xt, sq, rs = xts[ti % 2], sqs[ti % 2], rss[ti % 2]
            srcs = [x_src(l if i == 0 else 99, m, ti)[0] for m in range(KC)]
            src = xT_in if (l == 0 and i == 0) else xs
            P.dma("sp", xt[:, :, 0:n], src[:, t0:t0 + n].rearrange("(k p) t -> p k t", p=128), srcs, [xt])
            P.act(sq[:, :, 0:n], xt[:, :, 0:n], AF.Square, [xt], [sq])
            ps = P.psb()
            for k in range(KC):
                P.mm(ps[:, 0:n], C("ones", bf=True), sq[:, k, 0:n], k == 0, k == KC - 1, [cstb, sq], [ps])
            P.act(rs[:, 0:n], ps[:, 0:n], AF.Sqrt, [ps, epsD], [rs], scale=1.0 / D, bias=epsD[:, 0:1])
            P.recip(rs[:, 0:n], rs[:, 0:n], [rs], [rs])
            P.tt(xt[:, :, 0:n], xt[:, :, 0:n], rs[:, 0:n].unsqueeze(1).to_broadcast([128, KC, n]), ALU.mult,
                 [xt, rs], [xt])
            for k in range(KC):
                P.act(hT[:, k, t0:t0 + n], xt[:, k, 0:n], AF.Identity, [xt, mA, modT], [hT_v[ti]],
                      scale=mA[:, i, k, g:g + 1], bias=modT[:, sh0 + k, g:g + 1])

    def proj(l, c0, m, evac, wsrc=None, tiles=None):
        wb = load_w(w_in, w_in[l][:, c0:c0 + m].rearrange("(k p) n -> p k n", p=128), m=m)
        for ti, (t0, n) in enumerate(TILES):
            ps = P.psb()
            for k in range(KC):
                P.mm(ps[0:m, 0:n], wb[:, k, 0:m], hT[:, k, t0:t0 + n], k == 0, k == KC - 1, [wb, hT_v[ti]], [ps])
            evac(ps, ti, t0, n)

    def proj_full(l, c0, m, dst, eng="act"):
        def ev(ps, ti, t0, n):
            P.cp(dst[0:m, t0:t0 + n], ps[0:m, 0:n], [ps], [dst], eng=eng)
        proj(l, c0, m, ev)

    def rms_over_partitions(src_ap, n, rows, ones_ap, inv_count, eps_col, sqb, rsb, R):
        P.act(sqb[0:rows, 0:n], src_ap, AF.Square, R, [sqb])
        ps = P.psb()
        P.mm(ps[0:rows, 0:n], ones_ap, sqb[0:rows, 0:n], True, True, [cstb, sqb], [ps])
        P.act(rsb[0:rows, 0:n], ps[0:rows, 0:n], AF.Sqrt, [ps, epsD], [rsb], scale=inv_count,
              bias=epsD[0:rows, eps_col:eps_col + 1])
        P.recip(rsb[0:rows, 0:n], rsb[0:rows, 0:n], [rsb], [rsb])

    def stage_out(l):
        P.barrier()
        AF_.reset()
        AB_.reset()
        yts = [AB_.alloc("yt%d" % b, [KC, 512]) for b in range(2)]
        xps = Pool([AF_.alloc("xo%d" % b, [512]) for b in range(4)])
        for ti, (t0, n) in enumerate(TILES):
            g = 1 if ti == 0 else 0
            yt = yts[ti % 2]
            P.dma("sp", yt[:, :, 0:n], yT[:, t0:t0 + n].rearrange("(k p) t -> p k t", p=128),
                  [yT_v[m][ti] for m in range(KC)], [yt])
            for m in range(KC):
                wb = load_w(w_out, w_out[l][:, m * 128:(m + 1) * 128].rearrange("(k p) n -> p k n", p=128))
                xv, xsrc = x_src(l, m, ti)
                xo = xps.next()
                P.dma("sp", xo[:, 0:n], xsrc[m * 128:(m + 1) * 128, t0:t0 + n], [xv], [xo])
                ps = P.psb()
                for k in range(KC):
                    P.mm(ps[:, 0:n], wb[:, k, :], yt[:, k, 0:n], k == 0, k == KC - 1, [wb, yt], [ps])
                P.stt(xo[:, 0:n], ps[:, 0:n], modT[:, 32 + m, g:g + 1], xo[:, 0:n], ALU.mult, ALU.add,
                      [ps, modT, xo], [xo])
                P.dma("sp", xs[m * 128:(m + 1) * 128, t0:t0 + n], xo[:, 0:n], [xo], [xs_v[m][ti]])

    def stage_ffn(l, last):
        P.barrier()
        AF_.reset()
        AB_.reset()
        ups = [AF_.alloc("up%d" % b, [T]) for b in range(2)]
        cvs = [AF_.alloc("cv%d" % b, [T]) for b in range(2)]
        zb = Pool([AB_.alloc("zb%d" % b, [T]) for b in range(2)])
        dwo, _ = V128["dw"]
        for j in range(44):
            for half in range(2):
                cidx = j + 44 * half
                up, cv = ups[half], cvs[half]
                wb = load_w(ffn_up, ffn_up[l][:, cidx * 128:(cidx + 1) * 128].rearrange("(k p) n -> p k n", p=128))
                for ti, (t0, n) in enumerate(TILES):
                    ps = P.psb()
                    for k in range(KC):
                        P.mm(ps[:, 0:n], wb[:, k, :], hT[:, k, t0:t0 + n], k == 0, k == KC - 1, [wb, hT_v[ti]], [ps])
                    P.cp(up[:, t0:t0 + n], ps[:, 0:n], [ps], [up], eng="act" if ti % 2 else "dve")
                P.act(cv[:, :], up[:, :], AF.Identity, [up, vec], [cv],
                      scale=vec[:, dwo + 88 + cidx:dwo + 88 + cidx + 1], bias=V("db", c0=cidx, c1=cidx + 1))
                for (s0, s1) in SEGS:
                    P.stt(cv[:, s0 + 1:s1], up[:, s0:s1 - 1], vec[:, dwo + cidx:dwo + cidx + 1], cv[:, s0 + 1:s1],
                          ALU.mult, ALU.add, [up, vec, cv], [cv])
                    P.stt(cv[:, s0:s1 - 1], up[:, s0 + 1:s1], vec[:, dwo + 176 + cidx:dwo + 176 + cidx + 1],
                          cv[:, s0:s1 - 1], ALU.mult, ALU.add, [up, vec, cv], [cv])
            z = zb.next()
            P.act(cvs[0][:, :], cvs[0][:, :], AF.Silu, [cvs[0]], [cvs[0]])
            P.tt(z[:, :], cvs[0][:, :], cvs[1][:, :], ALU.mult, [cvs[0], cvs[1]], [z])
            P.dma("sp", zT[j * 128:(j + 1) * 128, :], z[:, :], [z], [zT_v[j]])
        P.barrier()
        AF_.reset()
        AB_.reset()
        zt = AB_.alloc("zt", [44, 512])
        wdp = Pool([AB_.alloc("wd%d" % b, [1024]) for b in range(3)])
        xps = Pool([AF_.alloc("xo%d" % b, [512]) for b in range(4)])
        for ti, (t0, n) in enumerate(TILES):
            g = 1 if ti == 0 else 0
            if last and ti == 0:
                continue
            P.dma("sp", zt[:, :, 0:n], zT[:, t0:t0 + n].rearrange("(k p) t -> p k t", p=128), zT_v, [zt])
            for half in range(2):
                pss = [P.psb() for _ in range(8)]
                for k in range(44):
                    wd = wdp.next()
                    P.dma("pool", wd[:, :], ffn_down[l][k * 128:(k + 1) * 128, half * 1024:(half + 1) * 1024],
                          [ffn_down], [wd])
                    for mm_ in range(8):
                        P.mm(pss[mm_][:, 0:n], wd[:, mm_ * 128:(mm_ + 1) * 128], zt[:, k, 0:n], k == 0, k == 43,
                             [wd, zt], [pss[mm_]])
                for mm_ in range(8):
                    m = half * 8 + mm_
                    xo = xps.next()
                    P.dma("sp", xo[:, 0:n], xs[m * 128:(m + 1) * 128, t0:t0 + n], [xs_v[m][ti]], [xo])
                    P.stt(xo[:, 0:n], pss[mm_][:, 0:n], modT[:, 80 + m, g:g + 1], xo[:, 0:n], ALU.mult, ALU.add,
                          [pss[mm_], modT, xo], [xo])
                    if last:
                        P.dma("sp", out[m * 128:(m + 1) * 128, t0 - NCTX:t0 - NCTX + n], xo[:, 0:n], [xo],
                              [out_v[m][ti]])
                    else:
                        P.dma("sp", xs[m * 128:(m + 1) * 128, t0:t0 + n], xo[:, 0:n], [xo], [xs_v[m][ti]])

    def token_shift(pf, sf, m, mu_ap):
        for (s0, s1) in SEGS:
            P.tt(sf[0:m, s0 + 1:s1 - 1], pf[0:m, s0:s1 - 2], pf[0:m, s0 + 2:s1], ALU.add, [pf], [sf])
            P.cp(sf[0:m, s0:s0 + 1], pf[0:m, s0 + 1:s0 + 2], [pf], [sf])
            P.cp(sf[0:m, s1 - 1:s1], pf[0:m, s1 - 2:s1 - 1], [pf], [sf])
        P.stt(sf[0:m, :], sf[0:m, :], 0.5, pf[0:m, :], ALU.mult, ALU.subtract, [sf, pf], [sf])
        P.stt(sf[0:m, :], sf[0:m, :], mu_ap, pf[0:m, :], ALU.mult, ALU.add, [sf, pf, vec], [sf])

    def proj_to(l, c0, m, buf, dst_fn, eng="act"):
        def ev(ps, ti, t0, n):
            P.cp(dst_fn(t0, n), ps[0:m, 0:n], [ps], [buf], eng=eng)
        proj(l, c0, m, ev)

    MLA_SCALE = 192.0 ** -0.5

    def mixer_mla(l):
        P.barrier()
        AF_.reset()
        cq = AB_.alloc("cq", [4, T])
        ckv = AB_.alloc("ckv", [2, T])
        krT = AB_.alloc("krT", [T])
        mark = _mem.off
        scr = AF_.alloc("scr", [4, T])
        sq4 = AB_.alloc("sq4", [4, 512])
        rs1 = AF_.alloc("rs1", [512])
        tA = AF_.alloc("tA", [512])
        tB = AF_.alloc("tB", [512])
        csT = AF_.alloc("csT", [2, 512])
        qb = AB_.alloc("qb", [512])
        sq1p = AB_.alloc("sq1p", [512])

        def rope(dst_bf, dst_buf, src32, src_buf, t0, n):
            P.dma("sp", csT[0:64, :, 0:n], cs_in[:, :, t0 - NCTX:t0 - NCTX + n].rearrange("c p t -> p c t"),
                  [cs_in], [csT])
            P.cp(qb[0:64, 0:n], src32, [src_buf], [qb], eng="act")
            ps = P.psb()
            P.mm(ps[0:64, 0:n], C("rotT", rows=64, bf=True), qb[0:64, 0:n], True, True, [cstb, qb], [ps])
            P.tt(tB[0:64, 0:n], ps[0:64, 0:n], csT[0:64, 1, 0:n], ALU.mult, [ps, csT], [tB])
            P.tt(src32, src32, csT[0:64, 0, 0:n], ALU.mult, [src_buf, csT], [src_buf])
            P.tt(dst_bf, src32, tB[0:64, 0:n], ALU.add, [src_buf, tB], [dst_buf])

        def norm_chunks(src3, nk, dstb, gname, cnt):
            for ti, (t0, n) in enumerate(TILES):
                P.act(sq4[:, 0:nk, 0:n], src3[:, 0:nk, t0:t0 + n], AF.Square, [scr], [sq4])
                ps = P.psb()
                for k in range(nk):
                    P.mm(ps[:, 0:n], C("ones", bf=True), sq4[:, k, 0:n], k == 0, k == nk - 1, [cstb, sq4], [ps])
                P.act(rs1[:, 0:n], ps[:, 0:n], AF.Sqrt, [ps, epsD], [rs1], scale=1.0 / cnt, bias=epsD[:, 0:1])
                P.recip(rs1[:, 0:n], rs1[:, 0:n], [rs1], [rs1])
                for k in range(nk):
                    P.stt(dstb[:, k, t0:t0 + n], src3[:, k, t0:t0 + n], V(gname, c0=k, c1=k + 1), rs1[:, 0:n],
                          ALU.mult, ALU.mult, [scr, vec, rs1], [dstb])

        for k in range(4):
            proj_to(l, ML0 + k * 128, 128, scr, lambda t0, n, k=k: scr[:, k, t0:t0 + n])
        norm_chunks(scr, 4, cq, "qnorm", 512)
        for k in range(2):
            proj_to(l, ML0 + 512 + k * 128, 128, scr, lambda t0, n, k=k: scr[:, k, t0:t0 + n])
        proj_to(l, ML0 + 768, 64, scr, lambda t0, n: scr[0:64, 2, t0:t0 + n])
        norm_chunks(scr, 2, ckv, "kvnorm", 256)
        for ti, (t0, n) in enumerate(TILES):
            rms_over_partitions(scr[0:64, 2, t0:t0 + n], n, 64, C("ones", rows=64, bf=True, c1=64), 1.0 / 64, 0,
                                sq1p, rs1, [scr])
            P.stt(tA[0:64, 0:n], scr[0:64, 2, t0:t0 + n], V("kr_g", rows=64), rs1[0:64, 0:n], ALU.mult, ALU.mult,
                  [scr, vec, rs1], [tA])
            if ti == 0:
                P.cp(krT[0:64, t0:t0 + n], tA[0:64, 0:n], [tA], [krT])
            else:
                rope(krT[0:64, t0:t0 + n], krT, tA[0:64, 0:n], tA, t0, n)
        P.barrier()
        _mem.off = mark
        P.memset(_mem.buf[:, mark:_mem.n], 0.0, [_mem.buf])
        P.barrier()
        qnT = AB_.alloc("qnT", [T])
        qrT = AB_.alloc("qrT", [T])
        knT = AB_.alloc("knT", [T])
        Vt = AB_.alloc("Vt", [18, 128])
        wq = AB_.alloc("wq", [4, 192])
        wkv = AB_.alloc("wkv", [2, 256])
        sq1 = AB_.alloc("sq1", [512])
        rs1 = AF_.alloc("rs1b", [512])
        tA = AF_.alloc("tAb", [512])
        tB = AF_.alloc("tBb", [512])
        csT = AF_.alloc("csTb", [2, 512])
        qb = AB_.alloc("qbb", [512])
        ptp = Pool([AB_.alloc("pt%d" % b, [512]) for b in range(3)])
        yb = Pool([AB_.alloc("yb%d" % b, [512]) for b in range(2)])
        for h in range(8):
            P.dma("pool", wq[:, :, :], w_uq[l][:, h * 192:(h + 1) * 192].rearrange("(k p) n -> p k n", p=128),
                  [w_uq], [wq])
            P.dma("pool", wkv[:, :, :], w_ukv[l][:, h * 256:(h + 1) * 256].rearrange("(k p) n -> p k n", p=128),
                  [w_ukv], [wkv])
            for ti, (t0, n) in enumerate(TILES):
                ps = P.psb()
                for k in range(4):
                    P.mm(ps[:, 0:n], wq[:, k, 0:128], cq[:, k, t0:t0 + n], k == 0, k == 3, [wq, cq], [ps])
                rms_over_partitions(ps[:, 0:n], n, 128, C("ones", bf=True), 1.0 / 128, 0, sq1, rs1, [ps])
                P.stt(qnT[:, t0:t0 + n], ps[:, 0:n], V("qn_g"), rs1[:, 0:n], ALU.mult, ALU.mult, [ps, vec, rs1], [qnT])
                ps = P.psb()
                for k in range(4):
                    P.mm(ps[0:64, 0:n], wq[:, k, 128:192], cq[:, k, t0:t0 + n], k == 0, k == 3, [wq, cq], [ps])
                rms_over_partitions(ps[0:64, 0:n], n, 64, C("ones", rows=64, bf=True, c1=64), 1.0 / 64, 0, sq1, rs1, [ps])
                P.stt(tA[0:64, 0:n], ps[0:64, 0:n], V("qr_g", rows=64), rs1[0:64, 0:n], ALU.mult, ALU.mult,
                      [ps, vec, rs1], [tA])
                if ti == 0:
                    P.cp(qrT[0:64, t0:t0 + n], tA[0:64, 0:n], [tA], [qrT])
                else:
                    rope(qrT[0:64, t0:t0 + n], qrT, tA[0:64, 0:n], tA, t0, n)
                ps = P.psb()
                for k in range(2):
                    P.mm(ps[:, 0:n], wkv[:, k, 0:128], ckv[:, k, t0:t0 + n], k == 0, k == 1, [wkv, ckv], [ps])
                rms_over_partitions(ps[:, 0:n], n, 128, C("ones", bf=True), 1.0 / 128, 0, sq1, rs1, [ps])
                P.stt(knT[:, t0:t0 + n], ps[:, 0:n], V("kn_g"), rs1[:, 0:n], ALU.mult, ALU.mult, [ps, vec, rs1], [knT])
            for b0 in range(0, 18, 4):
                nb = min(4, 18 - b0)
                ps = P.psb()
                for j in range(nb):
                    b = b0 + j
                    for k in range(2):
                        P.mm(ps[:, j * 128:(j + 1) * 128], ckv[:, k, b * 128:(b + 1) * 128], wkv[:, k, 128:256],
                             k == 0, k == 1, [ckv, wkv], [ps])
                P.cp(Vt[:, b0:b0 + nb, :], ps[:, 0:nb * 128].rearrange("p (a b) -> p a b", b=128), [ps], [Vt], eng="act")
            for ti, (t0, n) in enumerate(TILES):
                nkb = 2 if ti == 0 else 18
                pso = P.hold()
                psd = P.hold()
                for kb in range(nkb):
                    pss = P.psb()
                    ks = slice(kb * 128, (kb + 1) * 128)
                    P.mm(pss[:, 0:n], knT[:, ks], qnT[:, t0:t0 + n], True, False, [knT, qnT], [pss])
                    P.mm(pss[:, 0:n], krT[0:64, ks], qrT[0:64, t0:t0 + n], False, True, [krT, qrT], [pss])
                    pt = ptp.next()
                    P.act(pt[:, 0:n], pss[:, 0:n], AF.Exp, [pss], [pt], scale=MLA_SCALE)
                    P.mm(pso[:, 0:n], Vt[:, kb, :], pt[:, 0:n], kb == 0, kb == nkb - 1, [Vt, pt], [pso])
                    P.mm(psd[:, 0:n], C("ones", bf=True), pt[:, 0:n], kb == 0, kb == nkb - 1, [cstb, pt], [psd])
                P.recip(tA[:, 0:n], psd[:, 0:n], [psd], [tA])
                y = yb.next()
                P.tt(y[:, 0:n], pso[:, 0:n], tA[:, 0:n], ALU.mult, [pso, tA], [y])
                P.release(pso)
                P.release(psd)
                P.dma("sp", yT[(8 + h) * 128:(9 + h) * 128, t0:t0 + n], y[:, 0:n], [y], [yT_v[8 + h][ti]])
    def mixer_hgrn2(l):
        for h in range(4):
            P.barrier()
            AF_.reset()
            qf = AF_.alloc("qf", [T])
            vf = AF_.alloc("vf", [T])
            zfs = [AF_.alloc("zf%d" % d_, [T]) for d_ in range(2)]
            oTs = [AF_.alloc("oT%d" % d_, [T]) for d_ in range(2)]
            sq1 = AB_.alloc("sq1", [512])
            yb = Pool([AB_.alloc("yb%d" % b, [512]) for b in range(2)])
            proj_to(l, h * 128, 128, qf, lambda t0, n: qf[:, t0:t0 + n])
            proj_to(l, 1536 + h * 128, 128, vf, lambda t0, n: vf[:, t0:t0 + n])
            identb = C("ident", bf=True)

            def dir_gen(d):
                zf, oT = zfs[d], oTs[d]
                t1 = AF_.alloc("t1_%d" % d, [512])
                t2 = AF_.alloc("t2_%d" % d, [512])
                t3 = AF_.alloc("t3_%d" % d, [512])
                Fc = AF_.alloc("Fc%d" % d, [32])
                qt = AB_.alloc("qt%d" % d, [512])
                kt = AB_.alloc("kt%d" % d, [512])
                kb = AB_.alloc("kb%d" % d, [512])
                vb = AB_.alloc("vb%d" % d, [512])
                kvp = Pool([AB_.alloc("kvT%d_%d" % (d, b), [8, 128]) for b in range(3)])
                atp = Pool([AB_.alloc("aT%d_%d" % (d, b), [4, 16]) for b in range(3)])
                S32 = [AF_.alloc("S32_%d_%d" % (d, b), [128]) for b in range(2)]
                Sbf = Pool([AB_.alloc("Sbf%d_%d" % (d, b), [128]) for b in range(3)])
                proj_to(l, 512 * (1 + d) + h * 128, 128, zf, lambda t0, n: zf[:, t0:t0 + n])
                yield
                state = None
                cidx = 0
                for (t0, n, rev) in scan_tiles(d):
                    nch = n // 16
                    P.act(t1[:, 0:n], tsl(zf, t0, n, rev), AF.Sigmoid, [zf], [t1])
                    P.ts(t1[:, 0:n], t1[:, 0:n], omlb[:, h, d, l:l + 1], lbt[:, h, d, l:l + 1], ALU.mult, ALU.add,
                         [t1, omlb, lbt], [t1])
                    P.act(t2[:, 0:n], t1[:, 0:n], AF.Ln, [t1], [t2])
                    P.ts(t1[:, 0:n], t1[:, 0:n], -1.0, 1.0, ALU.mult, ALU.add, [t1], [t1])
                    P.scan(t3[:, 0:n], C("rs16", c1=n), t2[:, 0:n], [cst, t2], [t3])
                    P.act(t2[:, 0:n], t3[:, 0:n], AF.Exp, [t3], [t2])
                    P.tt(qt[:, 0:n], tsl(qf, t0, n, rev), t2[:, 0:n], ALU.mult, [qf, t2], [qt])
                    P.act(t2[:, 0:n], t3[:, 0:n], AF.Exp, [t3], [t2], scale=-1.0)
                    P.tt(kt[:, 0:n], t1[:, 0:n], t2[:, 0:n], ALU.mult, [t1, t2], [kt])
                    G3 = t3[:, 0:n].rearrange("p (c i) -> p c i", i=16)
                    P.tt(t2[:, 0:n].rearrange("p (c i) -> p c i", i=16), G3[:, :, 15:16].to_broadcast([128, nch, 16]),
                         G3, ALU.subtract, [t3], [t2])
                    P.act(t2[:, 0:n], t2[:, 0:n], AF.Exp, [t2], [t2])
                    P.tt(kb[:, 0:n], t1[:, 0:n], t2[:, 0:n], ALU.mult, [t1, t2], [kb])
                    P.act(Fc[:, 0:nch], G3[:, :, 15], AF.Exp, [t3], [Fc])
                    P.cp(vb[:, 0:n], tsl(vf, t0, n, rev), [vf], [vb])
                    yield
                    pso = P.hold()
                    for g0 in range(0, nch, 4):
                        psT = P.psb()
                        psTb = psT[:].bitcast(BF16)
                        for j in range(4):
                            cs = slice((g0 + j) * 16, (g0 + j + 1) * 16)
                            P.tr(psTb[0:16, j * 128:(j + 1) * 128], kb[:, cs], identb, [kb, cstb], [psT])
                            P.tr(psTb[0:16, (4 + j) * 128:(5 + j) * 128], vb[:, cs], identb, [vb, cstb], [psT])
                        kvT = kvp.next()
                        P.cp(kvT[0:16, :, :].rearrange("p a b -> p (a b)"), psTb[0:16, :], [psT], [kvT], eng="act")
                        psA = P.psb()
                        for j in range(4):
                            cs = slice((g0 + j) * 16, (g0 + j + 1) * 16)
                            P.mm(psA[0:16, j * 16:(j + 1) * 16], kt[:, cs], qt[:, cs], True, True, [kt, qt], [psA])
                        aT = atp.next()
                        P.tt(aT[0:16, :, :], psA[0:16, 0:64].rearrange("p (a b) -> p a b", b=16),
                             C("m16", rows=16).unsqueeze(1).to_broadcast([16, 4, 16]), ALU.mult, [psA, cst], [aT])
                        yield
                        psG = P.psb()
                        for j in range(4):
                            P.mm(psG[:, j * 128:(j + 1) * 128], kvT[0:16, j, :], kvT[0:16, 4 + j, :], True, True,
                                 [kvT], [psG])
                        for j in range(4):
                            c = g0 + j
                            cs = slice(c * 16, (c + 1) * 16)
                            P.mm(pso[:, cs], kvT[0:16, 4 + j, :], aT[0:16, j, :], True, state is None, [kvT, aT], [pso])
                            if state is not None:
                                P.mm(pso[:, cs], state[1][:, :], qt[:, cs], False, True, [state[1], qt], [pso])
                            new32 = S32[cidx % 2]
                            if state is None:
                                P.cp(new32[:, :], psG[:, j * 128:(j + 1) * 128], [psG], [new32])
                            else:
                                P.stt(new32[:, :], state[0][:, :], Fc[:, c:c + 1], psG[:, j * 128:(j + 1) * 128],
                                      ALU.mult, ALU.add, [state[0], Fc, psG], [new32])
                            nbf = Sbf.next()
                            P.cp(nbf[:, :], new32[:, :], [new32], [nbf], eng="act")
                            state = (new32, nbf)
                            cidx += 1
                            yield
                    P.cp(tsl(oT, t0, n, rev), pso[:, 0:n], [pso], [oT])
                    P.release(pso)

            live = [dir_gen(0), dir_gen(1)]
            while live:
                for g_ in list(live):
                    try:
                        next(g_)
                    except StopIteration:
                        live.remove(g_)
            zf = zfs[0]
            t1 = AF_.alloc("t1o", [512])
            t2 = AF_.alloc("t2o", [512])
            t3 = AF_.alloc("t3o", [512])
            proj_to(l, 2048 + h * 128, 128, zf, lambda t0, n: zf[:, t0:t0 + n])
            for ti, (t0, n) in enumerate(TILES):
                P.tt(t3[:, 0:n], oTs[0][:, t0:t0 + n], oTs[1][:, t0:t0 + n], ALU.add, [oTs[0], oTs[1]], [t3])
                rms_over_partitions(t3[:, 0:n], n, 128, C("ones", bf=True), 1.0 / 128, 0, sq1, t1, [t3])
                P.stt(t2[:, 0:n], t3[:, 0:n], V("hgn"), t1[:, 0:n], ALU.mult, ALU.mult, [t3, vec, t1], [t2])
                P.act(t3[:, 0:n], zf[:, t0:t0 + n], AF.Silu, [zf], [t3])
                y = yb.next()
                P.tt(y[:, 0:n], t2[:, 0:n], t3[:, 0:n], ALU.mult, [t2, t3], [y])
                P.dma("sp", yT[h * 128:(h + 1) * 128, t0:t0 + n], y[:, 0:n], [y], [yT_v[h][ti]])

    def rwkv_lora(l):
        P.barrier()
        AF_.reset()
        pf = AF_.alloc("pf", [T])
        sf = AF_.alloc("sf", [T])
        xb = AB_.alloc("xb", [T])
        w2b = AB_.alloc("w2b", [512])
        otp = Pool([AF_.alloc("ot%d" % b, [512]) for b in range(3)])
        groups = [(1536, 96, "w", 0, 0), (1632, 96, "w", 1, 1), (1728, 96, "a", 0, 2), (1824, 96, "a", 1, 3),
                  (1920, 64, "g", 0, 4)]
        for gi, (c0, m, kind, d, ai) in enumerate(groups):
            proj_to(l, RW0 + c0, m, pf, lambda t0, n, m=m: pf[0:m, t0:t0 + n])
            mu_ap = V("mul", rows=96, c0=gi, c1=gi + 1) if kind != "g" else V("mug", rows=64)
            token_shift(pf, sf, m, mu_ap)
            func = AF.Tanh if kind == "w" else (AF.Identity if kind == "a" else AF.Sigmoid)
            P.act(xb[0:m, :], sf[0:m, :], func, [sf], [xb])
            if kind == "w":
                wsrc, wb_ = rw_w2[l][d], rw_w2
            elif kind == "a":
                wsrc, wb_ = rw_a2[l][d], rw_a2
            else:
                wsrc, wb_ = rw_g2[l], rw_g2
            P.dma("pool", w2b[0:m, :], wsrc, [wb_], [w2b])
            for mc in range(4):
                for ti, (t0, n) in enumerate(TILES):
                    ps = P.psb()
                    P.mm(ps[:, 0:n], w2b[0:m, mc * 128:(mc + 1) * 128], xb[0:m, t0:t0 + n], True, True, [w2b, xb], [ps])
                    o = otp.next()
                    if kind == "w":
                        P.act(o[:, 0:n], ps[:, 0:n], AF.Sigmoid, [ps, vec], [o], bias=V("w0", c0=d * 4 + mc, c1=d * 4 + mc + 1))
                        P.ts(o[:, 0:n], o[:, 0:n], -DECAY_MAX, None, ALU.mult, ALU.bypass, [o], [o])
                    elif kind == "a":
                        P.act(o[:, 0:n], ps[:, 0:n], AF.Sigmoid, [ps, vec], [o], bias=V("a0", c0=d * 4 + mc, c1=d * 4 + mc + 1))
                    else:
                        P.cp(o[:, 0:n], ps[:, 0:n], [ps], [o])
                    P.dma("sp", rwaux[ai][mc * 128:(mc + 1) * 128, t0:t0 + n], o[:, 0:n], [o], [aux_v[ai][mc]])

    def mixer_rwkv(l):
        rwkv_lora(l)
        for m in range(4):
            P.barrier()
            AF_.reset()
            rf = AF_.alloc("rf", [T])
            kf = AF_.alloc("kf", [T])
            vf = AF_.alloc("vf", [T])
            kkf = AF_.alloc("kkf", [T])
            oT = AF_.alloc("oT", [T])
            pf = oT
            tlw = AF_.alloc("tlw", [512])
            tag = AF_.alloc("tag", [512])
            tcum = AF_.alloc("tcum", [512])
            te = AF_.alloc("te", [512])
            te2 = AF_.alloc("te2", [512])
            tb_ = AF_.alloc("tb", [512])
            tkd = AF_.alloc("tkd", [512])
            WC = AF_.alloc("WC", [4])
            omka = AF_.alloc("omka", [1])
            S32 = [AF_.alloc("S32_%d" % b, [64]) for b in range(2)]
            rt = AB_.alloc("rt", [512])
            bt = AB_.alloc("bt", [512])
            kt = AB_.alloc("kt", [512])
            at = AB_.alloc("at", [512])
            bb = AB_.alloc("bb", [512])
            kb = AB_.alloc("kb", [512])
            vb = AB_.alloc("vb", [512])
            sq1 = AB_.alloc("sq1", [512])
            atm = [AB_.alloc("atm%d" % b, [512]) for b in range(2)]
            rtm = [AB_.alloc("rtm%d" % b, [512]) for b in range(2)]
            for b_ in atm + rtm:
                P.memset(b_[:, :], 0.0, [b_])
            class Slot:
                pass
            slots = []
            for si in range(2):
                S_ = Slot()
                S_.Tp = Pool([AF_.alloc("Tj%d_%d" % (si, b), [2, 128]) for b in range(2)])
                S_.Lp = Pool([AF_.alloc("Lj%d_%d" % (si, b), [2, 128]) for b in range(2)])
                S_.Zp = Pool([AF_.alloc("Z%d_%d" % (si, b), [2, 128]) for b in range(3)])
                S_.tmpP = AF_.alloc("tmpP%d" % si, [128])
                S_.Gp = AF_.alloc("Gp%d" % si, [64])
                S_.Tak = AB_.alloc("Tak%d" % si, [2, 128])
                S_.Trb = AB_.alloc("Trb%d" % si, [2, 128])
                S_.Trk = AB_.alloc("Trk%d" % si, [2, 128])
                S_.tm = AB_.alloc("tm%d" % si, [4, 128])
                S_.Z1c = AB_.alloc("Z1c%d" % si, [128])
                S_.Z2c = AB_.alloc("Z2c%d" % si, [128])
                S_.PhiT = AB_.alloc("PhiT%d" % si, [128])
                S_.QeTm = [AB_.alloc("QeTm%d_%d" % (si, b), [128]) for b in range(2)]
                for b_ in S_.QeTm:
                    P.memset(b_[:, :], 0.0, [b_])
                slots.append(S_)
            identf = C("ident")
            Sbf = Pool([AB_.alloc("Sbf%d" % b, [64]) for b in range(3)])
            yb = Pool([AB_.alloc("yb%d" % b, [512]) for b in range(2)])
            identb = C("ident", bf=True)
            for (c0, dst, mi) in [(0, rf, m), (512, kf, 4 + m), (1024, vf, 8 + m)]:
                proj_to(l, RW0 + c0 + m * 128, 128, pf, lambda t0, n: pf[:, t0:t0 + n])
                token_shift(pf, dst, 128, V("mu", c0=mi, c1=mi + 1))
            P.ts(omka[:, 0:1], V("ka", c0=m, c1=m + 1), -1.0, 1.0, ALU.mult, ALU.add, [vec], [omka])
            for ti, (t0, n) in enumerate(TILES):
                P.ts(te[:, 0:n], kf[:, t0:t0 + n], V("kk", c0=m, c1=m + 1), None, ALU.mult, ALU.bypass, [kf, vec], [te])
                P.act(sq1[:, 0:n], te[:, 0:n], AF.Square, [te], [sq1])
                ps = P.psb()
                P.mm(ps[:, 0:n], C("blk", bf=True), sq1[:, 0:n], True, True, [cstb, sq1], [ps])
                P.act(te2[:, 0:n], ps[:, 0:n], AF.Sqrt, [ps, epsD], [te2], bias=epsD[:, 2:3])
                P.recip(te2[:, 0:n], te2[:, 0:n], [te2], [te2])
                P.tt(kkf[:, t0:t0 + n], te[:, 0:n], te2[:, 0:n], ALU.mult, [te, te2], [kkf])
            for d in range(2):
                state = None
                cidx = 0
                for (t0, n, rev) in scan_tiles(d):
                    nch = n // 128

                    def S(b_):
                        a_ = b_[:, 0:n]
                        return a_[:, ::-1] if rev else a_
                    P.dma("sp", tlw[:, 0:n], rwaux[d][m * 128:(m + 1) * 128, t0:t0 + n], [aux_v[d][m]], [tlw])
                    P.dma("sp", tag[:, 0:n], rwaux[2 + d][m * 128:(m + 1) * 128, t0:t0 + n], [aux_v[2 + d][m]], [tag])
                    P.scan(tcum[:, 0:n], C("rs128", c1=n), S(tlw), [cst, tlw], [tcum])
                    P.act(te[:, 0:n], tcum[:, 0:n], AF.Exp, [tcum], [te])
                    P.tt(rt[:, 0:n], tsl(rf, t0, n, rev), te[:, 0:n], ALU.mult, [rf, te], [rt])
                    P.act(te[:, 0:n], tcum[:, 0:n], AF.Exp, [tcum], [te], scale=-1.0)
                    P.tt(tb_[:, 0:n], tsl(kkf, t0, n, rev), S(tag), ALU.mult, [kkf, tag], [tb_])
                    P.tt(bt[:, 0:n], tb_[:, 0:n], te[:, 0:n], ALU.mult, [tb_, te], [bt])
                    P.ts(tkd[:, 0:n], S(tag), V("ka", c0=m, c1=m + 1), omka[:, 0:1], ALU.mult, ALU.add, [tag, vec, omka], [tkd])
                    P.tt(tkd[:, 0:n], tkd[:, 0:n], tsl(kf, t0, n, rev), ALU.mult, [tkd, kf], [tkd])
                    P.tt(kt[:, 0:n], tkd[:, 0:n], te[:, 0:n], ALU.mult, [tkd, te], [kt])
                    P.tt(te2[:, 0:n], tcum[:, 0:n], S(tlw), ALU.subtract, [tcum, tlw], [te2])
                    P.act(te2[:, 0:n], te2[:, 0:n], AF.Exp, [te2], [te2])
                    P.stt(at[:, 0:n], tsl(kkf, t0, n, rev), -1.0, te2[:, 0:n], ALU.mult, ALU.mult, [kkf, te2], [at])
                    c3 = tcum[:, 0:n].rearrange("p (c i) -> p c i", i=128)
                    P.tt(te2[:, 0:n].rearrange("p (c i) -> p c i", i=128), c3[:, :, 127:128].to_broadcast([128, nch, 128]),
                         c3, ALU.subtract, [tcum], [te2])
                    P.act(te2[:, 0:n], te2[:, 0:n], AF.Exp, [te2], [te2])
                    P.tt(bb[:, 0:n], tb_[:, 0:n], te2[:, 0:n], ALU.mult, [tb_, te2], [bb])
                    P.tt(kb[:, 0:n], tkd[:, 0:n], te2[:, 0:n], ALU.mult, [tkd, te2], [kb])
                    P.cp(vb[:, 0:n], tsl(vf, t0, n, rev), [vf], [vb])
                    P.act(WC[:, 0:nch], c3[:, :, 127], AF.Exp, [tcum], [WC])
                    for hh in range(2):
                        pr = slice(hh * 64, hh * 64 + 64)
                        P.cp(atm[hh][pr, 0:n], at[pr, 0:n], [at], [atm[hh]])
                        P.cp(rtm[hh][pr, 0:n], rt[pr, 0:n], [rt], [rtm[hh]], eng="act")
                    psO = P.hold()

                    def v3(ap):
                        return ap.rearrange("p (a b) -> p a b", b=128)

                    def msk(nm):
                        return C(nm).unsqueeze(1).to_broadcast([128, 2, 128])

                    def chunk_pre(c, S_):
                        cs = slice(c * 128, (c + 1) * 128)
                        A_, B_, C_ = P.psb(), P.psb(), P.psb()
                        for hh in range(2):
                            o0, o1 = hh * 128, 256 + hh * 128
                            am, rm = atm[hh], rtm[hh]
                            P.mm(A_[:, o0:o0 + 128], bt[:, cs], am[:, cs], True, True, [bt, am], [A_])
                            P.mm(A_[:, o1:o1 + 128], am[:, cs], bt[:, cs], True, True, [bt, am], [A_])
                            P.mm(B_[:, o0:o0 + 128], kt[:, cs], am[:, cs], True, True, [kt, am], [B_])
                            P.mm(B_[:, o1:o1 + 128], bt[:, cs], rm[:, cs], True, True, [bt, rm], [B_])
                            P.mm(C_[:, o0:o0 + 128], kt[:, cs], rm[:, cs], True, True, [kt, rm], [C_])
                        psT = P.psb()
                        psTb = psT[:].bitcast(BF16)
                        for i_, src in enumerate([bb, kb, vb, at]):
                            P.tr(psTb[:, i_ * 128:(i_ + 1) * 128], src[:, cs], identb, [src, cstb], [psT])
                        Tj, Lj = S_.Tp.next(), S_.Lp.next()
                        P.tt(Tj[:, :, :], v3(A_[:, 0:256]), msk("tri_s"), ALU.mult, [A_, cst], [Tj])
                        P.tt(Lj[:, :, :], v3(A_[:, 256:512]), msk("tri_sT"), ALU.mult, [A_, cst], [Lj])
                        P.tt(S_.Tak[:, :, :], v3(B_[:, 0:256]), msk("tri_s"), ALU.mult, [B_, cst], [S_.Tak])
                        P.tt(S_.Trb[:, :, :], v3(B_[:, 256:512]), msk("tri_i"), ALU.mult, [B_, cst], [S_.Trb])
                        P.tt(S_.Trk[:, :, :], v3(C_[:, 0:256]), msk("tri_i"), ALU.mult, [C_, cst], [S_.Trk])
                        tm = S_.tm
                        P.cp(tm[:, :, :].rearrange("p a b -> p (a b)"), psTb[:, 0:512], [psT], [tm], eng="act")
                        yield
                        psX = P.psb()
                        for hh in range(2):
                            P.mm(psX[:, hh * 64:(hh + 1) * 64], S_.Tak[:, hh, :], tm[:, 2, hh * 64:(hh + 1) * 64], True, True,
                                 [S_.Tak, tm], [psX])
                        Z = S_.Zp.next()
                        P.cp(Z[:, :, 0:64], tm[:, 3, :].rearrange("p (a b) -> p a b", b=64), [tm], [Z], eng="act")
                        P.cp(Z[:, :, 64:128], psX[:, 0:128].rearrange("p (a b) -> p a b", b=64), [psX], [Z])
                        yield
                        for j in range(7):
                            psZ = P.psb()
                            for hh in range(2):
                                o0 = hh * 128
                                P.mm(psZ[:, o0:o0 + 128], Tj[:, hh, :], Z[:, hh, :], True, False, [Tj, Z], [psZ])
                                P.mm(psZ[:, o0:o0 + 128], identf, Z[:, hh, :], False, True, [cst, Z], [psZ])
                            if j < 6:
                                psS = P.psb()
                                for hh in range(2):
                                    o0, o1 = hh * 128, 256 + hh * 128
                                    P.mm(psS[:, o0:o0 + 128], Lj[:, hh, :], Tj[:, hh, :], True, True, [Lj, Tj], [psS])
                                    P.mm(psS[:, o1:o1 + 128], Tj[:, hh, :], Lj[:, hh, :], True, True, [Lj, Tj], [psS])
                                Zn = S_.Zp.next()
                                P.cp(Zn[:, :, :], v3(psZ[:, 0:256]), [psZ], [Zn], eng="act")
                                Tn, Ln = S_.Tp.next(), S_.Lp.next()
                                P.cp(Tn[:, :, :], v3(psS[:, 0:256]), [psS], [Tn])
                                P.cp(Ln[:, :, :], v3(psS[:, 256:512]), [psS], [Ln], eng="act")
                                Z, Tj, Lj = Zn, Tn, Ln
                            else:
                                z3 = v3(psZ[:, 0:256])
                                P.cp(S_.Z1c[:, :].rearrange("p (a b) -> p a b", b=64), z3[:, :, 0:64], [psZ], [S_.Z1c], eng="act")
                                P.cp(S_.Z2c[:, :].rearrange("p (a b) -> p a b", b=64), z3[:, :, 64:128], [psZ], [S_.Z2c])
                            yield
                        psP = P.psb()
                        P.mm(psP[:, 0:128], S_.Z1c[:, :], tm[:, 0, :], True, True, [S_.Z1c, tm], [psP])
                        P.mm(psP[:, 128:256], tm[:, 0, :], S_.Z2c[:, :], True, False, [S_.Z2c, tm], [psP])
                        P.mm(psP[:, 128:256], tm[:, 1, :], tm[:, 2, :], False, True, [tm], [psP])
                        psQ = P.psb()
                        for hh in range(2):
                            pr = slice(hh * 64, hh * 64 + 64)
                            P.mm(psQ[pr, 0:128], S_.Z1c[:, pr], S_.Trb[:, hh, :], True, True, [S_.Z1c, S_.Trb], [psQ])
                        P.tt(S_.tmpP[:, :], psP[:, 0:128], C("blk"), ALU.mult, [psP, cst], [S_.tmpP])
                        P.stt(S_.PhiT[:, :], C("ident"), WC[:, c:c + 1], S_.tmpP[:, :], ALU.mult, ALU.add,
                              [cst, WC, S_.tmpP], [S_.PhiT])
                        P.cp(S_.Gp[0:64, :], psP[0:64, 128:192], [psP], [S_.Gp], eng="act")
                        P.cp(S_.Gp[64:128, :], psP[64:128, 192:256], [psP], [S_.Gp], eng="act")
                        for hh in range(2):
                            pr = slice(hh * 64, hh * 64 + 64)
                            P.tt(S_.QeTm[hh][pr, :], psQ[pr, 0:128], rt[pr, cs], ALU.add, [psQ, rt], [S_.QeTm[hh]])
                        yield

                    def chunk_post(c, S_, state, cidx):
                        cs = slice(c * 128, (c + 1) * 128)
                        tm = S_.tm
                        for hh in range(2):
                            pr = slice(hh * 64, hh * 64 + 64)
                            P.mm(psO[pr, cs], S_.Z2c[:, pr], S_.Trb[:, hh, :], True, False, [S_.Z2c, S_.Trb], [psO])
                            P.mm(psO[pr, cs], tm[:, 2, pr], S_.Trk[:, hh, :], False, state is None, [tm, S_.Trk], [psO])
                            if state is not None:
                                P.mm(psO[pr, cs], state[1][:, :], S_.QeTm[hh][:, :], False, True,
                                     [state[1], S_.QeTm[hh]], [psO])
                        new32 = S32[cidx % 2]
                        if state is None:
                            P.cp(new32[:, :], S_.Gp[:, :], [S_.Gp], [new32])
                        else:
                            psS2 = P.psb()
                            P.mm(psS2[:, 0:64], S_.PhiT[:, :], state[1][:, :], True, True, [S_.PhiT, state[1]], [psS2])
                            P.tt(new32[:, :], psS2[:, 0:64], S_.Gp[:, :], ALU.add, [psS2, S_.Gp], [new32])
                        nbf = Sbf.next()
                        P.cp(nbf[:, :], new32[:, :], [new32], [nbf], eng="act")
                        return (new32, nbf)

                    for c0_ in range(0, nch, 2):
                        cl = list(range(c0_, min(c0_ + 2, nch)))
                        gens = [chunk_pre(c, slots[c % 2]) for c in cl]
                        live = list(gens)
                        while live:
                            for g_ in list(live):
                                try:
                                    next(g_)
                                except StopIteration:
                                    live.remove(g_)
                        for c in cl:
                            state = chunk_post(c, slots[c % 2], state, cidx)
                            cidx += 1
                    if d == 0:
                        P.cp(oT[:, t0:t0 + n], psO[:, 0:n], [psO], [oT])
                    else:
                        P.tt(tsl(oT, t0, n, True), tsl(oT, t0, n, True), psO[:, 0:n], ALU.add, [oT, psO], [oT])
                    P.release(psO)
            for ti, (t0, n) in enumerate(TILES):
                osl = oT[:, t0:t0 + n]
                P.cp(sq1[:, 0:n], osl, [oT], [sq1], eng="act")
                ps = P.psb()
                P.mm(ps[:, 0:n], C("blk", bf=True), sq1[:, 0:n], True, True, [cstb, sq1], [ps])
                P.stt(te[:, 0:n], ps[:, 0:n], -1.0 / 64, osl, ALU.mult, ALU.add, [ps, oT], [te])
                P.act(sq1[:, 0:n], te[:, 0:n], AF.Square, [te], [sq1])
                ps = P.psb()
                P.mm(ps[:, 0:n], C("blk", bf=True), sq1[:, 0:n], True, True, [cstb, sq1], [ps])
                P.act(te2[:, 0:n], ps[:, 0:n], AF.Sqrt, [ps, epsD], [te2], scale=1.0 / 64, bias=epsD[:, 1:2])
                P.recip(te2[:, 0:n], te2[:, 0:n], [te2], [te2])
                P.tt(te[:, 0:n], te[:, 0:n], te2[:, 0:n], ALU.mult, [te, te2], [te])
                P.ts(te[:, 0:n], te[:, 0:n], V("gnw", c0=m, c1=m + 1), V("gnb", c0=m, c1=m + 1), ALU.mult, ALU.add,
                     [te, vec], [te])
                P.dma("sp", tlw[:, 0:n], rwaux[2][m * 128:(m + 1) * 128, t0:t0 + n], [aux_v[2][m]], [tlw])
                P.dma("sp", tag[:, 0:n], rwaux[3][m * 128:(m + 1) * 128, t0:t0 + n], [aux_v[3][m]], [tag])
                P.tt(tb_[:, 0:n], tlw[:, 0:n], tag[:, 0:n], ALU.add, [tlw, tag], [tb_])
                P.ts(tb_[:, 0:n], tb_[:, 0:n], -2.0, None, ALU.add, ALU.bypass, [tb_], [tb_])
                P.ts(tb_[:, 0:n], tb_[:, 0:n], V("ka", c0=m, c1=m + 1), 2.0, ALU.mult, ALU.add, [tb_, vec], [tb_])
                P.tt(tb_[:, 0:n], tb_[:, 0:n], kf[:, t0:t0 + n], ALU.mult, [tb_, kf], [tb_])
                P.tt(tb_[:, 0:n], tb_[:, 0:n], rf[:, t0:t0 + n], ALU.mult, [tb_, rf], [tb_])
                P.ts(sq1[:, 0:n], tb_[:, 0:n], V("rk", c0=m, c1=m + 1), None, ALU.mult, ALU.bypass, [tb_, vec], [sq1])
                ps = P.psb()
                P.mm(ps[:, 0:n], C("blk", bf=True), sq1[:, 0:n], True, True, [cstb, sq1], [ps])
                P.tt(tkd[:, 0:n], ps[:, 0:n], vf[:, t0:t0 + n], ALU.mult, [ps, vf], [tkd])
                P.tt(te[:, 0:n], te[:, 0:n], tkd[:, 0:n], ALU.add, [te, tkd], [te])
                P.dma("sp", tcum[:, 0:n], rwaux[4][m * 128:(m + 1) * 128, t0:t0 + n], [aux_v[4][m]], [tcum])
                y = yb.next()
                P.tt(y[:, 0:n], te[:, 0:n], tcum[:, 0:n], ALU.mult, [te, tcum], [y])
                P.dma("sp", yT[(4 + m) * 128:(5 + m) * 128, t0:t0 + n], y[:, 0:n], [y], [yT_v[4 + m][ti]])
    for l in range(nl):
        stage_mod(l)
        stage_norm(l, 0)
        mixer_hgrn2(l)
        mixer_rwkv(l)
        mixer_mla(l)
        stage_out(l)
        stage_norm(l, 1)
        stage_ffn(l, l == nl - 1)
    return P


_BIG = ["w_mod", "w_in", "w_out", "rw_w2", "rw_a2", "rw_g2", "mla_w_uq", "mla_w_ukv", "ffn_up", "ffn_down"]


def kernel(**inputs):
    inp = {k: np.asarray(v) for k, v in inputs.items()}
    P = build_program(L)
    nc = P.build()
    cst, cs = host_consts()
    vecs, lb = host_vecs(inp)
    big = {k: np.ascontiguousarray(inp[k], dtype=np.float32) for k in _BIG}
    in_maps = []
    for b in range(8):
        xT = np.ascontiguousarray(np.concatenate([inp["ctx"][b], inp["x"][b]], 0).T.astype(np.float32))
        cT = np.ascontiguousarray(np.stack([fm(inp["c"][b], 16), fm(inp["c_ctx"], 16)], -1).reshape(128, 32))
        in_maps.append(dict(xT=xT, cT=cT, vecs=vecs, hglb=lb, cst=cst, rope=cs, **big))
    res = run_bass_kernel_spmd(nc, in_maps, core_ids=list(range(8)))
    out = np.stack([np.ascontiguousarray(np.asarray(res.results[b]["out"]).T) for b in range(8)], 0)
    return out.astype(np.float32)
```

```python
import math
import numpy as np
import concourse.bass as bass
import concourse.mybir as mybir
from concourse.bass_utils import run_bass_kernel_spmd

F32 = mybir.dt.float32
BF16 = mybir.dt.bfloat16
ALU = mybir.AluOpType
AF = mybir.ActivationFunctionType
ENGS = ["pe", "act", "dve", "pool", "sp"]
NDMASEM = 16

L = 4
D = 2048
KC = 16
NCTX = 256
NLAT = 2048
T = NCTX + NLAT
TILES = [(0, 256), (256, 512), (768, 512), (1280, 512), (1792, 512)]
SEGS = [(0, 256), (256, 2304)]
DFF = 5632
IN_COLS = 5376
EPS = 1e-6
RW0 = 2560
ML0 = 4544
DECAY_MAX = math.exp(-0.5)


class Buf:
    __slots__ = ("name", "h", "writer", "readers", "excl")

    def __init__(self, name, h):
        self.name = name
        self.h = h
        self.writer = None
        self.readers = {}
        self.excl = False

    def __getitem__(self, idx):
        return self.h[idx]


class Pool:
    def __init__(self, bufs):
        self.bufs = bufs
        self.i = 0
        self.held = set()

    def next(self):
        while True:
            b = self.bufs[self.i % len(self.bufs)]
            self.i += 1
            if b.name not in self.held:
                return b

    def hold(self):
        b = self.next()
        self.held.add(b.name)
        return b

    def release(self, b):
        self.held.discard(b.name)


class Prog:
    def __init__(self):
        self.nc = bass.Bass("TRN2", target_bir_lowering=False)
        self.ops = {e: [] for e in ENGS}
        self.qcount = {}
        self.seen = {e: {} for e in ENGS}
        self.sems = {}
        self._stack = []
        self.floor = {}
        self.banks = None
        self.dma_idx = {}

    def sb(self, name, shape, dt=F32):
        g = self.nc.sbuf_tensor(name, list(shape), dt)
        h = g.__enter__()
        self._stack.append(g)
        return Buf(name, h)

    def pool(self, name, shape, dt, n):
        return Pool([self.sb("%s%d" % (name, i), shape, dt) for i in range(n)])

    def dram(self, name, shape, dt=F32, kind="Internal"):
        h = self.nc.dram_tensor(name, list(shape), dt, kind=kind)
        return Buf(name, h.ap())

    def view(self, buf, name=None):
        return Buf(name or buf.name, buf.h)

    def init_psum(self):
        bl = []
        for i in range(8):
            g = self.nc.psum_tensor("psb%d" % i, [128, 512], F32)
            h = g.__enter__()
            self._stack.append(g)
            b_ = Buf("psb%d" % i, h)
            b_.excl = True
            bl.append(b_)
        self.banks = Pool(bl)

    def psb(self):
        return self.banks.next()

    def hold(self):
        return self.banks.hold()

    def release(self, b):
        self.banks.release(b)

    def barrier(self):
        self.floor = dict(self.qcount)

    def emit(self, eng, fn, reads=(), writes=(), dma=False):
        if dma:
            k = self.dma_idx.get(eng, 0)
            self.dma_idx[eng] = k + 1
            q = "dma_%s_%d" % (eng, k % NDMASEM)
        else:
            q = eng
        excl = [b for b in reads if b.excl]
        if excl:
            reads = [b for b in reads if not b.excl]
            writes = list(writes) + excl
        deps = dict(self.floor)
        if dma and self.qcount.get(q, 0) > 0:
            deps[q] = self.qcount[q]

        def need(w):
            if w is None:
                return
            qq, c = w
            if qq == q and q == "pe":
                return
            if deps.get(qq, 0) < c:
                deps[qq] = c

        for b in reads:
            need(b.writer)
        for b in writes:
            need(b.writer)
            for qq, c in b.readers.items():
                need((qq, c))
        waits = []
        seen = self.seen[eng]
        for qq, c in deps.items():
            if qq == "pe" and q == "pe":
                continue
            if seen.get(qq, 0) < c:
                seen[qq] = c
                waits.append((qq, c))
        inc = 16 if dma else 1
        cnt = self.qcount.get(q, 0) + inc
        self.qcount[q] = cnt
        self.ops[eng].append((waits, fn, q, inc))
        for b in reads:
            b.readers[q] = cnt
        for b in writes:
            b.writer = (q, cnt)
            b.readers = {}
        return cnt

    def mm(self, out, lhsT, rhs, start, stop, R, W):
        self.emit("pe", lambda e: e.matmul(out, lhsT, rhs, start=start, stop=stop), R, W)

    def tr(self, out, in_, ident, R, W):
        self.emit("pe", lambda e: e.transpose(out, in_, ident), R, W)

    def act(self, out, in_, func, R, W, scale=1.0, bias=0.0):
        self.emit("act", lambda e: e.activation(out=out, in_=in_, func=func, scale=scale, bias=bias), R, W)

    def tt(self, out, in0, in1, op, R, W, eng="dve"):
        self.emit(eng, lambda e: e.tensor_tensor(out=out, in0=in0, in1=in1, op=op), R, W)

    def ts(self, out, in0, s1, s2, op0, op1, R, W, eng="dve"):
        self.emit(eng, lambda e: e.tensor_scalar(out=out, in0=in0, scalar1=s1, scalar2=s2, op0=op0, op1=op1), R, W)

    def stt(self, out, in0, scalar, in1, op0, op1, R, W):
        self.emit("dve", lambda e: e.scalar_tensor_tensor(out=out, in0=in0, scalar=scalar, in1=in1, op0=op0, op1=op1), R, W)

    def cp(self, out, in_, R, W, eng="dve"):
        if eng == "act":
            self.emit("act", lambda e: e.activation(out=out, in_=in_, func=AF.Copy), R, W)
        else:
            self.emit(eng, lambda e: e.tensor_copy(out=out, in_=in_), R, W)

    def recip(self, out, in_, R, W):
        self.emit("dve", lambda e: e.reciprocal(out=out, in_=in_), R, W)

    def scan(self, out, d0, d1, R, W):
        self.emit("dve", lambda e: e.tensor_tensor_scan(out=out, data0=d0, data1=d1, initial=0.0,
                                                       op0=ALU.mult, op1=ALU.add), R, W)

    def memset(self, ap, val, W, eng="dve"):
        self.emit(eng, lambda e: e.memset(ap, val), (), W)

    def dma(self, eng, out, in_, R, W):
        self.emit(eng, lambda e: e.dma_start(out=out, in_=in_), R, W, dma=True)

    def build(self):
        nc = self.nc
        qs = sorted(self.qcount.keys())
        guards = []
        for q in qs:
            g = nc.semaphore("s_" + q)
            self.sems[q] = g.__enter__()
            guards.append(g)
        ops, sems, qcount = self.ops, self.sems, self.qcount

        def run(engname):
            def body(e):
                for waits, fn, q, inc in ops[engname]:
                    for qq, c in waits:
                        e.wait_ge(sems[qq], c)
                    fn(e).then_inc(sems[q], inc)
                if engname == "sp":
                    for q in qs:
                        e.wait_ge(sems[q], qcount[q])
            return body

        with nc.Block() as block:
            block.sync(run("sp"))
            block.tensor(run("pe"))
            block.scalar(run("act"))
            block.vector(run("dve"))
            block.gpsimd(run("pool"))
        for g in guards:
            g.__exit__(None, None, None)
        while self._stack:
            self._stack.pop().__exit__(None, None, None)
        return nc


V128 = {}
_o = 0
for _n, _w in [("ng1", 16), ("ng2", 16), ("bm", 96), ("hgn", 1), ("mu", 12), ("w0", 8), ("a0", 8), ("kk", 4),
               ("ka", 4), ("rk", 4), ("gnw", 4), ("gnb", 4), ("qnorm", 4), ("kvnorm", 2), ("qn_g", 1),
               ("kn_g", 1), ("dw", 264), ("db", 88), ("mul", 4), ("mug", 1), ("qr_g", 1), ("kr_g", 1)]:
    V128[_n] = (_o, _w)
    _o += _w
NV = _o

C128 = {}
_o = 0
for _n, _w in [("ident", 128), ("ones", 128), ("blk", 128), ("tri_i", 128), ("tri_s", 128), ("tri_sT", 128),
               ("m16", 16), ("rs16", 512), ("rs128", 512), ("rotT", 64)]:
    C128[_n] = (_o, _w)
    _o += _w
NCST = _o


def fm(v, nch):
    return np.ascontiguousarray(np.asarray(v, np.float32).reshape(nch, 128).T)


def host_consts():
    c = np.zeros((128, NCST), np.float32)

    def put(n, a):
        o, w = C128[n]
        c[:a.shape[0], o:o + a.shape[1]] = a

    put("ident", np.eye(128, dtype=np.float32))
    put("ones", np.ones((128, 128), np.float32))
    blk = np.zeros((128, 128), np.float32)
    blk[:64, :64] = 1
    blk[64:, 64:] = 1
    put("blk", blk)
    put("tri_i", np.triu(np.ones((128, 128), np.float32)))
    put("tri_s", np.triu(np.ones((128, 128), np.float32), 1))
    put("tri_sT", np.tril(np.ones((128, 128), np.float32), -1))
    put("m16", np.triu(np.ones((16, 16), np.float32)))
    r16 = np.ones((128, 512), np.float32)
    r16[:, ::16] = 0
    put("rs16", r16)
    r128 = np.ones((128, 512), np.float32)
    r128[:, ::128] = 0
    put("rs128", r128)
    rot = np.zeros((64, 64), np.float32)
    rot[:32, 32:] = -np.eye(32)
    rot[32:, :32] = np.eye(32)
    put("rotT", np.ascontiguousarray(rot.T))
    rows = NLAT // 64
    row = np.repeat(np.arange(rows, dtype=np.float32), 64)
    col = np.tile(np.arange(64, dtype=np.float32), rows)
    inv = (10000.0 ** (-np.arange(0, 32, 2, dtype=np.float32) / 32)).astype(np.float32)
    ang = np.concatenate([row[:, None] * inv, col[:, None] * inv], -1).astype(np.float32)
    cos, sin = np.cos(ang).astype(np.float32), np.sin(ang).astype(np.float32)
    cs = np.zeros((2, 64, NLAT), np.float32)
    cs[0] = np.concatenate([cos.T, cos.T], 0)
    cs[1] = np.concatenate([sin.T, sin.T], 0)
    return c, cs


def host_vecs(inp):
    v = np.zeros((L, 128, NV), np.float32)

    def put(l, n, a):
        o, w = V128[n]
        a = np.asarray(a, np.float32)
        v[l, :a.shape[0], o:o + a.shape[1]] = a

    for l in range(L):
        put(l, "ng1", fm(inp["norm_g"][l, 0], 16))
        put(l, "ng2", fm(inp["norm_g"][l, 1], 16))
        put(l, "bm", fm(inp["b_mod"][l], 96))
        put(l, "hgn", inp["hg_gn"][l][:, None])
        put(l, "mu", fm(inp["rw_mu"][l, :1536], 12))
        put(l, "w0", np.concatenate([fm(inp["rw_w0"][l, 0], 4), fm(inp["rw_w0"][l, 1], 4)], 1))
        put(l, "a0", np.concatenate([fm(inp["rw_a0"][l, 0], 4), fm(inp["rw_a0"][l, 1], 4)], 1))
        put(l, "kk", fm(inp["rw_kk"][l], 4))
        put(l, "ka", fm(inp["rw_ka"][l], 4))
        put(l, "rk", fm(inp["rw_rk"][l].reshape(-1), 4))
        put(l, "gnw", fm(inp["rw_gn_w"][l], 4))
        put(l, "gnb", fm(inp["rw_gn_b"][l], 4))
        put(l, "qnorm", fm(inp["mla_q_norm"][l], 4))
        put(l, "kvnorm", fm(inp["mla_kv_norm"][l], 2))
        put(l, "qn_g", inp["mla_qn_g"][l][:, None])
        put(l, "kn_g", inp["mla_kn_g"][l][:, None])
        dw = inp["ffn_dw"][l]
        put(l, "dw", np.concatenate([fm(dw[0], 88), fm(dw[1], 88), fm(dw[2], 88)], 1))
        put(l, "db", fm(inp["ffn_db"][l], 88))
        mu = inp["rw_mu"][l]
        put(l, "mul", np.stack([mu[1536 + 96 * i:1536 + 96 * (i + 1)] for i in range(4)], 1))
        put(l, "mug", mu[1920:1984][:, None])
        put(l, "qr_g", inp["mla_qr_g"][l][:, None])
        put(l, "kr_g", inp["mla_kr_g"][l][:, None])
    lb = np.zeros((128, 4, 2, L), np.float32)
    for l in range(L):
        for d in range(2):
            lb[:, :, d, l] = fm(inp["hg_lb"][l, d], 4)
    return v, lb.reshape(128, 4 * 2 * L)


class ArenaMem:
    def __init__(self, P, name, nwords):
        self.buf = P.sb(name, [128, nwords], F32)
        self.n = nwords
        self.off = 0


class Arena:
    def __init__(self, mem, dt):
        self.mem = mem
        self.dt = dt

    def reset(self):
        self.mem.off = 0

    def alloc(self, name, shape):
        size = int(np.prod(shape))
        words = size if self.dt == F32 else (size + 1) // 2
        words = (words + 3) // 4 * 4
        m = self.mem
        assert m.off + words <= m.n, (name, m.off, words, m.n)
        ap = m.buf.h[:, m.off:m.off + words]
        m.off += words
        if self.dt != F32:
            ap = ap.bitcast(self.dt)
        ap = ap[:, 0:size]
        if len(shape) == 2:
            ap = ap.rearrange("p (a b) -> p a b", b=shape[1])
        elif len(shape) == 3:
            ap = ap.rearrange("p (a b c) -> p a b c", b=shape[1], c=shape[2])
        return Buf(name, ap)


def scan_tiles(d):
    if d == 0:
        return [(t0, n, False) for (t0, n) in TILES]
    return [(0, 256, True)] + [(t0, 512, True) for t0 in (1792, 1280, 768, 256)]


def tsl(ap2d, t0, n, rev):
    a = ap2d[:, t0:t0 + n]
    return a[:, ::-1] if rev else a


def build_program(nl=L, dbg=()):
    P = Prog()
    P.init_psum()
    EI = "ExternalInput"
    xT_in = P.dram("xT", [D, T], F32, EI)
    cT_in = P.dram("cT", [128, 32], F32, EI)
    vec_in = P.dram("vecs", [L, 128, NV], F32, EI)
    lb_in = P.dram("hglb", [128, 4 * 2 * L], F32, EI)
    cst_in = P.dram("cst", [128, NCST], F32, EI)
    cs_in = P.dram("rope", [2, 64, NLAT], F32, EI)
    w_mod = P.dram("w_mod", [L, D, 6 * D], F32, EI)
    w_in = P.dram("w_in", [L, D, IN_COLS], F32, EI)
    w_out = P.dram("w_out", [L, D, D], F32, EI)
    rw_w2 = P.dram("rw_w2", [L, 2, 96, 512], F32, EI)
    rw_a2 = P.dram("rw_a2", [L, 2, 96, 512], F32, EI)
    rw_g2 = P.dram("rw_g2", [L, 64, 512], F32, EI)
    w_uq = P.dram("mla_w_uq", [L, 512, 1536], F32, EI)
    w_ukv = P.dram("mla_w_ukv", [L, 256, 2048], F32, EI)
    ffn_up = P.dram("ffn_up", [L, D, 2 * DFF], F32, EI)
    ffn_down = P.dram("ffn_down", [L, DFF, D], F32, EI)
    out = P.dram("out", [D, NLAT], F32, "ExternalOutput")
    dbg_out = {}
    for n, shp in dbg:
        dbg_out[n] = P.dram("dbg_" + n, shp, F32, "ExternalOutput")

    xs = P.dram("xs", [D, T], F32)
    yT = P.dram("yT", [D, T], BF16)
    zT = P.dram("zT", [DFF, T], BF16)
    rwaux = P.dram("rwaux", [5, 512, T], F32)
    xs_v = [[P.view(xs, "xs_%d_%d" % (m, ti)) for ti in range(5)] for m in range(KC)]
    xin_v = [[P.view(xT_in, "xin_%d_%d" % (m, ti)) for ti in range(5)] for m in range(KC)]
    yT_v = [[P.view(yT, "yT_%d_%d" % (m, ti)) for ti in range(5)] for m in range(KC)]
    zT_v = [P.view(zT, "zT_%d" % j) for j in range(44)]
    aux_v = [[P.view(rwaux, "aux_%d_%d" % (a, m)) for m in range(4)] for a in range(5)]
    out_v = [[P.view(out, "out_%d_%d" % (m, ti)) for ti in range(5)] for m in range(KC)]

    cst = P.sb("cst_s", [128, NCST], F32)
    cstb = P.sb("cst_b", [128, NCST], BF16)
    P.dma("sp", cst[:], cst_in[:], [cst_in], [cst])
    P.cp(cstb[:], cst[:], [cst], [cstb])

    def C(n, rows=128, bf=False, c0=0, c1=None):
        o, w = C128[n]
        c1 = w if c1 is None else c1
        return (cstb if bf else cst)[0:rows, o + c0:o + c1]

    hT = P.sb("hT", [128, KC, T], BF16)
    hT_v = [P.view(hT, "hT_%d" % ti) for ti in range(5)]
    vec = P.sb("vec", [128, NV], F32)
    modT = P.sb("modT", [128, 96, 2], F32)
    mA = P.sb("mA", [128, 2, KC, 2], F32)
    lbt = P.sb("lbt", [128, 4, 2, L], F32)
    omlb = P.sb("omlb", [128, 4, 2, L], F32)
    cT = P.sb("cTs", [128, 32], F32)
    sT = P.sb("sTs", [128, KC, 2], BF16)
    epsD = P.sb("epsD", [128, 4], F32)
    wpool = P.pool("wp", [128, KC, 128], BF16, 3)
    _mem = ArenaMem(P, "arena", 27600)
    AF_ = Arena(_mem, F32)
    AB_ = Arena(_mem, BF16)

    def V(n, rows=128, c0=0, c1=None):
        o, w = V128[n]
        c1 = w if c1 is None else c1
        return vec[0:rows, o + c0:o + c1]

    P.memset(_mem.buf[:, :], 0.0, [_mem.buf])
    P.memset(hT[:].rearrange('p k t -> p (k t)'), 0.0, [hT])
    P.memset(epsD[:, 0:1], EPS, [epsD])
    P.memset(epsD[:, 1:2], 64e-5, [epsD])
    P.memset(epsD[:, 2:3], 1e-24, [epsD])
    P.memset(epsD[:, 3:4], 0.0, [epsD])

    P.dma("sp", cT[:], cT_in[:], [cT_in], [cT])
    P.act(sT[:].rearrange("p k g -> p (k g)"), cT[:], AF.Silu, [cT], [sT])

    lbe = P.sb("lbe", [128, 8, L], F32)
    lbs = P.sb("lbs", [128, 8], F32)
    P.dma("sp", lbe[:].rearrange("p a l -> p (a l)"), lb_in[:], [lb_in], [lbe])
    P.act(lbe[:], lbe[:], AF.Exp, [lbe], [lbe])
    P.emit("dve", lambda e: e.reduce_sum(out=lbs[:], in_=lbe[:], axis=mybir.AxisListType.X), [lbe], [lbs])
    P.recip(lbs[:], lbs[:], [lbs], [lbs])
    P.tt(lbe[:], lbe[:], lbs[:].unsqueeze(2).to_broadcast([128, 8, L]), ALU.mult, [lbe, lbs], [lbe])
    lb3 = lbt[:].rearrange("p h d l -> p (h d) l")
    P.memset(lb3[:, :, 0:1], 0.0, [lbt])
    for l in range(1, L):
        P.tt(lb3[:, :, l:l + 1], lb3[:, :, l - 1:l], lbe[:, :, l:l + 1], ALU.add, [lbt, lbe], [lbt])
    P.ts(omlb[:], lbt[:], -1.0, 1.0, ALU.mult, ALU.add, [lbt], [omlb])

    widx = [0]

    def load_w(src_buf, src_ap, rows=128, kc=KC, m=128):
        wb = wpool.next()
        P.dma("pool", wb[0:rows, 0:kc, 0:m], src_ap, [src_buf], [wb])
        return wb

    def x_src(l, m, ti):
        return (xin_v if l == 0 else xs_v)[m][ti], (xT_in if l == 0 else xs)

    def stage_mod(l):
        P.dma("sp", vec[:], vec_in[l], [vec_in], [vec])
        for j in range(96):
            wb = load_w(w_mod, w_mod[l][:, j * 128:(j + 1) * 128].rearrange("(k p) n -> p k n", p=128))
            ps = P.psb()
            for k in range(KC):
                P.mm(ps[:, 0:2], wb[:, k, :], sT[:, k, :], k == 0, k == KC - 1, [wb, sT], [ps])
            P.ts(modT[:, j, :], ps[:, 0:2], V("bm", c0=j, c1=j + 1), None, ALU.add, ALU.bypass, [ps, vec], [modT])
        for i, (ng, sc0) in enumerate([("ng1", 16), ("ng2", 64)]):
            P.ts(mA[:, i, :, :], modT[:, sc0:sc0 + 16, :], 1.0, None, ALU.add, ALU.bypass, [modT], [mA])
            P.tt(mA[:, i, :, :], mA[:, i, :, :], V(ng).unsqueeze(2).to_broadcast([128, 16, 2]), ALU.mult,
                 [mA, vec], [mA])

    def stage_norm(l, i):
        P.barrier()
        AF_.reset()
        AB_.reset()
        xts = [AF_.alloc("xt%d" % b, [KC, 512]) for b in range(2)]
        sqs = [AB_.alloc("sq%d" % b, [KC, 512]) for b in range(2)]
        rss = [AF_.alloc("rs%d" % b, [512]) for b in range(2)]
        sh0 = 0 if i == 0 else 48
        for ti, (t0, n) in enumerate(TILES):
            g = 1 if ti == 0 else 0
            xt, sq, rs = xts[ti % 2], sqs[ti % 2], rss[ti % 2]
            srcs = [x_src(l if i == 0 else 99, m, ti)[0] for m in range(KC)]
            src = xT_in if (l == 0 and i == 0) else xs
            P.dma("sp", xt[:, :, 0:n], src[:, t0:t0 + n].rearrange("(k p) t -> p k t", p=128), srcs, [xt])
            P.act(sq[:, :, 0:n], xt[:, :, 0:n], AF.Square, [xt], [sq])
            ps = P.psb()
            for k in range(KC):
                P.mm(ps[:, 0:n], C("ones", bf=True), sq[:, k, 0:n], k == 0, k == KC - 1, [cstb, sq], [ps])
            P.act(rs[:, 0:n], ps[:, 0:n], AF.Sqrt, [ps, epsD], [rs], scale=1.0 / D, bias=epsD[:, 0:1])
            P.recip(rs[:, 0:n], rs[:, 0:n], [rs], [rs])
            P.tt(xt[:, :, 0:n], xt[:, :, 0:n], rs[:, 0:n].unsqueeze(1).to_broadcast([128, KC, n]), ALU.mult,
                 [xt, rs], [xt])
            for k in range(KC):
                P.act(hT[:, k, t0:t0 + n], xt[:, k, 0:n], AF.Identity, [xt, mA, modT], [hT_v[ti]],
                      scale=mA[:, i, k, g:g + 1], bias=modT[:, sh0 + k, g:g + 1])

    def proj(l, c0, m, evac, wsrc=None, tiles=None):
        wb = load_w(w_in, w_in[l][:, c0:c0 + m].rearrange("(k p) n -> p k n", p=128), m=m)
        for ti, (t0, n) in enumerate(TILES):
            ps = P.psb()
            for k in range(KC):
                P.mm(ps[0:m, 0:n], wb[:, k, 0:m], hT[:, k, t0:t0 + n], k == 0, k == KC - 1, [wb, hT_v[ti]], [ps])
            evac(ps, ti, t0, n)

    def proj_full(l, c0, m, dst, eng="act"):
        def ev(ps, ti, t0, n):
            P.cp(dst[0:m, t0:t0 + n], ps[0:m, 0:n], [ps], [dst], eng=eng)
        proj(l, c0, m, ev)

    def rms_over_partitions(src_ap, n, rows, ones_ap, inv_count, eps_col, sqb, rsb, R):
        P.act(sqb[0:rows, 0:n], src_ap, AF.Square, R, [sqb])
        ps = P.psb()
        P.mm(ps[0:rows, 0:n], ones_ap, sqb[0:rows, 0:n], True, True, [cstb, sqb], [ps])
        P.act(rsb[0:rows, 0:n], ps[0:rows, 0:n], AF.Sqrt, [ps, epsD], [rsb], scale=inv_count,
              bias=epsD[0:rows, eps_col:eps_col + 1])
        P.recip(rsb[0:rows, 0:n], rsb[0:rows, 0:n], [rsb], [rsb])

    def stage_out(l):
        P.barrier()
        AF_.reset()
        yt = AB_.alloc("ytall", [KC, T])
        yt_v = [P.view(yt, "ytall_%d" % ti) for ti in range(5)]
        xps = Pool([AF_.alloc("xo%d" % b, [512]) for b in range(4)])
        for ti, (t0, n) in enumerate(TILES):
            P.dma("sp", yt[:, :, t0:t0 + n], yT[:, t0:t0 + n].rearrange("(k p) t -> p k t", p=128),
                  [yT_v[m][ti] for m in range(KC)], [yt_v[ti]])
        for m in range(KC):
            wb = load_w(w_out, w_out[l][:, m * 128:(m + 1) * 128].rearrange("(k p) n -> p k n", p=128))
            for ti, (t0, n) in enumerate(TILES):
                g = 1 if ti == 0 else 0
                xv, xsrc = x_src(l, m, ti)
                xo = xps.next()
                P.dma("sp", xo[:, 0:n], xsrc[m * 128:(m + 1) * 128, t0:t0 + n], [xv], [xo])
                ps = P.psb()
                for k in range(KC):
                    P.mm(ps[:, 0:n], wb[:, k, :], yt[:, k, t0:t0 + n], k == 0, k == KC - 1, [wb, yt_v[ti]], [ps])
                P.stt(xo[:, 0:n], ps[:, 0:n], modT[:, 32 + m, g:g + 1], xo[:, 0:n], ALU.mult, ALU.add,
                      [ps, modT, xo], [xo])
                P.dma("sp", xs[m * 128:(m + 1) * 128, t0:t0 + n], xo[:, 0:n], [xo], [xs_v[m][ti]])

    def stage_ffn(l, last):
        P.barrier()
        AF_.reset()
        AB_.reset()
        ups = [AF_.alloc("up%d" % b, [T]) for b in range(2)]
        cvs = [AF_.alloc("cv%d" % b, [T]) for b in range(2)]
        zb = Pool([AB_.alloc("zb%d" % b, [T]) for b in range(2)])
        dwo, _ = V128["dw"]
        for j in range(44):
            for half in range(2):
                cidx = j + 44 * half
                up, cv = ups[half], cvs[half]
                wb = load_w(ffn_up, ffn_up[l][:, cidx * 128:(cidx + 1) * 128].rearrange("(k p) n -> p k n", p=128))
                for ti, (t0, n) in enumerate(TILES):
                    ps = P.psb()
                    for k in range(KC):
                        P.mm(ps[:, 0:n], wb[:, k, :], hT[:, k, t0:t0 + n], k == 0, k == KC - 1, [wb, hT_v[ti]], [ps])
                    P.cp(up[:, t0:t0 + n], ps[:, 0:n], [ps], [up], eng="act" if ti % 2 else "dve")
                P.act(cv[:, :], up[:, :], AF.Identity, [up, vec], [cv],
                      scale=vec[:, dwo + 88 + cidx:dwo + 88 + cidx + 1], bias=V("db", c0=cidx, c1=cidx + 1))
                for (s0, s1) in SEGS:
                    P.stt(cv[:, s0 + 1:s1], up[:, s0:s1 - 1], vec[:, dwo + cidx:dwo + cidx + 1], cv[:, s0 + 1:s1],
                          ALU.mult, ALU.add, [up, vec, cv], [cv])
                    P.stt(cv[:, s0:s1 - 1], up[:, s0 + 1:s1], vec[:, dwo + 176 + cidx:dwo + 176 + cidx + 1],
                          cv[:, s0:s1 - 1], ALU.mult, ALU.add, [up, vec, cv], [cv])
            z = zb.next()
            P.act(cvs[0][:, :], cvs[0][:, :], AF.Silu, [cvs[0]], [cvs[0]])
            P.tt(z[:, :], cvs[0][:, :], cvs[1][:, :], ALU.mult, [cvs[0], cvs[1]], [z])
            P.dma("sp", zT[j * 128:(j + 1) * 128, :], z[:, :], [z], [zT_v[j]])
        P.barrier()
        AF_.reset()
        AB_.reset()
        zt = AB_.alloc("zt", [44, 1024])
        wdp = Pool([AB_.alloc("wd%d" % b, [512]) for b in range(3)])
        xps = Pool([AF_.alloc("xo%d" % b, [512]) for b in range(4)])
        tidx = {t0_: i_ for i_, (t0_, _n) in enumerate(TILES)}
        for (t0, n) in [(0, 256), (256, 1024), (1280, 1024)]:
            if last and t0 == 0:
                continue
            g = 1 if t0 == 0 else 0
            nh = (n + 511) // 512
            P.dma("sp", zt[:, :, 0:n], zT[:, t0:t0 + n].rearrange("(k p) t -> p k t", p=128), zT_v, [zt])
            for q in range(4):
                pss = [[P.psb() for _h in range(nh)] for _ in range(4)]
                for k in range(44):
                    wd = wdp.next()
                    P.dma("pool", wd[:, :], ffn_down[l][k * 128:(k + 1) * 128, q * 512:(q + 1) * 512],
                          [ffn_down], [wd])
                    for mm_ in range(4):
                        for hf in range(nh):
                            nn = min(512, n - hf * 512)
                            P.mm(pss[mm_][hf][:, 0:nn], wd[:, mm_ * 128:(mm_ + 1) * 128],
                                 zt[:, k, hf * 512:hf * 512 + nn], k == 0, k == 43, [wd, zt], [pss[mm_][hf]])
                for mm_ in range(4):
                    m = q * 4 + mm_
                    for hf in range(nh):
                        nn = min(512, n - hf * 512)
                        th = t0 + hf * 512
                        ti = tidx[th]
                        xo = xps.next()
                        P.dma("sp", xo[:, 0:nn], xs[m * 128:(m + 1) * 128, th:th + nn], [xs_v[m][ti]], [xo])
                        P.stt(xo[:, 0:nn], pss[mm_][hf][:, 0:nn], modT[:, 80 + m, g:g + 1], xo[:, 0:nn], ALU.mult,
                              ALU.add, [pss[mm_][hf], modT, xo], [xo])
                        if last:
                            P.dma("sp", out[m * 128:(m + 1) * 128, th - NCTX:th - NCTX + nn], xo[:, 0:nn], [xo],
                                  [out_v[m][ti]])
                        else:
                            P.dma("sp", xs[m * 128:(m + 1) * 128, th:th + nn], xo[:, 0:nn], [xo], [xs_v[m][ti]])

    def token_shift(pf, sf, m, mu_ap):
        for (s0, s1) in SEGS:
            P.tt(sf[0:m, s0 + 1:s1 - 1], pf[0:m, s0:s1 - 2], pf[0:m, s0 + 2:s1], ALU.add, [pf], [sf])
            P.cp(sf[0:m, s0:s0 + 1], pf[0:m, s0 + 1:s0 + 2], [pf], [sf])
            P.cp(sf[0:m, s1 - 1:s1], pf[0:m, s1 - 2:s1 - 1], [pf], [sf])
        P.stt(sf[0:m, :], sf[0:m, :], 0.5, pf[0:m, :], ALU.mult, ALU.subtract, [sf, pf], [sf])
        P.stt(sf[0:m, :], sf[0:m, :], mu_ap, pf[0:m, :], ALU.mult, ALU.add, [sf, pf, vec], [sf])

    def proj_to(l, c0, m, buf, dst_fn, eng="act"):
        def ev(ps, ti, t0, n):
            P.cp(dst_fn(t0, n), ps[0:m, 0:n], [ps], [buf], eng=eng)
        proj(l, c0, m, ev)

    MLA_SCALE = 192.0 ** -0.5

    def mixer_mla(l):
        P.barrier()
        AF_.reset()
        cq = AB_.alloc("cq", [4, T])
        ckv = AB_.alloc("ckv", [2, T])
        krT = AB_.alloc("krT", [T])
        mark = _mem.off
        scr = AF_.alloc("scr", [4, T])
        sq4 = AB_.alloc("sq4", [4, 512])
        rs1 = AF_.alloc("rs1", [512])
        tA = AF_.alloc("tA", [512])
        tB = AF_.alloc("tB", [512])
        csT = AF_.alloc("csT", [2, 512])
        qb = AB_.alloc("qb", [512])
        sq1p = AB_.alloc("sq1p", [512])

        def rope(dst_bf, dst_buf, src32, src_buf, t0, n):
            P.dma("sp", csT[0:64, :, 0:n], cs_in[:, :, t0 - NCTX:t0 - NCTX + n].rearrange("c p t -> p c t"),
                  [cs_in], [csT])
            P.cp(qb[0:64, 0:n], src32, [src_buf], [qb], eng="act")
            ps = P.psb()
            P.mm(ps[0:64, 0:n], C("rotT", rows=64, bf=True), qb[0:64, 0:n], True, True, [cstb, qb], [ps])
            P.tt(tB[0:64, 0:n], ps[0:64, 0:n], csT[0:64, 1, 0:n], ALU.mult, [ps, csT], [tB])
            P.tt(src32, src32, csT[0:64, 0, 0:n], ALU.mult, [src_buf, csT], [src_buf])
            P.tt(dst_bf, src32, tB[0:64, 0:n], ALU.add, [src_buf, tB], [dst_buf])

        def norm_chunks(src3, nk, dstb, gname, cnt):
            for ti, (t0, n) in enumerate(TILES):
                P.act(sq4[:, 0:nk, 0:n], src3[:, 0:nk, t0:t0 + n], AF.Square, [scr], [sq4])
                ps = P.psb()
                for k in range(nk):
                    P.mm(ps[:, 0:n], C("ones", bf=True), sq4[:, k, 0:n], k == 0, k == nk - 1, [cstb, sq4], [ps])
                P.act(rs1[:, 0:n], ps[:, 0:n], AF.Sqrt, [ps, epsD], [rs1], scale=1.0 / cnt, bias=epsD[:, 0:1])
                P.recip(rs1[:, 0:n], rs1[:, 0:n], [rs1], [rs1])
                for k in range(nk):
                    P.stt(dstb[:, k, t0:t0 + n], src3[:, k, t0:t0 + n], V(gname, c0=k, c1=k + 1), rs1[:, 0:n],
                          ALU.mult, ALU.mult, [scr, vec, rs1], [dstb])

        for k in range(4):
            proj_to(l, ML0 + k * 128, 128, scr, lambda t0, n, k=k: scr[:, k, t0:t0 + n])
        norm_chunks(scr, 4, cq, "qnorm", 512)
        for k in range(2):
            proj_to(l, ML0 + 512 + k * 128, 128, scr, lambda t0, n, k=k: scr[:, k, t0:t0 + n])
        proj_to(l, ML0 + 768, 64, scr, lambda t0, n: scr[0:64, 2, t0:t0 + n])
        norm_chunks(scr, 2, ckv, "kvnorm", 256)
        for ti, (t0, n) in enumerate(TILES):
            rms_over_partitions(scr[0:64, 2, t0:t0 + n], n, 64, C("ones", rows=64, bf=True, c1=64), 1.0 / 64, 0,
                                sq1p, rs1, [scr])
            P.stt(tA[0:64, 0:n], scr[0:64, 2, t0:t0 + n], V("kr_g", rows=64), rs1[0:64, 0:n], ALU.mult, ALU.mult,
                  [scr, vec, rs1], [tA])
            if ti == 0:
                P.cp(krT[0:64, t0:t0 + n], tA[0:64, 0:n], [tA], [krT])
            else:
                rope(krT[0:64, t0:t0 + n], krT, tA[0:64, 0:n], tA, t0, n)
        P.barrier()
        _mem.off = mark
        P.memset(_mem.buf[:, mark:_mem.n], 0.0, [_mem.buf])
        P.barrier()
        qnT = AB_.alloc("qnT", [T])
        qrT = AB_.alloc("qrT", [T])
        knT = AB_.alloc("knT", [T])
        Vt = AB_.alloc("Vt", [18, 128])
        wq = AB_.alloc("wq", [4, 192])
        wkv = AB_.alloc("wkv", [2, 256])
        sq1 = AB_.alloc("sq1", [512])
        rs1 = AF_.alloc("rs1b", [512])
        tA = AF_.alloc("tAb", [512])
        tB = AF_.alloc("tBb", [512])
        csT = AF_.alloc("csTb", [2, 512])
        qb = AB_.alloc("qbb", [512])
        ptp = Pool([AB_.alloc("pt%d" % b, [512]) for b in range(4)])
        yb = Pool([AB_.alloc("yb%d" % b, [512]) for b in range(2)])
        for h in range(8):
            P.dma("pool", wq[:, :, :], w_uq[l][:, h * 192:(h + 1) * 192].rearrange("(k p) n -> p k n", p=128),
                  [w_uq], [wq])
            P.dma("pool", wkv[:, :, :], w_ukv[l][:, h * 256:(h + 1) * 256].rearrange("(k p) n -> p k n", p=128),
                  [w_ukv], [wkv])
            for ti, (t0, n) in enumerate(TILES):
                ps = P.psb()
                for k in range(4):
                    P.mm(ps[:, 0:n], wq[:, k, 0:128], cq[:, k, t0:t0 + n], k == 0, k == 3, [wq, cq], [ps])
                rms_over_partitions(ps[:, 0:n], n, 128, C("ones", bf=True), 1.0 / 128, 0, sq1, rs1, [ps])
                P.stt(qnT[:, t0:t0 + n], ps[:, 0:n], V("qn_g"), rs1[:, 0:n], ALU.mult, ALU.mult, [ps, vec, rs1], [qnT])
                ps = P.psb()
                for k in range(4):
                    P.mm(ps[0:64, 0:n], wq[:, k, 128:192], cq[:, k, t0:t0 + n], k == 0, k == 3, [wq, cq], [ps])
                rms_over_partitions(ps[0:64, 0:n], n, 64, C("ones", rows=64, bf=True, c1=64), 1.0 / 64, 0, sq1, rs1, [ps])
                P.stt(tA[0:64, 0:n], ps[0:64, 0:n], V("qr_g", rows=64), rs1[0:64, 0:n], ALU.mult, ALU.mult,
                      [ps, vec, rs1], [tA])
                if ti == 0:
                    P.cp(qrT[0:64, t0:t0 + n], tA[0:64, 0:n], [tA], [qrT])
                else:
                    rope(qrT[0:64, t0:t0 + n], qrT, tA[0:64, 0:n], tA, t0, n)
                ps = P.psb()
                for k in range(2):
                    P.mm(ps[:, 0:n], wkv[:, k, 0:128], ckv[:, k, t0:t0 + n], k == 0, k == 1, [wkv, ckv], [ps])
                rms_over_partitions(ps[:, 0:n], n, 128, C("ones", bf=True), 1.0 / 128, 0, sq1, rs1, [ps])
                P.stt(knT[:, t0:t0 + n], ps[:, 0:n], V("kn_g"), rs1[:, 0:n], ALU.mult, ALU.mult, [ps, vec, rs1], [knT])
            for b0 in range(0, 18, 4):
                nb = min(4, 18 - b0)
                ps = P.psb()
                for j in range(nb):
                    b = b0 + j
                    for k in range(2):
                        P.mm(ps[:, j * 128:(j + 1) * 128], ckv[:, k, b * 128:(b + 1) * 128], wkv[:, k, 128:256],
                             k == 0, k == 1, [ckv, wkv], [ps])
                P.cp(Vt[:, b0:b0 + nb, :], ps[:, 0:nb * 128].rearrange("p (a b) -> p a b", b=128), [ps], [Vt], eng="act")
            for ti, (t0, n) in enumerate(TILES):
                nkb = 2 if ti == 0 else 18
                pso = P.hold()
                psd = P.hold()
                pend = None
                for kb in range(nkb + 1):
                    cur = None
                    if kb < nkb:
                        pss = P.psb()
                        ks = slice(kb * 128, (kb + 1) * 128)
                        P.mm(pss[:, 0:n], knT[:, ks], qnT[:, t0:t0 + n], True, False, [knT, qnT], [pss])
                        P.mm(pss[:, 0:n], krT[0:64, ks], qrT[0:64, t0:t0 + n], False, True, [krT, qrT], [pss])
                        pt = ptp.next()
                        P.act(pt[:, 0:n], pss[:, 0:n], AF.Exp, [pss], [pt], scale=MLA_SCALE)
                        cur = (pt, kb)
                    if pend is not None:
                        ppt, pkb = pend
                        P.mm(pso[:, 0:n], Vt[:, pkb, :], ppt[:, 0:n], pkb == 0, pkb == nkb - 1, [Vt, ppt], [pso])
                        P.mm(psd[:, 0:n], C("ones", bf=True), ppt[:, 0:n], pkb == 0, pkb == nkb - 1, [cstb, ppt], [psd])
                    pend = cur
                P.recip(tA[:, 0:n], psd[:, 0:n], [psd], [tA])
                y = yb.next()
                P.tt(y[:, 0:n], pso[:, 0:n], tA[:, 0:n], ALU.mult, [pso, tA], [y])
                P.release(pso)
                P.release(psd)
                P.dma("sp", yT[(8 + h) * 128:(9 + h) * 128, t0:t0 + n], y[:, 0:n], [y], [yT_v[8 + h][ti]])
    def mixer_hgrn2(l):
        for h in range(4):
            P.barrier()
            AF_.reset()
            qf = AF_.alloc("qf", [T])
            vf = AF_.alloc("vf", [T])
            zfs = [AF_.alloc("zf%d" % d_, [T]) for d_ in range(2)]
            oTs = [AF_.alloc("oT%d" % d_, [T]) for d_ in range(2)]
            sq1 = AB_.alloc("sq1", [512])
            yb = Pool([AB_.alloc("yb%d" % b, [512]) for b in range(2)])
            proj_to(l, h * 128, 128, qf, lambda t0, n: qf[:, t0:t0 + n])
            proj_to(l, 1536 + h * 128, 128, vf, lambda t0, n: vf[:, t0:t0 + n])
            identb = C("ident", bf=True)

            def dir_gen(d):
                zf, oT = zfs[d], oTs[d]
                t1 = AF_.alloc("t1_%d" % d, [512])
                t2 = AF_.alloc("t2_%d" % d, [512])
                t3 = AF_.alloc("t3_%d" % d, [512])
                Fc = AF_.alloc("Fc%d" % d, [32])
                qt = AB_.alloc("qt%d" % d, [512])
                kt = AB_.alloc("kt%d" % d, [512])
                kb = AB_.alloc("kb%d" % d, [512])
                vb = AB_.alloc("vb%d" % d, [512])
                kvp = Pool([AB_.alloc("kvT%d_%d" % (d, b), [8, 128]) for b in range(3)])
                atp = Pool([AB_.alloc("aT%d_%d" % (d, b), [4, 16]) for b in range(3)])
                S32 = [AF_.alloc("S32_%d_%d" % (d, b), [128]) for b in range(2)]
                Sbf = Pool([AB_.alloc("Sbf%d_%d" % (d, b), [128]) for b in range(3)])
                proj_to(l, 512 * (1 + d) + h * 128, 128, zf, lambda t0, n: zf[:, t0:t0 + n])
                yield
                state = None
                cidx = 0
                for (t0, n, rev) in scan_tiles(d):
                    nch = n // 16
                    P.act(t1[:, 0:n], tsl(zf, t0, n, rev), AF.Sigmoid, [zf], [t1])
                    P.ts(t1[:, 0:n], t1[:, 0:n], omlb[:, h, d, l:l + 1], lbt[:, h, d, l:l + 1], ALU.mult, ALU.add,
                         [t1, omlb, lbt], [t1])
                    P.act(t2[:, 0:n], t1[:, 0:n], AF.Ln, [t1], [t2])
                    P.ts(t1[:, 0:n], t1[:, 0:n], -1.0, 1.0, ALU.mult, ALU.add, [t1], [t1])
                    P.scan(t3[:, 0:n], C("rs16", c1=n), t2[:, 0:n], [cst, t2], [t3])
                    P.act(t2[:, 0:n], t3[:, 0:n], AF.Exp, [t3], [t2])
                    P.tt(qt[:, 0:n], tsl(qf, t0, n, rev), t2[:, 0:n], ALU.mult, [qf, t2], [qt])
                    P.act(t2[:, 0:n], t3[:, 0:n], AF.Exp, [t3], [t2], scale=-1.0)
                    P.tt(kt[:, 0:n], t1[:, 0:n], t2[:, 0:n], ALU.mult, [t1, t2], [kt])
                    G3 = t3[:, 0:n].rearrange("p (c i) -> p c i", i=16)
                    P.tt(t2[:, 0:n].rearrange("p (c i) -> p c i", i=16), G3[:, :, 15:16].to_broadcast([128, nch, 16]),
                         G3, ALU.subtract, [t3], [t2])
                    P.act(t2[:, 0:n], t2[:, 0:n], AF.Exp, [t2], [t2])
                    P.tt(kb[:, 0:n], t1[:, 0:n], t2[:, 0:n], ALU.mult, [t1, t2], [kb])
                    P.act(Fc[:, 0:nch], G3[:, :, 15], AF.Exp, [t3], [Fc])
                    P.cp(vb[:, 0:n], tsl(vf, t0, n, rev), [vf], [vb])
                    yield
                    pso = P.hold()
                    for g0 in range(0, nch, 4):
                        psT = P.psb()
                        psTb = psT[:].bitcast(BF16)
                        for j in range(4):
                            cs = slice((g0 + j) * 16, (g0 + j + 1) * 16)
                            P.tr(psTb[0:16, j * 128:(j + 1) * 128], kb[:, cs], identb, [kb, cstb], [psT])
                            P.tr(psTb[0:16, (4 + j) * 128:(5 + j) * 128], vb[:, cs], identb, [vb, cstb], [psT])
                        kvT = kvp.next()
                        P.cp(kvT[0:16, :, :].rearrange("p a b -> p (a b)"), psTb[0:16, :], [psT], [kvT], eng="act")
                        psA = P.psb()
                        for j in range(4):
                            cs = slice((g0 + j) * 16, (g0 + j + 1) * 16)
                            P.mm(psA[0:16, j * 16:(j + 1) * 16], kt[:, cs], qt[:, cs], True, True, [kt, qt], [psA])
                        aT = atp.next()
                        P.tt(aT[0:16, :, :], psA[0:16, 0:64].rearrange("p (a b) -> p a b", b=16),
                             C("m16", rows=16).unsqueeze(1).to_broadcast([16, 4, 16]), ALU.mult, [psA, cst], [aT])
                        yield
                        psG = P.psb()
                        for j in range(4):
                            P.mm(psG[:, j * 128:(j + 1) * 128], kvT[0:16, j, :], kvT[0:16, 4 + j, :], True, True,
                                 [kvT], [psG])
                        for j in range(4):
                            c = g0 + j
                            cs = slice(c * 16, (c + 1) * 16)
                            P.mm(pso[:, cs], kvT[0:16, 4 + j, :], aT[0:16, j, :], True, state is None, [kvT, aT], [pso])
                            if state is not None:
                                P.mm(pso[:, cs], state[1][:, :], qt[:, cs], False, True, [state[1], qt], [pso])
                            new32 = S32[cidx % 2]
                            nbf = Sbf.next()
                            if state is None:
                                P.cp(nbf[:, :], psG[:, j * 128:(j + 1) * 128], [psG], [nbf])
                                P.cp(new32[:, :], psG[:, j * 128:(j + 1) * 128], [psG], [new32])
                            else:
                                P.stt(nbf[:, :], state[0][:, :], Fc[:, c:c + 1], psG[:, j * 128:(j + 1) * 128],
                                      ALU.mult, ALU.add, [state[0], Fc, psG], [nbf])
                                P.stt(new32[:, :], state[0][:, :], Fc[:, c:c + 1], psG[:, j * 128:(j + 1) * 128],
                                      ALU.mult, ALU.add, [state[0], Fc, psG], [new32])
                            state = (new32, nbf)
                            cidx += 1
                            yield
                    P.cp(tsl(oT, t0, n, rev), pso[:, 0:n], [pso], [oT])
                    P.release(pso)

            live = [dir_gen(0), dir_gen(1)]
            while live:
                for g_ in list(live):
                    try:
                        next(g_)
                    except StopIteration:
                        live.remove(g_)
            zf = zfs[0]
            t1 = AF_.alloc("t1o", [512])
            t2 = AF_.alloc("t2o", [512])
            t3 = AF_.alloc("t3o", [512])
            proj_to(l, 2048 + h * 128, 128, zf, lambda t0, n: zf[:, t0:t0 + n])
            for ti, (t0, n) in enumerate(TILES):
                P.tt(t3[:, 0:n], oTs[0][:, t0:t0 + n], oTs[1][:, t0:t0 + n], ALU.add, [oTs[0], oTs[1]], [t3])
                rms_over_partitions(t3[:, 0:n], n, 128, C("ones", bf=True), 1.0 / 128, 0, sq1, t1, [t3])
                P.stt(t2[:, 0:n], t3[:, 0:n], V("hgn"), t1[:, 0:n], ALU.mult, ALU.mult, [t3, vec, t1], [t2])
                P.act(t3[:, 0:n], zf[:, t0:t0 + n], AF.Silu, [zf], [t3])
                y = yb.next()
                P.tt(y[:, 0:n], t2[:, 0:n], t3[:, 0:n], ALU.mult, [t2, t3], [y])
                P.dma("sp", yT[h * 128:(h + 1) * 128, t0:t0 + n], y[:, 0:n], [y], [yT_v[h][ti]])

    def rwkv_lora(l):
        P.barrier()
        AF_.reset()
        pf = AF_.alloc("pf", [T])
        sf = AF_.alloc("sf", [T])
        xb = AB_.alloc("xb", [T])
        w2b = AB_.alloc("w2b", [512])
        otp = Pool([AF_.alloc("ot%d" % b, [512]) for b in range(3)])
        groups = [(1536, 96, "w", 0, 0), (1632, 96, "w", 1, 1), (1728, 96, "a", 0, 2), (1824, 96, "a", 1, 3),
                  (1920, 64, "g", 0, 4)]
        for gi, (c0, m, kind, d, ai) in enumerate(groups):
            proj_to(l, RW0 + c0, m, pf, lambda t0, n, m=m: pf[0:m, t0:t0 + n])
            mu_ap = V("mul", rows=96, c0=gi, c1=gi + 1) if kind != "g" else V("mug", rows=64)
            token_shift(pf, sf, m, mu_ap)
            func = AF.Tanh if kind == "w" else (AF.Identity if kind == "a" else AF.Sigmoid)
            P.act(xb[0:m, :], sf[0:m, :], func, [sf], [xb])
            if kind == "w":
                wsrc, wb_ = rw_w2[l][d], rw_w2
            elif kind == "a":
                wsrc, wb_ = rw_a2[l][d], rw_a2
            else:
                wsrc, wb_ = rw_g2[l], rw_g2
            P.dma("pool", w2b[0:m, :], wsrc, [wb_], [w2b])
            for mc in range(4):
                for ti, (t0, n) in enumerate(TILES):
                    ps = P.psb()
                    P.mm(ps[:, 0:n], w2b[0:m, mc * 128:(mc + 1) * 128], xb[0:m, t0:t0 + n], True, True, [w2b, xb], [ps])
                    o = otp.next()
                    if kind == "w":
                        P.act(o[:, 0:n], ps[:, 0:n], AF.Sigmoid, [ps, vec], [o], bias=V("w0", c0=d * 4 + mc, c1=d * 4 + mc + 1))
                        P.ts(o[:, 0:n], o[:, 0:n], -DECAY_MAX, None, ALU.mult, ALU.bypass, [o], [o])
                    elif kind == "a":
                        P.act(o[:, 0:n], ps[:, 0:n], AF.Sigmoid, [ps, vec], [o], bias=V("a0", c0=d * 4 + mc, c1=d * 4 + mc + 1))
                    else:
                        P.cp(o[:, 0:n], ps[:, 0:n], [ps], [o])
                    P.dma("sp", rwaux[ai][mc * 128:(mc + 1) * 128, t0:t0 + n], o[:, 0:n], [o], [aux_v[ai][mc]])

    def mixer_rwkv(l):
        rwkv_lora(l)
        for m in range(4):
            P.barrier()
            AF_.reset()
            rf = AF_.alloc("rf", [T])
            kf = AF_.alloc("kf", [T])
            vf = AF_.alloc("vf", [T])
            kkf = AF_.alloc("kkf", [T])
            oT = AF_.alloc("oT", [T])
            pf = oT
            tlw = AF_.alloc("tlw", [512])
            tag = AF_.alloc("tag", [512])
            tcum = AF_.alloc("tcum", [512])
            te = AF_.alloc("te", [512])
            te2 = AF_.alloc("te2", [512])
            tb_ = AF_.alloc("tb", [512])
            tkd = AF_.alloc("tkd", [512])
            WC = AF_.alloc("WC", [4])
            omka = AF_.alloc("omka", [1])
            S32 = [AF_.alloc("S32_%d" % b, [64]) for b in range(2)]
            rt = AB_.alloc("rt", [512])
            bt = AB_.alloc("bt", [512])
            kt = AB_.alloc("kt", [512])
            at = AB_.alloc("at", [512])
            bb = AB_.alloc("bb", [512])
            kb = AB_.alloc("kb", [512])
            vb = AB_.alloc("vb", [512])
            sq1 = AB_.alloc("sq1", [512])
            atm = [AB_.alloc("atm%d" % b, [512]) for b in range(2)]
            rtm = [AB_.alloc("rtm%d" % b, [512]) for b in range(2)]
            for b_ in atm + rtm:
                P.memset(b_[:, :], 0.0, [b_])
            class Slot:
                pass
            slots = []
            for si in range(2):
                S_ = Slot()
                S_.Tp = Pool([AF_.alloc("Tj%d_%d" % (si, b), [2, 128]) for b in range(2)])
                S_.Lp = Pool([AF_.alloc("Lj%d_%d" % (si, b), [2, 128]) for b in range(2)])
                S_.Zp = Pool([AF_.alloc("Z%d_%d" % (si, b), [2, 128]) for b in range(3)])
                S_.tmpP = AF_.alloc("tmpP%d" % si, [128])
                S_.Gp = AF_.alloc("Gp%d" % si, [64])
                S_.Tak = AB_.alloc("Tak%d" % si, [2, 128])
                S_.Trb = AB_.alloc("Trb%d" % si, [2, 128])
                S_.Trk = AB_.alloc("Trk%d" % si, [2, 128])
                S_.tm = AB_.alloc("tm%d" % si, [4, 128])
                S_.Z1c = AB_.alloc("Z1c%d" % si, [128])
                S_.Z2c = AB_.alloc("Z2c%d" % si, [128])
                S_.PhiT = AB_.alloc("PhiT%d" % si, [128])
                S_.QeTm = [AB_.alloc("QeTm%d_%d" % (si, b), [128]) for b in range(2)]
                for b_ in S_.QeTm:
                    P.memset(b_[:, :], 0.0, [b_])
                slots.append(S_)
            identf = C("ident")
            Sbf = Pool([AB_.alloc("Sbf%d" % b, [64]) for b in range(3)])
            yb = Pool([AB_.alloc("yb%d" % b, [512]) for b in range(2)])
            identb = C("ident", bf=True)
            for (c0, dst, mi) in [(0, rf, m), (512, kf, 4 + m), (1024, vf, 8 + m)]:
                proj_to(l, RW0 + c0 + m * 128, 128, pf, lambda t0, n: pf[:, t0:t0 + n])
                token_shift(pf, dst, 128, V("mu", c0=mi, c1=mi + 1))
            P.ts(omka[:, 0:1], V("ka", c0=m, c1=m + 1), -1.0, 1.0, ALU.mult, ALU.add, [vec], [omka])
            for ti, (t0, n) in enumerate(TILES):
                P.ts(te[:, 0:n], kf[:, t0:t0 + n], V("kk", c0=m, c1=m + 1), None, ALU.mult, ALU.bypass, [kf, vec], [te])
                P.act(sq1[:, 0:n], te[:, 0:n], AF.Square, [te], [sq1])
                ps = P.psb()
                P.mm(ps[:, 0:n], C("blk", bf=True), sq1[:, 0:n], True, True, [cstb, sq1], [ps])
                P.act(te2[:, 0:n], ps[:, 0:n], AF.Sqrt, [ps, epsD], [te2], bias=epsD[:, 2:3])
                P.recip(te2[:, 0:n], te2[:, 0:n], [te2], [te2])
                P.tt(kkf[:, t0:t0 + n], te[:, 0:n], te2[:, 0:n], ALU.mult, [te, te2], [kkf])
            for d in range(2):
                state = None
                cidx = 0
                for (t0, n, rev) in scan_tiles(d):
                    nch = n // 128

                    def S(b_):
                        a_ = b_[:, 0:n]
                        return a_[:, ::-1] if rev else a_
                    P.dma("sp", tlw[:, 0:n], rwaux[d][m * 128:(m + 1) * 128, t0:t0 + n], [aux_v[d][m]], [tlw])
                    P.dma("sp", tag[:, 0:n], rwaux[2 + d][m * 128:(m + 1) * 128, t0:t0 + n], [aux_v[2 + d][m]], [tag])
                    P.scan(tcum[:, 0:n], C("rs128", c1=n), S(tlw), [cst, tlw], [tcum])
                    P.act(te[:, 0:n], tcum[:, 0:n], AF.Exp, [tcum], [te])
                    P.tt(rt[:, 0:n], tsl(rf, t0, n, rev), te[:, 0:n], ALU.mult, [rf, te], [rt])
                    P.act(te[:, 0:n], tcum[:, 0:n], AF.Exp, [tcum], [te], scale=-1.0)
                    P.tt(tb_[:, 0:n], tsl(kkf, t0, n, rev), S(tag), ALU.mult, [kkf, tag], [tb_])
                    P.tt(bt[:, 0:n], tb_[:, 0:n], te[:, 0:n], ALU.mult, [tb_, te], [bt])
                    P.ts(tkd[:, 0:n], S(tag), V("ka", c0=m, c1=m + 1), omka[:, 0:1], ALU.mult, ALU.add, [tag, vec, omka], [tkd])
                    P.tt(tkd[:, 0:n], tkd[:, 0:n], tsl(kf, t0, n, rev), ALU.mult, [tkd, kf], [tkd])
                    P.tt(kt[:, 0:n], tkd[:, 0:n], te[:, 0:n], ALU.mult, [tkd, te], [kt])
                    P.tt(te2[:, 0:n], tcum[:, 0:n], S(tlw), ALU.subtract, [tcum, tlw], [te2])
                    P.act(te2[:, 0:n], te2[:, 0:n], AF.Exp, [te2], [te2])
                    P.stt(at[:, 0:n], tsl(kkf, t0, n, rev), -1.0, te2[:, 0:n], ALU.mult, ALU.mult, [kkf, te2], [at])
                    c3 = tcum[:, 0:n].rearrange("p (c i) -> p c i", i=128)
                    P.tt(te2[:, 0:n].rearrange("p (c i) -> p c i", i=128), c3[:, :, 127:128].to_broadcast([128, nch, 128]),
                         c3, ALU.subtract, [tcum], [te2])
                    P.act(te2[:, 0:n], te2[:, 0:n], AF.Exp, [te2], [te2])
                    P.tt(bb[:, 0:n], tb_[:, 0:n], te2[:, 0:n], ALU.mult, [tb_, te2], [bb])
                    P.tt(kb[:, 0:n], tkd[:, 0:n], te2[:, 0:n], ALU.mult, [tkd, te2], [kb])
                    P.cp(vb[:, 0:n], tsl(vf, t0, n, rev), [vf], [vb])
                    P.act(WC[:, 0:nch], c3[:, :, 127], AF.Exp, [tcum], [WC])
                    for hh in range(2):
                        pr = slice(hh * 64, hh * 64 + 64)
                        P.cp(atm[hh][pr, 0:n], at[pr, 0:n], [at], [atm[hh]])
                        P.cp(rtm[hh][pr, 0:n], rt[pr, 0:n], [rt], [rtm[hh]], eng="act")
                    psO = P.hold()

                    def v3(ap):
                        return ap.rearrange("p (a b) -> p a b", b=128)

                    def msk(nm):
                        return C(nm).unsqueeze(1).to_broadcast([128, 2, 128])

                    def chunk_pre(c, S_):
                        cs = slice(c * 128, (c + 1) * 128)
                        A_, B_, C_ = P.psb(), P.psb(), P.psb()
                        for hh in range(2):
                            o0, o1 = hh * 128, 256 + hh * 128
                            am, rm = atm[hh], rtm[hh]
                            P.mm(A_[:, o0:o0 + 128], bt[:, cs], am[:, cs], True, True, [bt, am], [A_])
                            P.mm(A_[:, o1:o1 + 128], am[:, cs], bt[:, cs], True, True, [bt, am], [A_])
                            P.mm(B_[:, o0:o0 + 128], kt[:, cs], am[:, cs], True, True, [kt, am], [B_])
                            P.mm(B_[:, o1:o1 + 128], bt[:, cs], rm[:, cs], True, True, [bt, rm], [B_])
                            P.mm(C_[:, o0:o0 + 128], kt[:, cs], rm[:, cs], True, True, [kt, rm], [C_])
                        psT = P.psb()
                        psTb = psT[:].bitcast(BF16)
                        for i_, src in enumerate([bb, kb, vb, at]):
                            P.tr(psTb[:, i_ * 128:(i_ + 1) * 128], src[:, cs], identb, [src, cstb], [psT])
                        Tj, Lj = S_.Tp.next(), S_.Lp.next()
                        P.tt(Tj[:, :, :], v3(A_[:, 0:256]), msk("tri_s"), ALU.mult, [A_, cst], [Tj])
                        P.tt(Lj[:, :, :], v3(A_[:, 256:512]), msk("tri_sT"), ALU.mult, [A_, cst], [Lj])
                        P.tt(S_.Tak[:, :, :], v3(B_[:, 0:256]), msk("tri_s"), ALU.mult, [B_, cst], [S_.Tak])
                        P.tt(S_.Trb[:, :, :], v3(B_[:, 256:512]), msk("tri_i"), ALU.mult, [B_, cst], [S_.Trb])
                        P.tt(S_.Trk[:, :, :], v3(C_[:, 0:256]), msk("tri_i"), ALU.mult, [C_, cst], [S_.Trk])
                        tm = S_.tm
                        P.cp(tm[:, :, :].rearrange("p a b -> p (a b)"), psTb[:, 0:512], [psT], [tm], eng="act")
                        yield
                        psX = P.psb()
                        for hh in range(2):
                            P.mm(psX[:, hh * 64:(hh + 1) * 64], S_.Tak[:, hh, :], tm[:, 2, hh * 64:(hh + 1) * 64], True, True,
                                 [S_.Tak, tm], [psX])
                        Z = S_.Zp.next()
                        P.cp(Z[:, :, 0:64], tm[:, 3, :].rearrange("p (a b) -> p a b", b=64), [tm], [Z], eng="act")
                        P.cp(Z[:, :, 64:128], psX[:, 0:128].rearrange("p (a b) -> p a b", b=64), [psX], [Z])
                        yield
                        for j in range(7):
                            psZ = P.psb()
                            for hh in range(2):
                                o0 = hh * 128
                                P.mm(psZ[:, o0:o0 + 128], Tj[:, hh, :], Z[:, hh, :], True, False, [Tj, Z], [psZ])
                                P.mm(psZ[:, o0:o0 + 128], identf, Z[:, hh, :], False, True, [cst, Z], [psZ])
                            if j < 6:
                                psS = P.psb()
                                for hh in range(2):
                                    o0, o1 = hh * 128, 256 + hh * 128
                                    P.mm(psS[:, o0:o0 + 128], Lj[:, hh, :], Tj[:, hh, :], True, True, [Lj, Tj], [psS])
                                    P.mm(psS[:, o1:o1 + 128], Tj[:, hh, :], Lj[:, hh, :], True, True, [Lj, Tj], [psS])
                                Zn = S_.Zp.next()
                                P.cp(Zn[:, :, :], v3(psZ[:, 0:256]), [psZ], [Zn], eng="act")
                                Tn, Ln = S_.Tp.next(), S_.Lp.next()
                                P.cp(Tn[:, :, :], v3(psS[:, 0:256]), [psS], [Tn])
                                P.cp(Ln[:, :, :], v3(psS[:, 256:512]), [psS], [Ln], eng="act")
                                Z, Tj, Lj = Zn, Tn, Ln
                            else:
                                z3 = v3(psZ[:, 0:256])
                                P.cp(S_.Z1c[:, :].rearrange("p (a b) -> p a b", b=64), z3[:, :, 0:64], [psZ], [S_.Z1c], eng="act")
                                P.cp(S_.Z2c[:, :].rearrange("p (a b) -> p a b", b=64), z3[:, :, 64:128], [psZ], [S_.Z2c])
                            yield
                        psP = P.psb()
                        P.mm(psP[:, 0:128], S_.Z1c[:, :], tm[:, 0, :], True, True, [S_.Z1c, tm], [psP])
                        P.mm(psP[:, 128:256], tm[:, 0, :], S_.Z2c[:, :], True, False, [S_.Z2c, tm], [psP])
                        P.mm(psP[:, 128:256], tm[:, 1, :], tm[:, 2, :], False, True, [tm], [psP])
                        psQ = P.psb()
                        for hh in range(2):
                            pr = slice(hh * 64, hh * 64 + 64)
                            P.mm(psQ[pr, 0:128], S_.Z1c[:, pr], S_.Trb[:, hh, :], True, True, [S_.Z1c, S_.Trb], [psQ])
                        P.tt(S_.tmpP[:, :], psP[:, 0:128], C("blk"), ALU.mult, [psP, cst], [S_.tmpP])
                        P.stt(S_.PhiT[:, :], C("ident"), WC[:, c:c + 1], S_.tmpP[:, :], ALU.mult, ALU.add,
                              [cst, WC, S_.tmpP], [S_.PhiT])
                        P.cp(S_.Gp[0:64, :], psP[0:64, 128:192], [psP], [S_.Gp], eng="act")
                        P.cp(S_.Gp[64:128, :], psP[64:128, 192:256], [psP], [S_.Gp], eng="act")
                        for hh in range(2):
                            pr = slice(hh * 64, hh * 64 + 64)
                            P.tt(S_.QeTm[hh][pr, :], psQ[pr, 0:128], rt[pr, cs], ALU.add, [psQ, rt], [S_.QeTm[hh]])
                        yield

                    def chunk_post(c, S_, state, cidx):
                        cs = slice(c * 128, (c + 1) * 128)
                        tm = S_.tm
                        for hh in range(2):
                            pr = slice(hh * 64, hh * 64 + 64)
                            P.mm(psO[pr, cs], S_.Z2c[:, pr], S_.Trb[:, hh, :], True, False, [S_.Z2c, S_.Trb], [psO])
                            P.mm(psO[pr, cs], tm[:, 2, pr], S_.Trk[:, hh, :], False, state is None, [tm, S_.Trk], [psO])
                            if state is not None:
                                P.mm(psO[pr, cs], state[1][:, :], S_.QeTm[hh][:, :], False, True,
                                     [state[1], S_.QeTm[hh]], [psO])
                        new32 = S32[cidx % 2]
                        nbf = Sbf.next()
                        if state is None:
                            P.cp(nbf[:, :], S_.Gp[:, :], [S_.Gp], [nbf])
                            P.cp(new32[:, :], S_.Gp[:, :], [S_.Gp], [new32], eng="act")
                        else:
                            psS2 = P.psb()
                            P.mm(psS2[:, 0:64], S_.PhiT[:, :], state[1][:, :], True, True, [S_.PhiT, state[1]], [psS2])
                            P.tt(nbf[:, :], psS2[:, 0:64], S_.Gp[:, :], ALU.add, [psS2, S_.Gp], [nbf])
                            P.tt(new32[:, :], psS2[:, 0:64], S_.Gp[:, :], ALU.add, [psS2, S_.Gp], [new32])
                        return (new32, nbf)

                    for c0_ in range(0, nch, 2):
                        cl = list(range(c0_, min(c0_ + 2, nch)))
                        gens = [chunk_pre(c, slots[c % 2]) for c in cl]
                        live = list(gens)
                        while live:
                            for g_ in list(live):
                                try:
                                    next(g_)
                                except StopIteration:
                                    live.remove(g_)
                        for c in cl:
                            state = chunk_post(c, slots[c % 2], state, cidx)
                            cidx += 1
                    if d == 0:
                        P.cp(oT[:, t0:t0 + n], psO[:, 0:n], [psO], [oT])
                    else:
                        P.tt(tsl(oT, t0, n, True), tsl(oT, t0, n, True), psO[:, 0:n], ALU.add, [oT, psO], [oT])
                    P.release(psO)
            for ti, (t0, n) in enumerate(TILES):
                osl = oT[:, t0:t0 + n]
                P.cp(sq1[:, 0:n], osl, [oT], [sq1], eng="act")
                ps = P.psb()
                P.mm(ps[:, 0:n], C("blk", bf=True), sq1[:, 0:n], True, True, [cstb, sq1], [ps])
                P.stt(te[:, 0:n], ps[:, 0:n], -1.0 / 64, osl, ALU.mult, ALU.add, [ps, oT], [te])
                P.act(sq1[:, 0:n], te[:, 0:n], AF.Square, [te], [sq1])
                ps = P.psb()
                P.mm(ps[:, 0:n], C("blk", bf=True), sq1[:, 0:n], True, True, [cstb, sq1], [ps])
                P.act(te2[:, 0:n], ps[:, 0:n], AF.Sqrt, [ps, epsD], [te2], scale=1.0 / 64, bias=epsD[:, 1:2])
                P.recip(te2[:, 0:n], te2[:, 0:n], [te2], [te2])
                P.tt(te[:, 0:n], te[:, 0:n], te2[:, 0:n], ALU.mult, [te, te2], [te])
                P.ts(te[:, 0:n], te[:, 0:n], V("gnw", c0=m, c1=m + 1), V("gnb", c0=m, c1=m + 1), ALU.mult, ALU.add,
                     [te, vec], [te])
                P.dma("sp", tlw[:, 0:n], rwaux[2][m * 128:(m + 1) * 128, t0:t0 + n], [aux_v[2][m]], [tlw])
                P.dma("sp", tag[:, 0:n], rwaux[3][m * 128:(m + 1) * 128, t0:t0 + n], [aux_v[3][m]], [tag])
                P.tt(tb_[:, 0:n], tlw[:, 0:n], tag[:, 0:n], ALU.add, [tlw, tag], [tb_])
                P.ts(tb_[:, 0:n], tb_[:, 0:n], -2.0, None, ALU.add, ALU.bypass, [tb_], [tb_])
                P.ts(tb_[:, 0:n], tb_[:, 0:n], V("ka", c0=m, c1=m + 1), 2.0, ALU.mult, ALU.add, [tb_, vec], [tb_])
                P.tt(tb_[:, 0:n], tb_[:, 0:n], kf[:, t0:t0 + n], ALU.mult, [tb_, kf], [tb_])
                P.tt(tb_[:, 0:n], tb_[:, 0:n], rf[:, t0:t0 + n], ALU.mult, [tb_, rf], [tb_])
                P.ts(sq1[:, 0:n], tb_[:, 0:n], V("rk", c0=m, c1=m + 1), None, ALU.mult, ALU.bypass, [tb_, vec], [sq1])
                ps = P.psb()
                P.mm(ps[:, 0:n], C("blk", bf=True), sq1[:, 0:n], True, True, [cstb, sq1], [ps])
                P.tt(tkd[:, 0:n], ps[:, 0:n], vf[:, t0:t0 + n], ALU.mult, [ps, vf], [tkd])
                P.tt(te[:, 0:n], te[:, 0:n], tkd[:, 0:n], ALU.add, [te, tkd], [te])
                P.dma("sp", tcum[:, 0:n], rwaux[4][m * 128:(m + 1) * 128, t0:t0 + n], [aux_v[4][m]], [tcum])
                y = yb.next()
                P.tt(y[:, 0:n], te[:, 0:n], tcum[:, 0:n], ALU.mult, [te, tcum], [y])
                P.dma("sp", yT[(4 + m) * 128:(5 + m) * 128, t0:t0 + n], y[:, 0:n], [y], [yT_v[4 + m][ti]])
    for l in range(nl):
        stage_mod(l)
        stage_norm(l, 0)
        mixer_hgrn2(l)
        mixer_rwkv(l)
        mixer_mla(l)
        stage_out(l)
        stage_norm(l, 1)
        stage_ffn(l, l == nl - 1)
    return P


_BIG = ["w_mod", "w_in", "w_out", "rw_w2", "rw_a2", "rw_g2", "mla_w_uq", "mla_w_ukv", "ffn_up", "ffn_down"]


def kernel(**inputs):
    inp = {k: np.asarray(v) for k, v in inputs.items()}
    P = build_program(L)
    nc = P.build()
    cst, cs = host_consts()
    vecs, lb = host_vecs(inp)
    big = {k: np.ascontiguousarray(inp[k], dtype=np.float32) for k in _BIG}
    in_maps = []
    for b in range(8):
        xT = np.ascontiguousarray(np.concatenate([inp["ctx"][b], inp["x"][b]], 0).T.astype(np.float32))
        cT = np.ascontiguousarray(np.stack([fm(inp["c"][b], 16), fm(inp["c_ctx"], 16)], -1).reshape(128, 32))
        in_maps.append(dict(xT=xT, cT=cT, vecs=vecs, hglb=lb, cst=cst, rope=cs, **big))
    res = run_bass_kernel_spmd(nc, in_maps, core_ids=list(range(8)))
    out = np.stack([np.ascontiguousarray(np.asarray(res.results[b]["out"]).T) for b in range(8)], 0)
    return out.astype(np.float32)
```

```python
import math
import numpy as np
import concourse.bass as bass
import concourse.mybir as mybir
from concourse.bass_utils import run_bass_kernel_spmd

F32 = mybir.dt.float32
BF16 = mybir.dt.bfloat16
ALU = mybir.AluOpType
AF = mybir.ActivationFunctionType
ENGS = ["pe", "act", "dve", "pool", "sp"]
NDMASEM = 16

L = 4
D = 2048
KC = 16
NCTX = 256
NLAT = 2048
T = NCTX + NLAT
TILES = [(0, 256), (256, 512), (768, 512), (1280, 512), (1792, 512)]
SEGS = [(0, 256), (256, 2304)]
DFF = 5632
IN_COLS = 5376
EPS = 1e-6
RW0 = 2560
ML0 = 4544
DECAY_MAX = math.exp(-0.5)


class Buf:
    __slots__ = ("name", "h", "writer", "readers", "excl")

    def __init__(self, name, h):
        self.name = name
        self.h = h
        self.writer = None
        self.readers = {}
        self.excl = False

    def __getitem__(self, idx):
        return self.h[idx]


class Pool:
    def __init__(self, bufs):
        self.bufs = bufs
        self.i = 0
        self.held = set()

    def next(self):
        while True:
            b = self.bufs[self.i % len(self.bufs)]
            self.i += 1
            if b.name not in self.held:
                return b

    def hold(self):
        b = self.next()
        self.held.add(b.name)
        return b

    def release(self, b):
        self.held.discard(b.name)


class Prog:
    def __init__(self):
        self.nc = bass.Bass("TRN2", target_bir_lowering=False)
        self.ops = {e: [] for e in ENGS}
        self.qcount = {}
        self.seen = {e: {} for e in ENGS}
        self.sems = {}
        self._stack = []
        self.floor = {}
        self.banks = None
        self.dma_idx = {}

    def sb(self, name, shape, dt=F32):
        g = self.nc.sbuf_tensor(name, list(shape), dt)
        h = g.__enter__()
        self._stack.append(g)
        return Buf(name, h)

    def pool(self, name, shape, dt, n):
        return Pool([self.sb("%s%d" % (name, i), shape, dt) for i in range(n)])

    def dram(self, name, shape, dt=F32, kind="Internal"):
        h = self.nc.dram_tensor(name, list(shape), dt, kind=kind)
        return Buf(name, h.ap())

    def view(self, buf, name=None):
        return Buf(name or buf.name, buf.h)

    def init_psum(self):
        bl = []
        for i in range(8):
            g = self.nc.psum_tensor("psb%d" % i, [128, 512], F32)
            h = g.__enter__()
            self._stack.append(g)
            b_ = Buf("psb%d" % i, h)
            b_.excl = True
            bl.append(b_)
        self.banks = Pool(bl)

    def psb(self):
        return self.banks.next()

    def hold(self):
        return self.banks.hold()

    def release(self, b):
        self.banks.release(b)

    def barrier(self):
        self.floor = dict(self.qcount)

    def emit(self, eng, fn, reads=(), writes=(), dma=False):
        if dma:
            k = self.dma_idx.get(eng, 0)
            self.dma_idx[eng] = k + 1
            q = "dma_%s_%d" % (eng, k % NDMASEM)
        else:
            q = eng
        excl = [b for b in reads if b.excl]
        if excl:
            reads = [b for b in reads if not b.excl]
            writes = list(writes) + excl
        deps = dict(self.floor)
        if dma and self.qcount.get(q, 0) > 0:
            deps[q] = self.qcount[q]

        def need(w):
            if w is None:
                return
            qq, c = w
            if qq == q and q == "pe":
                return
            if deps.get(qq, 0) < c:
                deps[qq] = c

        for b in reads:
            need(b.writer)
        for b in writes:
            need(b.writer)
            for qq, c in b.readers.items():
                need((qq, c))
        waits = []
        seen = self.seen[eng]
        for qq, c in deps.items():
            if qq == "pe" and q == "pe":
                continue
            if seen.get(qq, 0) < c:
                seen[qq] = c
                waits.append((qq, c))
        inc = 16 if dma else 1
        cnt = self.qcount.get(q, 0) + inc
        self.qcount[q] = cnt
        self.ops[eng].append((waits, fn, q, inc))
        for b in reads:
            b.readers[q] = cnt
        for b in writes:
            b.writer = (q, cnt)
            b.readers = {}
        return cnt

    def mm(self, out, lhsT, rhs, start, stop, R, W):
        self.emit("pe", lambda e: e.matmul(out, lhsT, rhs, start=start, stop=stop), R, W)

    def tr(self, out, in_, ident, R, W):
        self.emit("pe", lambda e: e.transpose(out, in_, ident), R, W)

    def act(self, out, in_, func, R, W, scale=1.0, bias=0.0):
        self.emit("act", lambda e: e.activation(out=out, in_=in_, func=func, scale=scale, bias=bias), R, W)

    def tt(self, out, in0, in1, op, R, W, eng="dve"):
        self.emit(eng, lambda e: e.tensor_tensor(out=out, in0=in0, in1=in1, op=op), R, W)

    def ts(self, out, in0, s1, s2, op0, op1, R, W, eng="dve"):
        self.emit(eng, lambda e: e.tensor_scalar(out=out, in0=in0, scalar1=s1, scalar2=s2, op0=op0, op1=op1), R, W)

    def stt(self, out, in0, scalar, in1, op0, op1, R, W):
        self.emit("dve", lambda e: e.scalar_tensor_tensor(out=out, in0=in0, scalar=scalar, in1=in1, op0=op0, op1=op1), R, W)

    def cp(self, out, in_, R, W, eng="dve"):
        if eng == "act":
            self.emit("act", lambda e: e.activation(out=out, in_=in_, func=AF.Copy), R, W)
        else:
            self.emit(eng, lambda e: e.tensor_copy(out=out, in_=in_), R, W)

    def recip(self, out, in_, R, W):
        self.emit("dve", lambda e: e.reciprocal(out=out, in_=in_), R, W)

    def scan(self, out, d0, d1, R, W):
        self.emit("dve", lambda e: e.tensor_tensor_scan(out=out, data0=d0, data1=d1, initial=0.0,
                                                       op0=ALU.mult, op1=ALU.add), R, W)

    def memset(self, ap, val, W, eng="dve"):
        self.emit(eng, lambda e: e.memset(ap, val), (), W)

    def dma(self, eng, out, in_, R, W):
        self.emit(eng, lambda e: e.dma_start(out=out, in_=in_), R, W, dma=True)

    def build(self):
        nc = self.nc
        qs = sorted(self.qcount.keys())
        guards = []
        for q in qs:
            g = nc.semaphore("s_" + q)
            self.sems[q] = g.__enter__()
            guards.append(g)
        ops, sems, qcount = self.ops, self.sems, self.qcount

        def run(engname):
            def body(e):
                for waits, fn, q, inc in ops[engname]:
                    for qq, c in waits:
                        e.wait_ge(sems[qq], c)
                    fn(e).then_inc(sems[q], inc)
                if engname == "sp":
                    for q in qs:
                        e.wait_ge(sems[q], qcount[q])
            return body

        with nc.Block() as block:
            block.sync(run("sp"))
            block.tensor(run("pe"))
            block.scalar(run("act"))
            block.vector(run("dve"))
            block.gpsimd(run("pool"))
        for g in guards:
            g.__exit__(None, None, None)
        while self._stack:
            self._stack.pop().__exit__(None, None, None)
        return nc


V128 = {}
_o = 0
for _n, _w in [("ng1", 16), ("ng2", 16), ("bm", 96), ("hgn", 1), ("mu", 12), ("w0", 8), ("a0", 8), ("kk", 4),
               ("ka", 4), ("rk", 4), ("gnw", 4), ("gnb", 4), ("qnorm", 4), ("kvnorm", 2), ("qn_g", 1),
               ("kn_g", 1), ("dw", 264), ("db", 88), ("mul", 4), ("mug", 1), ("qr_g", 1), ("kr_g", 1)]:
    V128[_n] = (_o, _w)
    _o += _w
NV = _o

C128 = {}
_o = 0
for _n, _w in [("ident", 128), ("ones", 128), ("blk", 128), ("tri_i", 128), ("tri_s", 128), ("tri_sT", 128),
               ("m16", 16), ("rs16", 512), ("rs128", 512), ("rotT", 64)]:
    C128[_n] = (_o, _w)
    _o += _w
NCST = _o


def fm(v, nch):
    return np.ascontiguousarray(np.asarray(v, np.float32).reshape(nch, 128).T)


def host_consts():
    c = np.zeros((128, NCST), np.float32)

    def put(n, a):
        o, w = C128[n]
        c[:a.shape[0], o:o + a.shape[1]] = a

    put("ident", np.eye(128, dtype=np.float32))
    put("ones", np.ones((128, 128), np.float32))
    blk = np.zeros((128, 128), np.float32)
    blk[:64, :64] = 1
    blk[64:, 64:] = 1
    put("blk", blk)
    put("tri_i", np.triu(np.ones((128, 128), np.float32)))
    put("tri_s", np.triu(np.ones((128, 128), np.float32), 1))
    put("tri_sT", np.tril(np.ones((128, 128), np.float32), -1))
    put("m16", np.triu(np.ones((16, 16), np.float32)))
    r16 = np.ones((128, 512), np.float32)
    r16[:, ::16] = 0
    put("rs16", r16)
    r128 = np.ones((128, 512), np.float32)
    r128[:, ::128] = 0
    put("rs128", r128)
    rot = np.zeros((64, 64), np.float32)
    rot[:32, 32:] = -np.eye(32)
    rot[32:, :32] = np.eye(32)
    put("rotT", np.ascontiguousarray(rot.T))
    rows = NLAT // 64
    row = np.repeat(np.arange(rows, dtype=np.float32), 64)
    col = np.tile(np.arange(64, dtype=np.float32), rows)
    inv = (10000.0 ** (-np.arange(0, 32, 2, dtype=np.float32) / 32)).astype(np.float32)
    ang = np.concatenate([row[:, None] * inv, col[:, None] * inv], -1).astype(np.float32)
    cos, sin = np.cos(ang).astype(np.float32), np.sin(ang).astype(np.float32)
    cs = np.zeros((2, 64, NLAT), np.float32)
    cs[0] = np.concatenate([cos.T, cos.T], 0)
    cs[1] = np.concatenate([sin.T, sin.T], 0)
    return c, cs


def host_vecs(inp):
    v = np.zeros((L, 128, NV), np.float32)

    def put(l, n, a):
        o, w = V128[n]
        a = np.asarray(a, np.float32)
        v[l, :a.shape[0], o:o + a.shape[1]] = a

    for l in range(L):
        put(l, "ng1", fm(inp["norm_g"][l, 0], 16))
        put(l, "ng2", fm(inp["norm_g"][l, 1], 16))
        put(l, "bm", fm(inp["b_mod"][l], 96))
        put(l, "hgn", inp["hg_gn"][l][:, None])
        put(l, "mu", fm(inp["rw_mu"][l, :1536], 12))
        put(l, "w0", np.concatenate([fm(inp["rw_w0"][l, 0], 4), fm(inp["rw_w0"][l, 1], 4)], 1))
        put(l, "a0", np.concatenate([fm(inp["rw_a0"][l, 0], 4), fm(inp["rw_a0"][l, 1], 4)], 1))
        put(l, "kk", fm(inp["rw_kk"][l], 4))
        put(l, "ka", fm(inp["rw_ka"][l], 4))
        put(l, "rk", fm(inp["rw_rk"][l].reshape(-1), 4))
        put(l, "gnw", fm(inp["rw_gn_w"][l], 4))
        put(l, "gnb", fm(inp["rw_gn_b"][l], 4))
        put(l, "qnorm", fm(inp["mla_q_norm"][l], 4))
        put(l, "kvnorm", fm(inp["mla_kv_norm"][l], 2))
        put(l, "qn_g", inp["mla_qn_g"][l][:, None])
        put(l, "kn_g", inp["mla_kn_g"][l][:, None])
        dw = inp["ffn_dw"][l]
        put(l, "dw", np.concatenate([fm(dw[0], 88), fm(dw[1], 88), fm(dw[2], 88)], 1))
        put(l, "db", fm(inp["ffn_db"][l], 88))
        mu = inp["rw_mu"][l]
        put(l, "mul", np.stack([mu[1536 + 96 * i:1536 + 96 * (i + 1)] for i in range(4)], 1))
        put(l, "mug", mu[1920:1984][:, None])
        put(l, "qr_g", inp["mla_qr_g"][l][:, None])
        put(l, "kr_g", inp["mla_kr_g"][l][:, None])
    lb = np.zeros((128, 4, 2, L), np.float32)
    for l in range(L):
        for d in range(2):
            lb[:, :, d, l] = fm(inp["hg_lb"][l, d], 4)
    return v, lb.reshape(128, 4 * 2 * L)


class ArenaMem:
    def __init__(self, P, name, nwords):
        self.buf = P.sb(name, [128, nwords], F32)
        self.n = nwords
        self.off = 0


class Arena:
    def __init__(self, mem, dt):
        self.mem = mem
        self.dt = dt

    def reset(self):
        self.mem.off = 0

    def alloc(self, name, shape):
        size = int(np.prod(shape))
        words = size if self.dt == F32 else (size + 1) // 2
        words = (words + 3) // 4 * 4
        m = self.mem
        assert m.off + words <= m.n, (name, m.off, words, m.n)
        ap = m.buf.h[:, m.off:m.off + words]
        m.off += words
        if self.dt != F32:
            ap = ap.bitcast(self.dt)
        ap = ap[:, 0:size]
        if len(shape) == 2:
            ap = ap.rearrange("p (a b) -> p a b", b=shape[1])
        elif len(shape) == 3:
            ap = ap.rearrange("p (a b c) -> p a b c", b=shape[1], c=shape[2])
        return Buf(name, ap)


def scan_tiles(d):
    if d == 0:
        return [(t0, n, False) for (t0, n) in TILES]
    return [(0, 256, True)] + [(t0, 512, True) for t0 in (1792, 1280, 768, 256)]


def tsl(ap2d, t0, n, rev):
    a = ap2d[:, t0:t0 + n]
    return a[:, ::-1] if rev else a


def build_program(nl=L, dbg=()):
    P = Prog()
    P.init_psum()
    EI = "ExternalInput"
    xT_in = P.dram("xT", [D, T], F32, EI)
    cT_in = P.dram("cT", [128, 32], F32, EI)
    vec_in = P.dram("vecs", [L, 128, NV], F32, EI)
    lb_in = P.dram("hglb", [128, 4 * 2 * L], F32, EI)
    cst_in = P.dram("cst", [128, NCST], F32, EI)
    cs_in = P.dram("rope", [2, 64, NLAT], F32, EI)
    w_mod = P.dram("w_mod", [L, D, 6 * D], F32, EI)
    w_in = P.dram("w_in", [L, D, IN_COLS], F32, EI)
    w_out = P.dram("w_out", [L, D, D], F32, EI)
    rw_w2 = P.dram("rw_w2", [L, 2, 96, 512], F32, EI)
    rw_a2 = P.dram("rw_a2", [L, 2, 96, 512], F32, EI)
    rw_g2 = P.dram("rw_g2", [L, 64, 512], F32, EI)
    w_uq = P.dram("mla_w_uq", [L, 512, 1536], F32, EI)
    w_ukv = P.dram("mla_w_ukv", [L, 256, 2048], F32, EI)
    ffn_up = P.dram("ffn_up", [L, D, 2 * DFF], F32, EI)
    ffn_down = P.dram("ffn_down", [L, DFF, D], F32, EI)
    out = P.dram("out", [D, NLAT], F32, "ExternalOutput")
    dbg_out = {}
    for n, shp in dbg:
        dbg_out[n] = P.dram("dbg_" + n, shp, F32, "ExternalOutput")

    xs = P.dram("xs", [D, T], F32)
    yT = P.dram("yT", [D, T], BF16)
    zT = P.dram("zT", [DFF, T], BF16)
    rwaux = P.dram("rwaux", [5, 512, T], F32)
    xs_v = [[P.view(xs, "xs_%d_%d" % (m, ti)) for ti in range(5)] for m in range(KC)]
    xin_v = [[P.view(xT_in, "xin_%d_%d" % (m, ti)) for ti in range(5)] for m in range(KC)]
    yT_v = [[P.view(yT, "yT_%d_%d" % (m, ti)) for ti in range(5)] for m in range(KC)]
    zT_v = [P.view(zT, "zT_%d" % j) for j in range(44)]
    aux_v = [[P.view(rwaux, "aux_%d_%d" % (a, m)) for m in range(4)] for a in range(5)]
    out_v = [[P.view(out, "out_%d_%d" % (m, ti)) for ti in range(5)] for m in range(KC)]

    cst = P.sb("cst_s", [128, NCST], F32)
    cstb = P.sb("cst_b", [128, NCST], BF16)
    P.dma("sp", cst[:], cst_in[:], [cst_in], [cst])
    P.cp(cstb[:], cst[:], [cst], [cstb])

    def C(n, rows=128, bf=False, c0=0, c1=None):
        o, w = C128[n]
        c1 = w if c1 is None else c1
        return (cstb if bf else cst)[0:rows, o + c0:o + c1]

    hT = P.sb("hT", [128, KC, T], BF16)
    hT_v = [P.view(hT, "hT_%d" % ti) for ti in range(5)]
    vec = P.sb("vec", [128, NV], F32)
    modT = P.sb("modT", [128, 96, 2], F32)
    mA = P.sb("mA", [128, 2, KC, 2], F32)
    lbt = P.sb("lbt", [128, 4, 2, L], F32)
    omlb = P.sb("omlb", [128, 4, 2, L], F32)
    cT = P.sb("cTs", [128, 32], F32)
    sT = P.sb("sTs", [128, KC, 2], BF16)
    epsD = P.sb("epsD", [128, 4], F32)
    wpool = P.pool("wp", [128, KC, 128], BF16, 3)
    _mem = ArenaMem(P, "arena", 27600)
    AF_ = Arena(_mem, F32)
    AB_ = Arena(_mem, BF16)

    def V(n, rows=128, c0=0, c1=None):
        o, w = V128[n]
        c1 = w if c1 is None else c1
        return vec[0:rows, o + c0:o + c1]

    P.memset(_mem.buf[:, :], 0.0, [_mem.buf])
    P.memset(hT[:].rearrange('p k t -> p (k t)'), 0.0, [hT])
    P.memset(epsD[:, 0:1], EPS, [epsD])
    P.memset(epsD[:, 1:2], 64e-5, [epsD])
    P.memset(epsD[:, 2:3], 1e-24, [epsD])
    P.memset(epsD[:, 3:4], 0.0, [epsD])

    P.dma("sp", cT[:], cT_in[:], [cT_in], [cT])
    P.act(sT[:].rearrange("p k g -> p (k g)"), cT[:], AF.Silu, [cT], [sT])

    lbe = P.sb("lbe", [128, 8, L], F32)
    lbs = P.sb("lbs", [128, 8], F32)
    P.dma("sp", lbe[:].rearrange("p a l -> p (a l)"), lb_in[:], [lb_in], [lbe])
    P.act(lbe[:], lbe[:], AF.Exp, [lbe], [lbe])
    P.emit("dve", lambda e: e.reduce_sum(out=lbs[:], in_=lbe[:], axis=mybir.AxisListType.X), [lbe], [lbs])
    P.recip(lbs[:], lbs[:], [lbs], [lbs])
    P.tt(lbe[:], lbe[:], lbs[:].unsqueeze(2).to_broadcast([128, 8, L]), ALU.mult, [lbe, lbs], [lbe])
    lb3 = lbt[:].rearrange("p h d l -> p (h d) l")
    P.memset(lb3[:, :, 0:1], 0.0, [lbt])
    for l in range(1, L):
        P.tt(lb3[:, :, l:l + 1], lb3[:, :, l - 1:l], lbe[:, :, l:l + 1], ALU.add, [lbt, lbe], [lbt])
    P.ts(omlb[:], lbt[:], -1.0, 1.0, ALU.mult, ALU.add, [lbt], [omlb])

    widx = [0]

    def load_w(src_buf, src_ap, rows=128, kc=KC, m=128):
        wb = wpool.next()
        P.dma("pool", wb[0:rows, 0:kc, 0:m], src_ap, [src_buf], [wb])
        return wb

    def x_src(l, m, ti):
        return (xin_v if l == 0 else xs_v)[m][ti], (xT_in if l == 0 else xs)

    def stage_mod(l):
        P.dma("sp", vec[:], vec_in[l], [vec_in], [vec])
        for j in range(96):
            wb = load_w(w_mod, w_mod[l][:, j * 128:(j + 1) * 128].rearrange("(k p) n -> p k n", p=128))
            ps = P.psb()
            for k in range(KC):
                P.mm(ps[:, 0:2], wb[:, k, :], sT[:, k, :], k == 0, k == KC - 1, [wb, sT], [ps])
            P.ts(modT[:, j, :], ps[:, 0:2], V("bm", c0=j, c1=j + 1), None, ALU.add, ALU.bypass, [ps, vec], [modT])
        for i, (ng, sc0) in enumerate([("ng1", 16), ("ng2", 64)]):
            P.ts(mA[:, i, :, :], modT[:, sc0:sc0 + 16, :], 1.0, None, ALU.add, ALU.bypass, [modT], [mA])
            P.tt(mA[:, i, :, :], mA[:, i, :, :], V(ng).unsqueeze(2).to_broadcast([128, 16, 2]), ALU.mult,
                 [mA, vec], [mA])

    def stage_norm(l, i):
        P.barrier()
        AF_.reset()
        AB_.reset()
        xts = [AF_.alloc("xt%d" % b, [KC, 512]) for b in range(2)]
        sqs = [AB_.alloc("sq%d" % b, [KC, 512]) for b in range(2)]
        rss = [AF_.alloc("rs%d" % b, [512]) for b in range(2)]
        sh0 = 0 if i == 0 else 48
        for ti, (t0, n) in enumerate(TILES):
            g = 1 if ti == 0 else 0
            xt, sq, rs = xts[ti % 2], sqs[ti % 2], rss[ti % 2]
            srcs = [x_src(l if i == 0 else 99, m, ti)[0] for m in range(KC)]
            src = xT_in if (l == 0 and i == 0) else xs
            P.dma("sp", xt[:, :, 0:n], src[:, t0:t0 + n].rearrange("(k p) t -> p k t", p=128), srcs, [xt])
            P.act(sq[:, :, 0:n], xt[:, :, 0:n], AF.Square, [xt], [sq])
            ps = P.psb()
            for k in range(KC):
                P.mm(ps[:, 0:n], C("ones", bf=True), sq[:, k, 0:n], k == 0, k == KC - 1, [cstb, sq], [ps])
            P.act(rs[:, 0:n], ps[:, 0:n], AF.Sqrt, [ps, epsD], [rs], scale=1.0 / D, bias=epsD[:, 0:1])
            P.recip(rs[:, 0:n], rs[:, 0:n], [rs], [rs])
            P.tt(xt[:, :, 0:n], xt[:, :, 0:n], rs[:, 0:n].unsqueeze(1).to_broadcast([128, KC, n]), ALU.mult,
                 [xt, rs], [xt])
            for k in range(KC):
                P.act(hT[:, k, t0:t0 + n], xt[:, k, 0:n], AF.Identity, [xt, mA, modT], [hT_v[ti]],
                      scale=mA[:, i, k, g:g + 1], bias=modT[:, sh0 + k, g:g + 1])

    def proj(l, c0, m, evac, wsrc=None, tiles=None):
        wb = load_w(w_in, w_in[l][:, c0:c0 + m].rearrange("(k p) n -> p k n", p=128), m=m)
        for ti, (t0, n) in enumerate(TILES):
            ps = P.psb()
            for k in range(KC):
                P.mm(ps[0:m, 0:n], wb[:, k, 0:m], hT[:, k, t0:t0 + n], k == 0, k == KC - 1, [wb, hT_v[ti]], [ps])
            evac(ps, ti, t0, n)

    def proj_full(l, c0, m, dst, eng="act"):
        def ev(ps, ti, t0, n):
            P.cp(dst[0:m, t0:t0 + n], ps[0:m, 0:n], [ps], [dst], eng=eng)
        proj(l, c0, m, ev)

    def rms_over_partitions(src_ap, n, rows, ones_ap, inv_count, eps_col, sqb, rsb, R):
        P.act(sqb[0:rows, 0:n], src_ap, AF.Square, R, [sqb])
        ps = P.psb()
        P.mm(ps[0:rows, 0:n], ones_ap, sqb[0:rows, 0:n], True, True, [cstb, sqb], [ps])
        P.act(rsb[0:rows, 0:n], ps[0:rows, 0:n], AF.Sqrt, [ps, epsD], [rsb], scale=inv_count,
              bias=epsD[0:rows, eps_col:eps_col + 1])
        P.recip(rsb[0:rows, 0:n], rsb[0:rows, 0:n], [rsb], [rsb])

    def stage_out(l):
        P.barrier()
        AF_.reset()
        yt = AB_.alloc("ytall", [KC, T])
        yt_v = [P.view(yt, "ytall_%d" % ti) for ti in range(5)]
        xps = Pool([AF_.alloc("xo%d" % b, [512]) for b in range(4)])
        for ti, (t0, n) in enumerate(TILES):
            P.dma("sp", yt[:, :, t0:t0 + n], yT[:, t0:t0 + n].rearrange("(k p) t -> p k t", p=128),
                  [yT_v[m][ti] for m in range(KC)], [yt_v[ti]])
        for m in range(KC):
            wb = load_w(w_out, w_out[l][:, m * 128:(m + 1) * 128].rearrange("(k p) n -> p k n", p=128))
            for ti, (t0, n) in enumerate(TILES):
                g = 1 if ti == 0 else 0
                xv, xsrc = x_src(l, m, ti)
                xo = xps.next()
                P.dma("sp", xo[:, 0:n], xsrc[m * 128:(m + 1) * 128, t0:t0 + n], [xv], [xo])
                ps = P.psb()
                for k in range(KC):
                    P.mm(ps[:, 0:n], wb[:, k, :], yt[:, k, t0:t0 + n], k == 0, k == KC - 1, [wb, yt_v[ti]], [ps])
                P.stt(xo[:, 0:n], ps[:, 0:n], modT[:, 32 + m, g:g + 1], xo[:, 0:n], ALU.mult, ALU.add,
                      [ps, modT, xo], [xo])
                P.dma("sp", xs[m * 128:(m + 1) * 128, t0:t0 + n], xo[:, 0:n], [xo], [xs_v[m][ti]])

    def stage_ffn(l, last):
        P.barrier()
        AF_.reset()
        AB_.reset()
        ups = [AF_.alloc("up%d" % b, [T]) for b in range(2)]
        cvs = [AF_.alloc("cv%d" % b, [T]) for b in range(2)]
        zb = Pool([AB_.alloc("zb%d" % b, [T]) for b in range(2)])
        dwo, _ = V128["dw"]
        for j in range(44):
            for half in range(2):
                cidx = j + 44 * half
                up, cv = ups[half], cvs[half]
                wb = load_w(ffn_up, ffn_up[l][:, cidx * 128:(cidx + 1) * 128].rearrange("(k p) n -> p k n", p=128))
                for ti, (t0, n) in enumerate(TILES):
                    ps = P.psb()
                    for k in range(KC):
                        P.mm(ps[:, 0:n], wb[:, k, :], hT[:, k, t0:t0 + n], k == 0, k == KC - 1, [wb, hT_v[ti]], [ps])
                    P.cp(up[:, t0:t0 + n], ps[:, 0:n], [ps], [up], eng="act" if ti % 2 else "dve")
                P.act(cv[:, :], up[:, :], AF.Identity, [up, vec], [cv],
                      scale=vec[:, dwo + 88 + cidx:dwo + 88 + cidx + 1], bias=V("db", c0=cidx, c1=cidx + 1))
                for (s0, s1) in SEGS:
                    P.stt(cv[:, s0 + 1:s1], up[:, s0:s1 - 1], vec[:, dwo + cidx:dwo + cidx + 1], cv[:, s0 + 1:s1],
                          ALU.mult, ALU.add, [up, vec, cv], [cv])
                    P.stt(cv[:, s0:s1 - 1], up[:, s0 + 1:s1], vec[:, dwo + 176 + cidx:dwo + 176 + cidx + 1],
                          cv[:, s0:s1 - 1], ALU.mult, ALU.add, [up, vec, cv], [cv])
            z = zb.next()
            P.act(cvs[0][:, :], cvs[0][:, :], AF.Silu, [cvs[0]], [cvs[0]])
            P.tt(z[:, :], cvs[0][:, :], cvs[1][:, :], ALU.mult, [cvs[0], cvs[1]], [z])
            P.dma("sp", zT[j * 128:(j + 1) * 128, :], z[:, :], [z], [zT_v[j]])
        P.barrier()
        AF_.reset()
        AB_.reset()
        zt = AB_.alloc("zt", [44, 1024])
        wdp = Pool([AB_.alloc("wd%d" % b, [512]) for b in range(3)])
        wdf = Pool([AF_.alloc("wdf%d" % b, [512]) for b in range(3)])
        xps = Pool([AF_.alloc("xo%d" % b, [512]) for b in range(2)])
        tidx = {t0_: i_ for i_, (t0_, _n) in enumerate(TILES)}
        for (t0, n) in [(0, 256), (256, 1024), (1280, 1024)]:
            if last and t0 == 0:
                continue
            g = 1 if t0 == 0 else 0
            nh = (n + 511) // 512
            P.dma("sp", zt[:, :, 0:n], zT[:, t0:t0 + n].rearrange("(k p) t -> p k t", p=128), zT_v, [zt])
            for q in range(4):
                pss = [[P.psb() for _h in range(nh)] for _ in range(4)]
                for k in range(44):
                    wd, wf = wdp.next(), wdf.next()
                    P.dma("sp" if k % 2 else "act", wf[:, :], ffn_down[l][k * 128:(k + 1) * 128, q * 512:(q + 1) * 512],
                          [ffn_down], [wf])
                    P.cp(wd[:, :], wf[:, :], [wf], [wd], eng="act" if k % 2 else "dve")
                    for mm_ in range(4):
                        for hf in range(nh):
                            nn = min(512, n - hf * 512)
                            P.mm(pss[mm_][hf][:, 0:nn], wd[:, mm_ * 128:(mm_ + 1) * 128],
                                 zt[:, k, hf * 512:hf * 512 + nn], k == 0, k == 43, [wd, zt], [pss[mm_][hf]])
                for mm_ in range(4):
                    m = q * 4 + mm_
                    for hf in range(nh):
                        nn = min(512, n - hf * 512)
                        th = t0 + hf * 512
                        ti = tidx[th]
                        xo = xps.next()
                        P.dma("sp", xo[:, 0:nn], xs[m * 128:(m + 1) * 128, th:th + nn], [xs_v[m][ti]], [xo])
                        P.stt(xo[:, 0:nn], pss[mm_][hf][:, 0:nn], modT[:, 80 + m, g:g + 1], xo[:, 0:nn], ALU.mult,
                              ALU.add, [pss[mm_][hf], modT, xo], [xo])
                        if last:
                            P.dma("sp", out[m * 128:(m + 1) * 128, th - NCTX:th - NCTX + nn], xo[:, 0:nn], [xo],
                                  [out_v[m][ti]])
                        else:
                            P.dma("sp", xs[m * 128:(m + 1) * 128, th:th + nn], xo[:, 0:nn], [xo], [xs_v[m][ti]])

    def token_shift(pf, sf, m, mu_ap):
        for (s0, s1) in SEGS:
            P.tt(sf[0:m, s0 + 1:s1 - 1], pf[0:m, s0:s1 - 2], pf[0:m, s0 + 2:s1], ALU.add, [pf], [sf])
            P.cp(sf[0:m, s0:s0 + 1], pf[0:m, s0 + 1:s0 + 2], [pf], [sf])
            P.cp(sf[0:m, s1 - 1:s1], pf[0:m, s1 - 2:s1 - 1], [pf], [sf])
        P.stt(sf[0:m, :], sf[0:m, :], 0.5, pf[0:m, :], ALU.mult, ALU.subtract, [sf, pf], [sf])
        P.stt(sf[0:m, :], sf[0:m, :], mu_ap, pf[0:m, :], ALU.mult, ALU.add, [sf, pf, vec], [sf])

    def proj_to(l, c0, m, buf, dst_fn, eng="act"):
        def ev(ps, ti, t0, n):
            P.cp(dst_fn(t0, n), ps[0:m, 0:n], [ps], [buf], eng=eng)
        proj(l, c0, m, ev)

    MLA_SCALE = 192.0 ** -0.5

    def mixer_mla(l):
        P.barrier()
        AF_.reset()
        cq = AB_.alloc("cq", [4, T])
        ckv = AB_.alloc("ckv", [2, T])
        krT = AB_.alloc("krT", [T])
        mark = _mem.off
        scr = AF_.alloc("scr", [4, T])
        sq4 = AB_.alloc("sq4", [4, 512])
        rs1 = AF_.alloc("rs1", [512])
        tA = AF_.alloc("tA", [512])
        tB = AF_.alloc("tB", [512])
        csT = AF_.alloc("csT", [2, 512])
        qb = AB_.alloc("qb", [512])
        sq1p = AB_.alloc("sq1p", [512])

        def rope(dst_bf, dst_buf, src32, src_buf, t0, n):
            P.dma("sp", csT[0:64, :, 0:n], cs_in[:, :, t0 - NCTX:t0 - NCTX + n].rearrange("c p t -> p c t"),
                  [cs_in], [csT])
            P.cp(qb[0:64, 0:n], src32, [src_buf], [qb], eng="act")
            ps = P.psb()
            P.mm(ps[0:64, 0:n], C("rotT", rows=64, bf=True), qb[0:64, 0:n], True, True, [cstb, qb], [ps])
            P.tt(tB[0:64, 0:n], ps[0:64, 0:n], csT[0:64, 1, 0:n], ALU.mult, [ps, csT], [tB])
            P.tt(src32, src32, csT[0:64, 0, 0:n], ALU.mult, [src_buf, csT], [src_buf])
            P.tt(dst_bf, src32, tB[0:64, 0:n], ALU.add, [src_buf, tB], [dst_buf])

        def norm_chunks(src3, nk, dstb, gname, cnt):
            for ti, (t0, n) in enumerate(TILES):
                P.act(sq4[:, 0:nk, 0:n], src3[:, 0:nk, t0:t0 + n], AF.Square, [scr], [sq4])
                ps = P.psb()
                for k in range(nk):
                    P.mm(ps[:, 0:n], C("ones", bf=True), sq4[:, k, 0:n], k == 0, k == nk - 1, [cstb, sq4], [ps])
                P.act(rs1[:, 0:n], ps[:, 0:n], AF.Sqrt, [ps, epsD], [rs1], scale=1.0 / cnt, bias=epsD[:, 0:1])
                P.recip(rs1[:, 0:n], rs1[:, 0:n], [rs1], [rs1])
                for k in range(nk):
                    P.stt(dstb[:, k, t0:t0 + n], src3[:, k, t0:t0 + n], V(gname, c0=k, c1=k + 1), rs1[:, 0:n],
                          ALU.mult, ALU.mult, [scr, vec, rs1], [dstb])

        for k in range(4):
            proj_to(l, ML0 + k * 128, 128, scr, lambda t0, n, k=k: scr[:, k, t0:t0 + n])
        norm_chunks(scr, 4, cq, "qnorm", 512)
        for k in range(2):
            proj_to(l, ML0 + 512 + k * 128, 128, scr, lambda t0, n, k=k: scr[:, k, t0:t0 + n])
        proj_to(l, ML0 + 768, 64, scr, lambda t0, n: scr[0:64, 2, t0:t0 + n])
        norm_chunks(scr, 2, ckv, "kvnorm", 256)
        for ti, (t0, n) in enumerate(TILES):
            rms_over_partitions(scr[0:64, 2, t0:t0 + n], n, 64, C("ones", rows=64, bf=True, c1=64), 1.0 / 64, 0,
                                sq1p, rs1, [scr])
            P.stt(tA[0:64, 0:n], scr[0:64, 2, t0:t0 + n], V("kr_g", rows=64), rs1[0:64, 0:n], ALU.mult, ALU.mult,
                  [scr, vec, rs1], [tA])
            if ti == 0:
                P.cp(krT[0:64, t0:t0 + n], tA[0:64, 0:n], [tA], [krT])
            else:
                rope(krT[0:64, t0:t0 + n], krT, tA[0:64, 0:n], tA, t0, n)
        P.barrier()
        _mem.off = mark
        P.memset(_mem.buf[:, mark:_mem.n], 0.0, [_mem.buf])
        P.barrier()
        qnT = AB_.alloc("qnT", [T])
        qrT = AB_.alloc("qrT", [T])
        knT = AB_.alloc("knT", [T])
        Vt = AB_.alloc("Vt", [18, 128])
        wq = AB_.alloc("wq", [4, 192])
        wkv = AB_.alloc("wkv", [2, 256])
        sq1 = AB_.alloc("sq1", [512])
        rs1 = AF_.alloc("rs1b", [512])
        tA = AF_.alloc("tAb", [512])
        tB = AF_.alloc("tBb", [512])
        csT = AF_.alloc("csTb", [2, 512])
        qb = AB_.alloc("qbb", [512])
        ptp = Pool([AB_.alloc("pt%d" % b, [512]) for b in range(4)])
        yb = Pool([AB_.alloc("yb%d" % b, [512]) for b in range(2)])
        for h in range(8):
            P.dma("pool", wq[:, :, :], w_uq[l][:, h * 192:(h + 1) * 192].rearrange("(k p) n -> p k n", p=128),
                  [w_uq], [wq])
            P.dma("pool", wkv[:, :, :], w_ukv[l][:, h * 256:(h + 1) * 256].rearrange("(k p) n -> p k n", p=128),
                  [w_ukv], [wkv])
            for ti, (t0, n) in enumerate(TILES):
                ps = P.psb()
                for k in range(4):
                    P.mm(ps[:, 0:n], wq[:, k, 0:128], cq[:, k, t0:t0 + n], k == 0, k == 3, [wq, cq], [ps])
                rms_over_partitions(ps[:, 0:n], n, 128, C("ones", bf=True), 1.0 / 128, 0, sq1, rs1, [ps])
                P.stt(qnT[:, t0:t0 + n], ps[:, 0:n], V("qn_g"), rs1[:, 0:n], ALU.mult, ALU.mult, [ps, vec, rs1], [qnT])
                ps = P.psb()
                for k in range(4):
                    P.mm(ps[0:64, 0:n], wq[:, k, 128:192], cq[:, k, t0:t0 + n], k == 0, k == 3, [wq, cq], [ps])
                rms_over_partitions(ps[0:64, 0:n], n, 64, C("ones", rows=64, bf=True, c1=64), 1.0 / 64, 0, sq1, rs1, [ps])
                P.stt(tA[0:64, 0:n], ps[0:64, 0:n], V("qr_g", rows=64), rs1[0:64, 0:n], ALU.mult, ALU.mult,
                      [ps, vec, rs1], [tA])
                if ti == 0:
                    P.cp(qrT[0:64, t0:t0 + n], tA[0:64, 0:n], [tA], [qrT])
                else:
                    rope(qrT[0:64, t0:t0 + n], qrT, tA[0:64, 0:n], tA, t0, n)
                ps = P.psb()
                for k in range(2):
                    P.mm(ps[:, 0:n], wkv[:, k, 0:128], ckv[:, k, t0:t0 + n], k == 0, k == 1, [wkv, ckv], [ps])
                rms_over_partitions(ps[:, 0:n], n, 128, C("ones", bf=True), 1.0 / 128, 0, sq1, rs1, [ps])
                P.stt(knT[:, t0:t0 + n], ps[:, 0:n], V("kn_g"), rs1[:, 0:n], ALU.mult, ALU.mult, [ps, vec, rs1], [knT])
            for b0 in range(0, 18, 4):
                nb = min(4, 18 - b0)
                ps = P.psb()
                for j in range(nb):
                    b = b0 + j
                    for k in range(2):
                        P.mm(ps[:, j * 128:(j + 1) * 128], ckv[:, k, b * 128:(b + 1) * 128], wkv[:, k, 128:256],
                             k == 0, k == 1, [ckv, wkv], [ps])
                P.cp(Vt[:, b0:b0 + nb, :], ps[:, 0:nb * 128].rearrange("p (a b) -> p a b", b=128), [ps], [Vt], eng="act")
            for ti, (t0, n) in enumerate(TILES):
                nkb = 2 if ti == 0 else 18
                pso = P.hold()
                psd = P.hold()
                pend = None
                for kb in range(nkb + 1):
                    cur = None
                    if kb < nkb:
                        pss = P.psb()
                        ks = slice(kb * 128, (kb + 1) * 128)
                        P.mm(pss[:, 0:n], knT[:, ks], qnT[:, t0:t0 + n], True, False, [knT, qnT], [pss])
                        P.mm(pss[:, 0:n], krT[0:64, ks], qrT[0:64, t0:t0 + n], False, True, [krT, qrT], [pss])
                        pt = ptp.next()
                        P.act(pt[:, 0:n], pss[:, 0:n], AF.Exp, [pss], [pt], scale=MLA_SCALE)
                        cur = (pt, kb)
                    if pend is not None:
                        ppt, pkb = pend
                        P.mm(pso[:, 0:n], Vt[:, pkb, :], ppt[:, 0:n], pkb == 0, pkb == nkb - 1, [Vt, ppt], [pso])
                        P.mm(psd[:, 0:n], C("ones", bf=True), ppt[:, 0:n], pkb == 0, pkb == nkb - 1, [cstb, ppt], [psd])
                    pend = cur
                P.recip(tA[:, 0:n], psd[:, 0:n], [psd], [tA])
                y = yb.next()
                P.tt(y[:, 0:n], pso[:, 0:n], tA[:, 0:n], ALU.mult, [pso, tA], [y])
                P.release(pso)
                P.release(psd)
                P.dma("sp", yT[(8 + h) * 128:(9 + h) * 128, t0:t0 + n], y[:, 0:n], [y], [yT_v[8 + h][ti]])
    def mixer_hgrn2(l):
        for h in range(4):
            P.barrier()
            AF_.reset()
            qf = AF_.alloc("qf", [T])
            vf = AF_.alloc("vf", [T])
            zfs = [AF_.alloc("zf%d" % d_, [T]) for d_ in range(2)]
            oTs = [AF_.alloc("oT%d" % d_, [T]) for d_ in range(2)]
            sq1 = AB_.alloc("sq1", [512])
            yb = Pool([AB_.alloc("yb%d" % b, [512]) for b in range(2)])
            proj_to(l, h * 128, 128, qf, lambda t0, n: qf[:, t0:t0 + n])
            proj_to(l, 1536 + h * 128, 128, vf, lambda t0, n: vf[:, t0:t0 + n])
            identb = C("ident", bf=True)

            def dir_gen(d):
                zf, oT = zfs[d], oTs[d]
                t1 = AF_.alloc("t1_%d" % d, [512])
                t2 = AF_.alloc("t2_%d" % d, [512])
                t3 = AF_.alloc("t3_%d" % d, [512])
                Fc = AF_.alloc("Fc%d" % d, [32])
                qt = AB_.alloc("qt%d" % d, [512])
                kt = AB_.alloc("kt%d" % d, [512])
                kb = AB_.alloc("kb%d" % d, [512])
                vb = AB_.alloc("vb%d" % d, [512])
                kvp = Pool([AB_.alloc("kvT%d_%d" % (d, b), [8, 128]) for b in range(3)])
                atp = Pool([AB_.alloc("aT%d_%d" % (d, b), [4, 16]) for b in range(3)])
                S32 = [AF_.alloc("S32_%d_%d" % (d, b), [128]) for b in range(2)]
                Sbf = Pool([AB_.alloc("Sbf%d_%d" % (d, b), [128]) for b in range(3)])
                proj_to(l, 512 * (1 + d) + h * 128, 128, zf, lambda t0, n: zf[:, t0:t0 + n])
                yield
                state = None
                cidx = 0
                for (t0, n, rev) in scan_tiles(d):
                    nch = n // 16
                    P.act(t1[:, 0:n], tsl(zf, t0, n, rev), AF.Sigmoid, [zf], [t1])
                    P.ts(t1[:, 0:n], t1[:, 0:n], omlb[:, h, d, l:l + 1], lbt[:, h, d, l:l + 1], ALU.mult, ALU.add,
                         [t1, omlb, lbt], [t1])
                    P.act(t2[:, 0:n], t1[:, 0:n], AF.Ln, [t1], [t2])
                    P.ts(t1[:, 0:n], t1[:, 0:n], -1.0, 1.0, ALU.mult, ALU.add, [t1], [t1])
                    P.scan(t3[:, 0:n], C("rs16", c1=n), t2[:, 0:n], [cst, t2], [t3])
                    P.act(t2[:, 0:n], t3[:, 0:n], AF.Exp, [t3], [t2])
                    P.tt(qt[:, 0:n], tsl(qf, t0, n, rev), t2[:, 0:n], ALU.mult, [qf, t2], [qt])
                    P.act(t2[:, 0:n], t3[:, 0:n], AF.Exp, [t3], [t2], scale=-1.0)
                    P.tt(kt[:, 0:n], t1[:, 0:n], t2[:, 0:n], ALU.mult, [t1, t2], [kt])
                    G3 = t3[:, 0:n].rearrange("p (c i) -> p c i", i=16)
                    P.tt(t2[:, 0:n].rearrange("p (c i) -> p c i", i=16), G3[:, :, 15:16].to_broadcast([128, nch, 16]),
                         G3, ALU.subtract, [t3], [t2])
                    P.act(t2[:, 0:n], t2[:, 0:n], AF.Exp, [t2], [t2])
                    P.tt(kb[:, 0:n], t1[:, 0:n], t2[:, 0:n], ALU.mult, [t1, t2], [kb])
                    P.act(Fc[:, 0:nch], G3[:, :, 15], AF.Exp, [t3], [Fc])
                    P.cp(vb[:, 0:n], tsl(vf, t0, n, rev), [vf], [vb])
                    yield
                    pso = P.hold()
                    for g0 in range(0, nch, 4):
                        psT = P.psb()
                        psTb = psT[:].bitcast(BF16)
                        for j in range(4):
                            cs = slice((g0 + j) * 16, (g0 + j + 1) * 16)
                            P.tr(psTb[0:16, j * 128:(j + 1) * 128], kb[:, cs], identb, [kb, cstb], [psT])
                            P.tr(psTb[0:16, (4 + j) * 128:(5 + j) * 128], vb[:, cs], identb, [vb, cstb], [psT])
                        kvT = kvp.next()
                        P.cp(kvT[0:16, :, :].rearrange("p a b -> p (a b)"), psTb[0:16, :], [psT], [kvT], eng="act")
                        psA = P.psb()
                        for j in range(4):
                            cs = slice((g0 + j) * 16, (g0 + j + 1) * 16)
                            P.mm(psA[0:16, j * 16:(j + 1) * 16], kt[:, cs], qt[:, cs], True, True, [kt, qt], [psA])
                        aT = atp.next()
                        P.tt(aT[0:16, :, :], psA[0:16, 0:64].rearrange("p (a b) -> p a b", b=16),
                             C("m16", rows=16).unsqueeze(1).to_broadcast([16, 4, 16]), ALU.mult, [psA, cst], [aT])
                        yield
                        psG = P.psb()
                        for j in range(4):
                            P.mm(psG[:, j * 128:(j + 1) * 128], kvT[0:16, j, :], kvT[0:16, 4 + j, :], True, True,
                                 [kvT], [psG])
                        for j in range(4):
                            c = g0 + j
                            cs = slice(c * 16, (c + 1) * 16)
                            P.mm(pso[:, cs], kvT[0:16, 4 + j, :], aT[0:16, j, :], True, state is None, [kvT, aT], [pso])
                            if state is not None:
                                P.mm(pso[:, cs], state[1][:, :], qt[:, cs], False, True, [state[1], qt], [pso])
                            new32 = S32[cidx % 2]
                            nbf = Sbf.next()
                            if state is None:
                                P.cp(nbf[:, :], psG[:, j * 128:(j + 1) * 128], [psG], [nbf])
                                P.cp(new32[:, :], psG[:, j * 128:(j + 1) * 128], [psG], [new32])
                            else:
                                P.stt(nbf[:, :], state[0][:, :], Fc[:, c:c + 1], psG[:, j * 128:(j + 1) * 128],
                                      ALU.mult, ALU.add, [state[0], Fc, psG], [nbf])
                                P.stt(new32[:, :], state[0][:, :], Fc[:, c:c + 1], psG[:, j * 128:(j + 1) * 128],
                                      ALU.mult, ALU.add, [state[0], Fc, psG], [new32])
                            state = (new32, nbf)
                            cidx += 1
                            yield
                    P.cp(tsl(oT, t0, n, rev), pso[:, 0:n], [pso], [oT])
                    P.release(pso)

            live = [dir_gen(0), dir_gen(1)]
            while live:
                for g_ in list(live):
                    try:
                        next(g_)
                    except StopIteration:
                        live.remove(g_)
            zf = zfs[0]
            t1 = AF_.alloc("t1o", [512])
            t2 = AF_.alloc("t2o", [512])
            t3 = AF_.alloc("t3o", [512])
            proj_to(l, 2048 + h * 128, 128, zf, lambda t0, n: zf[:, t0:t0 + n])
            for ti, (t0, n) in enumerate(TILES):
                P.tt(t3[:, 0:n], oTs[0][:, t0:t0 + n], oTs[1][:, t0:t0 + n], ALU.add, [oTs[0], oTs[1]], [t3])
                rms_over_partitions(t3[:, 0:n], n, 128, C("ones", bf=True), 1.0 / 128, 0, sq1, t1, [t3])
                P.stt(t2[:, 0:n], t3[:, 0:n], V("hgn"), t1[:, 0:n], ALU.mult, ALU.mult, [t3, vec, t1], [t2])
                P.act(t3[:, 0:n], zf[:, t0:t0 + n], AF.Silu, [zf], [t3])
                y = yb.next()
                P.tt(y[:, 0:n], t2[:, 0:n], t3[:, 0:n], ALU.mult, [t2, t3], [y])
                P.dma("sp", yT[h * 128:(h + 1) * 128, t0:t0 + n], y[:, 0:n], [y], [yT_v[h][ti]])

    def rwkv_lora(l):
        P.barrier()
        AF_.reset()
        pf = AF_.alloc("pf", [T])
        sf = AF_.alloc("sf", [T])
        xb = AB_.alloc("xb", [T])
        w2b = AB_.alloc("w2b", [512])
        otp = Pool([AF_.alloc("ot%d" % b, [512]) for b in range(3)])
        groups = [(1536, 96, "w", 0, 0), (1632, 96, "w", 1, 1), (1728, 96, "a", 0, 2), (1824, 96, "a", 1, 3),
                  (1920, 64, "g", 0, 4)]
        for gi, (c0, m, kind, d, ai) in enumerate(groups):
            proj_to(l, RW0 + c0, m, pf, lambda t0, n, m=m: pf[0:m, t0:t0 + n])
            mu_ap = V("mul", rows=96, c0=gi, c1=gi + 1) if kind != "g" else V("mug", rows=64)
            token_shift(pf, sf, m, mu_ap)
            func = AF.Tanh if kind == "w" else (AF.Identity if kind == "a" else AF.Sigmoid)
            P.act(xb[0:m, :], sf[0:m, :], func, [sf], [xb])
            if kind == "w":
                wsrc, wb_ = rw_w2[l][d], rw_w2
            elif kind == "a":
                wsrc, wb_ = rw_a2[l][d], rw_a2
            else:
                wsrc, wb_ = rw_g2[l], rw_g2
            P.dma("pool", w2b[0:m, :], wsrc, [wb_], [w2b])
            for mc in range(4):
                for ti, (t0, n) in enumerate(TILES):
                    ps = P.psb()
                    P.mm(ps[:, 0:n], w2b[0:m, mc * 128:(mc + 1) * 128], xb[0:m, t0:t0 + n], True, True, [w2b, xb], [ps])
                    o = otp.next()
                    if kind == "w":
                        P.act(o[:, 0:n], ps[:, 0:n], AF.Sigmoid, [ps, vec], [o], bias=V("w0", c0=d * 4 + mc, c1=d * 4 + mc + 1))
                        P.ts(o[:, 0:n], o[:, 0:n], -DECAY_MAX, None, ALU.mult, ALU.bypass, [o], [o])
                    elif kind == "a":
                        P.act(o[:, 0:n], ps[:, 0:n], AF.Sigmoid, [ps, vec], [o], bias=V("a0", c0=d * 4 + mc, c1=d * 4 + mc + 1))
                    else:
                        P.cp(o[:, 0:n], ps[:, 0:n], [ps], [o])
                    P.dma("sp", rwaux[ai][mc * 128:(mc + 1) * 128, t0:t0 + n], o[:, 0:n], [o], [aux_v[ai][mc]])

    def mixer_rwkv(l):
        rwkv_lora(l)
        for m in range(4):
            P.barrier()
            AF_.reset()
            rf = AF_.alloc("rf", [T])
            kf = AF_.alloc("kf", [T])
            vf = AF_.alloc("vf", [T])
            kkf = AF_.alloc("kkf", [T])
            oT = AF_.alloc("oT", [T])
            pf = oT
            tlw = AF_.alloc("tlw", [512])
            tag = AF_.alloc("tag", [512])
            tcum = AF_.alloc("tcum", [512])
            te = AF_.alloc("te", [512])
            te2 = AF_.alloc("te2", [512])
            tb_ = AF_.alloc("tb", [512])
            tkd = AF_.alloc("tkd", [512])
            WC = AF_.alloc("WC", [4])
            omka = AF_.alloc("omka", [1])
            S32 = [AF_.alloc("S32_%d" % b, [64]) for b in range(2)]
            rt = AB_.alloc("rt", [512])
            bt = AB_.alloc("bt", [512])
            kt = AB_.alloc("kt", [512])
            at = AB_.alloc("at", [512])
            bb = AB_.alloc("bb", [512])
            kb = AB_.alloc("kb", [512])
            vb = AB_.alloc("vb", [512])
            sq1 = AB_.alloc("sq1", [512])
            atm = [AB_.alloc("atm%d" % b, [512]) for b in range(2)]
            rtm = [AB_.alloc("rtm%d" % b, [512]) for b in range(2)]
            for b_ in atm + rtm:
                P.memset(b_[:, :], 0.0, [b_])
            class Slot:
                pass
            slots = []
            for si in range(2):
                S_ = Slot()
                S_.Tp = Pool([AF_.alloc("Tj%d_%d" % (si, b), [2, 128]) for b in range(2)])
                S_.Lp = Pool([AF_.alloc("Lj%d_%d" % (si, b), [2, 128]) for b in range(2)])
                S_.Zp = Pool([AF_.alloc("Z%d_%d" % (si, b), [2, 128]) for b in range(3)])
                S_.tmpP = AF_.alloc("tmpP%d" % si, [128])
                S_.Gp = AF_.alloc("Gp%d" % si, [64])
                S_.Tak = AB_.alloc("Tak%d" % si, [2, 128])
                S_.Trb = AB_.alloc("Trb%d" % si, [2, 128])
                S_.Trk = AB_.alloc("Trk%d" % si, [2, 128])
                S_.tm = AB_.alloc("tm%d" % si, [4, 128])
                S_.Z1c = AB_.alloc("Z1c%d" % si, [128])
                S_.Z2c = AB_.alloc("Z2c%d" % si, [128])
                S_.PhiT = AB_.alloc("PhiT%d" % si, [128])
                S_.QeTm = [AB_.alloc("QeTm%d_%d" % (si, b), [128]) for b in range(2)]
                for b_ in S_.QeTm:
                    P.memset(b_[:, :], 0.0, [b_])
                slots.append(S_)
            identf = C("ident")
            Sbf = Pool([AB_.alloc("Sbf%d" % b, [64]) for b in range(3)])
            yb = Pool([AB_.alloc("yb%d" % b, [512]) for b in range(2)])
            identb = C("ident", bf=True)
            for (c0, dst, mi) in [(0, rf, m), (512, kf, 4 + m), (1024, vf, 8 + m)]:
                proj_to(l, RW0 + c0 + m * 128, 128, pf, lambda t0, n: pf[:, t0:t0 + n])
                token_shift(pf, dst, 128, V("mu", c0=mi, c1=mi + 1))
            P.ts(omka[:, 0:1], V("ka", c0=m, c1=m + 1), -1.0, 1.0, ALU.mult, ALU.add, [vec], [omka])
            for ti, (t0, n) in enumerate(TILES):
                P.ts(te[:, 0:n], kf[:, t0:t0 + n], V("kk", c0=m, c1=m + 1), None, ALU.mult, ALU.bypass, [kf, vec], [te])
                P.act(sq1[:, 0:n], te[:, 0:n], AF.Square, [te], [sq1])
                ps = P.psb()
                P.mm(ps[:, 0:n], C("blk", bf=True), sq1[:, 0:n], True, True, [cstb, sq1], [ps])
                P.act(te2[:, 0:n], ps[:, 0:n], AF.Sqrt, [ps, epsD], [te2], bias=epsD[:, 2:3])
                P.recip(te2[:, 0:n], te2[:, 0:n], [te2], [te2])
                P.tt(kkf[:, t0:t0 + n], te[:, 0:n], te2[:, 0:n], ALU.mult, [te, te2], [kkf])
            for d in range(2):
                state = None
                cidx = 0
                for (t0, n, rev) in scan_tiles(d):
                    nch = n // 128

                    def S(b_):
                        a_ = b_[:, 0:n]
                        return a_[:, ::-1] if rev else a_
                    P.dma("sp", tlw[:, 0:n], rwaux[d][m * 128:(m + 1) * 128, t0:t0 + n], [aux_v[d][m]], [tlw])
                    P.dma("sp", tag[:, 0:n], rwaux[2 + d][m * 128:(m + 1) * 128, t0:t0 + n], [aux_v[2 + d][m]], [tag])
                    P.scan(tcum[:, 0:n], C("rs128", c1=n), S(tlw), [cst, tlw], [tcum])
                    P.act(te[:, 0:n], tcum[:, 0:n], AF.Exp, [tcum], [te])
                    P.tt(rt[:, 0:n], tsl(rf, t0, n, rev), te[:, 0:n], ALU.mult, [rf, te], [rt])
                    P.act(te[:, 0:n], tcum[:, 0:n], AF.Exp, [tcum], [te], scale=-1.0)
                    P.tt(tb_[:, 0:n], tsl(kkf, t0, n, rev), S(tag), ALU.mult, [kkf, tag], [tb_])
                    P.tt(bt[:, 0:n], tb_[:, 0:n], te[:, 0:n], ALU.mult, [tb_, te], [bt])
                    P.ts(tkd[:, 0:n], S(tag), V("ka", c0=m, c1=m + 1), omka[:, 0:1], ALU.mult, ALU.add, [tag, vec, omka], [tkd])
                    P.tt(tkd[:, 0:n], tkd[:, 0:n], tsl(kf, t0, n, rev), ALU.mult, [tkd, kf], [tkd])
                    P.tt(kt[:, 0:n], tkd[:, 0:n], te[:, 0:n], ALU.mult, [tkd, te], [kt])
                    P.tt(te2[:, 0:n], tcum[:, 0:n], S(tlw), ALU.subtract, [tcum, tlw], [te2])
                    P.act(te2[:, 0:n], te2[:, 0:n], AF.Exp, [te2], [te2])
                    P.stt(at[:, 0:n], tsl(kkf, t0, n, rev), -1.0, te2[:, 0:n], ALU.mult, ALU.mult, [kkf, te2], [at])
                    c3 = tcum[:, 0:n].rearrange("p (c i) -> p c i", i=128)
                    P.tt(te2[:, 0:n].rearrange("p (c i) -> p c i", i=128), c3[:, :, 127:128].to_broadcast([128, nch, 128]),
                         c3, ALU.subtract, [tcum], [te2])
                    P.act(te2[:, 0:n], te2[:, 0:n], AF.Exp, [te2], [te2])
                    P.tt(bb[:, 0:n], tb_[:, 0:n], te2[:, 0:n], ALU.mult, [tb_, te2], [bb])
                    P.tt(kb[:, 0:n], tkd[:, 0:n], te2[:, 0:n], ALU.mult, [tkd, te2], [kb])
                    P.cp(vb[:, 0:n], tsl(vf, t0, n, rev), [vf], [vb])
                    P.act(WC[:, 0:nch], c3[:, :, 127], AF.Exp, [tcum], [WC])
                    for hh in range(2):
                        pr = slice(hh * 64, hh * 64 + 64)
                        P.cp(atm[hh][pr, 0:n], at[pr, 0:n], [at], [atm[hh]])
                        P.cp(rtm[hh][pr, 0:n], rt[pr, 0:n], [rt], [rtm[hh]], eng="act")
                    psO = P.hold()

                    def v3(ap):
                        return ap.rearrange("p (a b) -> p a b", b=128)

                    def msk(nm):
                        return C(nm).unsqueeze(1).to_broadcast([128, 2, 128])

                    def chunk_pre(c, S_):
                        cs = slice(c * 128, (c + 1) * 128)
                        A_, B_, C_ = P.psb(), P.psb(), P.psb()
                        for hh in range(2):
                            o0, o1 = hh * 128, 256 + hh * 128
                            am, rm = atm[hh], rtm[hh]
                            P.mm(A_[:, o0:o0 + 128], bt[:, cs], am[:, cs], True, True, [bt, am], [A_])
                            P.mm(A_[:, o1:o1 + 128], am[:, cs], bt[:, cs], True, True, [bt, am], [A_])
                            P.mm(B_[:, o0:o0 + 128], kt[:, cs], am[:, cs], True, True, [kt, am], [B_])
                            P.mm(B_[:, o1:o1 + 128], bt[:, cs], rm[:, cs], True, True, [bt, rm], [B_])
                            P.mm(C_[:, o0:o0 + 128], kt[:, cs], rm[:, cs], True, True, [kt, rm], [C_])
                        psT = P.psb()
                        psTb = psT[:].bitcast(BF16)
                        for i_, src in enumerate([bb, kb, vb, at]):
                            P.tr(psTb[:, i_ * 128:(i_ + 1) * 128], src[:, cs], identb, [src, cstb], [psT])
                        Tj, Lj = S_.Tp.next(), S_.Lp.next()
                        P.tt(Tj[:, :, :], v3(A_[:, 0:256]), msk("tri_s"), ALU.mult, [A_, cst], [Tj])
                        P.tt(Lj[:, :, :], v3(A_[:, 256:512]), msk("tri_sT"), ALU.mult, [A_, cst], [Lj])
                        P.tt(S_.Tak[:, :, :], v3(B_[:, 0:256]), msk("tri_s"), ALU.mult, [B_, cst], [S_.Tak])
                        P.tt(S_.Trb[:, :, :], v3(B_[:, 256:512]), msk("tri_i"), ALU.mult, [B_, cst], [S_.Trb])
                        P.tt(S_.Trk[:, :, :], v3(C_[:, 0:256]), msk("tri_i"), ALU.mult, [C_, cst], [S_.Trk])
                        tm = S_.tm
                        P.cp(tm[:, :, :].rearrange("p a b -> p (a b)"), psTb[:, 0:512], [psT], [tm], eng="act")
                        yield
                        psX = P.psb()
                        for hh in range(2):
                            P.mm(psX[:, hh * 64:(hh + 1) * 64], S_.Tak[:, hh, :], tm[:, 2, hh * 64:(hh + 1) * 64], True, True,
                                 [S_.Tak, tm], [psX])
                        Z = S_.Zp.next()
                        P.cp(Z[:, :, 0:64], tm[:, 3, :].rearrange("p (a b) -> p a b", b=64), [tm], [Z], eng="act")
                        P.cp(Z[:, :, 64:128], psX[:, 0:128].rearrange("p (a b) -> p a b", b=64), [psX], [Z])
                        yield
                        for j in range(7):
                            psZ = P.psb()
                            for hh in range(2):
                                o0 = hh * 128
                                P.mm(psZ[:, o0:o0 + 128], Tj[:, hh, :], Z[:, hh, :], True, True, [Tj, Z], [psZ])
                            if j < 6:
                                psS = P.psb()
                                for hh in range(2):
                                    o0, o1 = hh * 128, 256 + hh * 128
                                    P.mm(psS[:, o0:o0 + 128], Lj[:, hh, :], Tj[:, hh, :], True, True, [Lj, Tj], [psS])
                                    P.mm(psS[:, o1:o1 + 128], Tj[:, hh, :], Lj[:, hh, :], True, True, [Lj, Tj], [psS])
                                Zn = S_.Zp.next()
                                P.tt(Zn[:, :, :], v3(psZ[:, 0:256]), Z[:, :, :], ALU.add, [psZ, Z], [Zn])
                                Tn, Ln = S_.Tp.next(), S_.Lp.next()
                                P.cp(Tn[:, :, :], v3(psS[:, 0:256]), [psS], [Tn], eng="act")
                                P.cp(Ln[:, :, :], v3(psS[:, 256:512]), [psS], [Ln], eng="act")
                                Z, Tj, Lj = Zn, Tn, Ln
                            else:
                                z3 = v3(psZ[:, 0:256])
                                P.tt(S_.Z1c[:, :].rearrange("p (a b) -> p a b", b=64), z3[:, :, 0:64], Z[:, :, 0:64],
                                     ALU.add, [psZ, Z], [S_.Z1c])
                                P.tt(S_.Z2c[:, :].rearrange("p (a b) -> p a b", b=64), z3[:, :, 64:128], Z[:, :, 64:128],
                                     ALU.add, [psZ, Z], [S_.Z2c])
                            yield
                        psP = P.psb()
                        P.mm(psP[:, 0:128], S_.Z1c[:, :], tm[:, 0, :], True, True, [S_.Z1c, tm], [psP])
                        P.mm(psP[:, 128:256], tm[:, 0, :], S_.Z2c[:, :], True, False, [S_.Z2c, tm], [psP])
                        P.mm(psP[:, 128:256], tm[:, 1, :], tm[:, 2, :], False, True, [tm], [psP])
                        psQ = P.psb()
                        for hh in range(2):
                            pr = slice(hh * 64, hh * 64 + 64)
                            P.mm(psQ[pr, 0:128], S_.Z1c[:, pr], S_.Trb[:, hh, :], True, True, [S_.Z1c, S_.Trb], [psQ])
                        P.tt(S_.tmpP[:, :], psP[:, 0:128], C("blk"), ALU.mult, [psP, cst], [S_.tmpP])
                        P.stt(S_.PhiT[:, :], C("ident"), WC[:, c:c + 1], S_.tmpP[:, :], ALU.mult, ALU.add,
                              [cst, WC, S_.tmpP], [S_.PhiT])
                        P.cp(S_.Gp[0:64, :], psP[0:64, 128:192], [psP], [S_.Gp], eng="act")
                        P.cp(S_.Gp[64:128, :], psP[64:128, 192:256], [psP], [S_.Gp], eng="act")
                        for hh in range(2):
                            pr = slice(hh * 64, hh * 64 + 64)
                            P.tt(S_.QeTm[hh][pr, :], psQ[pr, 0:128], rt[pr, cs], ALU.add, [psQ, rt], [S_.QeTm[hh]])
                        yield

                    def chunk_post(c, S_, state, cidx):
                        cs = slice(c * 128, (c + 1) * 128)
                        tm = S_.tm
                        for hh in range(2):
                            pr = slice(hh * 64, hh * 64 + 64)
                            P.mm(psO[pr, cs], S_.Z2c[:, pr], S_.Trb[:, hh, :], True, False, [S_.Z2c, S_.Trb], [psO])
                            P.mm(psO[pr, cs], tm[:, 2, pr], S_.Trk[:, hh, :], False, state is None, [tm, S_.Trk], [psO])
                            if state is not None:
                                P.mm(psO[pr, cs], state[1][:, :], S_.QeTm[hh][:, :], False, True,
                                     [state[1], S_.QeTm[hh]], [psO])
                        new32 = S32[cidx % 2]
                        nbf = Sbf.next()
                        if state is None:
                            P.cp(nbf[:, :], S_.Gp[:, :], [S_.Gp], [nbf])
                            P.cp(new32[:, :], S_.Gp[:, :], [S_.Gp], [new32], eng="act")
                        else:
                            psS2 = P.psb()
                            P.mm(psS2[:, 0:64], S_.PhiT[:, :], state[1][:, :], True, True, [S_.PhiT, state[1]], [psS2])
                            P.tt(nbf[:, :], psS2[:, 0:64], S_.Gp[:, :], ALU.add, [psS2, S_.Gp], [nbf])
                            P.tt(new32[:, :], psS2[:, 0:64], S_.Gp[:, :], ALU.add, [psS2, S_.Gp], [new32])
                        return (new32, nbf)

                    for c0_ in range(0, nch, 2):
                        cl = list(range(c0_, min(c0_ + 2, nch)))
                        gens = [chunk_pre(c, slots[c % 2]) for c in cl]
                        live = list(gens)
                        while live:
                            for g_ in list(live):
                                try:
                                    next(g_)
                                except StopIteration:
                                    live.remove(g_)
                        for c in cl:
                            state = chunk_post(c, slots[c % 2], state, cidx)
                            cidx += 1
                    if d == 0:
                        P.cp(oT[:, t0:t0 + n], psO[:, 0:n], [psO], [oT])
                    else:
                        P.tt(tsl(oT, t0, n, True), tsl(oT, t0, n, True), psO[:, 0:n], ALU.add, [oT, psO], [oT])
                    P.release(psO)
            for ti, (t0, n) in enumerate(TILES):
                osl = oT[:, t0:t0 + n]
                P.cp(sq1[:, 0:n], osl, [oT], [sq1], eng="act")
                ps = P.psb()
                P.mm(ps[:, 0:n], C("blk", bf=True), sq1[:, 0:n], True, True, [cstb, sq1], [ps])
                P.stt(te[:, 0:n], ps[:, 0:n], -1.0 / 64, osl, ALU.mult, ALU.add, [ps, oT], [te])
                P.act(sq1[:, 0:n], te[:, 0:n], AF.Square, [te], [sq1])
                ps = P.psb()
                P.mm(ps[:, 0:n], C("blk", bf=True), sq1[:, 0:n], True, True, [cstb, sq1], [ps])
                P.act(te2[:, 0:n], ps[:, 0:n], AF.Sqrt, [ps, epsD], [te2], scale=1.0 / 64, bias=epsD[:, 1:2])
                P.recip(te2[:, 0:n], te2[:, 0:n], [te2], [te2])
                P.tt(te[:, 0:n], te[:, 0:n], te2[:, 0:n], ALU.mult, [te, te2], [te])
                P.ts(te[:, 0:n], te[:, 0:n], V("gnw", c0=m, c1=m + 1), V("gnb", c0=m, c1=m + 1), ALU.mult, ALU.add,
                     [te, vec], [te])
                P.dma("sp", tlw[:, 0:n], rwaux[2][m * 128:(m + 1) * 128, t0:t0 + n], [aux_v[2][m]], [tlw])
                P.dma("sp", tag[:, 0:n], rwaux[3][m * 128:(m + 1) * 128, t0:t0 + n], [aux_v[3][m]], [tag])
                P.tt(tb_[:, 0:n], tlw[:, 0:n], tag[:, 0:n], ALU.add, [tlw, tag], [tb_])
                P.ts(tb_[:, 0:n], tb_[:, 0:n], -2.0, None, ALU.add, ALU.bypass, [tb_], [tb_])
                P.ts(tb_[:, 0:n], tb_[:, 0:n], V("ka", c0=m, c1=m + 1), 2.0, ALU.mult, ALU.add, [tb_, vec], [tb_])
                P.tt(tb_[:, 0:n], tb_[:, 0:n], kf[:, t0:t0 + n], ALU.mult, [tb_, kf], [tb_])
                P.tt(tb_[:, 0:n], tb_[:, 0:n], rf[:, t0:t0 + n], ALU.mult, [tb_, rf], [tb_])
                P.ts(sq1[:, 0:n], tb_[:, 0:n], V("rk", c0=m, c1=m + 1), None, ALU.mult, ALU.bypass, [tb_, vec], [sq1])
                ps = P.psb()
                P.mm(ps[:, 0:n], C("blk", bf=True), sq1[:, 0:n], True, True, [cstb, sq1], [ps])
                P.tt(tkd[:, 0:n], ps[:, 0:n], vf[:, t0:t0 + n], ALU.mult, [ps, vf], [tkd])
                P.tt(te[:, 0:n], te[:, 0:n], tkd[:, 0:n], ALU.add, [te, tkd], [te])
                P.dma("sp", tcum[:, 0:n], rwaux[4][m * 128:(m + 1) * 128, t0:t0 + n], [aux_v[4][m]], [tcum])
                y = yb.next()
                P.tt(y[:, 0:n], te[:, 0:n], tcum[:, 0:n], ALU.mult, [te, tcum], [y])
                P.dma("sp", yT[(4 + m) * 128:(5 + m) * 128, t0:t0 + n], y[:, 0:n], [y], [yT_v[4 + m][ti]])
    for l in range(nl):
        stage_mod(l)
        stage_norm(l, 0)
        mixer_hgrn2(l)
        mixer_rwkv(l)
        mixer_mla(l)
        stage_out(l)
        stage_norm(l, 1)
        stage_ffn(l, l == nl - 1)
    return P


_BIG = ["w_mod", "w_in", "w_out", "rw_w2", "rw_a2", "rw_g2", "mla_w_uq", "mla_w_ukv", "ffn_up", "ffn_down"]


def kernel(**inputs):
    inp = {k: np.asarray(v) for k, v in inputs.items()}
    P = build_program(L)
    nc = P.build()
    cst, cs = host_consts()
    vecs, lb = host_vecs(inp)
    big = {k: np.ascontiguousarray(inp[k], dtype=np.float32) for k in _BIG}
    in_maps = []
    for b in range(8):
        xT = np.ascontiguousarray(np.concatenate([inp["ctx"][b], inp["x"][b]], 0).T.astype(np.float32))
        cT = np.ascontiguousarray(np.stack([fm(inp["c"][b], 16), fm(inp["c_ctx"], 16)], -1).reshape(128, 32))
        in_maps.append(dict(xT=xT, cT=cT, vecs=vecs, hglb=lb, cst=cst, rope=cs, **big))
    res = run_bass_kernel_spmd(nc, in_maps, core_ids=list(range(8)))
    out = np.stack([np.ascontiguousarray(np.asarray(res.results[b]["out"]).T) for b in range(8)], 0)
    return out.astype(np.float32)
```

```python
import math
import numpy as np
import concourse.bass as bass
import concourse.mybir as mybir
from concourse.bass_utils import run_bass_kernel_spmd

F32 = mybir.dt.float32
BF16 = mybir.dt.bfloat16
ALU = mybir.AluOpType
AF = mybir.ActivationFunctionType
ENGS = ["pe", "act", "dve", "pool", "sp"]
NDMASEM = 16

L = 4
D = 2048
KC = 16
NCTX = 256
NLAT = 2048
T = NCTX + NLAT
TILES = [(0, 256), (256, 512), (768, 512), (1280, 512), (1792, 512)]
SEGS = [(0, 256), (256, 2304)]
DFF = 5632
IN_COLS = 5376
EPS = 1e-6
RW0 = 2560
ML0 = 4544
DECAY_MAX = math.exp(-0.5)


class Buf:
    __slots__ = ("name", "h", "writer", "readers", "excl")

    def __init__(self, name, h):
        self.name = name
        self.h = h
        self.writer = None
        self.readers = {}
        self.excl = False

    def __getitem__(self, idx):
        return self.h[idx]


class Pool:
    def __init__(self, bufs):
        self.bufs = bufs
        self.i = 0
        self.held = set()

    def next(self):
        while True:
            b = self.bufs[self.i % len(self.bufs)]
            self.i += 1
            if b.name not in self.held:
                return b

    def hold(self):
        b = self.next()
        self.held.add(b.name)
        return b

    def release(self, b):
        self.held.discard(b.name)


class Prog:
    def __init__(self):
        self.nc = bass.Bass("TRN2", target_bir_lowering=False)
        self.ops = {e: [] for e in ENGS}
        self.qcount = {}
        self.seen = {e: {} for e in ENGS}
        self.sems = {}
        self._stack = []
        self.floor = {}
        self.banks = None
        self.dma_idx = {}

    def sb(self, name, shape, dt=F32):
        g = self.nc.sbuf_tensor(name, list(shape), dt)
        h = g.__enter__()
        self._stack.append(g)
        return Buf(name, h)

    def pool(self, name, shape, dt, n):
        return Pool([self.sb("%s%d" % (name, i), shape, dt) for i in range(n)])

    def dram(self, name, shape, dt=F32, kind="Internal"):
        h = self.nc.dram_tensor(name, list(shape), dt, kind=kind)
        return Buf(name, h.ap())

    def view(self, buf, name=None):
        return Buf(name or buf.name, buf.h)

    def init_psum(self):
        bl = []
        for i in range(8):
            g = self.nc.psum_tensor("psb%d" % i, [128, 512], F32)
            h = g.__enter__()
            self._stack.append(g)
            b_ = Buf("psb%d" % i, h)
            b_.excl = True
            bl.append(b_)
        self.banks = Pool(bl)

    def psb(self):
        return self.banks.next()

    def hold(self):
        return self.banks.hold()

    def release(self, b):
        self.banks.release(b)

    def barrier(self):
        self.floor = dict(self.qcount)

    def emit(self, eng, fn, reads=(), writes=(), dma=False):
        if dma:
            k = self.dma_idx.get(eng, 0)
            self.dma_idx[eng] = k + 1
            q = "dma_%s_%d" % (eng, k % NDMASEM)
        else:
            q = eng
        excl = [b for b in reads if b.excl]
        if excl:
            reads = [b for b in reads if not b.excl]
            writes = list(writes) + excl
        deps = dict(self.floor)
        if dma and self.qcount.get(q, 0) > 0:
            deps[q] = self.qcount[q]

        def need(w):
            if w is None:
                return
            qq, c = w
            if qq == q and q == "pe":
                return
            if deps.get(qq, 0) < c:
                deps[qq] = c

        for b in reads:
            need(b.writer)
        for b in writes:
            need(b.writer)
            for qq, c in b.readers.items():
                need((qq, c))
        waits = []
        seen = self.seen[eng]
        for qq, c in deps.items():
            if qq == "pe" and q == "pe":
                continue
            if seen.get(qq, 0) < c:
                seen[qq] = c
                waits.append((qq, c))
        inc = 16 if dma else 1
        cnt = self.qcount.get(q, 0) + inc
        self.qcount[q] = cnt
        self.ops[eng].append((waits, fn, q, inc))
        for b in reads:
            b.readers[q] = cnt
        for b in writes:
            b.writer = (q, cnt)
            b.readers = {}
        return cnt

    def mm(self, out, lhsT, rhs, start, stop, R, W):
        self.emit("pe", lambda e: e.matmul(out, lhsT, rhs, start=start, stop=stop), R, W)

    def tr(self, out, in_, ident, R, W):
        self.emit("pe", lambda e: e.transpose(out, in_, ident), R, W)

    def act(self, out, in_, func, R, W, scale=1.0, bias=0.0):
        self.emit("act", lambda e: e.activation(out=out, in_=in_, func=func, scale=scale, bias=bias), R, W)

    def tt(self, out, in0, in1, op, R, W, eng="dve"):
        self.emit(eng, lambda e: e.tensor_tensor(out=out, in0=in0, in1=in1, op=op), R, W)

    def ts(self, out, in0, s1, s2, op0, op1, R, W, eng="dve"):
        self.emit(eng, lambda e: e.tensor_scalar(out=out, in0=in0, scalar1=s1, scalar2=s2, op0=op0, op1=op1), R, W)

    def stt(self, out, in0, scalar, in1, op0, op1, R, W):
        self.emit("dve", lambda e: e.scalar_tensor_tensor(out=out, in0=in0, scalar=scalar, in1=in1, op0=op0, op1=op1), R, W)

    def cp(self, out, in_, R, W, eng="dve"):
        if eng == "act":
            self.emit("act", lambda e: e.activation(out=out, in_=in_, func=AF.Copy), R, W)
        else:
            self.emit(eng, lambda e: e.tensor_copy(out=out, in_=in_), R, W)

    def recip(self, out, in_, R, W):
        self.emit("dve", lambda e: e.reciprocal(out=out, in_=in_), R, W)

    def scan(self, out, d0, d1, R, W):
        self.emit("dve", lambda e: e.tensor_tensor_scan(out=out, data0=d0, data1=d1, initial=0.0,
                                                       op0=ALU.mult, op1=ALU.add), R, W)

    def memset(self, ap, val, W, eng="dve"):
        self.emit(eng, lambda e: e.memset(ap, val), (), W)

    def dma(self, eng, out, in_, R, W):
        self.emit(eng, lambda e: e.dma_start(out=out, in_=in_), R, W, dma=True)

    def build(self):
        nc = self.nc
        qs = sorted(self.qcount.keys())
        guards = []
        for q in qs:
            g = nc.semaphore("s_" + q)
            self.sems[q] = g.__enter__()
            guards.append(g)
        ops, sems, qcount = self.ops, self.sems, self.qcount

        def run(engname):
            def body(e):
                for waits, fn, q, inc in ops[engname]:
                    for qq, c in waits:
                        e.wait_ge(sems[qq], c)
                    fn(e).then_inc(sems[q], inc)
                if engname == "sp":
                    for q in qs:
                        e.wait_ge(sems[q], qcount[q])
            return body

        with nc.Block() as block:
            block.sync(run("sp"))
            block.tensor(run("pe"))
            block.scalar(run("act"))
            block.vector(run("dve"))
            block.gpsimd(run("pool"))
        for g in guards:
            g.__exit__(None, None, None)
        while self._stack:
            self._stack.pop().__exit__(None, None, None)
        return nc


V128 = {}
_o = 0
for _n, _w in [("ng1", 16), ("ng2", 16), ("bm", 96), ("hgn", 1), ("mu", 12), ("w0", 8), ("a0", 8), ("kk", 4),
               ("ka", 4), ("rk", 4), ("gnw", 4), ("gnb", 4), ("qnorm", 4), ("kvnorm", 2), ("qn_g", 1),
               ("kn_g", 1), ("dw", 264), ("db", 88), ("mul", 4), ("mug", 1), ("qr_g", 1), ("kr_g", 1)]:
    V128[_n] = (_o, _w)
    _o += _w
NV = _o

C128 = {}
_o = 0
for _n, _w in [("ident", 128), ("ones", 128), ("blk", 128), ("tri_i", 128), ("tri_s", 128), ("tri_sT", 128),
               ("m16", 16), ("rs16", 512), ("rs128", 512), ("rotT", 64)]:
    C128[_n] = (_o, _w)
    _o += _w
NCST = _o


def fm(v, nch):
    return np.ascontiguousarray(np.asarray(v, np.float32).reshape(nch, 128).T)


def host_consts():
    c = np.zeros((128, NCST), np.float32)

    def put(n, a):
        o, w = C128[n]
        c[:a.shape[0], o:o + a.shape[1]] = a

    put("ident", np.eye(128, dtype=np.float32))
    put("ones", np.ones((128, 128), np.float32))
    blk = np.zeros((128, 128), np.float32)
    blk[:64, :64] = 1
    blk[64:, 64:] = 1
    put("blk", blk)
    put("tri_i", np.triu(np.ones((128, 128), np.float32)))
    put("tri_s", np.triu(np.ones((128, 128), np.float32), 1))
    put("tri_sT", np.tril(np.ones((128, 128), np.float32), -1))
    put("m16", np.triu(np.ones((16, 16), np.float32)))
    r16 = np.ones((128, 512), np.float32)
    r16[:, ::16] = 0
    put("rs16", r16)
    r128 = np.ones((128, 512), np.float32)
    r128[:, ::128] = 0
    put("rs128", r128)
    rot = np.zeros((64, 64), np.float32)
    rot[:32, 32:] = -np.eye(32)
    rot[32:, :32] = np.eye(32)
    put("rotT", np.ascontiguousarray(rot.T))
    rows = NLAT // 64
    row = np.repeat(np.arange(rows, dtype=np.float32), 64)
    col = np.tile(np.arange(64, dtype=np.float32), rows)
    inv = (10000.0 ** (-np.arange(0, 32, 2, dtype=np.float32) / 32)).astype(np.float32)
    ang = np.concatenate([row[:, None] * inv, col[:, None] * inv], -1).astype(np.float32)
    cos, sin = np.cos(ang).astype(np.float32), np.sin(ang).astype(np.float32)
    cs = np.zeros((2, 64, NLAT), np.float32)
    cs[0] = np.concatenate([cos.T, cos.T], 0)
    cs[1] = np.concatenate([sin.T, sin.T], 0)
    return c, cs


def host_vecs(inp):
    v = np.zeros((L, 128, NV), np.float32)

    def put(l, n, a):
        o, w = V128[n]
        a = np.asarray(a, np.float32)
        v[l, :a.shape[0], o:o + a.shape[1]] = a

    for l in range(L):
        put(l, "ng1", fm(inp["norm_g"][l, 0], 16))
        put(l, "ng2", fm(inp["norm_g"][l, 1], 16))
        put(l, "bm", fm(inp["b_mod"][l], 96))
        put(l, "hgn", inp["hg_gn"][l][:, None])
        put(l, "mu", fm(inp["rw_mu"][l, :1536], 12))
        put(l, "w0", np.concatenate([fm(inp["rw_w0"][l, 0], 4), fm(inp["rw_w0"][l, 1], 4)], 1))
        put(l, "a0", np.concatenate([fm(inp["rw_a0"][l, 0], 4), fm(inp["rw_a0"][l, 1], 4)], 1))
        put(l, "kk", fm(inp["rw_kk"][l], 4))
        put(l, "ka", fm(inp["rw_ka"][l], 4))
        put(l, "rk", fm(inp["rw_rk"][l].reshape(-1), 4))
        put(l, "gnw", fm(inp["rw_gn_w"][l], 4))
        put(l, "gnb", fm(inp["rw_gn_b"][l], 4))
        put(l, "qnorm", fm(inp["mla_q_norm"][l], 4))
        put(l, "kvnorm", fm(inp["mla_kv_norm"][l], 2))
        put(l, "qn_g", inp["mla_qn_g"][l][:, None])
        put(l, "kn_g", inp["mla_kn_g"][l][:, None])
        dw = inp["ffn_dw"][l]
        put(l, "dw", np.concatenate([fm(dw[0], 88), fm(dw[1], 88), fm(dw[2], 88)], 1))
        put(l, "db", fm(inp["ffn_db"][l], 88))
        mu = inp["rw_mu"][l]
        put(l, "mul", np.stack([mu[1536 + 96 * i:1536 + 96 * (i + 1)] for i in range(4)], 1))
        put(l, "mug", mu[1920:1984][:, None])
        put(l, "qr_g", inp["mla_qr_g"][l][:, None])
        put(l, "kr_g", inp["mla_kr_g"][l][:, None])
    lb = np.zeros((128, 4, 2, L), np.float32)
    for l in range(L):
        for d in range(2):
            lb[:, :, d, l] = fm(inp["hg_lb"][l, d], 4)
    return v, lb.reshape(128, 4 * 2 * L)


class ArenaMem:
    def __init__(self, P, name, nwords):
        self.buf = P.sb(name, [128, nwords], F32)
        self.n = nwords
        self.off = 0


class Arena:
    def __init__(self, mem, dt):
        self.mem = mem
        self.dt = dt

    def reset(self):
        self.mem.off = 0

    def alloc(self, name, shape):
        size = int(np.prod(shape))
        words = size if self.dt == F32 else (size + 1) // 2
        words = (words + 3) // 4 * 4
        m = self.mem
        assert m.off + words <= m.n, (name, m.off, words, m.n)
        ap = m.buf.h[:, m.off:m.off + words]
        m.off += words
        if self.dt != F32:
            ap = ap.bitcast(self.dt)
        ap = ap[:, 0:size]
        if len(shape) == 2:
            ap = ap.rearrange("p (a b) -> p a b", b=shape[1])
        elif len(shape) == 3:
            ap = ap.rearrange("p (a b c) -> p a b c", b=shape[1], c=shape[2])
        return Buf(name, ap)


def scan_tiles(d):
    if d == 0:
        return [(t0, n, False) for (t0, n) in TILES]
    return [(0, 256, True)] + [(t0, 512, True) for t0 in (1792, 1280, 768, 256)]


def tsl(ap2d, t0, n, rev):
    a = ap2d[:, t0:t0 + n]
    return a[:, ::-1] if rev else a


def build_program(nl=L, dbg=()):
    P = Prog()
    P.init_psum()
    EI = "ExternalInput"
    xT_in = P.dram("xT", [D, T], F32, EI)
    cT_in = P.dram("cT", [128, 32], F32, EI)
    vec_in = P.dram("vecs", [L, 128, NV], F32, EI)
    lb_in = P.dram("hglb", [128, 4 * 2 * L], F32, EI)
    cst_in = P.dram("cst", [128, NCST], F32, EI)
    cs_in = P.dram("rope", [2, 64, NLAT], F32, EI)
    w_mod = P.dram("w_mod", [L, D, 6 * D], F32, EI)
    w_in = P.dram("w_in", [L, D, IN_COLS], F32, EI)
    w_out = P.dram("w_out", [L, D, D], F32, EI)
    rw_w2 = P.dram("rw_w2", [L, 2, 96, 512], F32, EI)
    rw_a2 = P.dram("rw_a2", [L, 2, 96, 512], F32, EI)
    rw_g2 = P.dram("rw_g2", [L, 64, 512], F32, EI)
    w_uq = P.dram("mla_w_uq", [L, 512, 1536], F32, EI)
    w_ukv = P.dram("mla_w_ukv", [L, 256, 2048], F32, EI)
    ffn_up = P.dram("ffn_up", [L, D, 2 * DFF], F32, EI)
    ffn_down = P.dram("ffn_down", [L, DFF, D], F32, EI)
    out = P.dram("out", [D, NLAT], F32, "ExternalOutput")
    dbg_out = {}
    for n, shp in dbg:
        dbg_out[n] = P.dram("dbg_" + n, shp, F32, "ExternalOutput")

    xs = P.dram("xs", [D, T], F32)
    yT = P.dram("yT", [D, T], BF16)
    zT = P.dram("zT", [DFF, T], BF16)
    rwaux = P.dram("rwaux", [5, 512, T], F32)
    xs_v = [[P.view(xs, "xs_%d_%d" % (m, ti)) for ti in range(5)] for m in range(KC)]
    xin_v = [[P.view(xT_in, "xin_%d_%d" % (m, ti)) for ti in range(5)] for m in range(KC)]
    yT_v = [[P.view(yT, "yT_%d_%d" % (m, ti)) for ti in range(5)] for m in range(KC)]
    zT_v = [P.view(zT, "zT_%d" % j) for j in range(44)]
    aux_v = [[P.view(rwaux, "aux_%d_%d" % (a, m)) for m in range(4)] for a in range(5)]
    out_v = [[P.view(out, "out_%d_%d" % (m, ti)) for ti in range(5)] for m in range(KC)]

    cst = P.sb("cst_s", [128, NCST], F32)
    cstb = P.sb("cst_b", [128, NCST], BF16)
    P.dma("sp", cst[:], cst_in[:], [cst_in], [cst])
    P.cp(cstb[:], cst[:], [cst], [cstb])

    def C(n, rows=128, bf=False, c0=0, c1=None):
        o, w = C128[n]
        c1 = w if c1 is None else c1
        return (cstb if bf else cst)[0:rows, o + c0:o + c1]

    hT = P.sb("hT", [128, KC, T], BF16)
    hT_v = [P.view(hT, "hT_%d" % ti) for ti in range(5)]
    vec = P.sb("vec", [128, NV], F32)
    modT = P.sb("modT", [128, 96, 2], F32)
    mA = P.sb("mA", [128, 2, KC, 2], F32)
    lbt = P.sb("lbt", [128, 4, 2, L], F32)
    omlb = P.sb("omlb", [128, 4, 2, L], F32)
    cT = P.sb("cTs", [128, 32], F32)
    sT = P.sb("sTs", [128, KC, 2], BF16)
    epsD = P.sb("epsD", [128, 4], F32)
    wpool = P.pool("wp", [128, KC, 128], BF16, 3)
    _mem = ArenaMem(P, "arena", 27600)
    AF_ = Arena(_mem, F32)
    AB_ = Arena(_mem, BF16)

    def V(n, rows=128, c0=0, c1=None):
        o, w = V128[n]
        c1 = w if c1 is None else c1
        return vec[0:rows, o + c0:o + c1]

    P.memset(_mem.buf[:, :], 0.0, [_mem.buf])
    P.memset(hT[:].rearrange('p k t -> p (k t)'), 0.0, [hT])
    P.memset(epsD[:, 0:1], EPS, [epsD])
    P.memset(epsD[:, 1:2], 64e-5, [epsD])
    P.memset(epsD[:, 2:3], 1e-24, [epsD])
    P.memset(epsD[:, 3:4], 0.0, [epsD])

    P.dma("sp", cT[:], cT_in[:], [cT_in], [cT])
    P.act(sT[:].rearrange("p k g -> p (k g)"), cT[:], AF.Silu, [cT], [sT])

    lbe = P.sb("lbe", [128, 8, L], F32)
    lbs = P.sb("lbs", [128, 8], F32)
    P.dma("sp", lbe[:].rearrange("p a l -> p (a l)"), lb_in[:], [lb_in], [lbe])
    P.act(lbe[:], lbe[:], AF.Exp, [lbe], [lbe])
    P.emit("dve", lambda e: e.reduce_sum(out=lbs[:], in_=lbe[:], axis=mybir.AxisListType.X), [lbe], [lbs])
    P.recip(lbs[:], lbs[:], [lbs], [lbs])
    P.tt(lbe[:], lbe[:], lbs[:].unsqueeze(2).to_broadcast([128, 8, L]), ALU.mult, [lbe, lbs], [lbe])
    lb3 = lbt[:].rearrange("p h d l -> p (h d) l")
    P.memset(lb3[:, :, 0:1], 0.0, [lbt])
    for l in range(1, L):
        P.tt(lb3[:, :, l:l + 1], lb3[:, :, l - 1:l], lbe[:, :, l:l + 1], ALU.add, [lbt, lbe], [lbt])
    P.ts(omlb[:], lbt[:], -1.0, 1.0, ALU.mult, ALU.add, [lbt], [omlb])

    widx = [0]

    def load_w(src_buf, src_ap, rows=128, kc=KC, m=128):
        wb = wpool.next()
        P.dma("pool", wb[0:rows, 0:kc, 0:m], src_ap, [src_buf], [wb])
        return wb

    def x_src(l, m, ti):
        return (xin_v if l == 0 else xs_v)[m][ti], (xT_in if l == 0 else xs)

    def stage_mod(l):
        P.dma("sp", vec[:], vec_in[l], [vec_in], [vec])
        for j in range(96):
            wb = load_w(w_mod, w_mod[l][:, j * 128:(j + 1) * 128].rearrange("(k p) n -> p k n", p=128))
            ps = P.psb()
            for k in range(KC):
                P.mm(ps[:, 0:2], wb[:, k, :], sT[:, k, :], k == 0, k == KC - 1, [wb, sT], [ps])
            P.ts(modT[:, j, :], ps[:, 0:2], V("bm", c0=j, c1=j + 1), None, ALU.add, ALU.bypass, [ps, vec], [modT])
        for i, (ng, sc0) in enumerate([("ng1", 16), ("ng2", 64)]):
            P.ts(mA[:, i, :, :], modT[:, sc0:sc0 + 16, :], 1.0, None, ALU.add, ALU.bypass, [modT], [mA])
            P.tt(mA[:, i, :, :], mA[:, i, :, :], V(ng).unsqueeze(2).to_broadcast([128, 16, 2]), ALU.mult,
                 [mA, vec], [mA])

    def stage_norm(l, i):
        P.barrier()
        AF_.reset()
        AB_.reset()
        xts = [AF_.alloc("xt%d" % b, [KC, 512]) for b in range(2)]
        sqs = [AB_.alloc("sq%d" % b, [KC, 512]) for b in range(2)]
        rss = [AF_.alloc("rs%d" % b, [512]) for b in range(2)]
        sh0 = 0 if i == 0 else 48
        for ti, (t0, n) in enumerate(TILES):
            g = 1 if ti == 0 else 0
            xt, sq, rs = xts[ti % 2], sqs[ti % 2], rss[ti % 2]
            srcs = [x_src(l if i == 0 else 99, m, ti)[0] for m in range(KC)]
            src = xT_in if (l == 0 and i == 0) else xs
            P.dma("sp", xt[:, :, 0:n], src[:, t0:t0 + n].rearrange("(k p) t -> p k t", p=128), srcs, [xt])
            P.act(sq[:, :, 0:n], xt[:, :, 0:n], AF.Square, [xt], [sq])
            ps = P.psb()
            for k in range(KC):
                P.mm(ps[:, 0:n], C("ones", bf=True), sq[:, k, 0:n], k == 0, k == KC - 1, [cstb, sq], [ps])
            P.act(rs[:, 0:n], ps[:, 0:n], AF.Sqrt, [ps, epsD], [rs], scale=1.0 / D, bias=epsD[:, 0:1])
            P.recip(rs[:, 0:n], rs[:, 0:n], [rs], [rs])
            P.tt(xt[:, :, 0:n], xt[:, :, 0:n], rs[:, 0:n].unsqueeze(1).to_broadcast([128, KC, n]), ALU.mult,
                 [xt, rs], [xt])
            for k in range(KC):
                P.act(hT[:, k, t0:t0 + n], xt[:, k, 0:n], AF.Identity, [xt, mA, modT], [hT_v[ti]],
                      scale=mA[:, i, k, g:g + 1], bias=modT[:, sh0 + k, g:g + 1])

    def proj(l, c0, m, evac, wsrc=None, tiles=None):
        wb = load_w(w_in, w_in[l][:, c0:c0 + m].rearrange("(k p) n -> p k n", p=128), m=m)
        for ti, (t0, n) in enumerate(TILES):
            ps = P.psb()
            for k in range(KC):
                P.mm(ps[0:m, 0:n], wb[:, k, 0:m], hT[:, k, t0:t0 + n], k == 0, k == KC - 1, [wb, hT_v[ti]], [ps])
            evac(ps, ti, t0, n)

    def proj_full(l, c0, m, dst, eng="act"):
        def ev(ps, ti, t0, n):
            P.cp(dst[0:m, t0:t0 + n], ps[0:m, 0:n], [ps], [dst], eng=eng)
        proj(l, c0, m, ev)

    def rms_over_partitions(src_ap, n, rows, ones_ap, inv_count, eps_col, sqb, rsb, R):
        P.act(sqb[0:rows, 0:n], src_ap, AF.Square, R, [sqb])
        ps = P.psb()
        P.mm(ps[0:rows, 0:n], ones_ap, sqb[0:rows, 0:n], True, True, [cstb, sqb], [ps])
        P.act(rsb[0:rows, 0:n], ps[0:rows, 0:n], AF.Sqrt, [ps, epsD], [rsb], scale=inv_count,
              bias=epsD[0:rows, eps_col:eps_col + 1])
        P.recip(rsb[0:rows, 0:n], rsb[0:rows, 0:n], [rsb], [rsb])

    def stage_out(l):
        P.barrier()
        AF_.reset()
        yt = AB_.alloc("ytall", [KC, T])
        yt_v = [P.view(yt, "ytall_%d" % ti) for ti in range(5)]
        xps = Pool([AF_.alloc("xo%d" % b, [512]) for b in range(4)])
        for ti, (t0, n) in enumerate(TILES):
            P.dma("sp", yt[:, :, t0:t0 + n], yT[:, t0:t0 + n].rearrange("(k p) t -> p k t", p=128),
                  [yT_v[m][ti] for m in range(KC)], [yt_v[ti]])
        for m in range(KC):
            wb = load_w(w_out, w_out[l][:, m * 128:(m + 1) * 128].rearrange("(k p) n -> p k n", p=128))
            for ti, (t0, n) in enumerate(TILES):
                g = 1 if ti == 0 else 0
                xv, xsrc = x_src(l, m, ti)
                xo = xps.next()
                P.dma("sp", xo[:, 0:n], xsrc[m * 128:(m + 1) * 128, t0:t0 + n], [xv], [xo])
                ps = P.psb()
                for k in range(KC):
                    P.mm(ps[:, 0:n], wb[:, k, :], yt[:, k, t0:t0 + n], k == 0, k == KC - 1, [wb, yt_v[ti]], [ps])
                P.stt(xo[:, 0:n], ps[:, 0:n], modT[:, 32 + m, g:g + 1], xo[:, 0:n], ALU.mult, ALU.add,
                      [ps, modT, xo], [xo])
                P.dma("sp", xs[m * 128:(m + 1) * 128, t0:t0 + n], xo[:, 0:n], [xo], [xs_v[m][ti]])

    def stage_ffn(l, last):
        P.barrier()
        AF_.reset()
        AB_.reset()
        ups = [AF_.alloc("up%d" % b, [T]) for b in range(2)]
        cvs = [AF_.alloc("cv%d" % b, [T]) for b in range(2)]
        zb = Pool([AB_.alloc("zb%d" % b, [T]) for b in range(2)])
        dwo, _ = V128["dw"]
        for j in range(44):
            for half in range(2):
                cidx = j + 44 * half
                up, cv = ups[half], cvs[half]
                wb = load_w(ffn_up, ffn_up[l][:, cidx * 128:(cidx + 1) * 128].rearrange("(k p) n -> p k n", p=128))
                for ti, (t0, n) in enumerate(TILES):
                    ps = P.psb()
                    for k in range(KC):
                        P.mm(ps[:, 0:n], wb[:, k, :], hT[:, k, t0:t0 + n], k == 0, k == KC - 1, [wb, hT_v[ti]], [ps])
                    P.cp(up[:, t0:t0 + n], ps[:, 0:n], [ps], [up], eng="act" if ti % 2 else "dve")
                P.act(cv[:, :], up[:, :], AF.Identity, [up, vec], [cv],
                      scale=vec[:, dwo + 88 + cidx:dwo + 88 + cidx + 1], bias=V("db", c0=cidx, c1=cidx + 1))
                for (s0, s1) in SEGS:
                    P.stt(cv[:, s0 + 1:s1], up[:, s0:s1 - 1], vec[:, dwo + cidx:dwo + cidx + 1], cv[:, s0 + 1:s1],
                          ALU.mult, ALU.add, [up, vec, cv], [cv])
                    P.stt(cv[:, s0:s1 - 1], up[:, s0 + 1:s1], vec[:, dwo + 176 + cidx:dwo + 176 + cidx + 1],
                          cv[:, s0:s1 - 1], ALU.mult, ALU.add, [up, vec, cv], [cv])
            z = zb.next()
            P.act(cvs[0][:, :], cvs[0][:, :], AF.Silu, [cvs[0]], [cvs[0]])
            P.tt(z[:, :], cvs[0][:, :], cvs[1][:, :], ALU.mult, [cvs[0], cvs[1]], [z])
            P.dma("sp", zT[j * 128:(j + 1) * 128, :], z[:, :], [z], [zT_v[j]])
        P.barrier()
        AF_.reset()
        AB_.reset()
        zt = AB_.alloc("zt", [44, 1024])
        wdp = Pool([AB_.alloc("wd%d" % b, [512]) for b in range(3)])
        wdf = Pool([AF_.alloc("wdf%d" % b, [512]) for b in range(4)])
        hTf = hT.h[:].rearrange("p k t -> p (k t)").bitcast(F32)
        xpre = [Buf("xoh%d" % b, hTf[:, b * 512:(b + 1) * 512]) for b in range(8)]
        tidx = {t0_: i_ for i_, (t0_, _n) in enumerate(TILES)}
        for (t0, n) in [(0, 256), (256, 1024), (1280, 1024)]:
            if last and t0 == 0:
                continue
            g = 1 if t0 == 0 else 0
            nh = (n + 511) // 512
            P.dma("sp", zt[:, :, 0:n], zT[:, t0:t0 + n].rearrange("(k p) t -> p k t", p=128), zT_v, [zt])
            for q in range(4):
                pss = [[P.psb() for _h in range(nh)] for _ in range(4)]
                for mm_ in range(4):
                    m = q * 4 + mm_
                    for hf in range(nh):
                        nn = min(512, n - hf * 512)
                        th = t0 + hf * 512
                        xo = xpre[mm_ * 2 + hf]
                        P.dma("pool", xo[:, 0:nn], xs[m * 128:(m + 1) * 128, th:th + nn], [xs_v[m][tidx[th]]], [xo])
                for k in range(44):
                    wd, wf = wdp.next(), wdf.next()
                    P.dma("sp", wf[:, :], ffn_down[l][k * 128:(k + 1) * 128, q * 512:(q + 1) * 512],
                          [ffn_down], [wf])
                    P.cp(wd[:, :], wf[:, :], [wf], [wd], eng="act" if k % 2 else "dve")
                    for mm_ in range(4):
                        for hf in range(nh):
                            nn = min(512, n - hf * 512)
                            P.mm(pss[mm_][hf][:, 0:nn], wd[:, mm_ * 128:(mm_ + 1) * 128],
                                 zt[:, k, hf * 512:hf * 512 + nn], k == 0, k == 43, [wd, zt], [pss[mm_][hf]])
                for mm_ in range(4):
                    m = q * 4 + mm_
                    for hf in range(nh):
                        nn = min(512, n - hf * 512)
                        th = t0 + hf * 512
                        ti = tidx[th]
                        xo = xpre[mm_ * 2 + hf]
                        P.stt(xo[:, 0:nn], pss[mm_][hf][:, 0:nn], modT[:, 80 + m, g:g + 1], xo[:, 0:nn], ALU.mult,
                              ALU.add, [pss[mm_][hf], modT, xo], [xo])
                        if last:
                            P.dma("pool", out[m * 128:(m + 1) * 128, th - NCTX:th - NCTX + nn], xo[:, 0:nn], [xo],
                                  [out_v[m][ti]])
                        else:
                            P.dma("pool", xs[m * 128:(m + 1) * 128, th:th + nn], xo[:, 0:nn], [xo], [xs_v[m][ti]])

    def token_shift(pf, sf, m, mu_ap):
        for (s0, s1) in SEGS:
            P.tt(sf[0:m, s0 + 1:s1 - 1], pf[0:m, s0:s1 - 2], pf[0:m, s0 + 2:s1], ALU.add, [pf], [sf])
            P.cp(sf[0:m, s0:s0 + 1], pf[0:m, s0 + 1:s0 + 2], [pf], [sf])
            P.cp(sf[0:m, s1 - 1:s1], pf[0:m, s1 - 2:s1 - 1], [pf], [sf])
        P.stt(sf[0:m, :], sf[0:m, :], 0.5, pf[0:m, :], ALU.mult, ALU.subtract, [sf, pf], [sf])
        P.stt(sf[0:m, :], sf[0:m, :], mu_ap, pf[0:m, :], ALU.mult, ALU.add, [sf, pf, vec], [sf])

    def proj_to(l, c0, m, buf, dst_fn, eng="act"):
        def ev(ps, ti, t0, n):
            P.cp(dst_fn(t0, n), ps[0:m, 0:n], [ps], [buf], eng=eng)
        proj(l, c0, m, ev)

    MLA_SCALE = 192.0 ** -0.5

    def mixer_mla(l):
        P.barrier()
        AF_.reset()
        cq = AB_.alloc("cq", [4, T])
        ckv = AB_.alloc("ckv", [2, T])
        krT = AB_.alloc("krT", [T])
        mark = _mem.off
        scr = AF_.alloc("scr", [4, T])
        sq4 = AB_.alloc("sq4", [4, 512])
        rs1 = AF_.alloc("rs1", [512])
        tA = AF_.alloc("tA", [512])
        tB = AF_.alloc("tB", [512])
        csT = AF_.alloc("csT", [2, 512])
        qb = AB_.alloc("qb", [512])
        sq1p = AB_.alloc("sq1p", [512])

        def rope(dst_bf, dst_buf, src32, src_buf, t0, n):
            P.dma("sp", csT[0:64, :, 0:n], cs_in[:, :, t0 - NCTX:t0 - NCTX + n].rearrange("c p t -> p c t"),
                  [cs_in], [csT])
            P.cp(qb[0:64, 0:n], src32, [src_buf], [qb], eng="act")
            ps = P.psb()
            P.mm(ps[0:64, 0:n], C("rotT", rows=64, bf=True), qb[0:64, 0:n], True, True, [cstb, qb], [ps])
            P.tt(tB[0:64, 0:n], ps[0:64, 0:n], csT[0:64, 1, 0:n], ALU.mult, [ps, csT], [tB])
            P.tt(src32, src32, csT[0:64, 0, 0:n], ALU.mult, [src_buf, csT], [src_buf])
            P.tt(dst_bf, src32, tB[0:64, 0:n], ALU.add, [src_buf, tB], [dst_buf])

        def norm_chunks(src3, nk, dstb, gname, cnt):
            for ti, (t0, n) in enumerate(TILES):
                P.act(sq4[:, 0:nk, 0:n], src3[:, 0:nk, t0:t0 + n], AF.Square, [scr], [sq4])
                ps = P.psb()
                for k in range(nk):
                    P.mm(ps[:, 0:n], C("ones", bf=True), sq4[:, k, 0:n], k == 0, k == nk - 1, [cstb, sq4], [ps])
                P.act(rs1[:, 0:n], ps[:, 0:n], AF.Sqrt, [ps, epsD], [rs1], scale=1.0 / cnt, bias=epsD[:, 0:1])
                P.recip(rs1[:, 0:n], rs1[:, 0:n], [rs1], [rs1])
                for k in range(nk):
                    P.stt(dstb[:, k, t0:t0 + n], src3[:, k, t0:t0 + n], V(gname, c0=k, c1=k + 1), rs1[:, 0:n],
                          ALU.mult, ALU.mult, [scr, vec, rs1], [dstb])

        for k in range(4):
            proj_to(l, ML0 + k * 128, 128, scr, lambda t0, n, k=k: scr[:, k, t0:t0 + n])
        norm_chunks(scr, 4, cq, "qnorm", 512)
        for k in range(2):
            proj_to(l, ML0 + 512 + k * 128, 128, scr, lambda t0, n, k=k: scr[:, k, t0:t0 + n])
        proj_to(l, ML0 + 768, 64, scr, lambda t0, n: scr[0:64, 2, t0:t0 + n])
        norm_chunks(scr, 2, ckv, "kvnorm", 256)
        for ti, (t0, n) in enumerate(TILES):
            rms_over_partitions(scr[0:64, 2, t0:t0 + n], n, 64, C("ones", rows=64, bf=True, c1=64), 1.0 / 64, 0,
                                sq1p, rs1, [scr])
            P.stt(tA[0:64, 0:n], scr[0:64, 2, t0:t0 + n], V("kr_g", rows=64), rs1[0:64, 0:n], ALU.mult, ALU.mult,
                  [scr, vec, rs1], [tA])
            if ti == 0:
                P.cp(krT[0:64, t0:t0 + n], tA[0:64, 0:n], [tA], [krT])
            else:
                rope(krT[0:64, t0:t0 + n], krT, tA[0:64, 0:n], tA, t0, n)
        P.barrier()
        _mem.off = mark
        P.memset(_mem.buf[:, mark:_mem.n], 0.0, [_mem.buf])
        P.barrier()
        qnT = AB_.alloc("qnT", [T])
        qrT = AB_.alloc("qrT", [T])
        knT = AB_.alloc("knT", [T])
        Vt = AB_.alloc("Vt", [18, 128])
        wq = AB_.alloc("wq", [4, 192])
        wkv = AB_.alloc("wkv", [2, 256])
        sq1 = AB_.alloc("sq1", [512])
        rs1 = AF_.alloc("rs1b", [512])
        tA = AF_.alloc("tAb", [512])
        tB = AF_.alloc("tBb", [512])
        csT = AF_.alloc("csTb", [2, 512])
        qb = AB_.alloc("qbb", [512])
        ptp = Pool([AB_.alloc("pt%d" % b, [512]) for b in range(4)])
        yb = Pool([AB_.alloc("yb%d" % b, [512]) for b in range(2)])
        for h in range(8):
            P.dma("pool", wq[:, :, :], w_uq[l][:, h * 192:(h + 1) * 192].rearrange("(k p) n -> p k n", p=128),
                  [w_uq], [wq])
            P.dma("pool", wkv[:, :, :], w_ukv[l][:, h * 256:(h + 1) * 256].rearrange("(k p) n -> p k n", p=128),
                  [w_ukv], [wkv])
            for ti, (t0, n) in enumerate(TILES):
                ps = P.psb()
                for k in range(4):
                    P.mm(ps[:, 0:n], wq[:, k, 0:128], cq[:, k, t0:t0 + n], k == 0, k == 3, [wq, cq], [ps])
                rms_over_partitions(ps[:, 0:n], n, 128, C("ones", bf=True), 1.0 / 128, 0, sq1, rs1, [ps])
                P.stt(qnT[:, t0:t0 + n], ps[:, 0:n], V("qn_g"), rs1[:, 0:n], ALU.mult, ALU.mult, [ps, vec, rs1], [qnT])
                ps = P.psb()
                for k in range(4):
                    P.mm(ps[0:64, 0:n], wq[:, k, 128:192], cq[:, k, t0:t0 + n], k == 0, k == 3, [wq, cq], [ps])
                rms_over_partitions(ps[0:64, 0:n], n, 64, C("ones", rows=64, bf=True, c1=64), 1.0 / 64, 0, sq1, rs1, [ps])
                P.stt(tA[0:64, 0:n], ps[0:64, 0:n], V("qr_g", rows=64), rs1[0:64, 0:n], ALU.mult, ALU.mult,
                      [ps, vec, rs1], [tA])
                if ti == 0:
                    P.cp(qrT[0:64, t0:t0 + n], tA[0:64, 0:n], [tA], [qrT])
                else:
                    rope(qrT[0:64, t0:t0 + n], qrT, tA[0:64, 0:n], tA, t0, n)
                ps = P.psb()
                for k in range(2):
                    P.mm(ps[:, 0:n], wkv[:, k, 0:128], ckv[:, k, t0:t0 + n], k == 0, k == 1, [wkv, ckv], [ps])
                rms_over_partitions(ps[:, 0:n], n, 128, C("ones", bf=True), 1.0 / 128, 0, sq1, rs1, [ps])
                P.stt(knT[:, t0:t0 + n], ps[:, 0:n], V("kn_g"), rs1[:, 0:n], ALU.mult, ALU.mult, [ps, vec, rs1], [knT])
            for b0 in range(0, 18, 4):
                nb = min(4, 18 - b0)
                ps = P.psb()
                for j in range(nb):
                    b = b0 + j
                    for k in range(2):
                        P.mm(ps[:, j * 128:(j + 1) * 128], ckv[:, k, b * 128:(b + 1) * 128], wkv[:, k, 128:256],
                             k == 0, k == 1, [ckv, wkv], [ps])
                P.cp(Vt[:, b0:b0 + nb, :], ps[:, 0:nb * 128].rearrange("p (a b) -> p a b", b=128), [ps], [Vt], eng="act")
            for ti, (t0, n) in enumerate(TILES):
                nkb = 2 if ti == 0 else 18
                pso = P.hold()
                psd = P.hold()
                pend = None
                for kb in range(nkb + 1):
                    cur = None
                    if kb < nkb:
                        pss = P.psb()
                        ks = slice(kb * 128, (kb + 1) * 128)
                        P.mm(pss[:, 0:n], knT[:, ks], qnT[:, t0:t0 + n], True, False, [knT, qnT], [pss])
                        P.mm(pss[:, 0:n], krT[0:64, ks], qrT[0:64, t0:t0 + n], False, True, [krT, qrT], [pss])
                        pt = ptp.next()
                        P.act(pt[:, 0:n], pss[:, 0:n], AF.Exp, [pss], [pt], scale=MLA_SCALE)
                        cur = (pt, kb)
                    if pend is not None:
                        ppt, pkb = pend
                        P.mm(pso[:, 0:n], Vt[:, pkb, :], ppt[:, 0:n], pkb == 0, pkb == nkb - 1, [Vt, ppt], [pso])
                        P.mm(psd[:, 0:n], C("ones", bf=True), ppt[:, 0:n], pkb == 0, pkb == nkb - 1, [cstb, ppt], [psd])
                    pend = cur
                P.recip(tA[:, 0:n], psd[:, 0:n], [psd], [tA])
                y = yb.next()
                P.tt(y[:, 0:n], pso[:, 0:n], tA[:, 0:n], ALU.mult, [pso, tA], [y])
                P.release(pso)
                P.release(psd)
                P.dma("sp", yT[(8 + h) * 128:(9 + h) * 128, t0:t0 + n], y[:, 0:n], [y], [yT_v[8 + h][ti]])
    def mixer_hgrn2(l):
        for h in range(4):
            P.barrier()
            AF_.reset()
            qf = AF_.alloc("qf", [T])
            vf = AF_.alloc("vf", [T])
            zfs = [AF_.alloc("zf%d" % d_, [T]) for d_ in range(2)]
            oTs = [AF_.alloc("oT%d" % d_, [T]) for d_ in range(2)]
            sq1 = AB_.alloc("sq1", [512])
            yb = Pool([AB_.alloc("yb%d" % b, [512]) for b in range(2)])
            proj_to(l, h * 128, 128, qf, lambda t0, n: qf[:, t0:t0 + n])
            proj_to(l, 1536 + h * 128, 128, vf, lambda t0, n: vf[:, t0:t0 + n])
            identb = C("ident", bf=True)

            def dir_gen(d):
                zf, oT = zfs[d], oTs[d]
                t1 = AF_.alloc("t1_%d" % d, [512])
                t2 = AF_.alloc("t2_%d" % d, [512])
                t3 = AF_.alloc("t3_%d" % d, [512])
                Fc = AF_.alloc("Fc%d" % d, [32])
                qt = AB_.alloc("qt%d" % d, [512])
                kt = AB_.alloc("kt%d" % d, [512])
                kb = AB_.alloc("kb%d" % d, [512])
                vb = AB_.alloc("vb%d" % d, [512])
                kvp = Pool([AB_.alloc("kvT%d_%d" % (d, b), [8, 128]) for b in range(3)])
                atp = Pool([AB_.alloc("aT%d_%d" % (d, b), [4, 16]) for b in range(3)])
                S32 = [AF_.alloc("S32_%d_%d" % (d, b), [128]) for b in range(2)]
                Sbf = Pool([AB_.alloc("Sbf%d_%d" % (d, b), [128]) for b in range(3)])
                proj_to(l, 512 * (1 + d) + h * 128, 128, zf, lambda t0, n: zf[:, t0:t0 + n])
                yield
                state = None
                cidx = 0
                for (t0, n, rev) in scan_tiles(d):
                    nch = n // 16
                    P.act(t1[:, 0:n], tsl(zf, t0, n, rev), AF.Sigmoid, [zf], [t1])
                    P.ts(t1[:, 0:n], t1[:, 0:n], omlb[:, h, d, l:l + 1], lbt[:, h, d, l:l + 1], ALU.mult, ALU.add,
                         [t1, omlb, lbt], [t1])
                    P.act(t2[:, 0:n], t1[:, 0:n], AF.Ln, [t1], [t2])
                    P.ts(t1[:, 0:n], t1[:, 0:n], -1.0, 1.0, ALU.mult, ALU.add, [t1], [t1])
                    P.scan(t3[:, 0:n], C("rs16", c1=n), t2[:, 0:n], [cst, t2], [t3])
                    P.act(t2[:, 0:n], t3[:, 0:n], AF.Exp, [t3], [t2])
                    P.tt(qt[:, 0:n], tsl(qf, t0, n, rev), t2[:, 0:n], ALU.mult, [qf, t2], [qt])
                    P.act(t2[:, 0:n], t3[:, 0:n], AF.Exp, [t3], [t2], scale=-1.0)
                    P.tt(kt[:, 0:n], t1[:, 0:n], t2[:, 0:n], ALU.mult, [t1, t2], [kt])
                    G3 = t3[:, 0:n].rearrange("p (c i) -> p c i", i=16)
                    P.tt(t2[:, 0:n].rearrange("p (c i) -> p c i", i=16), G3[:, :, 15:16].to_broadcast([128, nch, 16]),
                         G3, ALU.subtract, [t3], [t2])
                    P.act(t2[:, 0:n], t2[:, 0:n], AF.Exp, [t2], [t2])
                    P.tt(kb[:, 0:n], t1[:, 0:n], t2[:, 0:n], ALU.mult, [t1, t2], [kb])
                    P.act(Fc[:, 0:nch], G3[:, :, 15], AF.Exp, [t3], [Fc])
                    P.cp(vb[:, 0:n], tsl(vf, t0, n, rev), [vf], [vb])
                    yield
                    pso = P.hold()
                    for g0 in range(0, nch, 4):
                        psT = P.psb()
                        psTb = psT[:].bitcast(BF16)
                        for j in range(4):
                            cs = slice((g0 + j) * 16, (g0 + j + 1) * 16)
                            P.tr(psTb[0:16, j * 128:(j + 1) * 128], kb[:, cs], identb, [kb, cstb], [psT])
                            P.tr(psTb[0:16, (4 + j) * 128:(5 + j) * 128], vb[:, cs], identb, [vb, cstb], [psT])
                        kvT = kvp.next()
                        P.cp(kvT[0:16, :, :].rearrange("p a b -> p (a b)"), psTb[0:16, :], [psT], [kvT], eng="act")
                        psA = P.psb()
                        for j in range(4):
                            cs = slice((g0 + j) * 16, (g0 + j + 1) * 16)
                            P.mm(psA[0:16, j * 16:(j + 1) * 16], kt[:, cs], qt[:, cs], True, True, [kt, qt], [psA])
                        aT = atp.next()
                        P.tt(aT[0:16, :, :], psA[0:16, 0:64].rearrange("p (a b) -> p a b", b=16),
                             C("m16", rows=16).unsqueeze(1).to_broadcast([16, 4, 16]), ALU.mult, [psA, cst], [aT])
                        yield
                        psG = P.psb()
                        for j in range(4):
                            P.mm(psG[:, j * 128:(j + 1) * 128], kvT[0:16, j, :], kvT[0:16, 4 + j, :], True, True,
                                 [kvT], [psG])
                        for j in range(4):
                            c = g0 + j
                            cs = slice(c * 16, (c + 1) * 16)
                            P.mm(pso[:, cs], kvT[0:16, 4 + j, :], aT[0:16, j, :], True, state is None, [kvT, aT], [pso])
                            if state is not None:
                                P.mm(pso[:, cs], state[1][:, :], qt[:, cs], False, True, [state[1], qt], [pso])
                            new32 = S32[cidx % 2]
                            nbf = Sbf.next()
                            if state is None:
                                P.cp(nbf[:, :], psG[:, j * 128:(j + 1) * 128], [psG], [nbf])
                                P.cp(new32[:, :], psG[:, j * 128:(j + 1) * 128], [psG], [new32])
                            else:
                                P.stt(nbf[:, :], state[0][:, :], Fc[:, c:c + 1], psG[:, j * 128:(j + 1) * 128],
                                      ALU.mult, ALU.add, [state[0], Fc, psG], [nbf])
                                P.stt(new32[:, :], state[0][:, :], Fc[:, c:c + 1], psG[:, j * 128:(j + 1) * 128],
                                      ALU.mult, ALU.add, [state[0], Fc, psG], [new32])
                            state = (new32, nbf)
                            cidx += 1
                            yield
                    P.cp(tsl(oT, t0, n, rev), pso[:, 0:n], [pso], [oT])
                    P.release(pso)

            live = [dir_gen(0), dir_gen(1)]
            while live:
                for g_ in list(live):
                    try:
                        next(g_)
                    except StopIteration:
                        live.remove(g_)
            zf = zfs[0]
            t1 = AF_.alloc("t1o", [512])
            t2 = AF_.alloc("t2o", [512])
            t3 = AF_.alloc("t3o", [512])
            proj_to(l, 2048 + h * 128, 128, zf, lambda t0, n: zf[:, t0:t0 + n])
            for ti, (t0, n) in enumerate(TILES):
                P.tt(t3[:, 0:n], oTs[0][:, t0:t0 + n], oTs[1][:, t0:t0 + n], ALU.add, [oTs[0], oTs[1]], [t3])
                rms_over_partitions(t3[:, 0:n], n, 128, C("ones", bf=True), 1.0 / 128, 0, sq1, t1, [t3])
                P.stt(t2[:, 0:n], t3[:, 0:n], V("hgn"), t1[:, 0:n], ALU.mult, ALU.mult, [t3, vec, t1], [t2])
                P.act(t3[:, 0:n], zf[:, t0:t0 + n], AF.Silu, [zf], [t3])
                y = yb.next()
                P.tt(y[:, 0:n], t2[:, 0:n], t3[:, 0:n], ALU.mult, [t2, t3], [y])
                P.dma("sp", yT[h * 128:(h + 1) * 128, t0:t0 + n], y[:, 0:n], [y], [yT_v[h][ti]])

    def rwkv_lora(l):
        P.barrier()
        AF_.reset()
        pf = AF_.alloc("pf", [T])
        sf = AF_.alloc("sf", [T])
        xb = AB_.alloc("xb", [T])
        w2b = AB_.alloc("w2b", [512])
        otp = Pool([AF_.alloc("ot%d" % b, [512]) for b in range(3)])
        groups = [(1536, 96, "w", 0, 0), (1632, 96, "w", 1, 1), (1728, 96, "a", 0, 2), (1824, 96, "a", 1, 3),
                  (1920, 64, "g", 0, 4)]
        for gi, (c0, m, kind, d, ai) in enumerate(groups):
            proj_to(l, RW0 + c0, m, pf, lambda t0, n, m=m: pf[0:m, t0:t0 + n])
            mu_ap = V("mul", rows=96, c0=gi, c1=gi + 1) if kind != "g" else V("mug", rows=64)
            token_shift(pf, sf, m, mu_ap)
            func = AF.Tanh if kind == "w" else (AF.Identity if kind == "a" else AF.Sigmoid)
            P.act(xb[0:m, :], sf[0:m, :], func, [sf], [xb])
            if kind == "w":
                wsrc, wb_ = rw_w2[l][d], rw_w2
            elif kind == "a":
                wsrc, wb_ = rw_a2[l][d], rw_a2
            else:
                wsrc, wb_ = rw_g2[l], rw_g2
            P.dma("pool", w2b[0:m, :], wsrc, [wb_], [w2b])
            for mc in range(4):
                for ti, (t0, n) in enumerate(TILES):
                    ps = P.psb()
                    P.mm(ps[:, 0:n], w2b[0:m, mc * 128:(mc + 1) * 128], xb[0:m, t0:t0 + n], True, True, [w2b, xb], [ps])
                    o = otp.next()
                    if kind == "w":
                        P.act(o[:, 0:n], ps[:, 0:n], AF.Sigmoid, [ps, vec], [o], bias=V("w0", c0=d * 4 + mc, c1=d * 4 + mc + 1))
                        P.ts(o[:, 0:n], o[:, 0:n], -DECAY_MAX, None, ALU.mult, ALU.bypass, [o], [o])
                    elif kind == "a":
                        P.act(o[:, 0:n], ps[:, 0:n], AF.Sigmoid, [ps, vec], [o], bias=V("a0", c0=d * 4 + mc, c1=d * 4 + mc + 1))
                    else:
                        P.cp(o[:, 0:n], ps[:, 0:n], [ps], [o])
                    P.dma("sp", rwaux[ai][mc * 128:(mc + 1) * 128, t0:t0 + n], o[:, 0:n], [o], [aux_v[ai][mc]])

    def mixer_rwkv(l):
        rwkv_lora(l)
        for m in range(4):
            P.barrier()
            AF_.reset()
            rf = AF_.alloc("rf", [T])
            kf = AF_.alloc("kf", [T])
            vf = AF_.alloc("vf", [T])
            kkf = AF_.alloc("kkf", [T])
            oT = AF_.alloc("oT", [T])
            pf = oT
            tlw = AF_.alloc("tlw", [512])
            tag = AF_.alloc("tag", [512])
            tcum = AF_.alloc("tcum", [512])
            te = AF_.alloc("te", [512])
            te2 = AF_.alloc("te2", [512])
            tb_ = AF_.alloc("tb", [512])
            tkd = AF_.alloc("tkd", [512])
            WC = AF_.alloc("WC", [4])
            omka = AF_.alloc("omka", [1])
            S32 = [AF_.alloc("S32_%d" % b, [64]) for b in range(2)]
            rt = AB_.alloc("rt", [512])
            bt = AB_.alloc("bt", [512])
            kt = AB_.alloc("kt", [512])
            at = AB_.alloc("at", [512])
            bb = AB_.alloc("bb", [512])
            kb = AB_.alloc("kb", [512])
            vb = AB_.alloc("vb", [512])
            sq1 = AB_.alloc("sq1", [512])
            atm = [AB_.alloc("atm%d" % b, [512]) for b in range(2)]
            rtm = [AB_.alloc("rtm%d" % b, [512]) for b in range(2)]
            for b_ in atm + rtm:
                P.memset(b_[:, :], 0.0, [b_])
            class Slot:
                pass
            slots = []
            for si in range(2):
                S_ = Slot()
                S_.Tp = Pool([AF_.alloc("Tj%d_%d" % (si, b), [2, 128]) for b in range(2)])
                S_.Lp = Pool([AF_.alloc("Lj%d_%d" % (si, b), [2, 128]) for b in range(2)])
                S_.Zp = Pool([AF_.alloc("Z%d_%d" % (si, b), [2, 128]) for b in range(3)])
                S_.tmpP = AF_.alloc("tmpP%d" % si, [128])
                S_.Gp = AF_.alloc("Gp%d" % si, [64])
                S_.Tak = AB_.alloc("Tak%d" % si, [2, 128])
                S_.Trb = AB_.alloc("Trb%d" % si, [2, 128])
                S_.Trk = AB_.alloc("Trk%d" % si, [2, 128])
                S_.tm = AB_.alloc("tm%d" % si, [4, 128])
                S_.Z1c = AB_.alloc("Z1c%d" % si, [128])
                S_.Z2c = AB_.alloc("Z2c%d" % si, [128])
                S_.PhiT = AB_.alloc("PhiT%d" % si, [128])
                S_.QeTm = [AB_.alloc("QeTm%d_%d" % (si, b), [128]) for b in range(2)]
                for b_ in S_.QeTm:
                    P.memset(b_[:, :], 0.0, [b_])
                slots.append(S_)
            identf = C("ident")
            Sbf = Pool([AB_.alloc("Sbf%d" % b, [64]) for b in range(3)])
            yb = Pool([AB_.alloc("yb%d" % b, [512]) for b in range(2)])
            identb = C("ident", bf=True)
            for (c0, dst, mi) in [(0, rf, m), (512, kf, 4 + m), (1024, vf, 8 + m)]:
                proj_to(l, RW0 + c0 + m * 128, 128, pf, lambda t0, n: pf[:, t0:t0 + n])
                token_shift(pf, dst, 128, V("mu", c0=mi, c1=mi + 1))
            P.ts(omka[:, 0:1], V("ka", c0=m, c1=m + 1), -1.0, 1.0, ALU.mult, ALU.add, [vec], [omka])
            for ti, (t0, n) in enumerate(TILES):
                P.ts(te[:, 0:n], kf[:, t0:t0 + n], V("kk", c0=m, c1=m + 1), None, ALU.mult, ALU.bypass, [kf, vec], [te])
                P.act(sq1[:, 0:n], te[:, 0:n], AF.Square, [te], [sq1])
                ps = P.psb()
                P.mm(ps[:, 0:n], C("blk", bf=True), sq1[:, 0:n], True, True, [cstb, sq1], [ps])
                P.act(te2[:, 0:n], ps[:, 0:n], AF.Sqrt, [ps, epsD], [te2], bias=epsD[:, 2:3])
                P.recip(te2[:, 0:n], te2[:, 0:n], [te2], [te2])
                P.tt(kkf[:, t0:t0 + n], te[:, 0:n], te2[:, 0:n], ALU.mult, [te, te2], [kkf])
            for d in range(2):
                state = None
                cidx = 0
                for (t0, n, rev) in scan_tiles(d):
                    nch = n // 128

                    def S(b_):
                        a_ = b_[:, 0:n]
                        return a_[:, ::-1] if rev else a_
                    P.dma("sp", tlw[:, 0:n], rwaux[d][m * 128:(m + 1) * 128, t0:t0 + n], [aux_v[d][m]], [tlw])
                    P.dma("sp", tag[:, 0:n], rwaux[2 + d][m * 128:(m + 1) * 128, t0:t0 + n], [aux_v[2 + d][m]], [tag])
                    P.scan(tcum[:, 0:n], C("rs128", c1=n), S(tlw), [cst, tlw], [tcum])
                    P.act(te[:, 0:n], tcum[:, 0:n], AF.Exp, [tcum], [te])
                    P.tt(rt[:, 0:n], tsl(rf, t0, n, rev), te[:, 0:n], ALU.mult, [rf, te], [rt])
                    P.act(te[:, 0:n], tcum[:, 0:n], AF.Exp, [tcum], [te], scale=-1.0)
                    P.tt(tb_[:, 0:n], tsl(kkf, t0, n, rev), S(tag), ALU.mult, [kkf, tag], [tb_])
                    P.tt(bt[:, 0:n], tb_[:, 0:n], te[:, 0:n], ALU.mult, [tb_, te], [bt])
                    P.ts(tkd[:, 0:n], S(tag), V("ka", c0=m, c1=m + 1), omka[:, 0:1], ALU.mult, ALU.add, [tag, vec, omka], [tkd])
                    P.tt(tkd[:, 0:n], tkd[:, 0:n], tsl(kf, t0, n, rev), ALU.mult, [tkd, kf], [tkd])
                    P.tt(kt[:, 0:n], tkd[:, 0:n], te[:, 0:n], ALU.mult, [tkd, te], [kt])
                    P.tt(te2[:, 0:n], tcum[:, 0:n], S(tlw), ALU.subtract, [tcum, tlw], [te2])
                    P.act(te2[:, 0:n], te2[:, 0:n], AF.Exp, [te2], [te2])
                    P.stt(at[:, 0:n], tsl(kkf, t0, n, rev), -1.0, te2[:, 0:n], ALU.mult, ALU.mult, [kkf, te2], [at])
                    c3 = tcum[:, 0:n].rearrange("p (c i) -> p c i", i=128)
                    P.tt(te2[:, 0:n].rearrange("p (c i) -> p c i", i=128), c3[:, :, 127:128].to_broadcast([128, nch, 128]),
                         c3, ALU.subtract, [tcum], [te2])
                    P.act(te2[:, 0:n], te2[:, 0:n], AF.Exp, [te2], [te2])
                    P.tt(bb[:, 0:n], tb_[:, 0:n], te2[:, 0:n], ALU.mult, [tb_, te2], [bb])
                    P.tt(kb[:, 0:n], tkd[:, 0:n], te2[:, 0:n], ALU.mult, [tkd, te2], [kb])
                    P.cp(vb[:, 0:n], tsl(vf, t0, n, rev), [vf], [vb])
                    P.act(WC[:, 0:nch], c3[:, :, 127], AF.Exp, [tcum], [WC])
                    for hh in range(2):
                        pr = slice(hh * 64, hh * 64 + 64)
                        P.cp(atm[hh][pr, 0:n], at[pr, 0:n], [at], [atm[hh]])
                        P.cp(rtm[hh][pr, 0:n], rt[pr, 0:n], [rt], [rtm[hh]], eng="act")
                    psO = P.hold()

                    def v3(ap):
                        return ap.rearrange("p (a b) -> p a b", b=128)

                    def msk(nm):
                        return C(nm).unsqueeze(1).to_broadcast([128, 2, 128])

                    def chunk_pre(c, S_):
                        cs = slice(c * 128, (c + 1) * 128)
                        A_, B_, C_ = P.psb(), P.psb(), P.psb()
                        for hh in range(2):
                            o0, o1 = hh * 128, 256 + hh * 128
                            am, rm = atm[hh], rtm[hh]
                            P.mm(A_[:, o0:o0 + 128], bt[:, cs], am[:, cs], True, True, [bt, am], [A_])
                            P.mm(A_[:, o1:o1 + 128], am[:, cs], bt[:, cs], True, True, [bt, am], [A_])
                            P.mm(B_[:, o0:o0 + 128], kt[:, cs], am[:, cs], True, True, [kt, am], [B_])
                            P.mm(B_[:, o1:o1 + 128], bt[:, cs], rm[:, cs], True, True, [bt, rm], [B_])
                            P.mm(C_[:, o0:o0 + 128], kt[:, cs], rm[:, cs], True, True, [kt, rm], [C_])
                        psT = P.psb()
                        psTb = psT[:].bitcast(BF16)
                        for i_, src in enumerate([bb, kb, vb, at]):
                            P.tr(psTb[:, i_ * 128:(i_ + 1) * 128], src[:, cs], identb, [src, cstb], [psT])
                        Tj, Lj = S_.Tp.next(), S_.Lp.next()
                        P.tt(Tj[:, :, :], v3(A_[:, 0:256]), msk("tri_s"), ALU.mult, [A_, cst], [Tj])
                        P.tt(Lj[:, :, :], v3(A_[:, 256:512]), msk("tri_sT"), ALU.mult, [A_, cst], [Lj])
                        P.tt(S_.Tak[:, :, :], v3(B_[:, 0:256]), msk("tri_s"), ALU.mult, [B_, cst], [S_.Tak])
                        P.tt(S_.Trb[:, :, :], v3(B_[:, 256:512]), msk("tri_i"), ALU.mult, [B_, cst], [S_.Trb])
                        P.tt(S_.Trk[:, :, :], v3(C_[:, 0:256]), msk("tri_i"), ALU.mult, [C_, cst], [S_.Trk])
                        tm = S_.tm
                        P.cp(tm[:, :, :].rearrange("p a b -> p (a b)"), psTb[:, 0:512], [psT], [tm], eng="act")
                        yield
                        psX = P.psb()
                        for hh in range(2):
                            P.mm(psX[:, hh * 64:(hh + 1) * 64], S_.Tak[:, hh, :], tm[:, 2, hh * 64:(hh + 1) * 64], True, True,
                                 [S_.Tak, tm], [psX])
                        Z = S_.Zp.next()
                        P.cp(Z[:, :, 0:64], tm[:, 3, :].rearrange("p (a b) -> p a b", b=64), [tm], [Z], eng="act")
                        P.cp(Z[:, :, 64:128], psX[:, 0:128].rearrange("p (a b) -> p a b", b=64), [psX], [Z])
                        yield
                        for j in range(7):
                            psZ = P.psb()
                            for hh in range(2):
                                o0 = hh * 128
                                P.mm(psZ[:, o0:o0 + 128], Tj[:, hh, :], Z[:, hh, :], True, True, [Tj, Z], [psZ])
                            if j < 6:
                                psS = P.psb()
                                for hh in range(2):
                                    o0, o1 = hh * 128, 256 + hh * 128
                                    P.mm(psS[:, o0:o0 + 128], Lj[:, hh, :], Tj[:, hh, :], True, True, [Lj, Tj], [psS])
                                    P.mm(psS[:, o1:o1 + 128], Tj[:, hh, :], Lj[:, hh, :], True, True, [Lj, Tj], [psS])
                                Zn = S_.Zp.next()
                                P.tt(Zn[:, :, :], v3(psZ[:, 0:256]), Z[:, :, :], ALU.add, [psZ, Z], [Zn])
                                Tn, Ln = S_.Tp.next(), S_.Lp.next()
                                P.cp(Tn[:, :, :], v3(psS[:, 0:256]), [psS], [Tn], eng="act")
                                P.cp(Ln[:, :, :], v3(psS[:, 256:512]), [psS], [Ln], eng="act")
                                Z, Tj, Lj = Zn, Tn, Ln
                            else:
                                z3 = v3(psZ[:, 0:256])
                                P.tt(S_.Z1c[:, :].rearrange("p (a b) -> p a b", b=64), z3[:, :, 0:64], Z[:, :, 0:64],
                                     ALU.add, [psZ, Z], [S_.Z1c])
                                P.tt(S_.Z2c[:, :].rearrange("p (a b) -> p a b", b=64), z3[:, :, 64:128], Z[:, :, 64:128],
                                     ALU.add, [psZ, Z], [S_.Z2c])
                            yield
                        psP = P.psb()
                        P.mm(psP[:, 0:128], S_.Z1c[:, :], tm[:, 0, :], True, True, [S_.Z1c, tm], [psP])
                        P.mm(psP[:, 128:256], tm[:, 0, :], S_.Z2c[:, :], True, False, [S_.Z2c, tm], [psP])
                        P.mm(psP[:, 128:256], tm[:, 1, :], tm[:, 2, :], False, True, [tm], [psP])
                        psQ = P.psb()
                        for hh in range(2):
                            pr = slice(hh * 64, hh * 64 + 64)
                            P.mm(psQ[pr, 0:128], S_.Z1c[:, pr], S_.Trb[:, hh, :], True, True, [S_.Z1c, S_.Trb], [psQ])
                        P.tt(S_.tmpP[:, :], psP[:, 0:128], C("blk"), ALU.mult, [psP, cst], [S_.tmpP])
                        P.stt(S_.PhiT[:, :], C("ident"), WC[:, c:c + 1], S_.tmpP[:, :], ALU.mult, ALU.add,
                              [cst, WC, S_.tmpP], [S_.PhiT])
                        P.cp(S_.Gp[0:64, :], psP[0:64, 128:192], [psP], [S_.Gp], eng="act")
                        P.cp(S_.Gp[64:128, :], psP[64:128, 192:256], [psP], [S_.Gp], eng="act")
                        for hh in range(2):
                            pr = slice(hh * 64, hh * 64 + 64)
                            P.tt(S_.QeTm[hh][pr, :], psQ[pr, 0:128], rt[pr, cs], ALU.add, [psQ, rt], [S_.QeTm[hh]])
                        yield

                    def chunk_post(c, S_, state, cidx):
                        cs = slice(c * 128, (c + 1) * 128)
                        tm = S_.tm
                        for hh in range(2):
                            pr = slice(hh * 64, hh * 64 + 64)
                            P.mm(psO[pr, cs], S_.Z2c[:, pr], S_.Trb[:, hh, :], True, False, [S_.Z2c, S_.Trb], [psO])
                            P.mm(psO[pr, cs], tm[:, 2, pr], S_.Trk[:, hh, :], False, state is None, [tm, S_.Trk], [psO])
                            if state is not None:
                                P.mm(psO[pr, cs], state[1][:, :], S_.QeTm[hh][:, :], False, True,
                                     [state[1], S_.QeTm[hh]], [psO])
                        new32 = S32[cidx % 2]
                        nbf = Sbf.next()
                        if state is None:
                            P.cp(nbf[:, :], S_.Gp[:, :], [S_.Gp], [nbf])
                            P.cp(new32[:, :], S_.Gp[:, :], [S_.Gp], [new32], eng="act")
                        else:
                            psS2 = P.psb()
                            P.mm(psS2[:, 0:64], S_.PhiT[:, :], state[1][:, :], True, True, [S_.PhiT, state[1]], [psS2])
                            P.tt(nbf[:, :], psS2[:, 0:64], S_.Gp[:, :], ALU.add, [psS2, S_.Gp], [nbf])
                            P.tt(new32[:, :], psS2[:, 0:64], S_.Gp[:, :], ALU.add, [psS2, S_.Gp], [new32])
                        return (new32, nbf)

                    for c0_ in range(0, nch, 2):
                        cl = list(range(c0_, min(c0_ + 2, nch)))
                        gens = [chunk_pre(c, slots[c % 2]) for c in cl]
                        live = list(gens)
                        while live:
                            for g_ in list(live):
                                try:
                                    next(g_)
                                except StopIteration:
                                    live.remove(g_)
                        for c in cl:
                            state = chunk_post(c, slots[c % 2], state, cidx)
                            cidx += 1
                    if d == 0:
                        P.cp(oT[:, t0:t0 + n], psO[:, 0:n], [psO], [oT])
                    else:
                        P.tt(tsl(oT, t0, n, True), tsl(oT, t0, n, True), psO[:, 0:n], ALU.add, [oT, psO], [oT])
                    P.release(psO)
            for ti, (t0, n) in enumerate(TILES):
                osl = oT[:, t0:t0 + n]
                P.cp(sq1[:, 0:n], osl, [oT], [sq1], eng="act")
                ps = P.psb()
                P.mm(ps[:, 0:n], C("blk", bf=True), sq1[:, 0:n], True, True, [cstb, sq1], [ps])
                P.stt(te[:, 0:n], ps[:, 0:n], -1.0 / 64, osl, ALU.mult, ALU.add, [ps, oT], [te])
                P.act(sq1[:, 0:n], te[:, 0:n], AF.Square, [te], [sq1])
                ps = P.psb()
                P.mm(ps[:, 0:n], C("blk", bf=True), sq1[:, 0:n], True, True, [cstb, sq1], [ps])
                P.act(te2[:, 0:n], ps[:, 0:n], AF.Sqrt, [ps, epsD], [te2], scale=1.0 / 64, bias=epsD[:, 1:2])
                P.recip(te2[:, 0:n], te2[:, 0:n], [te2], [te2])
                P.tt(te[:, 0:n], te[:, 0:n], te2[:, 0:n], ALU.mult, [te, te2], [te])
                P.ts(te[:, 0:n], te[:, 0:n], V("gnw", c0=m, c1=m + 1), V("gnb", c0=m, c1=m + 1), ALU.mult, ALU.add,
                     [te, vec], [te])
                P.dma("sp", tlw[:, 0:n], rwaux[2][m * 128:(m + 1) * 128, t0:t0 + n], [aux_v[2][m]], [tlw])
                P.dma("sp", tag[:, 0:n], rwaux[3][m * 128:(m + 1) * 128, t0:t0 + n], [aux_v[3][m]], [tag])
                P.tt(tb_[:, 0:n], tlw[:, 0:n], tag[:, 0:n], ALU.add, [tlw, tag], [tb_])
                P.ts(tb_[:, 0:n], tb_[:, 0:n], -2.0, None, ALU.add, ALU.bypass, [tb_], [tb_])
                P.ts(tb_[:, 0:n], tb_[:, 0:n], V("ka", c0=m, c1=m + 1), 2.0, ALU.mult, ALU.add, [tb_, vec], [tb_])
                P.tt(tb_[:, 0:n], tb_[:, 0:n], kf[:, t0:t0 + n], ALU.mult, [tb_, kf], [tb_])
                P.tt(tb_[:, 0:n], tb_[:, 0:n], rf[:, t0:t0 + n], ALU.mult, [tb_, rf], [tb_])
                P.ts(sq1[:, 0:n], tb_[:, 0:n], V("rk", c0=m, c1=m + 1), None, ALU.mult, ALU.bypass, [tb_, vec], [sq1])
                ps = P.psb()
                P.mm(ps[:, 0:n], C("blk", bf=True), sq1[:, 0:n], True, True, [cstb, sq1], [ps])
                P.tt(tkd[:, 0:n], ps[:, 0:n], vf[:, t0:t0 + n], ALU.mult, [ps, vf], [tkd])
                P.tt(te[:, 0:n], te[:, 0:n], tkd[:, 0:n], ALU.add, [te, tkd], [te])
                P.dma("sp", tcum[:, 0:n], rwaux[4][m * 128:(m + 1) * 128, t0:t0 + n], [aux_v[4][m]], [tcum])
                y = yb.next()
                P.tt(y[:, 0:n], te[:, 0:n], tcum[:, 0:n], ALU.mult, [te, tcum], [y])
                P.dma("sp", yT[(4 + m) * 128:(5 + m) * 128, t0:t0 + n], y[:, 0:n], [y], [yT_v[4 + m][ti]])
    for l in range(nl):
        stage_mod(l)
        stage_norm(l, 0)
        mixer_hgrn2(l)
        mixer_rwkv(l)
        mixer_mla(l)
        stage_out(l)
        stage_norm(l, 1)
        stage_ffn(l, l == nl - 1)
    return P


_BIG = ["w_mod", "w_in", "w_out", "rw_w2", "rw_a2", "rw_g2", "mla_w_uq", "mla_w_ukv", "ffn_up", "ffn_down"]


def kernel(**inputs):
    inp = {k: np.asarray(v) for k, v in inputs.items()}
    P = build_program(L)
    nc = P.build()
    cst, cs = host_consts()
    vecs, lb = host_vecs(inp)
    big = {k: np.ascontiguousarray(inp[k], dtype=np.float32) for k in _BIG}
    in_maps = []
    for b in range(8):
        xT = np.ascontiguousarray(np.concatenate([inp["ctx"][b], inp["x"][b]], 0).T.astype(np.float32))
        cT = np.ascontiguousarray(np.stack([fm(inp["c"][b], 16), fm(inp["c_ctx"], 16)], -1).reshape(128, 32))
        in_maps.append(dict(xT=xT, cT=cT, vecs=vecs, hglb=lb, cst=cst, rope=cs, **big))
    res = run_bass_kernel_spmd(nc, in_maps, core_ids=list(range(8)))
    out = np.stack([np.ascontiguousarray(np.asarray(res.results[b]["out"]).T) for b in range(8)], 0)
    return out.astype(np.float32)
```

```python
import math
import numpy as np
import concourse.bass as bass
import concourse.mybir as mybir
from concourse.bass_utils import run_bass_kernel_spmd

F32 = mybir.dt.float32
BF16 = mybir.dt.bfloat16
ALU = mybir.AluOpType
AF = mybir.ActivationFunctionType
ENGS = ["pe", "act", "dve", "pool", "sp"]
NDMASEM = 16

L = 4
D = 2048
KC = 16
NCTX = 256
NLAT = 2048
T = NCTX + NLAT
TILES = [(0, 256), (256, 512), (768, 512), (1280, 512), (1792, 512)]
SEGS = [(0, 256), (256, 2304)]
DFF = 5632
IN_COLS = 5376
EPS = 1e-6
RW0 = 2560
ML0 = 4544
DECAY_MAX = math.exp(-0.5)


class Buf:
    __slots__ = ("name", "h", "writer", "readers", "excl")

    def __init__(self, name, h):
        self.name = name
        self.h = h
        self.writer = None
        self.readers = {}
        self.excl = False

    def __getitem__(self, idx):
        return self.h[idx]


class Pool:
    def __init__(self, bufs):
        self.bufs = bufs
        self.i = 0
        self.held = set()

    def next(self):
        while True:
            b = self.bufs[self.i % len(self.bufs)]
            self.i += 1
            if b.name not in self.held:
                return b

    def hold(self):
        b = self.next()
        self.held.add(b.name)
        return b

    def release(self, b):
        self.held.discard(b.name)


class Prog:
    def __init__(self):
        self.nc = bass.Bass("TRN2", target_bir_lowering=False)
        self.ops = {e: [] for e in ENGS}
        self.qcount = {}
        self.seen = {e: {} for e in ENGS}
        self.sems = {}
        self._stack = []
        self.floor = {}
        self.banks = None
        self.dma_idx = {}

    def sb(self, name, shape, dt=F32):
        g = self.nc.sbuf_tensor(name, list(shape), dt)
        h = g.__enter__()
        self._stack.append(g)
        return Buf(name, h)

    def pool(self, name, shape, dt, n):
        return Pool([self.sb("%s%d" % (name, i), shape, dt) for i in range(n)])

    def dram(self, name, shape, dt=F32, kind="Internal"):
        h = self.nc.dram_tensor(name, list(shape), dt, kind=kind)
        return Buf(name, h.ap())

    def view(self, buf, name=None):
        return Buf(name or buf.name, buf.h)

    def init_psum(self):
        bl = []
        for i in range(8):
            g = self.nc.psum_tensor("psb%d" % i, [128, 512], F32)
            h = g.__enter__()
            self._stack.append(g)
            b_ = Buf("psb%d" % i, h)
            b_.excl = True
            bl.append(b_)
        self.banks = Pool(bl)

    def psb(self):
        return self.banks.next()

    def hold(self):
        return self.banks.hold()

    def release(self, b):
        self.banks.release(b)

    def barrier(self):
        self.floor = dict(self.qcount)

    def emit(self, eng, fn, reads=(), writes=(), dma=False):
        if dma:
            k = self.dma_idx.get(eng, 0)
            self.dma_idx[eng] = k + 1
            q = "dma_%s_%d" % (eng, k % NDMASEM)
        else:
            q = eng
        excl = [b for b in reads if b.excl]
        if excl:
            reads = [b for b in reads if not b.excl]
            writes = list(writes) + excl
        deps = dict(self.floor)
        if dma and self.qcount.get(q, 0) > 0:
            deps[q] = self.qcount[q]

        def need(w):
            if w is None:
                return
            qq, c = w
            if qq == q and q == "pe":
                return
            if deps.get(qq, 0) < c:
                deps[qq] = c

        for b in reads:
            need(b.writer)
        for b in writes:
            need(b.writer)
            for qq, c in b.readers.items():
                need((qq, c))
        waits = []
        seen = self.seen[eng]
        for qq, c in deps.items():
            if qq == "pe" and q == "pe":
                continue
            if seen.get(qq, 0) < c:
                seen[qq] = c
                waits.append((qq, c))
        inc = 16 if dma else 1
        cnt = self.qcount.get(q, 0) + inc
        self.qcount[q] = cnt
        self.ops[eng].append((waits, fn, q, inc))
        for b in reads:
            b.readers[q] = cnt
        for b in writes:
            b.writer = (q, cnt)
            b.readers = {}
        return cnt

    def mm(self, out, lhsT, rhs, start, stop, R, W):
        self.emit("pe", lambda e: e.matmul(out, lhsT, rhs, start=start, stop=stop), R, W)

    def tr(self, out, in_, ident, R, W):
        self.emit("pe", lambda e: e.transpose(out, in_, ident), R, W)

    def act(self, out, in_, func, R, W, scale=1.0, bias=0.0):
        self.emit("act", lambda e: e.activation(out=out, in_=in_, func=func, scale=scale, bias=bias), R, W)

    def tt(self, out, in0, in1, op, R, W, eng="dve"):
        self.emit(eng, lambda e: e.tensor_tensor(out=out, in0=in0, in1=in1, op=op), R, W)

    def ts(self, out, in0, s1, s2, op0, op1, R, W, eng="dve"):
        self.emit(eng, lambda e: e.tensor_scalar(out=out, in0=in0, scalar1=s1, scalar2=s2, op0=op0, op1=op1), R, W)

    def stt(self, out, in0, scalar, in1, op0, op1, R, W):
        self.emit("dve", lambda e: e.scalar_tensor_tensor(out=out, in0=in0, scalar=scalar, in1=in1, op0=op0, op1=op1), R, W)

    def cp(self, out, in_, R, W, eng="dve"):
        if eng == "act":
            self.emit("act", lambda e: e.activation(out=out, in_=in_, func=AF.Copy), R, W)
        else:
            self.emit(eng, lambda e: e.tensor_copy(out=out, in_=in_), R, W)

    def recip(self, out, in_, R, W):
        self.emit("dve", lambda e: e.reciprocal(out=out, in_=in_), R, W)

    def scan(self, out, d0, d1, R, W):
        self.emit("dve", lambda e: e.tensor_tensor_scan(out=out, data0=d0, data1=d1, initial=0.0,
                                                       op0=ALU.mult, op1=ALU.add), R, W)

    def memset(self, ap, val, W, eng="dve"):
        self.emit(eng, lambda e: e.memset(ap, val), (), W)

    def dma(self, eng, out, in_, R, W):
        self.emit(eng, lambda e: e.dma_start(out=out, in_=in_), R, W, dma=True)

    def build(self):
        nc = self.nc
        qs = sorted(self.qcount.keys())
        guards = []
        for q in qs:
            g = nc.semaphore("s_" + q)
            self.sems[q] = g.__enter__()
            guards.append(g)
        ops, sems, qcount = self.ops, self.sems, self.qcount

        def run(engname):
            def body(e):
                for waits, fn, q, inc in ops[engname]:
                    for qq, c in waits:
                        e.wait_ge(sems[qq], c)
                    fn(e).then_inc(sems[q], inc)
                if engname == "sp":
                    for q in qs:
                        e.wait_ge(sems[q], qcount[q])
            return body

        with nc.Block() as block:
            block.sync(run("sp"))
            block.tensor(run("pe"))
            block.scalar(run("act"))
            block.vector(run("dve"))
            block.gpsimd(run("pool"))
        for g in guards:
            g.__exit__(None, None, None)
        while self._stack:
            self._stack.pop().__exit__(None, None, None)
        return nc


V128 = {}
_o = 0
for _n, _w in [("ng1", 16), ("ng2", 16), ("bm", 96), ("hgn", 1), ("mu", 12), ("w0", 8), ("a0", 8), ("kk", 4),
               ("ka", 4), ("rk", 4), ("gnw", 4), ("gnb", 4), ("qnorm", 4), ("kvnorm", 2), ("qn_g", 1),
               ("kn_g", 1), ("dw", 264), ("db", 88), ("mul", 4), ("mug", 1), ("qr_g", 1), ("kr_g", 1)]:
    V128[_n] = (_o, _w)
    _o += _w
NV = _o

C128 = {}
_o = 0
for _n, _w in [("ident", 128), ("ones", 128), ("blk", 128), ("tri_i", 128), ("tri_s", 128), ("tri_sT", 128),
               ("m16", 16), ("rs16", 512), ("rs128", 512), ("rotT", 64)]:
    C128[_n] = (_o, _w)
    _o += _w
NCST = _o


def fm(v, nch):
    return np.ascontiguousarray(np.asarray(v, np.float32).reshape(nch, 128).T)


def host_consts():
    c = np.zeros((128, NCST), np.float32)

    def put(n, a):
        o, w = C128[n]
        c[:a.shape[0], o:o + a.shape[1]] = a

    put("ident", np.eye(128, dtype=np.float32))
    put("ones", np.ones((128, 128), np.float32))
    blk = np.zeros((128, 128), np.float32)
    blk[:64, :64] = 1
    blk[64:, 64:] = 1
    put("blk", blk)
    put("tri_i", np.triu(np.ones((128, 128), np.float32)))
    put("tri_s", np.triu(np.ones((128, 128), np.float32), 1))
    put("tri_sT", np.tril(np.ones((128, 128), np.float32), -1))
    put("m16", np.triu(np.ones((16, 16), np.float32)))
    r16 = np.ones((128, 512), np.float32)
    r16[:, ::16] = 0
    put("rs16", r16)
    r128 = np.ones((128, 512), np.float32)
    r128[:, ::128] = 0
    put("rs128", r128)
    rot = np.zeros((64, 64), np.float32)
    rot[:32, 32:] = -np.eye(32)
    rot[32:, :32] = np.eye(32)
    put("rotT", np.ascontiguousarray(rot.T))
    rows = NLAT // 64
    row = np.repeat(np.arange(rows, dtype=np.float32), 64)
    col = np.tile(np.arange(64, dtype=np.float32), rows)
    inv = (10000.0 ** (-np.arange(0, 32, 2, dtype=np.float32) / 32)).astype(np.float32)
    ang = np.concatenate([row[:, None] * inv, col[:, None] * inv], -1).astype(np.float32)
    cos, sin = np.cos(ang).astype(np.float32), np.sin(ang).astype(np.float32)
    cs = np.zeros((2, 64, NLAT), np.float32)
    cs[0] = np.concatenate([cos.T, cos.T], 0)
    cs[1] = np.concatenate([sin.T, sin.T], 0)
    return c, cs


def host_vecs(inp):
    v = np.zeros((L, 128, NV), np.float32)

    def put(l, n, a):
        o, w = V128[n]
        a = np.asarray(a, np.float32)
        v[l, :a.shape[0], o:o + a.shape[1]] = a

    for l in range(L):
        put(l, "ng1", fm(inp["norm_g"][l, 0], 16))
        put(l, "ng2", fm(inp["norm_g"][l, 1], 16))
        put(l, "bm", fm(inp["b_mod"][l], 96))
        put(l, "hgn", inp["hg_gn"][l][:, None])
        put(l, "mu", fm(inp["rw_mu"][l, :1536], 12))
        put(l, "w0", np.concatenate([fm(inp["rw_w0"][l, 0], 4), fm(inp["rw_w0"][l, 1], 4)], 1))
        put(l, "a0", np.concatenate([fm(inp["rw_a0"][l, 0], 4), fm(inp["rw_a0"][l, 1], 4)], 1))
        put(l, "kk", fm(inp["rw_kk"][l], 4))
        put(l, "ka", fm(inp["rw_ka"][l], 4))
        put(l, "rk", fm(inp["rw_rk"][l].reshape(-1), 4))
        put(l, "gnw", fm(inp["rw_gn_w"][l], 4))
        put(l, "gnb", fm(inp["rw_gn_b"][l], 4))
        put(l, "qnorm", fm(inp["mla_q_norm"][l], 4))
        put(l, "kvnorm", fm(inp["mla_kv_norm"][l], 2))
        put(l, "qn_g", inp["mla_qn_g"][l][:, None])
        put(l, "kn_g", inp["mla_kn_g"][l][:, None])
        dw = inp["ffn_dw"][l]
        put(l, "dw", np.concatenate([fm(dw[0], 88), fm(dw[1], 88), fm(dw[2], 88)], 1))
        put(l, "db", fm(inp["ffn_db"][l], 88))
        mu = inp["rw_mu"][l]
        put(l, "mul", np.stack([mu[1536 + 96 * i:1536 + 96 * (i + 1)] for i in range(4)], 1))
        put(l, "mug", mu[1920:1984][:, None])
        put(l, "qr_g", inp["mla_qr_g"][l][:, None])
        put(l, "kr_g", inp["mla_kr_g"][l][:, None])
    lb = np.zeros((128, 4, 2, L), np.float32)
    for l in range(L):
        for d in range(2):
            lb[:, :, d, l] = fm(inp["hg_lb"][l, d], 4)
    return v, lb.reshape(128, 4 * 2 * L)


class ArenaMem:
    def __init__(self, P, name, nwords):
        self.buf = P.sb(name, [128, nwords], F32)
        self.n = nwords
        self.off = 0


class Arena:
    def __init__(self, mem, dt):
        self.mem = mem
        self.dt = dt

    def reset(self):
        self.mem.off = 0

    def alloc(self, name, shape):
        size = int(np.prod(shape))
        words = size if self.dt == F32 else (size + 1) // 2
        words = (words + 3) // 4 * 4
        m = self.mem
        assert m.off + words <= m.n, (name, m.off, words, m.n)
        ap = m.buf.h[:, m.off:m.off + words]
        m.off += words
        if self.dt != F32:
            ap = ap.bitcast(self.dt)
        ap = ap[:, 0:size]
        if len(shape) == 2:
            ap = ap.rearrange("p (a b) -> p a b", b=shape[1])
        elif len(shape) == 3:
            ap = ap.rearrange("p (a b c) -> p a b c", b=shape[1], c=shape[2])
        return Buf(name, ap)


def scan_tiles(d):
    if d == 0:
        return [(t0, n, False) for (t0, n) in TILES]
    return [(0, 256, True)] + [(t0, 512, True) for t0 in (1792, 1280, 768, 256)]


def tsl(ap2d, t0, n, rev):
    a = ap2d[:, t0:t0 + n]
    return a[:, ::-1] if rev else a


def build_program(nl=L, dbg=()):
    P = Prog()
    P.init_psum()
    EI = "ExternalInput"
    xT_in = P.dram("xT", [D, T], F32, EI)
    cT_in = P.dram("cT", [128, 32], F32, EI)
    vec_in = P.dram("vecs", [L, 128, NV], F32, EI)
    lb_in = P.dram("hglb", [128, 4 * 2 * L], F32, EI)
    cst_in = P.dram("cst", [128, NCST], F32, EI)
    cs_in = P.dram("rope", [2, 64, NLAT], F32, EI)
    w_mod = P.dram("w_mod", [L, D, 6 * D], F32, EI)
    w_in = P.dram("w_in", [L, D, IN_COLS], F32, EI)
    w_out = P.dram("w_out", [L, D, D], F32, EI)
    rw_w2 = P.dram("rw_w2", [L, 2, 96, 512], F32, EI)
    rw_a2 = P.dram("rw_a2", [L, 2, 96, 512], F32, EI)
    rw_g2 = P.dram("rw_g2", [L, 64, 512], F32, EI)
    w_uq = P.dram("mla_w_uq", [L, 512, 1536], F32, EI)
    w_ukv = P.dram("mla_w_ukv", [L, 256, 2048], F32, EI)
    ffn_up = P.dram("ffn_up", [L, D, 2 * DFF], F32, EI)
    ffn_down = P.dram("ffn_down", [L, DFF, D], F32, EI)
    out = P.dram("out", [D, NLAT], F32, "ExternalOutput")
    dbg_out = {}
    for n, shp in dbg:
        dbg_out[n] = P.dram("dbg_" + n, shp, F32, "ExternalOutput")

    xs = P.dram("xs", [D, T], F32)
    yT = P.dram("yT", [D, T], BF16)
    zT = P.dram("zT", [DFF, T], BF16)
    rwaux = P.dram("rwaux", [5, 512, T], F32)
    xs_v = [[P.view(xs, "xs_%d_%d" % (m, ti)) for ti in range(5)] for m in range(KC)]
    xin_v = [[P.view(xT_in, "xin_%d_%d" % (m, ti)) for ti in range(5)] for m in range(KC)]
    yT_v = [[P.view(yT, "yT_%d_%d" % (m, ti)) for ti in range(5)] for m in range(KC)]
    zT_v = [P.view(zT, "zT_%d" % j) for j in range(44)]
    aux_v = [[P.view(rwaux, "aux_%d_%d" % (a, m)) for m in range(4)] for a in range(5)]
    out_v = [[P.view(out, "out_%d_%d" % (m, ti)) for ti in range(5)] for m in range(KC)]

    cst = P.sb("cst_s", [128, NCST], F32)
    cstb = P.sb("cst_b", [128, NCST], BF16)
    P.dma("sp", cst[:], cst_in[:], [cst_in], [cst])
    P.cp(cstb[:], cst[:], [cst], [cstb])

    def C(n, rows=128, bf=False, c0=0, c1=None):
        o, w = C128[n]
        c1 = w if c1 is None else c1
        return (cstb if bf else cst)[0:rows, o + c0:o + c1]

    hT = P.sb("hT", [128, KC, T], BF16)
    hT_v = [P.view(hT, "hT_%d" % ti) for ti in range(5)]
    vec = P.sb("vec", [128, NV], F32)
    modT = P.sb("modT", [128, 96, 2], F32)
    mA = P.sb("mA", [128, 2, KC, 2], F32)
    lbt = P.sb("lbt", [128, 4, 2, L], F32)
    omlb = P.sb("omlb", [128, 4, 2, L], F32)
    cT = P.sb("cTs", [128, 32], F32)
    sT = P.sb("sTs", [128, KC, 2], BF16)
    epsD = P.sb("epsD", [128, 4], F32)
    wpool = P.pool("wp", [128, KC, 128], BF16, 3)
    _mem = ArenaMem(P, "arena", 27600)
    AF_ = Arena(_mem, F32)
    AB_ = Arena(_mem, BF16)

    def V(n, rows=128, c0=0, c1=None):
        o, w = V128[n]
        c1 = w if c1 is None else c1
        return vec[0:rows, o + c0:o + c1]

    P.memset(_mem.buf[:, :], 0.0, [_mem.buf])
    P.memset(hT[:].rearrange('p k t -> p (k t)'), 0.0, [hT])
    P.memset(epsD[:, 0:1], EPS, [epsD])
    P.memset(epsD[:, 1:2], 64e-5, [epsD])
    P.memset(epsD[:, 2:3], 1e-24, [epsD])
    P.memset(epsD[:, 3:4], 0.0, [epsD])

    P.dma("sp", cT[:], cT_in[:], [cT_in], [cT])
    P.act(sT[:].rearrange("p k g -> p (k g)"), cT[:], AF.Silu, [cT], [sT])

    lbe = P.sb("lbe", [128, 8, L], F32)
    lbs = P.sb("lbs", [128, 8], F32)
    P.dma("sp", lbe[:].rearrange("p a l -> p (a l)"), lb_in[:], [lb_in], [lbe])
    P.act(lbe[:], lbe[:], AF.Exp, [lbe], [lbe])
    P.emit("dve", lambda e: e.reduce_sum(out=lbs[:], in_=lbe[:], axis=mybir.AxisListType.X), [lbe], [lbs])
    P.recip(lbs[:], lbs[:], [lbs], [lbs])
    P.tt(lbe[:], lbe[:], lbs[:].unsqueeze(2).to_broadcast([128, 8, L]), ALU.mult, [lbe, lbs], [lbe])
    lb3 = lbt[:].rearrange("p h d l -> p (h d) l")
    P.memset(lb3[:, :, 0:1], 0.0, [lbt])
    for l in range(1, L):
        P.tt(lb3[:, :, l:l + 1], lb3[:, :, l - 1:l], lbe[:, :, l:l + 1], ALU.add, [lbt, lbe], [lbt])
    P.ts(omlb[:], lbt[:], -1.0, 1.0, ALU.mult, ALU.add, [lbt], [omlb])

    widx = [0]

    def load_w(src_buf, src_ap, rows=128, kc=KC, m=128):
        wb = wpool.next()
        P.dma("pool", wb[0:rows, 0:kc, 0:m], src_ap, [src_buf], [wb])
        return wb

    def x_src(l, m, ti):
        return (xin_v if l == 0 else xs_v)[m][ti], (xT_in if l == 0 else xs)

    def stage_mod(l):
        P.dma("sp", vec[:], vec_in[l], [vec_in], [vec])
        for j in range(96):
            wb = load_w(w_mod, w_mod[l][:, j * 128:(j + 1) * 128].rearrange("(k p) n -> p k n", p=128))
            ps = P.psb()
            for k in range(KC):
                P.mm(ps[:, 0:2], wb[:, k, :], sT[:, k, :], k == 0, k == KC - 1, [wb, sT], [ps])
            P.ts(modT[:, j, :], ps[:, 0:2], V("bm", c0=j, c1=j + 1), None, ALU.add, ALU.bypass, [ps, vec], [modT])
        for i, (ng, sc0) in enumerate([("ng1", 16), ("ng2", 64)]):
            P.ts(mA[:, i, :, :], modT[:, sc0:sc0 + 16, :], 1.0, None, ALU.add, ALU.bypass, [modT], [mA])
            P.tt(mA[:, i, :, :], mA[:, i, :, :], V(ng).unsqueeze(2).to_broadcast([128, 16, 2]), ALU.mult,
                 [mA, vec], [mA])

    def stage_norm(l, i):
        P.barrier()
        AF_.reset()
        AB_.reset()
        xts = [AF_.alloc("xt%d" % b, [KC, 512]) for b in range(2)]
        sqs = [AB_.alloc("sq%d" % b, [KC, 512]) for b in range(2)]
        rss = [AF_.alloc("rs%d" % b, [512]) for b in range(2)]
        sh0 = 0 if i == 0 else 48
        for ti, (t0, n) in enumerate(TILES):
            g = 1 if ti == 0 else 0
            xt, sq, rs = xts[ti % 2], sqs[ti % 2], rss[ti % 2]
            srcs = [x_src(l if i == 0 else 99, m, ti)[0] for m in range(KC)]
            src = xT_in if (l == 0 and i == 0) else xs
            P.dma("sp", xt[:, :, 0:n], src[:, t0:t0 + n].rearrange("(k p) t -> p k t", p=128), srcs, [xt])
            P.act(sq[:, :, 0:n], xt[:, :, 0:n], AF.Square, [xt], [sq])
            ps = P.psb()
            for k in range(KC):
                P.mm(ps[:, 0:n], C("ones", bf=True), sq[:, k, 0:n], k == 0, k == KC - 1, [cstb, sq], [ps])
            P.act(rs[:, 0:n], ps[:, 0:n], AF.Sqrt, [ps, epsD], [rs], scale=1.0 / D, bias=epsD[:, 0:1])
            P.recip(rs[:, 0:n], rs[:, 0:n], [rs], [rs])
            P.tt(xt[:, :, 0:n], xt[:, :, 0:n], rs[:, 0:n].unsqueeze(1).to_broadcast([128, KC, n]), ALU.mult,
                 [xt, rs], [xt])
            for k in range(KC):
                P.act(hT[:, k, t0:t0 + n], xt[:, k, 0:n], AF.Identity, [xt, mA, modT], [hT_v[ti]],
                      scale=mA[:, i, k, g:g + 1], bias=modT[:, sh0 + k, g:g + 1])

    def proj(l, c0, m, evac, wsrc=None, tiles=None):
        wb = load_w(w_in, w_in[l][:, c0:c0 + m].rearrange("(k p) n -> p k n", p=128), m=m)
        for ti, (t0, n) in enumerate(TILES):
            ps = P.psb()
            for k in range(KC):
                P.mm(ps[0:m, 0:n], wb[:, k, 0:m], hT[:, k, t0:t0 + n], k == 0, k == KC - 1, [wb, hT_v[ti]], [ps])
            evac(ps, ti, t0, n)

    def proj_full(l, c0, m, dst, eng="act"):
        def ev(ps, ti, t0, n):
            P.cp(dst[0:m, t0:t0 + n], ps[0:m, 0:n], [ps], [dst], eng=eng)
        proj(l, c0, m, ev)

    def rms_over_partitions(src_ap, n, rows, ones_ap, inv_count, eps_col, sqb, rsb, R):
        P.act(sqb[0:rows, 0:n], src_ap, AF.Square, R, [sqb])
        ps = P.psb()
        P.mm(ps[0:rows, 0:n], ones_ap, sqb[0:rows, 0:n], True, True, [cstb, sqb], [ps])
        P.act(rsb[0:rows, 0:n], ps[0:rows, 0:n], AF.Sqrt, [ps, epsD], [rsb], scale=inv_count,
              bias=epsD[0:rows, eps_col:eps_col + 1])
        P.recip(rsb[0:rows, 0:n], rsb[0:rows, 0:n], [rsb], [rsb])

    def stage_out(l):
        P.barrier()
        AF_.reset()
        yt = AB_.alloc("ytall", [KC, T])
        yt_v = [P.view(yt, "ytall_%d" % ti) for ti in range(5)]
        xps = Pool([AF_.alloc("xo%d" % b, [512]) for b in range(4)])
        for ti, (t0, n) in enumerate(TILES):
            P.dma("sp", yt[:, :, t0:t0 + n], yT[:, t0:t0 + n].rearrange("(k p) t -> p k t", p=128),
                  [yT_v[m][ti] for m in range(KC)], [yt_v[ti]])
        for m in range(KC):
            wb = load_w(w_out, w_out[l][:, m * 128:(m + 1) * 128].rearrange("(k p) n -> p k n", p=128))
            for ti, (t0, n) in enumerate(TILES):
                g = 1 if ti == 0 else 0
                xv, xsrc = x_src(l, m, ti)
                xo = xps.next()
                P.dma("sp", xo[:, 0:n], xsrc[m * 128:(m + 1) * 128, t0:t0 + n], [xv], [xo])
                ps = P.psb()
                for k in range(KC):
                    P.mm(ps[:, 0:n], wb[:, k, :], yt[:, k, t0:t0 + n], k == 0, k == KC - 1, [wb, yt_v[ti]], [ps])
                P.stt(xo[:, 0:n], ps[:, 0:n], modT[:, 32 + m, g:g + 1], xo[:, 0:n], ALU.mult, ALU.add,
                      [ps, modT, xo], [xo])
                P.dma("sp", xs[m * 128:(m + 1) * 128, t0:t0 + n], xo[:, 0:n], [xo], [xs_v[m][ti]])

    def stage_ffn(l, last):
        P.barrier()
        AF_.reset()
        AB_.reset()
        ups = [AF_.alloc("up%d" % b, [T]) for b in range(2)]
        cvs = [AF_.alloc("cv%d" % b, [T]) for b in range(2)]
        zb = Pool([AB_.alloc("zb%d" % b, [T]) for b in range(2)])
        dwo, _ = V128["dw"]
        for j in range(44):
            for half in range(2):
                cidx = j + 44 * half
                up, cv = ups[half], cvs[half]
                wb = load_w(ffn_up, ffn_up[l][:, cidx * 128:(cidx + 1) * 128].rearrange("(k p) n -> p k n", p=128))
                for ti, (t0, n) in enumerate(TILES):
                    ps = P.psb()
                    for k in range(KC):
                        P.mm(ps[:, 0:n], wb[:, k, :], hT[:, k, t0:t0 + n], k == 0, k == KC - 1, [wb, hT_v[ti]], [ps])
                    P.cp(up[:, t0:t0 + n], ps[:, 0:n], [ps], [up], eng="act" if ti % 2 else "dve")
                P.act(cv[:, :], up[:, :], AF.Identity, [up, vec], [cv],
                      scale=vec[:, dwo + 88 + cidx:dwo + 88 + cidx + 1], bias=V("db", c0=cidx, c1=cidx + 1))
                for (s0, s1) in SEGS:
                    P.stt(cv[:, s0 + 1:s1], up[:, s0:s1 - 1], vec[:, dwo + cidx:dwo + cidx + 1], cv[:, s0 + 1:s1],
                          ALU.mult, ALU.add, [up, vec, cv], [cv])
                    P.stt(cv[:, s0:s1 - 1], up[:, s0 + 1:s1], vec[:, dwo + 176 + cidx:dwo + 176 + cidx + 1],
                          cv[:, s0:s1 - 1], ALU.mult, ALU.add, [up, vec, cv], [cv])
            z = zb.next()
            P.act(cvs[0][:, :], cvs[0][:, :], AF.Silu, [cvs[0]], [cvs[0]])
            P.tt(z[:, :], cvs[0][:, :], cvs[1][:, :], ALU.mult, [cvs[0], cvs[1]], [z])
            P.dma("sp", zT[j * 128:(j + 1) * 128, :], z[:, :], [z], [zT_v[j]])
        P.barrier()
        AF_.reset()
        AB_.reset()
        zt = AB_.alloc("zt", [44, 1024])
        zt_q = [P.view(zt, "zt_q%d" % i_) for i_ in range(4)]
        wdp = Pool([AB_.alloc("wd%d" % b, [512]) for b in range(3)])
        wdf = Pool([AF_.alloc("wdf%d" % b, [512]) for b in range(4)])
        hTf = hT.h[:].rearrange("p k t -> p (k t)").bitcast(F32)
        xpre = [Buf("xoh%d" % b, hTf[:, b * 512:(b + 1) * 512]) for b in range(8)]
        tidx = {t0_: i_ for i_, (t0_, _n) in enumerate(TILES)}
        for (t0, n) in [(0, 256), (256, 1024), (1280, 1024)]:
            if last and t0 == 0:
                continue
            g = 1 if t0 == 0 else 0
            nh = (n + 511) // 512
            for i_ in range(4):
                P.dma("sp", zt[:, 11 * i_:11 * (i_ + 1), 0:n],
                      zT[11 * i_ * 128:11 * (i_ + 1) * 128, t0:t0 + n].rearrange("(k p) t -> p k t", p=128),
                      zT_v[11 * i_:11 * (i_ + 1)], [zt_q[i_]])
            for q in range(4):
                pss = [[P.psb() for _h in range(nh)] for _ in range(4)]
                for mm_ in range(4):
                    m = q * 4 + mm_
                    for hf in range(nh):
                        nn = min(512, n - hf * 512)
                        th = t0 + hf * 512
                        xo = xpre[mm_ * 2 + hf]
                        P.dma("pool", xo[:, 0:nn], xs[m * 128:(m + 1) * 128, th:th + nn], [xs_v[m][tidx[th]]], [xo])
                for k in range(44):
                    wd, wf = wdp.next(), wdf.next()
                    P.dma("sp", wf[:, :], ffn_down[l][k * 128:(k + 1) * 128, q * 512:(q + 1) * 512],
                          [ffn_down], [wf])
                    P.cp(wd[:, :], wf[:, :], [wf], [wd], eng="act" if k % 2 else "dve")
                    for mm_ in range(4):
                        for hf in range(nh):
                            nn = min(512, n - hf * 512)
                            P.mm(pss[mm_][hf][:, 0:nn], wd[:, mm_ * 128:(mm_ + 1) * 128],
                                 zt[:, k, hf * 512:hf * 512 + nn], k == 0, k == 43, [wd, zt_q[k // 11]], [pss[mm_][hf]])
                for mm_ in range(4):
                    m = q * 4 + mm_
                    for hf in range(nh):
                        nn = min(512, n - hf * 512)
                        th = t0 + hf * 512
                        ti = tidx[th]
                        xo = xpre[mm_ * 2 + hf]
                        P.stt(xo[:, 0:nn], pss[mm_][hf][:, 0:nn], modT[:, 80 + m, g:g + 1], xo[:, 0:nn], ALU.mult,
                              ALU.add, [pss[mm_][hf], modT, xo], [xo])
                        if last:
                            P.dma("pool", out[m * 128:(m + 1) * 128, th - NCTX:th - NCTX + nn], xo[:, 0:nn], [xo],
                                  [out_v[m][ti]])
                        else:
                            P.dma("pool", xs[m * 128:(m + 1) * 128, th:th + nn], xo[:, 0:nn], [xo], [xs_v[m][ti]])

    def token_shift(pf, sf, m, mu_ap):
        for (s0, s1) in SEGS:
            P.tt(sf[0:m, s0 + 1:s1 - 1], pf[0:m, s0:s1 - 2], pf[0:m, s0 + 2:s1], ALU.add, [pf], [sf])
            P.cp(sf[0:m, s0:s0 + 1], pf[0:m, s0 + 1:s0 + 2], [pf], [sf])
            P.cp(sf[0:m, s1 - 1:s1], pf[0:m, s1 - 2:s1 - 1], [pf], [sf])
        P.stt(sf[0:m, :], sf[0:m, :], 0.5, pf[0:m, :], ALU.mult, ALU.subtract, [sf, pf], [sf])
        P.stt(sf[0:m, :], sf[0:m, :], mu_ap, pf[0:m, :], ALU.mult, ALU.add, [sf, pf, vec], [sf])

    def proj_to(l, c0, m, buf, dst_fn, eng="act"):
        def ev(ps, ti, t0, n):
            P.cp(dst_fn(t0, n), ps[0:m, 0:n], [ps], [buf], eng=eng)
        proj(l, c0, m, ev)

    MLA_SCALE = 192.0 ** -0.5

    def mixer_mla(l):
        P.barrier()
        AF_.reset()
        cq = AB_.alloc("cq", [4, T])
        ckv = AB_.alloc("ckv", [2, T])
        krT = AB_.alloc("krT", [T])
        mark = _mem.off
        scr = AF_.alloc("scr", [4, T])
        sq4 = AB_.alloc("sq4", [4, 512])
        rs1 = AF_.alloc("rs1", [512])
        tA = AF_.alloc("tA", [512])
        tB = AF_.alloc("tB", [512])
        csT = AF_.alloc("csT", [2, 512])
        qb = AB_.alloc("qb", [512])
        sq1p = AB_.alloc("sq1p", [512])

        def rope(dst_bf, dst_buf, src32, src_buf, t0, n):
            P.dma("sp", csT[0:64, :, 0:n], cs_in[:, :, t0 - NCTX:t0 - NCTX + n].rearrange("c p t -> p c t"),
                  [cs_in], [csT])
            P.cp(qb[0:64, 0:n], src32, [src_buf], [qb], eng="act")
            ps = P.psb()
            P.mm(ps[0:64, 0:n], C("rotT", rows=64, bf=True), qb[0:64, 0:n], True, True, [cstb, qb], [ps])
            P.tt(tB[0:64, 0:n], ps[0:64, 0:n], csT[0:64, 1, 0:n], ALU.mult, [ps, csT], [tB])
            P.tt(src32, src32, csT[0:64, 0, 0:n], ALU.mult, [src_buf, csT], [src_buf])
            P.tt(dst_bf, src32, tB[0:64, 0:n], ALU.add, [src_buf, tB], [dst_buf])

        def norm_chunks(src3, nk, dstb, gname, cnt):
            for ti, (t0, n) in enumerate(TILES):
                P.act(sq4[:, 0:nk, 0:n], src3[:, 0:nk, t0:t0 + n], AF.Square, [scr], [sq4])
                ps = P.psb()
                for k in range(nk):
                    P.mm(ps[:, 0:n], C("ones", bf=True), sq4[:, k, 0:n], k == 0, k == nk - 1, [cstb, sq4], [ps])
                P.act(rs1[:, 0:n], ps[:, 0:n], AF.Sqrt, [ps, epsD], [rs1], scale=1.0 / cnt, bias=epsD[:, 0:1])
                P.recip(rs1[:, 0:n], rs1[:, 0:n], [rs1], [rs1])
                for k in range(nk):
                    P.stt(dstb[:, k, t0:t0 + n], src3[:, k, t0:t0 + n], V(gname, c0=k, c1=k + 1), rs1[:, 0:n],
                          ALU.mult, ALU.mult, [scr, vec, rs1], [dstb])

        for k in range(4):
            proj_to(l, ML0 + k * 128, 128, scr, lambda t0, n, k=k: scr[:, k, t0:t0 + n])
        norm_chunks(scr, 4, cq, "qnorm", 512)
        for k in range(2):
            proj_to(l, ML0 + 512 + k * 128, 128, scr, lambda t0, n, k=k: scr[:, k, t0:t0 + n])
        proj_to(l, ML0 + 768, 64, scr, lambda t0, n: scr[0:64, 2, t0:t0 + n])
        norm_chunks(scr, 2, ckv, "kvnorm", 256)
        for ti, (t0, n) in enumerate(TILES):
            rms_over_partitions(scr[0:64, 2, t0:t0 + n], n, 64, C("ones", rows=64, bf=True, c1=64), 1.0 / 64, 0,
                                sq1p, rs1, [scr])
            P.stt(tA[0:64, 0:n], scr[0:64, 2, t0:t0 + n], V("kr_g", rows=64), rs1[0:64, 0:n], ALU.mult, ALU.mult,
                  [scr, vec, rs1], [tA])
            if ti == 0:
                P.cp(krT[0:64, t0:t0 + n], tA[0:64, 0:n], [tA], [krT])
            else:
                rope(krT[0:64, t0:t0 + n], krT, tA[0:64, 0:n], tA, t0, n)
        P.barrier()
        _mem.off = mark
        P.memset(_mem.buf[:, mark:_mem.n], 0.0, [_mem.buf])
        P.barrier()
        qnT = AB_.alloc("qnT", [T])
        qrT = AB_.alloc("qrT", [T])
        knT = AB_.alloc("knT", [T])
        Vt = AB_.alloc("Vt", [18, 128])
        wq = AB_.alloc("wq", [4, 192])
        wkv = AB_.alloc("wkv", [2, 256])
        sq1 = AB_.alloc("sq1", [512])
        rs1 = AF_.alloc("rs1b", [512])
        tA = AF_.alloc("tAb", [512])
        tB = AF_.alloc("tBb", [512])
        csT = AF_.alloc("csTb", [2, 512])
        qb = AB_.alloc("qbb", [512])
        ptp = Pool([AB_.alloc("pt%d" % b, [512]) for b in range(4)])
        yb = Pool([AB_.alloc("yb%d" % b, [512]) for b in range(2)])
        for h in range(8):
            P.dma("pool", wq[:, :, :], w_uq[l][:, h * 192:(h + 1) * 192].rearrange("(k p) n -> p k n", p=128),
                  [w_uq], [wq])
            P.dma("pool", wkv[:, :, :], w_ukv[l][:, h * 256:(h + 1) * 256].rearrange("(k p) n -> p k n", p=128),
                  [w_ukv], [wkv])
            for ti, (t0, n) in enumerate(TILES):
                ps = P.psb()
                for k in range(4):
                    P.mm(ps[:, 0:n], wq[:, k, 0:128], cq[:, k, t0:t0 + n], k == 0, k == 3, [wq, cq], [ps])
                rms_over_partitions(ps[:, 0:n], n, 128, C("ones", bf=True), 1.0 / 128, 0, sq1, rs1, [ps])
                P.stt(qnT[:, t0:t0 + n], ps[:, 0:n], V("qn_g"), rs1[:, 0:n], ALU.mult, ALU.mult, [ps, vec, rs1], [qnT])
                ps = P.psb()
                for k in range(4):
                    P.mm(ps[0:64, 0:n], wq[:, k, 128:192], cq[:, k, t0:t0 + n], k == 0, k == 3, [wq, cq], [ps])
                rms_over_partitions(ps[0:64, 0:n], n, 64, C("ones", rows=64, bf=True, c1=64), 1.0 / 64, 0, sq1, rs1, [ps])
                P.stt(tA[0:64, 0:n], ps[0:64, 0:n], V("qr_g", rows=64), rs1[0:64, 0:n], ALU.mult, ALU.mult,
                      [ps, vec, rs1], [tA])
                if ti == 0:
                    P.cp(qrT[0:64, t0:t0 + n], tA[0:64, 0:n], [tA], [qrT])
                else:
                    rope(qrT[0:64, t0:t0 + n], qrT, tA[0:64, 0:n], tA, t0, n)
                ps = P.psb()
                for k in range(2):
                    P.mm(ps[:, 0:n], wkv[:, k, 0:128], ckv[:, k, t0:t0 + n], k == 0, k == 1, [wkv, ckv], [ps])
                rms_over_partitions(ps[:, 0:n], n, 128, C("ones", bf=True), 1.0 / 128, 0, sq1, rs1, [ps])
                P.stt(knT[:, t0:t0 + n], ps[:, 0:n], V("kn_g"), rs1[:, 0:n], ALU.mult, ALU.mult, [ps, vec, rs1], [knT])
            for b0 in range(0, 18, 4):
                nb = min(4, 18 - b0)
                ps = P.psb()
                for j in range(nb):
                    b = b0 + j
                    for k in range(2):
                        P.mm(ps[:, j * 128:(j + 1) * 128], ckv[:, k, b * 128:(b + 1) * 128], wkv[:, k, 128:256],
                             k == 0, k == 1, [ckv, wkv], [ps])
                P.cp(Vt[:, b0:b0 + nb, :], ps[:, 0:nb * 128].rearrange("p (a b) -> p a b", b=128), [ps], [Vt], eng="act")
            for ti, (t0, n) in enumerate(TILES):
                nkb = 2 if ti == 0 else 18
                pso = P.hold()
                psd = P.hold()
                pend = None
                for kb in range(nkb + 1):
                    cur = None
                    if kb < nkb:
                        pss = P.psb()
                        ks = slice(kb * 128, (kb + 1) * 128)
                        P.mm(pss[:, 0:n], knT[:, ks], qnT[:, t0:t0 + n], True, False, [knT, qnT], [pss])
                        P.mm(pss[:, 0:n], krT[0:64, ks], qrT[0:64, t0:t0 + n], False, True, [krT, qrT], [pss])
                        pt = ptp.next()
                        P.act(pt[:, 0:n], pss[:, 0:n], AF.Exp, [pss], [pt], scale=MLA_SCALE)
                        cur = (pt, kb)
                    if pend is not None:
                        ppt, pkb = pend
                        P.mm(pso[:, 0:n], Vt[:, pkb, :], ppt[:, 0:n], pkb == 0, pkb == nkb - 1, [Vt, ppt], [pso])
                        P.mm(psd[:, 0:n], C("ones", bf=True), ppt[:, 0:n], pkb == 0, pkb == nkb - 1, [cstb, ppt], [psd])
                    pend = cur
                P.recip(tA[:, 0:n], psd[:, 0:n], [psd], [tA])
                y = yb.next()
                P.tt(y[:, 0:n], pso[:, 0:n], tA[:, 0:n], ALU.mult, [pso, tA], [y])
                P.release(pso)
                P.release(psd)
                P.dma("sp", yT[(8 + h) * 128:(9 + h) * 128, t0:t0 + n], y[:, 0:n], [y], [yT_v[8 + h][ti]])
    def mixer_hgrn2(l):
        for h in range(4):
            P.barrier()
            AF_.reset()
            qf = AF_.alloc("qf", [T])
            vf = AF_.alloc("vf", [T])
            zfs = [AF_.alloc("zf%d" % d_, [T]) for d_ in range(2)]
            oTs = [AF_.alloc("oT%d" % d_, [T]) for d_ in range(2)]
            sq1 = AB_.alloc("sq1", [512])
            yb = Pool([AB_.alloc("yb%d" % b, [512]) for b in range(2)])
            proj_to(l, h * 128, 128, qf, lambda t0, n: qf[:, t0:t0 + n])
            proj_to(l, 1536 + h * 128, 128, vf, lambda t0, n: vf[:, t0:t0 + n])
            identb = C("ident", bf=True)

            def dir_gen(d):
                zf, oT = zfs[d], oTs[d]
                t1 = AF_.alloc("t1_%d" % d, [512])
                t2 = AF_.alloc("t2_%d" % d, [512])
                t3 = AF_.alloc("t3_%d" % d, [512])
                Fc = AF_.alloc("Fc%d" % d, [32])
                qt = AB_.alloc("qt%d" % d, [512])
                kt = AB_.alloc("kt%d" % d, [512])
                kb = AB_.alloc("kb%d" % d, [512])
                vb = AB_.alloc("vb%d" % d, [512])
                kvp = Pool([AB_.alloc("kvT%d_%d" % (d, b), [8, 128]) for b in range(3)])
                atp = Pool([AB_.alloc("aT%d_%d" % (d, b), [4, 16]) for b in range(3)])
                S32 = [AF_.alloc("S32_%d_%d" % (d, b), [128]) for b in range(2)]
                Sbf = Pool([AB_.alloc("Sbf%d_%d" % (d, b), [128]) for b in range(3)])
                proj_to(l, 512 * (1 + d) + h * 128, 128, zf, lambda t0, n: zf[:, t0:t0 + n])
                yield
                state = None
                cidx = 0
                for (t0, n, rev) in scan_tiles(d):
                    nch = n // 16
                    P.act(t1[:, 0:n], tsl(zf, t0, n, rev), AF.Sigmoid, [zf], [t1])
                    P.ts(t1[:, 0:n], t1[:, 0:n], omlb[:, h, d, l:l + 1], lbt[:, h, d, l:l + 1], ALU.mult, ALU.add,
                         [t1, omlb, lbt], [t1])
                    P.act(t2[:, 0:n], t1[:, 0:n], AF.Ln, [t1], [t2])
                    P.ts(t1[:, 0:n], t1[:, 0:n], -1.0, 1.0, ALU.mult, ALU.add, [t1], [t1])
                    P.scan(t3[:, 0:n], C("rs16", c1=n), t2[:, 0:n], [cst, t2], [t3])
                    P.act(t2[:, 0:n], t3[:, 0:n], AF.Exp, [t3], [t2])
                    P.tt(qt[:, 0:n], tsl(qf, t0, n, rev), t2[:, 0:n], ALU.mult, [qf, t2], [qt])
                    P.act(t2[:, 0:n], t3[:, 0:n], AF.Exp, [t3], [t2], scale=-1.0)
                    P.tt(kt[:, 0:n], t1[:, 0:n], t2[:, 0:n], ALU.mult, [t1, t2], [kt])
                    G3 = t3[:, 0:n].rearrange("p (c i) -> p c i", i=16)
                    P.tt(t2[:, 0:n].rearrange("p (c i) -> p c i", i=16), G3[:, :, 15:16].to_broadcast([128, nch, 16]),
                         G3, ALU.subtract, [t3], [t2])
                    P.act(t2[:, 0:n], t2[:, 0:n], AF.Exp, [t2], [t2])
                    P.tt(kb[:, 0:n], t1[:, 0:n], t2[:, 0:n], ALU.mult, [t1, t2], [kb])
                    P.act(Fc[:, 0:nch], G3[:, :, 15], AF.Exp, [t3], [Fc])
                    P.cp(vb[:, 0:n], tsl(vf, t0, n, rev), [vf], [vb])
                    yield
                    pso = P.hold()
                    for g0 in range(0, nch, 4):
                        psT = P.psb()
                        psTb = psT[:].bitcast(BF16)
                        for j in range(4):
                            cs = slice((g0 + j) * 16, (g0 + j + 1) * 16)
                            P.tr(psTb[0:16, j * 128:(j + 1) * 128], kb[:, cs], identb, [kb, cstb], [psT])
                            P.tr(psTb[0:16, (4 + j) * 128:(5 + j) * 128], vb[:, cs], identb, [vb, cstb], [psT])
                        kvT = kvp.next()
                        P.cp(kvT[0:16, :, :].rearrange("p a b -> p (a b)"), psTb[0:16, :], [psT], [kvT], eng="act")
                        psA = P.psb()
                        for j in range(4):
                            cs = slice((g0 + j) * 16, (g0 + j + 1) * 16)
                            P.mm(psA[0:16, j * 16:(j + 1) * 16], kt[:, cs], qt[:, cs], True, True, [kt, qt], [psA])
                        aT = atp.next()
                        P.tt(aT[0:16, :, :], psA[0:16, 0:64].rearrange("p (a b) -> p a b", b=16),
                             C("m16", rows=16).unsqueeze(1).to_broadcast([16, 4, 16]), ALU.mult, [psA, cst], [aT])
                        yield
                        psG = P.psb()
                        for j in range(4):
                            P.mm(psG[:, j * 128:(j + 1) * 128], kvT[0:16, j, :], kvT[0:16, 4 + j, :], True, True,
                                 [kvT], [psG])
                        for j in range(4):
                            c = g0 + j
                            cs = slice(c * 16, (c + 1) * 16)
                            P.mm(pso[:, cs], kvT[0:16, 4 + j, :], aT[0:16, j, :], True, state is None, [kvT, aT], [pso])
                            if state is not None:
                                P.mm(pso[:, cs], state[1][:, :], qt[:, cs], False, True, [state[1], qt], [pso])
                            new32 = S32[cidx % 2]
                            nbf = Sbf.next()
                            if state is None:
                                P.cp(nbf[:, :], psG[:, j * 128:(j + 1) * 128], [psG], [nbf])
                                P.cp(new32[:, :], psG[:, j * 128:(j + 1) * 128], [psG], [new32])
                            else:
                                P.stt(nbf[:, :], state[0][:, :], Fc[:, c:c + 1], psG[:, j * 128:(j + 1) * 128],
                                      ALU.mult, ALU.add, [state[0], Fc, psG], [nbf])
                                P.stt(new32[:, :], state[0][:, :], Fc[:, c:c + 1], psG[:, j * 128:(j + 1) * 128],
                                      ALU.mult, ALU.add, [state[0], Fc, psG], [new32])
                            state = (new32, nbf)
                            cidx += 1
                            yield
                    P.cp(tsl(oT, t0, n, rev), pso[:, 0:n], [pso], [oT])
                    P.release(pso)

            live = [dir_gen(0), dir_gen(1)]
            while live:
                for g_ in list(live):
                    try:
                        next(g_)
                    except StopIteration:
                        live.remove(g_)
            zf = zfs[0]
            t1 = AF_.alloc("t1o", [512])
            t2 = AF_.alloc("t2o", [512])
            t3 = AF_.alloc("t3o", [512])
            proj_to(l, 2048 + h * 128, 128, zf, lambda t0, n: zf[:, t0:t0 + n])
            for ti, (t0, n) in enumerate(TILES):
                P.tt(t3[:, 0:n], oTs[0][:, t0:t0 + n], oTs[1][:, t0:t0 + n], ALU.add, [oTs[0], oTs[1]], [t3])
                rms_over_partitions(t3[:, 0:n], n, 128, C("ones", bf=True), 1.0 / 128, 0, sq1, t1, [t3])
                P.stt(t2[:, 0:n], t3[:, 0:n], V("hgn"), t1[:, 0:n], ALU.mult, ALU.mult, [t3, vec, t1], [t2])
                P.act(t3[:, 0:n], zf[:, t0:t0 + n], AF.Silu, [zf], [t3])
                y = yb.next()
                P.tt(y[:, 0:n], t2[:, 0:n], t3[:, 0:n], ALU.mult, [t2, t3], [y])
                P.dma("sp", yT[h * 128:(h + 1) * 128, t0:t0 + n], y[:, 0:n], [y], [yT_v[h][ti]])

    def rwkv_lora(l):
        P.barrier()
        AF_.reset()
        pf = AF_.alloc("pf", [T])
        sf = AF_.alloc("sf", [T])
        xb = AB_.alloc("xb", [T])
        w2b = AB_.alloc("w2b", [512])
        otp = Pool([AF_.alloc("ot%d" % b, [512]) for b in range(3)])
        groups = [(1536, 96, "w", 0, 0), (1632, 96, "w", 1, 1), (1728, 96, "a", 0, 2), (1824, 96, "a", 1, 3),
                  (1920, 64, "g", 0, 4)]
        for gi, (c0, m, kind, d, ai) in enumerate(groups):
            proj_to(l, RW0 + c0, m, pf, lambda t0, n, m=m: pf[0:m, t0:t0 + n])
            mu_ap = V("mul", rows=96, c0=gi, c1=gi + 1) if kind != "g" else V("mug", rows=64)
            token_shift(pf, sf, m, mu_ap)
            func = AF.Tanh if kind == "w" else (AF.Identity if kind == "a" else AF.Sigmoid)
            P.act(xb[0:m, :], sf[0:m, :], func, [sf], [xb])
            if kind == "w":
                wsrc, wb_ = rw_w2[l][d], rw_w2
            elif kind == "a":
                wsrc, wb_ = rw_a2[l][d], rw_a2
            else:
                wsrc, wb_ = rw_g2[l], rw_g2
            P.dma("pool", w2b[0:m, :], wsrc, [wb_], [w2b])
            for mc in range(4):
                for ti, (t0, n) in enumerate(TILES):
                    ps = P.psb()
                    P.mm(ps[:, 0:n], w2b[0:m, mc * 128:(mc + 1) * 128], xb[0:m, t0:t0 + n], True, True, [w2b, xb], [ps])
                    o = otp.next()
                    if kind == "w":
                        P.act(o[:, 0:n], ps[:, 0:n], AF.Sigmoid, [ps, vec], [o], bias=V("w0", c0=d * 4 + mc, c1=d * 4 + mc + 1))
                        P.ts(o[:, 0:n], o[:, 0:n], -DECAY_MAX, None, ALU.mult, ALU.bypass, [o], [o])
                    elif kind == "a":
                        P.act(o[:, 0:n], ps[:, 0:n], AF.Sigmoid, [ps, vec], [o], bias=V("a0", c0=d * 4 + mc, c1=d * 4 + mc + 1))
                    else:
                        P.cp(o[:, 0:n], ps[:, 0:n], [ps], [o])
                    P.dma("sp", rwaux[ai][mc * 128:(mc + 1) * 128, t0:t0 + n], o[:, 0:n], [o], [aux_v[ai][mc]])

    def mixer_rwkv(l):
        rwkv_lora(l)
        for m in range(4):
            P.barrier()
            AF_.reset()
            rf = AF_.alloc("rf", [T])
            kf = AF_.alloc("kf", [T])
            vf = AF_.alloc("vf", [T])
            kkf = AF_.alloc("kkf", [T])
            oT = AF_.alloc("oT", [T])
            pf = oT
            tlw = AF_.alloc("tlw", [512])
            tag = AF_.alloc("tag", [512])
            tcum = AF_.alloc("tcum", [512])
            te = AF_.alloc("te", [512])
            te2 = AF_.alloc("te2", [512])
            tb_ = AF_.alloc("tb", [512])
            tkd = AF_.alloc("tkd", [512])
            WC = AF_.alloc("WC", [4])
            omka = AF_.alloc("omka", [1])
            S32 = [AF_.alloc("S32_%d" % b, [64]) for b in range(2)]
            rt = AB_.alloc("rt", [512])
            bt = AB_.alloc("bt", [512])
            kt = AB_.alloc("kt", [512])
            at = AB_.alloc("at", [512])
            bb = AB_.alloc("bb", [512])
            kb = AB_.alloc("kb", [512])
            vb = AB_.alloc("vb", [512])
            sq1 = AB_.alloc("sq1", [512])
            atm = [AB_.alloc("atm%d" % b, [512]) for b in range(2)]
            rtm = [AB_.alloc("rtm%d" % b, [512]) for b in range(2)]
            for b_ in atm + rtm:
                P.memset(b_[:, :], 0.0, [b_])
            class Slot:
                pass
            slots = []
            for si in range(2):
                S_ = Slot()
                S_.Tp = Pool([AF_.alloc("Tj%d_%d" % (si, b), [2, 128]) for b in range(2)])
                S_.Lp = Pool([AF_.alloc("Lj%d_%d" % (si, b), [2, 128]) for b in range(2)])
                S_.Zp = Pool([AF_.alloc("Z%d_%d" % (si, b), [2, 128]) for b in range(3)])
                S_.tmpP = AF_.alloc("tmpP%d" % si, [128])
                S_.Gp = AF_.alloc("Gp%d" % si, [64])
                S_.Tak = AB_.alloc("Tak%d" % si, [2, 128])
                S_.Trb = AB_.alloc("Trb%d" % si, [2, 128])
                S_.Trk = AB_.alloc("Trk%d" % si, [2, 128])
                S_.tm = AB_.alloc("tm%d" % si, [4, 128])
                S_.Z1c = AB_.alloc("Z1c%d" % si, [128])
                S_.Z2c = AB_.alloc("Z2c%d" % si, [128])
                S_.PhiT = AB_.alloc("PhiT%d" % si, [128])
                S_.QeTm = [AB_.alloc("QeTm%d_%d" % (si, b), [128]) for b in range(2)]
                for b_ in S_.QeTm:
                    P.memset(b_[:, :], 0.0, [b_])
                slots.append(S_)
            identf = C("ident")
            Sbf = Pool([AB_.alloc("Sbf%d" % b, [64]) for b in range(3)])
            yb = Pool([AB_.alloc("yb%d" % b, [512]) for b in range(2)])
            identb = C("ident", bf=True)
            for (c0, dst, mi) in [(0, rf, m), (512, kf, 4 + m), (1024, vf, 8 + m)]:
                proj_to(l, RW0 + c0 + m * 128, 128, pf, lambda t0, n: pf[:, t0:t0 + n])
                token_shift(pf, dst, 128, V("mu", c0=mi, c1=mi + 1))
            P.ts(omka[:, 0:1], V("ka", c0=m, c1=m + 1), -1.0, 1.0, ALU.mult, ALU.add, [vec], [omka])
            for ti, (t0, n) in enumerate(TILES):
                P.ts(te[:, 0:n], kf[:, t0:t0 + n], V("kk", c0=m, c1=m + 1), None, ALU.mult, ALU.bypass, [kf, vec], [te])
                P.act(sq1[:, 0:n], te[:, 0:n], AF.Square, [te], [sq1])
                ps = P.psb()
                P.mm(ps[:, 0:n], C("blk", bf=True), sq1[:, 0:n], True, True, [cstb, sq1], [ps])
                P.act(te2[:, 0:n], ps[:, 0:n], AF.Sqrt, [ps, epsD], [te2], bias=epsD[:, 2:3])
                P.recip(te2[:, 0:n], te2[:, 0:n], [te2], [te2])
                P.tt(kkf[:, t0:t0 + n], te[:, 0:n], te2[:, 0:n], ALU.mult, [te, te2], [kkf])
            for d in range(2):
                state = None
                cidx = 0
                for (t0, n, rev) in scan_tiles(d):
                    nch = n // 128

                    def S(b_):
                        a_ = b_[:, 0:n]
                        return a_[:, ::-1] if rev else a_
                    P.dma("sp", tlw[:, 0:n], rwaux[d][m * 128:(m + 1) * 128, t0:t0 + n], [aux_v[d][m]], [tlw])
                    P.dma("sp", tag[:, 0:n], rwaux[2 + d][m * 128:(m + 1) * 128, t0:t0 + n], [aux_v[2 + d][m]], [tag])
                    P.scan(tcum[:, 0:n], C("rs128", c1=n), S(tlw), [cst, tlw], [tcum])
                    P.act(te[:, 0:n], tcum[:, 0:n], AF.Exp, [tcum], [te])
                    P.tt(rt[:, 0:n], tsl(rf, t0, n, rev), te[:, 0:n], ALU.mult, [rf, te], [rt])
                    P.act(te[:, 0:n], tcum[:, 0:n], AF.Exp, [tcum], [te], scale=-1.0)
                    P.tt(tb_[:, 0:n], tsl(kkf, t0, n, rev), S(tag), ALU.mult, [kkf, tag], [tb_])
                    P.tt(bt[:, 0:n], tb_[:, 0:n], te[:, 0:n], ALU.mult, [tb_, te], [bt])
                    P.ts(tkd[:, 0:n], S(tag), V("ka", c0=m, c1=m + 1), omka[:, 0:1], ALU.mult, ALU.add, [tag, vec, omka], [tkd])
                    P.tt(tkd[:, 0:n], tkd[:, 0:n], tsl(kf, t0, n, rev), ALU.mult, [tkd, kf], [tkd])
                    P.tt(kt[:, 0:n], tkd[:, 0:n], te[:, 0:n], ALU.mult, [tkd, te], [kt])
                    P.tt(te2[:, 0:n], tcum[:, 0:n], S(tlw), ALU.subtract, [tcum, tlw], [te2])
                    P.act(te2[:, 0:n], te2[:, 0:n], AF.Exp, [te2], [te2])
                    P.stt(at[:, 0:n], tsl(kkf, t0, n, rev), -1.0, te2[:, 0:n], ALU.mult, ALU.mult, [kkf, te2], [at])
                    c3 = tcum[:, 0:n].rearrange("p (c i) -> p c i", i=128)
                    P.tt(te2[:, 0:n].rearrange("p (c i) -> p c i", i=128), c3[:, :, 127:128].to_broadcast([128, nch, 128]),
                         c3, ALU.subtract, [tcum], [te2])
                    P.act(te2[:, 0:n], te2[:, 0:n], AF.Exp, [te2], [te2])
                    P.tt(bb[:, 0:n], tb_[:, 0:n], te2[:, 0:n], ALU.mult, [tb_, te2], [bb])
                    P.tt(kb[:, 0:n], tkd[:, 0:n], te2[:, 0:n], ALU.mult, [tkd, te2], [kb])
                    P.cp(vb[:, 0:n], tsl(vf, t0, n, rev), [vf], [vb])
                    P.act(WC[:, 0:nch], c3[:, :, 127], AF.Exp, [tcum], [WC])
                    for hh in range(2):
                        pr = slice(hh * 64, hh * 64 + 64)
                        P.cp(atm[hh][pr, 0:n], at[pr, 0:n], [at], [atm[hh]])
                        P.cp(rtm[hh][pr, 0:n], rt[pr, 0:n], [rt], [rtm[hh]], eng="act")
                    psO = P.hold()

                    def v3(ap):
                        return ap.rearrange("p (a b) -> p a b", b=128)

                    def msk(nm):
                        return C(nm).unsqueeze(1).to_broadcast([128, 2, 128])

                    def chunk_pre(c, S_):
                        cs = slice(c * 128, (c + 1) * 128)
                        A_, B_, C_ = P.psb(), P.psb(), P.psb()
                        for hh in range(2):
                            o0, o1 = hh * 128, 256 + hh * 128
                            am, rm = atm[hh], rtm[hh]
                            P.mm(A_[:, o0:o0 + 128], bt[:, cs], am[:, cs], True, True, [bt, am], [A_])
                            P.mm(A_[:, o1:o1 + 128], am[:, cs], bt[:, cs], True, True, [bt, am], [A_])
                            P.mm(B_[:, o0:o0 + 128], kt[:, cs], am[:, cs], True, True, [kt, am], [B_])
                            P.mm(B_[:, o1:o1 + 128], bt[:, cs], rm[:, cs], True, True, [bt, rm], [B_])
                            P.mm(C_[:, o0:o0 + 128], kt[:, cs], rm[:, cs], True, True, [kt, rm], [C_])
                        psT = P.psb()
                        psTb = psT[:].bitcast(BF16)
                        for i_, src in enumerate([bb, kb, vb, at]):
                            P.tr(psTb[:, i_ * 128:(i_ + 1) * 128], src[:, cs], identb, [src, cstb], [psT])
                        Tj, Lj = S_.Tp.next(), S_.Lp.next()
                        P.tt(Tj[:, :, :], v3(A_[:, 0:256]), msk("tri_s"), ALU.mult, [A_, cst], [Tj])
                        P.tt(Lj[:, :, :], v3(A_[:, 256:512]), msk("tri_sT"), ALU.mult, [A_, cst], [Lj])
                        P.tt(S_.Tak[:, :, :], v3(B_[:, 0:256]), msk("tri_s"), ALU.mult, [B_, cst], [S_.Tak])
                        P.tt(S_.Trb[:, :, :], v3(B_[:, 256:512]), msk("tri_i"), ALU.mult, [B_, cst], [S_.Trb])
                        P.tt(S_.Trk[:, :, :], v3(C_[:, 0:256]), msk("tri_i"), ALU.mult, [C_, cst], [S_.Trk])
                        tm = S_.tm
                        P.cp(tm[:, :, :].rearrange("p a b -> p (a b)"), psTb[:, 0:512], [psT], [tm], eng="act")
                        yield
                        psX = P.psb()
                        for hh in range(2):
                            P.mm(psX[:, hh * 64:(hh + 1) * 64], S_.Tak[:, hh, :], tm[:, 2, hh * 64:(hh + 1) * 64], True, True,
                                 [S_.Tak, tm], [psX])
                        Z = S_.Zp.next()
                        P.cp(Z[:, :, 0:64], tm[:, 3, :].rearrange("p (a b) -> p a b", b=64), [tm], [Z], eng="act")
                        P.cp(Z[:, :, 64:128], psX[:, 0:128].rearrange("p (a b) -> p a b", b=64), [psX], [Z])
                        yield
                        for j in range(7):
                            psZ = P.psb()
                            for hh in range(2):
                                o0 = hh * 128
                                P.mm(psZ[:, o0:o0 + 128], Tj[:, hh, :], Z[:, hh, :], True, True, [Tj, Z], [psZ])
                            if j < 6:
                                psS = P.psb()
                                for hh in range(2):
                                    o0, o1 = hh * 128, 256 + hh * 128
                                    P.mm(psS[:, o0:o0 + 128], Lj[:, hh, :], Tj[:, hh, :], True, True, [Lj, Tj], [psS])
                                    P.mm(psS[:, o1:o1 + 128], Tj[:, hh, :], Lj[:, hh, :], True, True, [Lj, Tj], [psS])
                                Zn = S_.Zp.next()
                                P.tt(Zn[:, :, :], v3(psZ[:, 0:256]), Z[:, :, :], ALU.add, [psZ, Z], [Zn])
                                Tn, Ln = S_.Tp.next(), S_.Lp.next()
                                P.cp(Tn[:, :, :], v3(psS[:, 0:256]), [psS], [Tn], eng="act")
                                P.cp(Ln[:, :, :], v3(psS[:, 256:512]), [psS], [Ln], eng="act")
                                Z, Tj, Lj = Zn, Tn, Ln
                            else:
                                z3 = v3(psZ[:, 0:256])
                                P.tt(S_.Z1c[:, :].rearrange("p (a b) -> p a b", b=64), z3[:, :, 0:64], Z[:, :, 0:64],
                                     ALU.add, [psZ, Z], [S_.Z1c])
                                P.tt(S_.Z2c[:, :].rearrange("p (a b) -> p a b", b=64), z3[:, :, 64:128], Z[:, :, 64:128],
                                     ALU.add, [psZ, Z], [S_.Z2c])
                            yield
                        psP = P.psb()
                        P.mm(psP[:, 0:128], S_.Z1c[:, :], tm[:, 0, :], True, True, [S_.Z1c, tm], [psP])
                        P.mm(psP[:, 128:256], tm[:, 0, :], S_.Z2c[:, :], True, False, [S_.Z2c, tm], [psP])
                        P.mm(psP[:, 128:256], tm[:, 1, :], tm[:, 2, :], False, True, [tm], [psP])
                        psQ = P.psb()
                        for hh in range(2):
                            pr = slice(hh * 64, hh * 64 + 64)
                            P.mm(psQ[pr, 0:128], S_.Z1c[:, pr], S_.Trb[:, hh, :], True, True, [S_.Z1c, S_.Trb], [psQ])
                        P.tt(S_.tmpP[:, :], psP[:, 0:128], C("blk"), ALU.mult, [psP, cst], [S_.tmpP])
                        P.stt(S_.PhiT[:, :], C("ident"), WC[:, c:c + 1], S_.tmpP[:, :], ALU.mult, ALU.add,
                              [cst, WC, S_.tmpP], [S_.PhiT])
                        P.cp(S_.Gp[0:64, :], psP[0:64, 128:192], [psP], [S_.Gp], eng="act")
                        P.cp(S_.Gp[64:128, :], psP[64:128, 192:256], [psP], [S_.Gp], eng="act")
                        for hh in range(2):
                            pr = slice(hh * 64, hh * 64 + 64)
                            P.tt(S_.QeTm[hh][pr, :], psQ[pr, 0:128], rt[pr, cs], ALU.add, [psQ, rt], [S_.QeTm[hh]])
                        yield

                    def chunk_post(c, S_, state, cidx):
                        cs = slice(c * 128, (c + 1) * 128)
                        tm = S_.tm
                        for hh in range(2):
                            pr = slice(hh * 64, hh * 64 + 64)
                            P.mm(psO[pr, cs], S_.Z2c[:, pr], S_.Trb[:, hh, :], True, False, [S_.Z2c, S_.Trb], [psO])
                            P.mm(psO[pr, cs], tm[:, 2, pr], S_.Trk[:, hh, :], False, state is None, [tm, S_.Trk], [psO])
                            if state is not None:
                                P.mm(psO[pr, cs], state[1][:, :], S_.QeTm[hh][:, :], False, True,
                                     [state[1], S_.QeTm[hh]], [psO])
                        new32 = S32[cidx % 2]
                        nbf = Sbf.next()
                        if state is None:
                            P.cp(nbf[:, :], S_.Gp[:, :], [S_.Gp], [nbf])
                            P.cp(new32[:, :], S_.Gp[:, :], [S_.Gp], [new32], eng="act")
                        else:
                            psS2 = P.psb()
                            P.mm(psS2[:, 0:64], S_.PhiT[:, :], state[1][:, :], True, True, [S_.PhiT, state[1]], [psS2])
                            P.tt(nbf[:, :], psS2[:, 0:64], S_.Gp[:, :], ALU.add, [psS2, S_.Gp], [nbf])
                            P.tt(new32[:, :], psS2[:, 0:64], S_.Gp[:, :], ALU.add, [psS2, S_.Gp], [new32])
                        return (new32, nbf)

                    for c0_ in range(0, nch, 2):
                        cl = list(range(c0_, min(c0_ + 2, nch)))
                        gens = [chunk_pre(c, slots[c % 2]) for c in cl]
                        live = list(gens)
                        while live:
                            for g_ in list(live):
                                try:
                                    next(g_)
                                except StopIteration:
                                    live.remove(g_)
                        for c in cl:
                            state = chunk_post(c, slots[c % 2], state, cidx)
                            cidx += 1
                    if d == 0:
                        P.cp(oT[:, t0:t0 + n], psO[:, 0:n], [psO], [oT])
                    else:
                        P.tt(tsl(oT, t0, n, True), tsl(oT, t0, n, True), psO[:, 0:n], ALU.add, [oT, psO], [oT])
                    P.release(psO)
            for ti, (t0, n) in enumerate(TILES):
                osl = oT[:, t0:t0 + n]
                P.cp(sq1[:, 0:n], osl, [oT], [sq1], eng="act")
                ps = P.psb()
                P.mm(ps[:, 0:n], C("blk", bf=True), sq1[:, 0:n], True, True, [cstb, sq1], [ps])
                P.stt(te[:, 0:n], ps[:, 0:n], -1.0 / 64, osl, ALU.mult, ALU.add, [ps, oT], [te])
                P.act(sq1[:, 0:n], te[:, 0:n], AF.Square, [te], [sq1])
                ps = P.psb()
                P.mm(ps[:, 0:n], C("blk", bf=True), sq1[:, 0:n], True, True, [cstb, sq1], [ps])
                P.act(te2[:, 0:n], ps[:, 0:n], AF.Sqrt, [ps, epsD], [te2], scale=1.0 / 64, bias=epsD[:, 1:2])
                P.recip(te2[:, 0:n], te2[:, 0:n], [te2], [te2])
                P.tt(te[:, 0:n], te[:, 0:n], te2[:, 0:n], ALU.mult, [te, te2], [te])
                P.ts(te[:, 0:n], te[:, 0:n], V("gnw", c0=m, c1=m + 1), V("gnb", c0=m, c1=m + 1), ALU.mult, ALU.add,
                     [te, vec], [te])
                P.dma("sp", tlw[:, 0:n], rwaux[2][m * 128:(m + 1) * 128, t0:t0 + n], [aux_v[2][m]], [tlw])
                P.dma("sp", tag[:, 0:n], rwaux[3][m * 128:(m + 1) * 128, t0:t0 + n], [aux_v[3][m]], [tag])
                P.tt(tb_[:, 0:n], tlw[:, 0:n], tag[:, 0:n], ALU.add, [tlw, tag], [tb_])
                P.ts(tb_[:, 0:n], tb_[:, 0:n], -2.0, None, ALU.add, ALU.bypass, [tb_], [tb_])
                P.ts(tb_[:, 0:n], tb_[:, 0:n], V("ka", c0=m, c1=m + 1), 2.0, ALU.mult, ALU.add, [tb_, vec], [tb_])
                P.tt(tb_[:, 0:n], tb_[:, 0:n], kf[:, t0:t0 + n], ALU.mult, [tb_, kf], [tb_])
                P.tt(tb_[:, 0:n], tb_[:, 0:n], rf[:, t0:t0 + n], ALU.mult, [tb_, rf], [tb_])
                P.ts(sq1[:, 0:n], tb_[:, 0:n], V("rk", c0=m, c1=m + 1), None, ALU.mult, ALU.bypass, [tb_, vec], [sq1])
                ps = P.psb()
                P.mm(ps[:, 0:n], C("blk", bf=True), sq1[:, 0:n], True, True, [cstb, sq1], [ps])
                P.tt(tkd[:, 0:n], ps[:, 0:n], vf[:, t0:t0 + n], ALU.mult, [ps, vf], [tkd])
                P.tt(te[:, 0:n], te[:, 0:n], tkd[:, 0:n], ALU.add, [te, tkd], [te])
                P.dma("sp", tcum[:, 0:n], rwaux[4][m * 128:(m + 1) * 128, t0:t0 + n], [aux_v[4][m]], [tcum])
                y = yb.next()
                P.tt(y[:, 0:n], te[:, 0:n], tcum[:, 0:n], ALU.mult, [te, tcum], [y])
                P.dma("sp", yT[(4 + m) * 128:(5 + m) * 128, t0:t0 + n], y[:, 0:n], [y], [yT_v[4 + m][ti]])
    for l in range(nl):
        stage_mod(l)
        stage_norm(l, 0)
        mixer_hgrn2(l)
        mixer_rwkv(l)
        mixer_mla(l)
        stage_out(l)
        stage_norm(l, 1)
        stage_ffn(l, l == nl - 1)
    return P


_BIG = ["w_mod", "w_in", "w_out", "rw_w2", "rw_a2", "rw_g2", "mla_w_uq", "mla_w_ukv", "ffn_up", "ffn_down"]


def kernel(**inputs):
    inp = {k: np.asarray(v) for k, v in inputs.items()}
    P = build_program(L)
    nc = P.build()
    cst, cs = host_consts()
    vecs, lb = host_vecs(inp)
    big = {k: np.ascontiguousarray(inp[k], dtype=np.float32) for k in _BIG}
    in_maps = []
    for b in range(8):
        xT = np.ascontiguousarray(np.concatenate([inp["ctx"][b], inp["x"][b]], 0).T.astype(np.float32))
        cT = np.ascontiguousarray(np.stack([fm(inp["c"][b], 16), fm(inp["c_ctx"], 16)], -1).reshape(128, 32))
        in_maps.append(dict(xT=xT, cT=cT, vecs=vecs, hglb=lb, cst=cst, rope=cs, **big))
    res = run_bass_kernel_spmd(nc, in_maps, core_ids=list(range(8)))
    out = np.stack([np.ascontiguousarray(np.asarray(res.results[b]["out"]).T) for b in range(8)], 0)
    return out.astype(np.float32)
```
